# Optimizing a Trainium2 kernel written in Bass

```python
import jax, jax.numpy as jnp
from jax import lax
import numpy as np

D_MODEL = 1024
BATCH = 8
SEQ = 2048
DEPTH = 1
DEC_BATCH = 8
DEC_SEQ = 16
PAST_LEN = 2048

CHUNK = 64
N_META = 16
D_CONV = D_MODEL // 2
CONV_W = 3
N_HEADS = 8
HEAD_DIM = 64
D_ATTN = N_HEADS * HEAD_DIM
D_FF = 4 * D_MODEL
Q_BLOCK = 128
EPS = 1e-6
ATTN_SCALE = HEAD_DIM ** -0.5
IN_COLS = 3 * D_CONV + 3 * D_ATTN + N_HEADS + 2 * D_MODEL

kernel_name = "hybrid_shortconv_fox_gated_stream_step"


def rms_norm(x, g):
    xf = x.astype(jnp.float32)
    n = xf * lax.rsqrt(jnp.mean(xf * xf, axis=-1, keepdims=True) + EPS)
    return (n * g.astype(jnp.float32)).astype(x.dtype)


def mixer_inputs(h, norm_g, w_in, b_f, q_g, k_g):
    xn = rms_norm(h, norm_g)
    u = xn @ w_in
    cuts = [D_CONV, 2 * D_CONV, 3 * D_CONV,
            3 * D_CONV + D_ATTN, 3 * D_CONV + 2 * D_ATTN, 3 * D_CONV + 3 * D_ATTN,
            3 * D_CONV + 3 * D_ATTN + N_HEADS]
    cb, cc, ch, q, k, v, fl, gl = jnp.split(u, cuts, axis=-1)
    bt = h.shape[:2]
    q = rms_norm(q.reshape(*bt, N_HEADS, HEAD_DIM), q_g)
    k = rms_norm(k.reshape(*bt, N_HEADS, HEAD_DIM), k_g)
    v = v.reshape(*bt, N_HEADS, HEAD_DIM)
    logf = jax.nn.log_sigmoid(fl.astype(jnp.float32) + b_f.astype(jnp.float32))
    return cb, cc, ch, q, k, v, logf, gl


def fox_attend(q, cq, qpos, k, ck, kpos, v):
    s = jnp.einsum("bqhd,bkhd->bhqk", q, k).astype(jnp.float32) * ATTN_SCALE
    s = s + jnp.swapaxes(cq, 1, 2)[..., :, None] - jnp.swapaxes(ck, 1, 2)[..., None, :]
    s = jnp.where(kpos[None, :] <= qpos[:, None], s, -jnp.inf)
    p = jax.nn.softmax(s, axis=-1).astype(v.dtype)
    return jnp.einsum("bhqk,bkhd->bqhd", p, v)


def short_conv(cb, cc, ch, left, conv_w, conv_b):
    z = cc * ch
    zp = jnp.concatenate([left.astype(z.dtype), z], axis=1)
    t = z.shape[1]
    y = sum(zp[:, i:i + t] * conv_w[i] for i in range(CONV_W)) + conv_b
    return cb * y, zp[:, -(CONV_W - 1):]


def gated_merge(conv_out, attn_out, gl, w_br_conv, w_br_attn, w_out):
    g_conv, g_attn = jnp.split(jax.nn.sigmoid(gl), 2, axis=-1)
    attn_flat = attn_out.reshape(*attn_out.shape[:2], D_ATTN)
    merged = g_conv * (conv_out @ w_br_conv) + g_attn * (attn_flat @ w_br_attn)
    return merged @ w_out


def sq_relu_mlp(h, norm_g, w_up, w_down):
    a = jax.nn.relu(rms_norm(h, norm_g) @ w_up)
    return h + (a * a) @ w_down


def setup_inputs(seed: int = 0) -> dict:
    key = jax.random.key(seed)
    ks = jax.random.split(key, 24)

    def nrm(k, shape, scale=1.0):
        return jax.random.normal(k, shape, jnp.float32) * scale

    return {
        "x_prompt": nrm(ks[0], (BATCH, SEQ, D_MODEL)),
        "x_sample": nrm(ks[1], (DEC_BATCH, DEC_SEQ, D_MODEL)),
        "cache_k": nrm(ks[2], (DEPTH, DEC_BATCH, PAST_LEN, N_HEADS, HEAD_DIM)),
        "cache_v": nrm(ks[3], (DEPTH, DEC_BATCH, PAST_LEN, N_HEADS, HEAD_DIM)),
        "cache_logf": jax.nn.log_sigmoid(nrm(ks[4], (DEPTH, DEC_BATCH, PAST_LEN, N_HEADS)) + 2.0),
        "state_conv": nrm(ks[5], (DEPTH, DEC_BATCH, CONV_W - 1, D_CONV)),
        "meta": nrm(ks[6], (N_META, D_MODEL)),
        "norm1_g": 1.0 + nrm(ks[7], (DEPTH, D_MODEL), 0.02),
        "w_in": nrm(ks[8], (DEPTH, D_MODEL, IN_COLS), D_MODEL ** -0.5),
        "b_f": 1.0 + nrm(ks[9], (DEPTH, N_HEADS), 0.1),
        "conv_w": nrm(ks[10], (DEPTH, CONV_W, D_CONV), CONV_W ** -0.5),
        "conv_b": nrm(ks[11], (DEPTH, D_CONV), 0.01),
        "q_norm_g": 1.0 + nrm(ks[12], (DEPTH, HEAD_DIM), 0.02),
        "k_norm_g": 1.0 + nrm(ks[13], (DEPTH, HEAD_DIM), 0.02),
        "w_br_conv": nrm(ks[14], (DEPTH, D_CONV, D_MODEL), D_CONV ** -0.5),
        "w_br_attn": nrm(ks[15], (DEPTH, D_ATTN, D_MODEL), D_ATTN ** -0.5),
        "w_out": nrm(ks[16], (DEPTH, D_MODEL, D_MODEL), D_MODEL ** -0.5),
        "norm2_g": 1.0 + nrm(ks[17], (DEPTH, D_MODEL), 0.02),
        "w_up": nrm(ks[18], (DEPTH, D_MODEL, D_FF), D_MODEL ** -0.5),
        "w_down": nrm(ks[19], (DEPTH, D_FF, D_MODEL), D_FF ** -0.5),
    }


def reference(x_prompt, x_sample, cache_k, cache_v, cache_logf, state_conv, meta,
              norm1_g, w_in, b_f, conv_w, conv_b, q_norm_g, k_norm_g,
              w_br_conv, w_br_attn, w_out, norm2_g, w_up, w_down):
    b, seq, _ = x_prompt.shape
    n_blk = seq // Q_BLOCK
    length = N_META + seq
    pos = jnp.arange(length)
    blk_pos = (N_META + jnp.arange(seq)).reshape(n_blk, Q_BLOCK)

    past = cache_k.shape[2]
    dec_seq = x_sample.shape[1]
    kpos_s = jnp.arange(past + dec_seq)
    qpos_s = past + jnp.arange(dec_seq)

    hp = jnp.concatenate(
        [jnp.broadcast_to(meta[None].astype(x_prompt.dtype), (b, N_META, D_MODEL)), x_prompt], axis=1)
    hs = x_sample

    pk, pv, pf, pc, sk, sv, sf, sc = [], [], [], [], [], [], [], []
    for l in range(DEPTH):
        cb, cc, ch, q, k, v, logf, gl = mixer_inputs(hp, norm1_g[l], w_in[l], b_f[l], q_norm_g[l], k_norm_g[l])
        c = jnp.cumsum(logf, axis=1)
        o_meta = fox_attend(q[:, :N_META], c[:, :N_META], pos[:N_META],
                            k[:, :N_META], c[:, :N_META], pos[:N_META], v[:, :N_META])
        qb = jnp.swapaxes(q[:, N_META:].reshape(b, n_blk, Q_BLOCK, N_HEADS, HEAD_DIM), 0, 1)
        cqb = jnp.swapaxes(c[:, N_META:].reshape(b, n_blk, Q_BLOCK, N_HEADS), 0, 1)
        o_blk = lax.map(lambda a: fox_attend(a[0], a[1], a[2], k, c, pos, v), (qb, cqb, blk_pos))
        o_real = jnp.swapaxes(o_blk, 0, 1).reshape(b, seq, N_HEADS, HEAD_DIM)
        attn_p = jnp.concatenate([o_meta, o_real], axis=1)
        conv_p, conv_rows_p = short_conv(cb, cc, ch, jnp.zeros((b, CONV_W - 1, D_CONV), ch.dtype),
                                         conv_w[l], conv_b[l])
        hp = hp + gated_merge(conv_p, attn_p, gl, w_br_conv[l], w_br_attn[l], w_out[l])
        pk.append(k); pv.append(v); pf.append(logf); pc.append(conv_rows_p)
        if l == DEPTH - 1:
            hp = hp[:, N_META:]
        hp = sq_relu_mlp(hp, norm2_g[l], w_up[l], w_down[l])

        cb, cc, ch, q, k, v, logf, gl = mixer_inputs(hs, norm1_g[l], w_in[l], b_f[l], q_norm_g[l], k_norm_g[l])
        cum = jnp.cumsum(cache_logf[l].astype(jnp.float32), axis=1)
        ck_cache = cum - cum[:, -1:]
        cq = jnp.cumsum(logf, axis=1)
        k_all = jnp.concatenate([cache_k[l].astype(k.dtype), k], axis=1)
        v_all = jnp.concatenate([cache_v[l].astype(v.dtype), v], axis=1)
        ck_all = jnp.concatenate([ck_cache, cq], axis=1)
        attn_s = fox_attend(q, cq, qpos_s, k_all, ck_all, kpos_s, v_all)
        conv_s, conv_rows_s = short_conv(cb, cc, ch, state_conv[l], conv_w[l], conv_b[l])
        hs = hs + gated_merge(conv_s, attn_s, gl, w_br_conv[l], w_br_attn[l], w_out[l])
        hs = sq_relu_mlp(hs, norm2_g[l], w_up[l], w_down[l])
        sk.append(k); sv.append(v); sf.append(logf); sc.append(conv_rows_s)

    return (hp, hs,
            jnp.stack(pk), jnp.stack(pv), jnp.stack(pf), jnp.stack(pc),
            jnp.stack(sk), jnp.stack(sv), jnp.stack(sf), jnp.stack(sc))
```

```python
import os
import numpy as np
import concourse.bass as bass
import concourse.mybir as mybir
from concourse.bass_utils import run_bass_kernel_spmd

F32 = mybir.dt.float32
BF16 = mybir.dt.bfloat16
AF = mybir.ActivationFunctionType
ALU = mybir.AluOpType
AX = mybir.AxisListType

P = 128
D = 1024
KD = 8
SEQ = 2048
NMETA = 16
L = SEQ + NMETA
NS = 16
NT = L + NS
NTILE = 17
PAST = 2048
H = 8
HD = 64
DC = 512
DA = 512
DFF = 4096
INC = 5128
EPS = 1e-6
C_B, C_C, C_H, C_Q, C_K, C_V, C_F, C_G = 0, 512, 1024, 1536, 2048, 2560, 3072, 3080
BLKS = [(0, 512), (512, 512), (1024, 512), (1536, 512), (2048, 32)]


def tile_rows(i):
    return 128 if i < 16 else 32


class Sched:
    ENG = ("pe", "act", "dve", "pool", "sp")

    def __init__(self, nc):
        self.nc = nc
        self.q = {e: [] for e in self.ENG}
        self.cnt = {}
        self.sems = {}
        self.lastw = {}
        self.lastr = {}
        self.seen = {e: {} for e in self.ENG}
        self.pending = {e: {} for e in self.ENG}
        for e in ("pe", "act", "dve", "pool"):
            self._sem(e)

    def _sem(self, name):
        if name not in self.sems:
            self.sems[name] = self.nc.alloc_semaphore("s_" + name)
            self.cnt[name] = 0
        return self.sems[name]

    def _deps(self, eng, reads, writes):
        deps = dict(self.pending[eng])
        self.pending[eng] = {}

        def merge(src, raw):
            for s, v in src.items():
                if s == eng and eng == "pe":
                    continue
                if deps.get(s, 0) < v:
                    deps[s] = v
        for k in reads:
            merge(self.lastw.get(k, {}), True)
        for k in writes:
            merge(self.lastw.get(k, {}), False)
            merge(self.lastr.get(k, {}), False)
        out = []
        seen = self.seen[eng]
        for s, v in deps.items():
            if seen.get(s, 0) < v:
                seen[s] = v
                out.append((s, v))
        return out

    def _record(self, s, v, reads, writes):
        for k in reads:
            d = self.lastr.setdefault(k, {})
            if d.get(s, 0) < v:
                d[s] = v
        for k in writes:
            d = self.lastw.setdefault(k, {})
            if d.get(s, 0) < v:
                d[s] = v

    def op(self, eng, fn, reads=(), writes=(), sig=True):
        waits = self._deps(eng, reads, writes)
        if sig:
            self.cnt[eng] += 1
            v = self.cnt[eng]
            inc = (eng, 1)
        else:
            v = self.cnt[eng] + 1
            inc = None
        self._record(eng, v, reads, writes)
        self.q[eng].append((waits, fn, inc))

    def dma(self, queue, sem, fn, reads=(), writes=()):
        self._sem(sem)
        waits = self._deps(queue, reads, writes)
        self.cnt[sem] += 16
        self._record(sem, self.cnt[sem], reads, writes)
        self.q[queue].append((waits, fn, (sem, 16)))

    def barrier(self, engines=("pe", "act", "dve", "sp"), exclude=()):
        snap = {s: v for s, v in self.cnt.items() if v > 0 and s not in exclude
                and not s.startswith(("d_ring", "d_kc", "d_vc", "d_wfl"))}
        for e in engines:
            for s, v in snap.items():
                if s == e and e == "pe":
                    continue
                if self.pending[e].get(s, 0) < v:
                    self.pending[e][s] = v

    def finish(self, eng="sp"):
        waits = []
        for s, v in self.cnt.items():
            if v > 0 and self.seen[eng].get(s, 0) < v:
                waits.append((s, v))
        self.q[eng].append((waits, None, None))

    def replay(self, name, e):
        for waits, fn, inc in self.q[name]:
            for s, v in waits:
                e.wait_ge(self.sems[s], v)
            if fn is None:
                continue
            ins = fn(e)
            if inc is not None:
                ins.then_inc(self.sems[inc[0]], inc[1])

    def emit(self):
        nc = self.nc
        with nc.Block() as block:
            @block.tensor
            def _(e):
                self.replay("pe", e)

            @block.scalar
            def _(e):
                self.replay("act", e)

            @block.vector
            def _(e):
                self.replay("dve", e)

            @block.gpsimd
            def _(e):
                self.replay("pool", e)

            @block.sync
            def _(e):
                self.replay("sp", e)


class Mem:
    def __init__(self, nc):
        self.nc = nc
        self.base = 16512
        self.top = 229344
        self.n = 0

    def at(self, off, shape, dtype, name):
        self.n += 1
        nb = int(np.prod(shape[1:])) * (4 if dtype == F32 else 2)
        assert off % 32 == 0, (name, off)
        assert self.base <= off and off + nb <= self.top, (name, off, nb, self.top)
        return self.nc.alloc_sbuf_tensor_at("%s_%d" % (name, self.n), list(shape), dtype, offset=off)


def build(dbg=None, stop_after=99):
    dbg = dbg or []
    nc = bass.Bass("TRN2", target_bir_lowering=False)
    S = Sched(nc)
    M = Mem(nc)

    def din(name, shape):
        return nc.dram_tensor(name, list(shape), F32, kind="ExternalInput")

    def dout(name, shape):
        return nc.dram_tensor(name, list(shape), F32, kind="ExternalOutput")

    x_prompt = din("x_prompt", (SEQ, D))
    x_sample = din("x_sample", (NS, D))
    cache_k = din("cache_k", (PAST, 512))
    cache_v = din("cache_v", (PAST, 512))
    cache_logf = din("cache_logf", (PAST, H))
    meta = din("meta", (NMETA, D))
    w_in = din("w_in", (D, INC))
    w_br_conv = din("w_br_conv", (DC, D))
    w_br_attn = din("w_br_attn", (DA, D))
    w_out = din("w_out", (D, D))
    w_up = din("w_up", (D, DFF))
    w_down = din("w_down", (DFF, D))
    NPK = 64
    ppk = din("ppk", (P, NPK))
    stc = din("state_conv_t", (P, 8))

    y_prompt = dout("y_prompt", (SEQ, D))
    y_sample = dout("y_sample", (NS, D))
    nk_p = dout("nk_p", (L, 512))
    nv_p = dout("nv_p", (L, 512))
    nf_p = dout("nf_p", (L, H))
    nc_p = dout("nc_p", (2, DC))
    nk_s = dout("nk_s", (NS, 512))
    nv_s = dout("nv_s", (NS, 512))
    nf_s = dout("nf_s", (NS, H))
    nc_s = dout("nc_s", (2, DC))
    dbg_out = {}
    for (name, shape, dt_) in dbg:
        dbg_out[name] = nc.dram_tensor("dbg_" + name, list(shape), dt_, kind="ExternalOutput")

    if os.environ.get("PAIR_EXP", "1") == "1":
        PS2 = [nc.alloc_psum_tensor("ps2_%d" % i, [P, 1024], F32) for i in range(4)]
        PS = [PS2[i // 2][:, (i % 2) * 512:(i % 2 + 1) * 512] for i in range(8)]
    else:
        PS2 = None
        PS = [nc.alloc_psum_tensor("ps%d" % i, [P, 512], F32) for i in range(8)]

    o = M.base
    RING_SLOTS = 5
    ring = [M.at(o + i * 8192, (P, 4096), BF16, "ring") for i in range(RING_SLOTS)]
    o += RING_SLOTS * 8192
    pk = M.at(o, (P, NPK), F32, "pk"); o += NPK * 4
    identb = M.at(o, (P, P), BF16, "identb"); o += 256
    identf = M.at(o, (P, P), F32, "identf"); o += 512
    small = [o]
    o += 13312
    def sm(shape, dtype, name):
        nb = int(np.prod(shape[1:])) * (4 if dtype == F32 else 2)
        nb = (nb + 31) // 32 * 32
        t = M.at(small[0], shape, dtype, name)
        small[0] += nb
        assert small[0] <= o_small_end
        return t
    o_small_end = o
    AUG = o; o += 12544
    R2 = o; o += 33280
    R1 = o; o += 33280
    R3 = o; o += 34304
    R45 = o
    R45_SIZE = M.top - o
    assert R45_SIZE >= 41472, R45_SIZE

    xnT = M.at(R1, (P, KD, NT), BF16, "xnT")
    NXT = 6
    xt = [M.at(R3 + i * 4096, (P, D), F32, "xt") for i in range(2)] + \
         [M.at(R45 + 26624 + i * 4096, (P, D), F32, "xt") for i in range(4)]
    xs = [M.at(R3 + 8192 + i * 2048, (P, D), BF16, "xs") for i in range(2)] + [M.at(AUG, (P, D), BF16, "xs")]
    junk = M.at(R3 + 12288, (P, D), BF16, "junk")
    ssq = sm((P, 32), F32, "ssq")
    rstd = sm((P, 32), F32, "rstd")

    S.dma("sp", "d_pk", lambda e: e.dma_start(out=pk[:, :], in_=ppk[:, :]), writes=["pk"])
    def mk_ident(t, key):
        S.op("pool", lambda e: e.memset(t[:, :], 1.0), writes=[key])
        S.op("pool", lambda e: e.affine_select(t[:, :], t[:, :], [[-1, P]], ALU.is_equal, 0.0,
                                               base=0, channel_multiplier=1), reads=[key], writes=[key])
    mk_ident(identb, "identb")
    mk_ident(identf, "identf")

    epsc = sm((P, 1), F32, "epsc")
    S.op("pool", lambda e: e.memset(epsc[:, :], EPS), writes=["epsc"])

    def load_xtile(i, buf, key, sem):
        if i == 0:
            S.dma("sp", sem, lambda e: e.dma_start(out=buf[0:16, :], in_=meta[:, :]), writes=[key])
            S.dma("sp", sem, lambda e: e.dma_start(out=buf[16:128, :], in_=x_prompt[0:112, :]), writes=[key])
        elif i < 16:
            S.dma("sp", sem, lambda e: e.dma_start(out=buf[:, :], in_=x_prompt[128 * i - 16:128 * i + 112, :]), writes=[key])
        else:
            S.dma("sp", sem, lambda e: e.dma_start(out=buf[0:16, :], in_=x_prompt[2032:2048, :]), writes=[key])
            S.dma("sp", sem, lambda e: e.dma_start(out=buf[16:32, :], in_=x_sample[:, :]), writes=[key])

    def rms_rstd(src, r, col, inv_n, kin, tagk, junk=junk, jkey="junk"):
        S.op("act", lambda e: e.activation(out=junk[0:r, :], in_=src, func=AF.Square,
                                           accum_out=ssq[0:r, col:col + 1]),
             reads=[kin, "epsc"], writes=[jkey, "ssq%s%d" % (tagk, col)])
        S.op("act", lambda e: e.activation(out=ssq[0:r, col:col + 1], in_=ssq[0:r, col:col + 1], func=AF.Ln,
                                           bias=epsc[0:r, :], scale=inv_n),
             reads=["ssq%s%d" % (tagk, col)], writes=["ssq%s%d" % (tagk, col)])
        S.op("act", lambda e: e.activation(out=rstd[0:r, col:col + 1], in_=ssq[0:r, col:col + 1], func=AF.Exp,
                                           scale=-0.5),
             reads=["ssq%s%d" % (tagk, col)], writes=["rstd%s%d" % (tagk, col)])

    for i in range(NTILE):
        r = tile_rows(i)
        b = i % NXT
        b3 = i % 3
        kx, ks = "xt%d" % b, "xs%d" % b3
        load_xtile(i, xt[b], kx, "d_xt%d" % b)
        rms_rstd(xt[b][0:r, :], r, i, 1.0 / D, kx, "a")
        S.op("dve", lambda e, b=b, b3=b3, r=r, i=i: e.tensor_scalar(out=xs[b3][0:r, :], in0=xt[b][0:r, :],
                                                          scalar1=rstd[0:r, i:i + 1], scalar2=None, op0=ALU.mult),
             reads=[kx, "rstda%d" % i], writes=[ks])
        pb = i % 2
        pst = PS[pb][:, :].bitcast(BF16)
        for k in range(KD):
            S.op("pe", lambda e, k=k, b3=b3, r=r, pst=pst: e.transpose(out=pst[:, k * 128:k * 128 + r],
                                                                    in_=xs[b3][0:r, k * 128:(k + 1) * 128],
                                                                    identity=identb[0:r, 0:r]),
                 reads=[ks, "identb"], writes=["ps%d" % pb], sig=(k == KD - 1))
        c0 = 128 * i
        S.op("dve", lambda e, pst=pst, r=r, c0=c0: e.tensor_tensor(
            out=xnT[:, :, c0:c0 + r],
            in0=pst.rearrange("p (k t) -> p k t", k=KD)[:, :, 0:r],
            in1=pk[:, 0:KD].unsqueeze(2).to_broadcast([P, KD, r]), op=ALU.mult),
             reads=["ps%d" % pb, "pk"], writes=["xnT%d" % i])

    if "xnT" in dbg_out:
        S.dma("sp", "d_dbg", lambda e: e.dma_start(out=dbg_out["xnT"][:, :, :], in_=xnT[:, :, :]),
              reads=["xnT%d" % i for i in range(NTILE)])

    class _Stop(Exception):
        pass

    def chk(st):
        if stop_after < st:
            raise _Stop()

    try:
        XN_ALL = ["xnT%d" % i for i in range(NTILE)]

        def xn_keys(c0, n):
            return ["xnT%d" % i for i in range(c0 // 128, (c0 + n - 1) // 128 + 1)]

        wlist = []

        def wg_cols(w, c0, kch=KD, ncol=512):
            return (lambda t: t[:, 0:kch * ncol].rearrange("p (k c) -> p k c", k=kch),
                    w[0:kch * 128, c0:c0 + ncol].rearrange("(k p) c -> p k c", p=P))

        G_Q, G_K, G_V = 0, 1, 2
        wlist.append(wg_cols(w_in, C_Q))
        wlist.append(wg_cols(w_in, C_K))
        wlist.append(wg_cols(w_in, C_V))
        G_CV = [3, 4, 5, 6]
        for cch in range(4):
            wlist.append((lambda t: t[:, 0:KD * 384].rearrange("p (k c) -> p k c", k=KD),
                          [((128 * j3, 128 * (j3 + 1)),
                            w_in[:, base + 128 * cch:base + 128 * (cch + 1)].rearrange("(k p) c -> p k c", p=P))
                           for j3, base in enumerate((C_B, C_C, C_H))]))
        G_BRC, G_BRA = 7, 8
        G_GP = [9, 10, 11, 12]
        wlist.append(wg_cols(w_br_conv, 0, kch=4, ncol=1024))
        wlist.append(wg_cols(w_br_attn, 0, kch=4, ncol=1024))
        for pr in range(4):
            wlist.append((lambda t: t[:, 0:KD * 512].rearrange("p (k c) -> p k c", k=KD),
                          [((0, 256), w_in[:, C_G + 256 * pr:C_G + 256 * (pr + 1)].rearrange("(k p) c -> p k c", p=P)),
                           ((256, 512), w_in[:, C_G + 1024 + 256 * pr:C_G + 1024 + 256 * (pr + 1)].rearrange("(k p) c -> p k c", p=P))]))
        G_OUT0 = 13
        wlist.append(wg_cols(w_out, 0))
        wlist.append(wg_cols(w_out, 512))
        G_MLP = 15
        for g in range(8):
            wlist.append(wg_cols(w_up, 512 * g))
            wlist.append((lambda t: t[:, :].rearrange("p (k c) -> p k c", k=4),
                          w_down[512 * g:512 * (g + 1), :].rearrange("(k p) c -> p k c", p=P)))
        wstate = {"issued": 0, "free": list(range(RING_SLOTS)), "slot": {}}

        def w_try_issue(limit=None, after=()):
            while wstate["issued"] < len(wlist) and wstate["free"] and (limit is None or wstate["issued"] < limit):
                g = wstate["issued"]
                slot = wstate["free"].pop(0)
                wstate["slot"][g] = slot
                vf, srcs = wlist[g]
                dst = vf(ring[slot])
                if not isinstance(srcs, list):
                    srcs = [(None, srcs)]
                for (sub, src) in srcs:
                    d = dst if sub is None else dst[:, :, sub[0]:sub[1]]
                    S.dma("pool", "d_ring%d" % slot, lambda e, d=d, src=src: e.dma_start(out=d, in_=src),
                          reads=list(after), writes=["ring%d" % slot])
                wstate["issued"] += 1

        def w_get(g):
            if g not in wstate["slot"]:
                w_try_issue(g + 1)
            slot = wstate["slot"][g]
            return wlist[g][0](ring[slot]), "ring%d" % slot

        def w_done(g):
            wstate["free"].append(wstate["slot"][g])
            w_try_issue()

        w_try_issue(1)
        w_try_issue(2, after=["xt%d" % (7 % NXT)])
        w_try_issue(3, after=["xt%d" % (12 % NXT)])

        blockones = sm((P, P), BF16, "blockones")
        S.op("pool", lambda e: e.memset(blockones[:, :], 0.0), writes=["blockones"])
        S.op("pool", lambda e: e.memset(blockones[0:64, 0:64], 1.0), writes=["blockones"])
        S.op("pool", lambda e: e.memset(blockones[64:128, 64:128], 1.0), writes=["blockones"])
        onesf = sm((P, P), F32, "onesf")
        S.op("pool", lambda e: e.memset(onesf[:, :], 1.0), writes=["onesf"])
        trif = sm((P, P), F32, "trif")
        S.op("pool", lambda e: e.memset(trif[:, :], 1.0), writes=["trif"])
        S.op("pool", lambda e: e.affine_select(trif[:, :], trif[:, :], [[1, P]], ALU.is_ge, 0.0,
                                               base=0, channel_multiplier=-1), reads=["trif"], writes=["trif"])
        maskb = sm((P, P), BF16, "maskb")
        S.op("pool", lambda e: e.memset(maskb[:, :], 1.0), writes=["maskb"])
        S.op("pool", lambda e: e.affine_select(maskb[:, :], maskb[:, :], [[1, P]], ALU.is_ge, 0.0,
                                               base=0, channel_multiplier=-1), reads=["maskb"], writes=["maskb"])
        maskneg = sm((P, P), BF16, "maskneg")
        S.op("pool", lambda e: e.memset(maskneg[:, :], 0.0), writes=["maskneg"])
        S.op("pool", lambda e: e.affine_select(maskneg[:, :], maskneg[:, :], [[1, P]], ALU.is_ge, -9984.0,
                                               base=0, channel_multiplier=-1), reads=["maskneg"], writes=["maskneg"])
        ones3 = sm((3, P), BF16, "ones3")
        S.op("pool", lambda e: e.memset(ones3[:, :], 1.0), writes=["ones3"])
        onecol = sm((P, 1), F32, "onecol")
        S.op("pool", lambda e: e.memset(onecol[:, :], 1.0), writes=["onecol"])
        wfl = sm((P, KD, 8), BF16, "wfl")
        if not os.environ.get("SKIP_WFL"):
          S.dma("pool", "d_wfl", lambda e: e.dma_start(out=wfl[:, :, :],
                                                     in_=w_in[:, C_F:C_F + 8].rearrange("(k p) c -> p k c", p=P)),
              writes=["wfl"])
        fl_all = sm((P, NTILE, 8), F32, "fl_all")
        lf_all = sm((P, NTILE, 8), F32, "lf_all")
        S.op("pool", lambda e: e.memset(fl_all[:, :, :], 0.0), writes=["fl_all"])
        S.op("pool", lambda e: e.memset(lf_all[:, :, :], 0.0), writes=["lf_all"])
        c_all = sm((P, NTILE, 8), F32, "c_all")
        S.op("pool", lambda e: e.memset(c_all[:, :, :], 0.0), writes=["c_all"])
        negc = sm((P, NTILE, 8), F32, "negc")
        csplit = sm((P, NTILE, 3, 8), BF16, "csplit")
        cres = sm((P, NTILE, 8), F32, "cres")
        cT24 = M.at(AUG, (24, NT), BF16, "cT24")
        qaug = [M.at(AUG + 4160 + i * 4160, (3, NT), BF16, "qaug") for i in range(2)]

        qT = M.at(R2, (P, 4, NT), BF16, "qT")
        kT = M.at(R2 + 16640, (P, 4, NT), BF16, "kT")
        Vp = M.at(R3 + 14336, (P, NTILE, H, 66), BF16, "Vp")
        sqb = [M.at(R45 + i * 1024, (P, 512), BF16, "sqb") for i in range(2)]
        rsb = [M.at(R45 + 2048 + i * 2048, (P, 512), F32, "rsb") for i in range(2)]
        kf = M.at(R45 + 6144, (P, 4, 512), F32, "kf")
        ktok = [M.at(R45 + 14336 + i * 2048, (P, 512), F32, "ktok") for i in range(2)]
        vtok = [M.at(R45 + 18432 + i * 2048, (P, 512), F32, "vtok") for i in range(2)]

        psn = {"n": 0}

        def next_ps(lo=0, hi=8, key="n"):
            psn[key] = psn.get(key, lo - 1) + 1
            if psn[key] >= hi or psn[key] < lo:
                psn[key] = lo
            return psn[key]

        if not os.environ.get("SKIP_VPMEM"):
            S.op("pool", lambda e: e.memset(Vp[:, :, :, 64:65], 1.0), writes=["Vp_ones"])

        chk(1)
        sqb3 = [M.at(R45 + 22528 + i * 1024, (P, 512), BF16, "sqb3") for i in range(4)]
        cnt1 = {"kt": 0}
        units1 = []
        for which, G in (("q", G_Q), ("k", G_K)):
            for (c0, n) in BLKS:
                for m in range(4):
                    units1.append(dict(which=which, G=G, c0=c0, n=n, m=m, idx=len(units1)))

        def s1A(u):
            which, G, c0, n, m = u["which"], u["G"], u["c0"], u["n"], u["m"]
            wv, wkey = w_get(G)
            A = next_ps(0, 4, "qa")
            sj = u["idx"] % 4
            u["A"], u["sj"] = A, sj
            for k in range(KD):
                S.op("pe", lambda e, k=k: e.matmul(
                    PS[A][:, 0:n], lhsT=wv[:, k, m * 128:(m + 1) * 128], rhs=xnT[:, k, c0:c0 + n],
                    start=(k == 0), stop=(k == KD - 1)),
                     reads=[wkey] + xn_keys(c0, n), writes=["ps%d" % A], sig=(k == KD - 1))
            S.op("act", lambda e: e.activation(out=sqb3[sj][:, 0:n], in_=PS[A][:, 0:n], func=AF.Square),
                 reads=["ps%d" % A], writes=["sqb%d" % sj])

        def s1B(u):
            which, G, c0, n, m = u["which"], u["G"], u["c0"], u["n"], u["m"]
            A, sj = u["A"], u["sj"]
            j = u["idx"] % 2
            B = next_ps(4, 6, "qb")
            S.op("pe", lambda e: e.matmul(PS[B][:, 0:n], lhsT=blockones[:, :], rhs=sqb3[sj][:, 0:n], start=True, stop=True),
                 reads=["sqb%d" % sj, "blockones"], writes=["ps%d" % B])
            S.op("act", lambda e: e.activation(out=rsb[j][:, 0:n], in_=PS[B][:, 0:n], func=AF.Ln,
                                               bias=epsc[:, :], scale=1.0 / HD),
                 reads=["ps%d" % B, "epsc"], writes=["rsb%d" % j])
            S.op("act", lambda e: e.activation(out=rsb[j][:, 0:n], in_=rsb[j][:, 0:n], func=AF.Exp, scale=-0.5),
                 reads=["rsb%d" % j], writes=["rsb%d" % j])
            if which == "q":
                S.op("dve", lambda e: e.scalar_tensor_tensor(
                    out=qT[:, m, c0:c0 + n], in0=PS[A][:, 0:n], scalar=pk[:, 40:41], in1=rsb[j][:, 0:n],
                    op0=ALU.mult, op1=ALU.mult),
                     reads=["ps%d" % A, "rsb%d" % j, "pk"], writes=["qT%d_%d" % (m, c0)])
            else:
                S.op("dve", lambda e: e.scalar_tensor_tensor(
                    out=kf[:, m, 0:n], in0=PS[A][:, 0:n], scalar=pk[:, 41:42], in1=rsb[j][:, 0:n],
                    op0=ALU.mult, op1=ALU.mult),
                     reads=["ps%d" % A, "rsb%d" % j, "pk"], writes=["kf%d" % m])
                S.op("dve", lambda e: e.tensor_copy(out=kT[:, m, c0:c0 + n], in_=kf[:, m, 0:n]),
                     reads=["kf%d" % m], writes=["kT%d_%d" % (m, c0)])
                if m == 3:
                    for tt in range((n + 127) // 128):
                        r = min(128, n - tt * 128)
                        jj = cnt1["kt"] % 2
                        cnt1["kt"] += 1
                        Cb = next_ps(6, 8, "kt")
                        for mm in range(4):
                            S.op("pe", lambda e, Cb=Cb, mm=mm, tt=tt, r=r: e.transpose(
                                out=PS[Cb][0:r, mm * 128:(mm + 1) * 128], in_=kf[:, mm, tt * 128:tt * 128 + r], identity=identf[:, :]),
                                 reads=["kf%d" % mm, "identf"], writes=["ps%d" % Cb], sig=(mm == 3))
                        S.op("dve", lambda e, Cb=Cb, jj=jj, r=r: e.tensor_copy(out=ktok[jj][0:r, :], in_=PS[Cb][0:r, :]),
                             reads=["ps%d" % Cb], writes=["ktok%d" % jj])
                        p0 = c0 + tt * 128
                        if p0 < 2048:
                            S.dma("sp", "d_ktok%d" % jj, lambda e, jj=jj, p0=p0: e.dma_start(out=nk_p[p0:p0 + 128, :], in_=ktok[jj][:, :]),
                                  reads=["ktok%d" % jj], writes=["nk_p_%d" % p0])
                        else:
                            S.dma("sp", "d_ktok%d" % jj, lambda e, jj=jj: e.dma_start(out=nk_p[2048:2064, :], in_=ktok[jj][0:16, :]),
                                  reads=["ktok%d" % jj], writes=["nk_p_%d" % p0])
                            S.dma("sp", "d_ktok%d" % jj, lambda e, jj=jj: e.dma_start(out=nk_s[:, :], in_=ktok[jj][16:32, :]),
                                  reads=["ktok%d" % jj], writes=["nk_s"])
            if m == 3 and c0 == 2048:
                w_done(G)

        LA1 = 3
        for i in range(LA1):
            s1A(units1[i])
        for i in range(len(units1)):
            s1B(units1[i])
            if i + LA1 < len(units1):
                s1A(units1[i + LA1])

        chk(2)
        wv, wkey = w_get(G_V)
        FB = 4
        for i in range(NTILE):
            r = tile_rows(i)
            c0 = 128 * i
            for k in range(KD):
                S.op("pe", lambda e, k=k, r=r, c0=c0, i=i: e.matmul(
                    PS[FB][0:r, 8 * i:8 * i + 8], lhsT=xnT[:, k, c0:c0 + r], rhs=wfl[:, k, :], start=(k == 0), stop=(k == KD - 1)),
                     reads=["wfl", "xnT%d" % i], writes=["ps%d" % FB], sig=(k == KD - 1))
        S.op("dve", lambda e: e.tensor_tensor(out=fl_all[:, 0:16, :], in0=PS[FB][:, 0:128].rearrange("p (i h) -> p i h", h=8),
                                              in1=pk[:, 16:24].unsqueeze(1).to_broadcast([P, 16, 8]), op=ALU.add),
             reads=["ps%d" % FB, "pk"], writes=["fl_all"])
        S.op("dve", lambda e: e.tensor_tensor(out=fl_all[0:32, 16, :], in0=PS[FB][0:32, 128:136], in1=pk[0:32, 16:24], op=ALU.add),
             reads=["ps%d" % FB, "pk"], writes=["fl_all"])

        def logsig(dst, src, r, kin, kout):
            S.op("act", lambda e: e.activation(out=dst, in_=src, func=AF.Exp, scale=-1.0), reads=[kin], writes=[kout])
            S.op("act", lambda e: e.activation(out=dst, in_=dst, func=AF.Ln, bias=onecol[0:r, :], scale=1.0),
                 reads=[kout, "onecol"], writes=[kout])
            S.op("dve", lambda e: e.tensor_scalar(out=dst, in0=dst, scalar1=-1.0, scalar2=None, op0=ALU.mult),
                 reads=[kout], writes=[kout])
        logsig(lf_all[:, 0:16, :], fl_all[:, 0:16, :], P, "fl_all", "lf_all")
        logsig(lf_all[0:32, 16, :], fl_all[0:32, 16, :], 32, "fl_all", "lf_all")
        S.dma("sp", "d_lf", lambda e: e.dma_start(out=nf_p[0:2048, :].rearrange("(i p) h -> p i h", p=P), in_=lf_all[:, 0:16, :]),
              reads=["lf_all"], writes=["nf_p_a"])
        S.dma("sp", "d_lf", lambda e: e.dma_start(out=nf_p[2048:2064, :], in_=lf_all[0:16, 16, :]), reads=["lf_all"], writes=["nf_p_b"])
        S.dma("sp", "d_lf", lambda e: e.dma_start(out=nf_s[:, :], in_=lf_all[16:32, 16, :]), reads=["lf_all"], writes=["nf_s"])

        carr = sm((P, NTILE, 8), F32, "carr")
        lfc = sm((P, 16, 8), F32, "lfc")
        S.dma("sp", "d_lfc", lambda e: e.dma_start(out=lfc[:, :, :], in_=cache_logf[:, :].rearrange("(i p) h -> p i h", p=P)),
              writes=["lfc"])
        c_s = sm((P, NTILE, 8), F32, "c_s")
        S.op("pool", lambda e: e.memset(c_s[:, :, :], 0.0), writes=["c_s"])
        negc_s = sm((P, NTILE, 8), F32, "negc_s")
        msel = sm((32, 16), F32, "msel")
        S.op("pool", lambda e: e.memset(msel[:, :], 1.0), writes=["msel"])
        S.op("pool", lambda e: e.affine_select(msel[:, :], msel[:, :], [[1, 16]], ALU.is_ge, 0.0,
                                               base=16, channel_multiplier=-1), reads=["msel"], writes=["msel"])
        S.op("pool", lambda e: e.memset(msel[0:16, :], 0.0), reads=["msel"], writes=["msel"])
        carr_s = sm((P, 17, 8), F32, "carr_s")
        Cb = 5
        S.op("pe", lambda e: e.matmul(PS[Cb][:, 0:136], lhsT=trif[:, :], rhs=lf_all[:, :, :].rearrange("p i h -> p (i h)"), start=True, stop=True),
             reads=["lf_all", "trif"], writes=["ps%d" % Cb])
        S.op("pe", lambda e: e.matmul(PS[Cb][:, 136:272], lhsT=onesf[:, :], rhs=lf_all[:, :, :].rearrange("p i h -> p (i h)"), start=True, stop=True),
             reads=["lf_all", "onesf"], writes=["ps%d" % Cb])
        Cs = 6
        S.op("pe", lambda e: e.matmul(PS[Cs][:, 0:128], lhsT=trif[:, :], rhs=lfc[:, :, :].rearrange("p i h -> p (i h)"), start=True, stop=True),
             reads=["lfc", "trif"], writes=["ps%d" % Cs])
        S.op("pe", lambda e: e.matmul(PS[Cs][:, 128:256], lhsT=onesf[:, :], rhs=lfc[:, :, :].rearrange("p i h -> p (i h)"), start=True, stop=True),
             reads=["lfc", "onesf"], writes=["ps%d" % Cs])
        S.op("pe", lambda e: e.matmul(PS[Cs][0:16, 256:264], lhsT=msel[:, :], rhs=lf_all[0:32, 16, :], start=True, stop=True),
             reads=["lf_all", "msel"], writes=["ps%d" % Cs])

        chain = []

        def DF(fn, **kw):
            chain.append(lambda: S.op("dve", fn, **kw))
        DF(lambda e: e.memset(carr[:, 0, :], 0.0), writes=["carr"])
        for i in range(1, NTILE):
            DF(lambda e, i=i: e.tensor_tensor(out=carr[:, i, :], in0=carr[:, i - 1, :], in1=PS[Cb][:, 136 + 8 * (i - 1):136 + 8 * i], op=ALU.add),
              reads=["carr", "ps%d" % Cb], writes=["carr"])
        DF(lambda e: e.tensor_tensor(out=c_all[:, :, :], in0=carr[:, :, :], in1=PS[Cb][:, 0:136].rearrange("p (i h) -> p i h", h=8), op=ALU.add),
          reads=["carr", "ps%d" % Cb], writes=["c_all"])
        key = "c_all"
        DF(lambda e: e.tensor_scalar(out=negc[:, :, :], in0=c_all[:, :, :], scalar1=-1.0, scalar2=None, op0=ALU.mult),
          reads=[key], writes=[key + "_neg"])
        DF(lambda e: e.tensor_copy(out=csplit[:, :, 0, :], in_=c_all[:, :, :]), reads=[key], writes=[key + "_s"])
        DF(lambda e: e.tensor_tensor(out=cres[:, :, :], in0=c_all[:, :, :], in1=csplit[:, :, 0, :], op=ALU.subtract),
          reads=[key, key + "_s"], writes=[key + "_r"])
        DF(lambda e: e.tensor_copy(out=csplit[:, :, 1, :], in_=cres[:, :, :]), reads=[key + "_r"], writes=[key + "_s"])
        DF(lambda e: e.tensor_tensor(out=cres[:, :, :], in0=cres[:, :, :], in1=csplit[:, :, 1, :], op=ALU.subtract),
          reads=[key + "_r", key + "_s"], writes=[key + "_r"])
        DF(lambda e: e.tensor_copy(out=csplit[:, :, 2, :], in_=cres[:, :, :]), reads=[key + "_r"], writes=[key + "_s"])
        DF(lambda e: e.memset(carr_s[:, 0, :], 0.0), writes=["carr_s"])
        for i in range(1, 17):
            DF(lambda e, i=i: e.tensor_tensor(out=carr_s[:, i, :], in0=carr_s[:, i - 1, :], in1=PS[Cs][:, 128 + 8 * (i - 1):128 + 8 * i], op=ALU.add),
              reads=["carr_s", "ps%d" % Cs], writes=["carr_s"])
        DF(lambda e: e.tensor_tensor(out=c_s[:, 0:16, :], in0=carr_s[:, 0:16, :], in1=PS[Cs][:, 0:128].rearrange("p (i h) -> p i h", h=8), op=ALU.add),
          reads=["carr_s", "ps%d" % Cs], writes=["c_s"])
        DF(lambda e: e.tensor_tensor(out=c_s[:, 0:16, :], in0=c_s[:, 0:16, :],
                                    in1=carr_s[:, 16, :].unsqueeze(1).to_broadcast([P, 16, 8]), op=ALU.subtract),
          reads=["carr_s", "c_s"], writes=["c_s"])
        DF(lambda e: e.tensor_copy(out=c_s[0:16, 16, :], in_=PS[Cs][0:16, 256:264]), reads=["ps%d" % Cs], writes=["c_s"])
        DF(lambda e: e.tensor_scalar(out=negc_s[:, :, :], in0=c_s[:, :, :], scalar1=-1.0, scalar2=None, op0=ALU.mult),
          reads=["c_s"], writes=["c_s_neg"])

        def run_chain(n):
            for _ in range(n):
                if chain:
                    chain.pop(0)()

        for i in range(NTILE):
            r = tile_rows(i)
            c0 = 128 * i
            jj = i % 2
            A = next_ps(0, 4, "vp")
            for k in range(KD):
                S.op("pe", lambda e, A=A, k=k, r=r, c0=c0, wv=wv: e.matmul(
                    PS[A][0:r, :], lhsT=xnT[:, k, c0:c0 + r], rhs=wv[:, k, :], start=(k == 0), stop=(k == KD - 1)),
                     reads=[wkey, "xnT%d" % i], writes=["ps%d" % A], sig=(k == KD - 1))
            S.op("dve", lambda e, A=A, jj=jj, r=r: e.tensor_copy(out=vtok[jj][0:r, :], in_=PS[A][0:r, :]),
                 reads=["ps%d" % A], writes=["vtok%d" % jj])
            S.op("dve", lambda e, A=A, r=r, i=i: e.tensor_copy(
                out=Vp[0:r, i, :, 0:64], in_=PS[A][0:r, :].rearrange("p (h d) -> p h d", h=H)),
                 reads=["ps%d" % A], writes=["Vp%d" % i])
            if i < 16:
                S.dma("sp", "d_vtok%d" % jj, lambda e, jj=jj, c0=c0: e.dma_start(out=nv_p[c0:c0 + 128, :], in_=vtok[jj][:, :]),
                      reads=["vtok%d" % jj], writes=["nv_p_%d" % i])
            else:
                S.dma("sp", "d_vtok%d" % jj, lambda e, jj=jj: e.dma_start(out=nv_p[2048:2064, :], in_=vtok[jj][0:16, :]),
                      reads=["vtok%d" % jj], writes=["nv_p_%d" % i])
                S.dma("sp", "d_vtok%d" % jj, lambda e, jj=jj: e.dma_start(out=nv_s[:, :], in_=vtok[jj][16:32, :]),
                      reads=["vtok%d" % jj], writes=["nv_s"])
            run_chain(4)
        run_chain(len(chain))
        Vsn = sm((16, H, 66), BF16, "Vsn")
        S.op("pool", lambda e: e.memset(Vsn[:, :, 64:65], 1.0), writes=["Vsn_ones"])
        A = next_ps(0, 4, "vp")
        for k in range(KD):
            S.op("pe", lambda e, A=A, k=k, wv=wv: e.matmul(PS[A][0:16, :], lhsT=xnT[:, k, L:NT], rhs=wv[:, k, :],
                                                          start=(k == 0), stop=(k == KD - 1)),
                 reads=[wkey, "xnT16"], writes=["ps%d" % A], sig=(k == KD - 1))
        S.op("dve", lambda e, A=A: e.tensor_copy(out=Vsn[:, :, 0:64], in_=PS[A][0:16, :].rearrange("p (h d) -> p h d", h=H)),
             reads=["ps%d" % A], writes=["Vsn"])
        w_done(G_V)

        for bnk in range(3):
            Tb = next_ps(4, 8, "ct")
            pst = PS[Tb][:, :].bitcast(BF16)
            tiles = list(range(8 * bnk, min(NTILE, 8 * bnk + 8)))
            for i in tiles:
                r = 128 if i < 16 else 16
                S.op("pe", lambda e, i=i, r=r, pst=pst: e.transpose(
                    out=pst[0:24, 128 * (i % 8):128 * (i % 8) + r], in_=csplit[0:r, i, :, :].rearrange("p j h -> p (j h)"),
                    identity=identb[0:r, 0:r]),
                     reads=["c_all_s", "identb"], writes=["ps%d" % Tb], sig=(i == tiles[-1]))
            w0 = 128 * tiles[0]
            wn = sum(128 if i < 16 else 16 for i in tiles)
            S.op("dve", lambda e, pst=pst, w0=w0, wn=wn: e.tensor_scalar(out=cT24[:, w0:w0 + wn], in0=pst[0:24, 0:wn],
                                                                  scalar1=8.0, scalar2=None, op0=ALU.mult),
                 reads=["ps%d" % Tb], writes=["c_all_T"])

        ncT24 = M.at(AUG + 4160, (24, NT), BF16, "ncT24")
        S.op("dve", lambda e: e.tensor_scalar(out=ncT24[:, 0:L], in0=cT24[:, 0:L], scalar1=-1.0, scalar2=None, op0=ALU.mult),
             reads=["c_all_T"], writes=["ncT24"])
        chk(6)
        S.barrier(engines=("pe", "act", "dve", "sp", "pool"))

        attnT = M.at(R45, (P, 4, NT), BF16, "attnT")
        attn_tok = M.at(R45 + 16640, (P, NTILE, 512), BF16, "attn_tok")
        NPB = 4
        Pb = [M.at(R45 + 34048 + i * 2048, (P, 1024), BF16, "Pb") for i in range(NPB)]
        Qh = [M.at(R45 + i * 4160, (P, NT), BF16, "Qh") for i in range(2)]
        Kh = [M.at(R45 + 8320 + i * 4160, (P, NT), BF16, "Kh") for i in range(2)]
        for i in range(2):
            S.op("pool", lambda e, i=i: e.memset(Qh[i][64:128, :], 0.0), writes=["Qh%d" % i])
            S.op("pool", lambda e, i=i: e.memset(Kh[i][64:128, :], 0.0), writes=["Kh%d" % i])
        S.op("dve", lambda e: e.memset(Qh[0][64:70, :], 1.0), writes=["Qh0"])
        S.dma("sp", "d_qh1", lambda e: e.dma_start(out=Qh[1][67:70, 0:L], in_=Qh[0][67:70, 0:L]), reads=["Qh0"], writes=["Qh1"])
        S.dma("sp", "d_kh0", lambda e: e.dma_start(out=Kh[0][64:67, 0:L], in_=Qh[0][67:70, 0:L]), reads=["Qh0"], writes=["Kh0"])
        S.dma("sp", "d_kh1", lambda e: e.dma_start(out=Kh[1][64:67, 0:L], in_=Qh[0][67:70, 0:L]), reads=["Qh0"], writes=["Kh1"])
        rsum = sm((P, 4), F32, "rsum")
        QG = [(0, 512), (512, 512), (1024, 512), (1536, 512), (2048, 16)]
        units = []
        for h in range(H):
            for gi, (q0, qn) in enumerate(QG):
                kt_last = (q0 + qn - 1) // 128
                kts = list(range(kt_last + 1))
                groups = []
                full = [kt for kt in kts if kt * 128 < q0 and qn == 512]
                rest = [kt for kt in kts if kt not in full]
                for j in range(0, len(full), 2):
                    groups.append(full[j:j + 2])
                for kt in rest:
                    groups.append([kt])
                for gj, g in enumerate(groups):
                    units.append(dict(h=h, q0=q0, qn=qn, kts=g, first_h=(gi == 0 and gj == 0), first_g=(gj == 0),
                                      last_g=(gj == len(groups) - 1), idx=len(units)))
        ostate = {}

        def emitA(u):
            h, q0, qn, kts = u["h"], u["q0"], u["qn"], u["kts"]
            hp, hoff = h // 2, (h % 2) * 64
            hb = h % 2
            nqb = (qn + 127) // 128
            if u["first_h"]:
                S.dma("sp", "d_qh%d" % hb, lambda e: e.dma_start(out=Qh[hb][0:64, :], in_=qT[hoff:hoff + 64, hp, :]),
                      reads=["qT_all"], writes=["Qh%d" % hb])
                S.dma("sp", "d_kh%d" % hb, lambda e: e.dma_start(out=Kh[hb][0:64, :], in_=kT[hoff:hoff + 64, hp, :]),
                      reads=["kT_all"], writes=["Kh%d" % hb])
                for j3 in range(3):
                    S.dma("sp", "d_qh%d" % hb, lambda e, j3=j3: e.dma_start(
                        out=Qh[hb][64 + j3:65 + j3, 0:L], in_=cT24[8 * j3 + h:8 * j3 + h + 1, 0:L]),
                          reads=["c_all_T"], writes=["Qh%d" % hb])
                    S.dma("sp", "d_kh%d" % hb, lambda e, j3=j3: e.dma_start(
                        out=Kh[hb][67 + j3:68 + j3, 0:L], in_=ncT24[8 * j3 + h:8 * j3 + h + 1, 0:L]),
                          reads=["ncT24"], writes=["Kh%d" % hb])
            if u["first_g"]:
                O = next_ps(6, 8, "o")
                ostate[(h, q0)] = O
                S.op("dve", lambda e, O=O, nqb=nqb: e.memset(PS[O][:, 0:65 * nqb], 0.0), writes=["ps%d" % O])
            pp = next_ps(0, 3, "sp2")
            pj = u["idx"] % NPB
            u["pj"] = pj
            u["geo"] = []
            for hf, kt in enumerate(kts):
                kr = 128 if kt < 16 else 16
                qs = max(q0, kt * 128)
                nn = q0 + qn - qs
                u["geo"].append((kt, kr, qs, nn))
                diag = (kt * 128 >= q0)
                dst = PS2[pp][0:kr, hf * 512:hf * 512 + nn] if PS2 is not None else PS[2 * pp + hf][0:kr, 0:nn]
                S.op("pe", lambda e, dst=dst, kt=kt, kr=kr, qs=qs, nn=nn, diag=diag: e.matmul(
                    dst, lhsT=Kh[hb][:, kt * 128:kt * 128 + kr], rhs=Qh[hb][:, qs:qs + nn], start=True, stop=not diag),
                     reads=["Kh%d" % hb, "Qh%d" % hb], writes=["ps%d" % (2 * pp), "ps%d" % (2 * pp + 1)], sig=not diag)
                if diag:
                    dn = min(128, nn)
                    S.op("pe", lambda e, kr=kr, dn=dn, hf=hf: e.matmul(
                        (PS2[pp][0:kr, hf * 512:hf * 512 + dn] if PS2 is not None else PS[2 * pp + hf][0:kr, 0:dn]),
                        lhsT=identb[0:kr, 0:kr], rhs=maskneg[0:kr, 0:dn], start=False, stop=True),
                         reads=["identb", "maskneg"], writes=["ps%d" % (2 * pp), "ps%d" % (2 * pp + 1)])
            if len(kts) == 2 and os.environ.get("PAIR_EXP", "1") == "1":
                S.op("act", lambda e: e.activation(out=Pb[pj][:, :], in_=PS2[pp][:, :], func=AF.Exp, scale=0.125),
                     reads=["ps%d" % (2 * pp), "ps%d" % (2 * pp + 1)], writes=["Pb%d" % pj])
            elif len(kts) == 2:
                for hf in range(2):
                    S.op("act", lambda e, hf=hf: e.activation(out=Pb[pj][:, hf * 512:(hf + 1) * 512],
                                                             in_=(PS2[pp][:, hf * 512:(hf + 1) * 512] if PS2 is not None else PS[2 * pp + hf][:, :]),
                                                             func=AF.Exp, scale=0.125),
                         reads=["ps%d" % (2 * pp), "ps%d" % (2 * pp + 1)], writes=["Pb%d" % pj])
            else:
                kt, kr, qs, nn = u["geo"][0]
                S.op("act", lambda e: e.activation(out=Pb[pj][0:kr, 0:nn], in_=(PS2[pp][0:kr, 0:nn] if PS2 is not None else PS[2 * pp][0:kr, 0:nn]),
                                                   func=AF.Exp, scale=0.125),
                     reads=["ps%d" % (2 * pp), "ps%d" % (2 * pp + 1)], writes=["Pb%d" % pj])

        def emitB(u):
            h, q0, qn = u["h"], u["q0"], u["qn"]
            pj = u["pj"]
            nqb = (qn + 127) // 128
            O = ostate[(h, q0)]
            nk = len(u["geo"])
            for hf, (kt, kr, qs, nn) in enumerate(u["geo"]):
                for qb in range(nqb):
                    qcol = q0 + qb * 128
                    qr = min(128, q0 + qn - qcol)
                    if qcol + qr - 1 < kt * 128:
                        continue
                    S.op("pe", lambda e, qb=qb, qr=qr, qcol=qcol, hf=hf, kt=kt, kr=kr, qs=qs: e.matmul(
                        PS[O][0:qr, 65 * qb:65 * qb + 65], lhsT=Pb[pj][0:kr, hf * 512 + qcol - qs:hf * 512 + qcol - qs + qr],
                        rhs=Vp[0:kr, kt, h, 0:65], start=False, stop=(kt == qcol // 128), skip_group_check=True),
                         reads=["Pb%d" % pj, "Vp%d" % kt, "Vp_ones"], writes=["ps%d" % O],
                         sig=(qb == nqb - 1 and hf == nk - 1))
            if u["last_g"]:
                for qb in range(nqb):
                    qcol = q0 + qb * 128
                    qr = min(128, q0 + qn - qcol)
                    S.op("dve", lambda e, qb=qb, qr=qr: e.reciprocal(out=rsum[0:qr, qb:qb + 1], in_=PS[O][0:qr, 65 * qb + 64:65 * qb + 65]),
                         reads=["ps%d" % O], writes=["rsum%d" % qb])
                    S.op("dve", lambda e, qb=qb, qr=qr, qcol=qcol: e.tensor_scalar(
                        out=attn_tok[0:qr, qcol // 128, h * 64:(h + 1) * 64], in0=PS[O][0:qr, 65 * qb:65 * qb + 64],
                        scalar1=rsum[0:qr, qb:qb + 1], scalar2=None, op0=ALU.mult),
                         reads=["ps%d" % O, "rsum%d" % qb], writes=["attn_tok%d" % (qcol // 128)])

        LA = 3
        for i in range(min(LA, len(units))):
            emitA(units[i])
        for i in range(len(units)):
            emitB(units[i])
            if i + LA < len(units):
                emitA(units[i + LA])

        chk(7)
        for i in range(NTILE):
            r = 128 if i < 16 else 16
            Tb = next_ps(0, 4, "ta")
            pst = PS[Tb][:, :].bitcast(BF16)
            for hp in range(4):
                S.op("pe", lambda e, i=i, r=r, hp=hp, pst=pst: e.transpose(
                    out=pst[:, hp * 128:hp * 128 + r], in_=attn_tok[0:r, i, hp * 128:(hp + 1) * 128], identity=identb[0:r, 0:r]),
                     reads=["attn_tok%d" % i, "identb"], writes=["ps%d" % Tb], sig=(hp == 3))
            S.op("dve", lambda e, i=i, r=r, pst=pst: e.tensor_copy(
                out=attnT[:, :, 128 * i:128 * i + r], in_=pst[:, 0:512].rearrange("p (h t) -> p h t", h=4)[:, :, 0:r]),
                 reads=["ps%d" % Tb], writes=["attnT%d" % i])
        KC = 256
        kc_tm = [M.at(R3 + i * 2048, (P, 2, 512), BF16, "kc_tm") for i in range(2)]
        vc_tm = [M.at(R3 + 4096 + i * 2048, (P, 2, 512), BF16, "vc_tm") for i in range(2)]
        KcT = [M.at(R3 + 8192 + i * 2048, (P, 2, 4, 128), BF16, "KcT") for i in range(2)]
        Psb = [M.at(R3 + 12288 + i * 512, (P, 2, H, 16), BF16, "Psb") for i in range(2)]
        Vw = [M.at(AUG + 8320 + i * 2112, (P, 2, H, 66), BF16, "Vw") for i in range(2)]
        attn_s = sm((16, 512), BF16, "attn_s")
        rsum_s = sm((16, 8), F32, "rsum_s")
        wts = sm((P, NTILE, 8), F32, "wts")
        Qz = sm((P, H, 16), BF16, "Qz")
        S.op("act", lambda e: e.activation(out=wts[:, :, :].rearrange("p i h -> p (i h)"), in_=negc_s[:, :, :].rearrange("p i h -> p (i h)"), func=AF.Exp),
             reads=["c_s_neg"], writes=["wts"])
        S.op("dve", lambda e: e.memset(Qz[:, :, :], 0.0), writes=["Qz"])
        for h in range(H):
            hp, hoff = h // 2, (h % 2) * 64
            S.op("dve", lambda e, h=h, hp=hp, hoff=hoff: e.tensor_copy(out=Qz[hoff:hoff + 64, h, :], in_=qT[hoff:hoff + 64, hp, L:NT]),
                 reads=["qT_all"], writes=["Qz"])

        def load_cache_chunk(c):
            j = c % 2
            S.dma("pool", "d_kc%d" % j, lambda e, c=c, j=j: e.dma_start(
                out=kc_tm[j][:, :, :], in_=cache_k[KC * c:KC * (c + 1), :].rearrange("(i p) d -> p i d", p=P)),
                  writes=["kc_tm%d" % j])
            S.dma("pool", "d_vc%d" % j, lambda e, c=c, j=j: e.dma_start(
                out=vc_tm[j][:, :, :], in_=cache_v[KC * c:KC * (c + 1), :].rearrange("(i p) d -> p i d", p=P)),
                  writes=["vc_tm%d" % j])
        load_cache_chunk(0)
        load_cache_chunk(1)
        OS = (6, 7)
        for ob in OS:
            S.op("dve", lambda e, ob=ob: e.memset(PS[ob][0:16, 0:260], 0.0), writes=["ps%d" % ob])
        NCH = PAST // KC

        def sA(c):
            j = c % 2
            Tk = next_ps(0, 2, "tk")
            pstk = PS[Tk][:, :].bitcast(BF16)
            for t in range(2):
                for hp in range(4):
                    S.op("pe", lambda e, t=t, hp=hp: e.transpose(
                        out=pstk[:, (t * 4 + hp) * 128:(t * 4 + hp + 1) * 128], in_=kc_tm[j][:, t, hp * 128:(hp + 1) * 128],
                        identity=identb[:, :]),
                         reads=["kc_tm%d" % j, "identb"], writes=["ps%d" % Tk], sig=(t == 1 and hp == 3))
            S.op("act", lambda e: e.activation(out=KcT[j][:, :, :, :].rearrange("p t h k -> p (t h k)"), in_=pstk[:, :], func=AF.Copy),
                 reads=["ps%d" % Tk], writes=["KcT%d" % j])
            for t in range(2):
                kt = 2 * c + t
                S.op("dve", lambda e, t=t, kt=kt: e.tensor_tensor(
                    out=Vw[j][:, t, :, 0:64], in0=vc_tm[j][:, t, :].rearrange("p (h d) -> p h d", h=H),
                    in1=wts[:, kt, :].unsqueeze(2).to_broadcast([P, H, 64]), op=ALU.mult),
                     reads=["vc_tm%d" % j, "wts"], writes=["Vw%d" % j])
                S.op("dve", lambda e, t=t, kt=kt: e.tensor_copy(out=Vw[j][:, t, :, 64], in_=wts[:, kt, :]),
                     reads=["wts"], writes=["Vw%d" % j])
            Sb = next_ps(2, 4, "sb")
            for t in range(2):
                for h in range(H):
                    hp = h // 2
                    col = (t * H + h) * 16
                    S.op("pe", lambda e, col=col, t=t, hp=hp, h=h: e.matmul(
                        PS[Sb][:, col:col + 16], lhsT=KcT[j][:, t, hp, :], rhs=Qz[:, h, :], start=True, stop=True),
                         reads=["KcT%d" % j, "Qz"], writes=["ps%d" % Sb], sig=(t == 1 and h == H - 1))
            S.op("act", lambda e: e.activation(out=Psb[j][:, :, :, :].rearrange("p t h q -> p (t h q)"), in_=PS[Sb][:, 0:256],
                                               func=AF.Exp, scale=0.125),
                 reads=["ps%d" % Sb], writes=["Psb%d" % j])

        def sB(c):
            j = c % 2
            for t in range(2):
                for h in range(H):
                    ob = OS[h // 4]
                    oc = (h % 4) * 65
                    S.op("pe", lambda e, ob=ob, oc=oc, t=t, h=h: e.matmul(
                        PS[ob][0:16, oc:oc + 65], lhsT=Psb[j][:, t, h, :], rhs=Vw[j][:, t, h, 0:65],
                        start=False, stop=False, skip_group_check=True),
                         reads=["Psb%d" % j, "Vw%d" % j], writes=["ps%d" % ob], sig=(t == 1 and h == H - 1))
            if c + 2 < NCH:
                load_cache_chunk(c + 2)
        sA(0)
        for c in range(NCH):
            if c + 1 < NCH:
                sA(c + 1)
            sB(c)
        Sb = next_ps(2, 4, "sb")
        Pn = sm((16, H, 16), BF16, "Pn")
        Vsw = sm((16, H, 66), BF16, "Vsw")
        for h in range(H):
            hp = h // 2
            S.op("pe", lambda e, h=h, hp=hp: e.matmul(
                PS[Sb][0:16, 16 * h:16 * h + 16], lhsT=kT[:, hp, L:NT], rhs=Qz[:, h, :], start=True, stop=False),
                 reads=["Qz"], writes=["ps%d" % Sb], sig=False)
            S.op("pe", lambda e, h=h: e.matmul(
                PS[Sb][0:16, 16 * h:16 * h + 16], lhsT=identb[0:16, 0:16], rhs=maskneg[0:16, 0:16], start=False, stop=True),
                 reads=["identb", "maskneg"], writes=["ps%d" % Sb], sig=(h == H - 1))
        S.op("act", lambda e: e.activation(out=Pn[:, :, :].rearrange("p h q -> p (h q)"), in_=PS[Sb][0:16, 0:128], func=AF.Exp, scale=0.125),
             reads=["ps%d" % Sb], writes=["Pn"])
        S.op("dve", lambda e: e.tensor_tensor(out=Vsw[:, :, 0:65], in0=Vsn[:, :, 0:65],
                                              in1=wts[0:16, 16, :].unsqueeze(2).to_broadcast([16, H, 65]), op=ALU.mult),
             reads=["Vsn", "Vsn_ones", "wts"], writes=["Vsw"])
        for h in range(H):
            ob = OS[h // 4]
            oc = (h % 4) * 65
            S.op("pe", lambda e, ob=ob, oc=oc, h=h: e.matmul(
                PS[ob][0:16, oc:oc + 65], lhsT=Pn[:, h, :], rhs=Vsw[:, h, 0:65], start=False, stop=True, skip_group_check=True),
                 reads=["Pn", "Vsw"], writes=["ps%d" % ob], sig=(h % 4 == 3))
        for h in range(H):
            ob = OS[h // 4]
            oc = (h % 4) * 65
            S.op("dve", lambda e, ob=ob, oc=oc, h=h: e.reciprocal(out=rsum_s[:, h:h + 1], in_=PS[ob][0:16, oc + 64:oc + 65]),
                 reads=["ps%d" % ob], writes=["rsum_s"])
            S.op("dve", lambda e, ob=ob, oc=oc, h=h: e.tensor_scalar(
                out=attn_s[:, h * 64:(h + 1) * 64], in0=PS[ob][0:16, oc:oc + 64], scalar1=rsum_s[:, h:h + 1], scalar2=None, op0=ALU.mult),
                 reads=["ps%d" % ob, "rsum_s"], writes=["attn_s"])

        Tb = next_ps(0, 4, "ta")
        pst = PS[Tb][:, :].bitcast(BF16)
        for hp in range(4):
            S.op("pe", lambda e, hp=hp, pst=pst: e.transpose(
                out=pst[:, hp * 128:hp * 128 + 16], in_=attn_s[0:16, hp * 128:(hp + 1) * 128], identity=identb[0:16, 0:16]),
                 reads=["attn_s", "identb"], writes=["ps%d" % Tb], sig=(hp == 3))
        S.op("dve", lambda e, pst=pst: e.tensor_copy(
            out=attnT[:, :, L:NT], in_=pst[:, 0:512].rearrange("p (h t) -> p h t", h=4)[:, :, 0:16]),
             reads=["ps%d" % Tb], writes=["attnT17"])
        if "attnT" in dbg_out:
            S.dma("sp", "d_dbg", lambda e: e.dma_start(out=dbg_out["attnT"][:, :, :], in_=attnT[:, :, :]),
                  reads=["attnT%d" % i for i in range(18)])

        chk(8)
        S.barrier()
        ZW = 2088
        zbuf = [M.at(R3 + i * (ZW * 4), (P, ZW), F32, "zbuf") for i in range(2)]
        convT = M.at(R3 + 16896, (P, 4, NT), BF16, "convT")
        tmpC = [M.at(R45 + 16640 + i * 2048, (P, 512), F32, "tmpC") for i in range(2)]
        ytmp = [M.at(R45 + 20736 + i * 2048, (P, 512), F32, "ytmp") for i in range(2)]
        tmpB = [M.at(AUG + 4160 + i * 2048, (P, 512), F32, "tmpB") for i in range(2)]
        stc_sb = sm((P, 8), F32, "stc_sb")
        zlast = sm((P, 4, 4), F32, "zlast")
        S.dma("sp", "d_stc", lambda e: e.dma_start(out=stc_sb[:, :], in_=stc[:, :]), writes=["stc_sb"])
        cc3 = {"n": 0}
        for c in range(4):
            wX, kX = w_get(G_CV[c])
            zb = zbuf[c % 2]
            zk = "zbuf%d" % (c % 2)
            S.op("dve", lambda e, zb=zb: e.memset(zb[:, 0:2], 0.0), writes=[zk])
            S.op("dve", lambda e, zb=zb, c=c: e.tensor_copy(out=zb[:, 2066:2068], in_=stc_sb[:, 2 * c:2 * c + 2]), reads=["stc_sb"], writes=[zk])
            for (c0, n) in BLKS:
                jj = cc3["n"] % 2
                cc3["n"] += 1
                banks = []
                for j3 in range(3):
                    A = next_ps()
                    banks.append(A)
                    for k in range(KD):
                        S.op("pe", lambda e, A=A, k=k, j3=j3, c0=c0, n=n, wX=wX: e.matmul(
                            PS[A][:, 0:n], lhsT=wX[:, k, 128 * j3:128 * (j3 + 1)], rhs=xnT[:, k, c0:c0 + n],
                            start=(k == 0), stop=(k == KD - 1)), reads=[kX], writes=["ps%d" % A], sig=(k == KD - 1))
                bB, bC, bH = banks
                S.op("act", lambda e, bC=bC, jj=jj, n=n: e.activation(out=tmpC[jj][:, 0:n], in_=PS[bC][:, 0:n], func=AF.Copy),
                     reads=["ps%d" % bC], writes=["tmpC%d" % jj])
                S.op("act", lambda e, bB=bB, jj=jj, n=n: e.activation(out=tmpB[jj][:, 0:n], in_=PS[bB][:, 0:n], func=AF.Copy),
                     reads=["ps%d" % bB], writes=["tmpB%d" % jj])
                segs = [(0, n, c0 + 2)] if n == 512 else [(0, 16, 2050), (16, 16, 2068)]
                for (so, sn, zc) in segs:
                    S.op("dve", lambda e, bH=bH, jj=jj, so=so, sn=sn, zc=zc, zb=zb: e.tensor_tensor(
                        out=zb[:, zc:zc + sn], in0=tmpC[jj][:, so:so + sn], in1=PS[bH][:, so:so + sn], op=ALU.mult),
                         reads=["tmpC%d" % jj, "ps%d" % bH], writes=[zk])
                for (so, sn, zc) in segs:
                    S.op("dve", lambda e, jj=jj, so=so, sn=sn, zc=zc, zb=zb, c=c: e.tensor_scalar(
                        out=ytmp[jj][:, so:so + sn], in0=zb[:, zc:zc + sn], scalar1=pk[:, 24 + 3 * c + 2:24 + 3 * c + 3],
                        scalar2=pk[:, 36 + c:37 + c], op0=ALU.mult, op1=ALU.add),
                         reads=[zk, "pk"], writes=["ytmp%d" % jj])
                    for tap, sh in ((1, 1), (0, 2)):
                        S.op("dve", lambda e, jj=jj, so=so, sn=sn, zc=zc, zb=zb, c=c, tap=tap, sh=sh: e.scalar_tensor_tensor(
                            out=ytmp[jj][:, so:so + sn], in0=zb[:, zc - sh:zc - sh + sn], scalar=pk[:, 24 + 3 * c + tap:24 + 3 * c + tap + 1],
                            in1=ytmp[jj][:, so:so + sn], op0=ALU.mult, op1=ALU.add),
                             reads=[zk, "pk", "ytmp%d" % jj], writes=["ytmp%d" % jj])
                S.op("dve", lambda e, jj=jj, n=n, c=c, c0=c0: e.tensor_tensor(
                    out=convT[:, c, c0:c0 + n], in0=tmpB[jj][:, 0:n], in1=ytmp[jj][:, 0:n], op=ALU.mult),
                     reads=["tmpB%d" % jj, "ytmp%d" % jj], writes=["convT"])
            S.op("dve", lambda e, zb=zb, c=c: e.tensor_copy(out=zlast[:, c, 0:2], in_=zb[:, 2064:2066]), reads=[zk], writes=["zlast"])
            S.op("dve", lambda e, zb=zb, c=c: e.tensor_copy(out=zlast[:, c, 2:4], in_=zb[:, 2082:2084]), reads=[zk], writes=["zlast"])
            w_done(G_CV[c])
        with nc.allow_non_contiguous_dma(reason="tiny transposed conv-state rows"):
            for c in range(4):
                S.dma("sp", "d_ncp", lambda e, c=c: e.dma_start(
                    out=nc_p[:, 128 * c:128 * (c + 1)].rearrange("r p -> p r"), in_=zlast[:, c, 0:2], allow_slow_non_contiguous=True),
                      reads=["zlast"], writes=["nc_p%d" % c])
                S.dma("sp", "d_ncs", lambda e, c=c: e.dma_start(
                    out=nc_s[:, 128 * c:128 * (c + 1)].rearrange("r p -> p r"), in_=zlast[:, c, 2:4], allow_slow_non_contiguous=True),
                      reads=["zlast"], writes=["nc_s%d" % c])

        chk(9)
        mergedT = M.at(R2, (P, KD, NT), BF16, "mergedT")
        gt = [[M.at(R45 + 24832 + (i * 4 + q_) * 2048, (P, 512), F32, "gt") for q_ in range(4)] for i in range(2)]
        g4 = {"n": 0}
        wbrc, kbrc = w_get(G_BRC)
        wbra, kbra = w_get(G_BRA)
        for pr in range(4):
            wgp, kgp = w_get(G_GP[pr])
            for jj2 in range(2):
                j = 2 * pr + jj2
                for (c0, n) in BLKS:
                    si = g4["n"] % 2
                    g4["n"] += 1
                    sA, sB, t1, t2 = gt[si]
                    ba = next_ps(); bb = next_ps(); bc = next_ps(); bd = next_ps()
                    for (bank, wv, wk, src, nk, col0) in ((ba, wgp, kgp, xnT, KD, jj2 * 128), (bb, wgp, kgp, xnT, KD, 256 + jj2 * 128),
                                                        (bc, wbrc, kbrc, convT, 4, j * 128), (bd, wbra, kbra, attnT, 4, j * 128)):
                        for k in range(nk):
                            S.op("pe", lambda e, bank=bank, wv=wv, src=src, k=k, nk=nk, col0=col0, c0=c0, n=n: e.matmul(
                                PS[bank][:, 0:n], lhsT=wv[:, k, col0:col0 + 128], rhs=src[:, k, c0:c0 + n],
                                start=(k == 0), stop=(k == nk - 1)),
                                 reads=[wk, "convT"] + ["attnT%d" % i for i in range(18)], writes=["ps%d" % bank], sig=(k == nk - 1))
                    S.op("act", lambda e, ba=ba, sA=sA, n=n: e.activation(out=sA[:, 0:n], in_=PS[ba][:, 0:n], func=AF.Sigmoid),
                         reads=["ps%d" % ba], writes=["gtA%d" % si])
                    S.op("act", lambda e, bb=bb, sB=sB, n=n: e.activation(out=sB[:, 0:n], in_=PS[bb][:, 0:n], func=AF.Sigmoid),
                         reads=["ps%d" % bb], writes=["gtB%d" % si])
                    S.op("dve", lambda e, bc=bc, sA=sA, t1=t1, n=n: e.tensor_tensor(out=t1[:, 0:n], in0=sA[:, 0:n], in1=PS[bc][:, 0:n], op=ALU.mult),
                         reads=["ps%d" % bc, "gtA%d" % si], writes=["gt1%d" % si])
                    S.op("dve", lambda e, bd=bd, sB=sB, t2=t2, n=n: e.tensor_tensor(out=t2[:, 0:n], in0=sB[:, 0:n], in1=PS[bd][:, 0:n], op=ALU.mult),
                         reads=["ps%d" % bd, "gtB%d" % si], writes=["gt2%d" % si])
                    S.op("dve", lambda e, t1=t1, t2=t2, j=j, c0=c0, n=n: e.tensor_tensor(out=mergedT[:, j, c0:c0 + n], in0=t1[:, 0:n], in1=t2[:, 0:n], op=ALU.add),
                         reads=["gt1%d" % si, "gt2%d" % si], writes=["mergedT"])
            w_done(G_GP[pr])
        w_done(G_BRC); w_done(G_BRA)

        chk(10)
        S.barrier()
        yacc = M.at(R1, (P, 16, D), F32, "yacc")
        yacc16 = M.at(R45 + 33280, (P, D), F32, "yacc16")
        hnT = M.at(R45, (P, KD, NT), BF16, "hnT")
        hsb = [M.at(AUG + i * 2048, (P, D), BF16, "hs") for i in range(3)]
        junk2 = M.at(R45 + 41472, (P, D), BF16, "junk2")
        wo0, ko0 = w_get(G_OUT0)
        wo1, ko1 = w_get(G_OUT0 + 1)

        def ytile(i):
            return yacc[:, i, :] if i < 16 else yacc16[:, :]

        def s5A(i):
            r = tile_rows(i)
            c0 = 128 * i
            b = i % 3
            yt = ytile(i)
            for half, (wo, ko) in enumerate(((wo0, ko0), (wo1, ko1))):
                A = next_ps(0, 6, "w5")
                for k in range(KD):
                    S.op("pe", lambda e, A=A, k=k, wo=wo: e.matmul(
                        PS[A][0:r, :], lhsT=mergedT[:, k, c0:c0 + r], rhs=wo[:, k, :], start=(k == 0), stop=(k == KD - 1)),
                         reads=[ko, "mergedT"], writes=["ps%d" % A], sig=(k == KD - 1))
                S.op("dve", lambda e, A=A, half=half: e.tensor_tensor(
                    out=yt[0:r, half * 512:(half + 1) * 512], in0=yt[0:r, half * 512:(half + 1) * 512], in1=PS[A][0:r, :], op=ALU.add),
                     reads=["ps%d" % A, "yacc%d" % i], writes=["yacc%d" % i])
            rms_rstd(yt[0:r, :], r, i, 1.0 / D, "yacc%d" % i, "b", junk=junk2, jkey="junk2")

        def s5A2(i):
            r = tile_rows(i)
            b = i % 3
            yt = ytile(i)
            S.op("dve", lambda e: e.tensor_scalar(out=hsb[b][0:r, :], in0=yt[0:r, :], scalar1=rstd[0:r, i:i + 1],
                                                  scalar2=None, op0=ALU.mult),
                 reads=["yacc%d" % i, "rstdb%d" % i], writes=["hs%d" % b])

        def s5B(i):
            r = tile_rows(i)
            c0 = 128 * i
            b = i % 3
            Tb = next_ps(6, 8, "t5")
            pst = PS[Tb][:, :].bitcast(BF16)
            for k in range(KD):
                S.op("pe", lambda e, k=k: e.transpose(out=pst[:, k * 128:k * 128 + r], in_=hsb[b][0:r, k * 128:(k + 1) * 128],
                                                     identity=identb[0:r, 0:r]),
                     reads=["hs%d" % b, "identb"], writes=["ps%d" % Tb], sig=(k == KD - 1))
            S.op("dve", lambda e: e.tensor_tensor(
                out=hnT[:, :, c0:c0 + r], in0=pst.rearrange("p (k t) -> p k t", k=KD)[:, :, 0:r],
                in1=pk[:, 8:16].unsqueeze(2).to_broadcast([P, KD, r]), op=ALU.mult),
                 reads=["ps%d" % Tb, "pk"], writes=["hnT"])
        for i in range(NTILE):
            load_xtile(i, ytile(i), "yacc%d" % i, "d_xr%d" % i)
        s5A(0)
        s5A(1)
        s5A2(0)
        for i in range(NTILE):
            if i + 2 < NTILE:
                s5A(i + 2)
            if i + 1 < NTILE:
                s5A2(i + 1)
            s5B(i)
        w_done(G_OUT0); w_done(G_OUT0 + 1)

        chk(11)
        S.barrier()
        aTb = [M.at(R2 + i * 16640, (P, 4, NT), BF16, "aT") for i in range(2)]
        rtmp = [M.at(R45 + 37376 + i * 2048, (P, 512), F32, "rtmp") for i in range(2)]
        r6 = {"n": 0}
        for g in range(8):
            wu, ku = w_get(G_MLP + 2 * g)
            wd, kd = w_get(G_MLP + 2 * g + 1)
            aT = aTb[g % 2]
            ak = "aT%d" % (g % 2)
            for fc in range(4):
                for (c0, n) in BLKS:
                    A = next_ps()
                    for k in range(KD):
                        S.op("pe", lambda e, A=A, k=k, fc=fc, c0=c0, n=n, wu=wu: e.matmul(
                            PS[A][:, 0:n], lhsT=wu[:, k, fc * 128:(fc + 1) * 128], rhs=hnT[:, k, c0:c0 + n],
                            start=(k == 0), stop=(k == KD - 1)), reads=[ku, "hnT"], writes=["ps%d" % A], sig=(k == KD - 1))
                    rj = r6["n"] % 2
                    r6["n"] += 1
                    S.op("act", lambda e, A=A, n=n, rj=rj: e.activation(out=rtmp[rj][:, 0:n], in_=PS[A][:, 0:n], func=AF.Relu),
                         reads=["ps%d" % A], writes=["rtmp%d" % rj])
                    S.op("dve", lambda e, fc=fc, c0=c0, n=n, aT=aT, rj=rj: e.tensor_tensor(
                        out=aT[:, fc, c0:c0 + n], in0=rtmp[rj][:, 0:n], in1=rtmp[rj][:, 0:n], op=ALU.mult),
                         reads=["rtmp%d" % rj], writes=[ak])
            for i in range(NTILE):
                r = tile_rows(i)
                c0 = 128 * i
                yt = ytile(i)
                for half in range(2):
                    A = next_ps()
                    for fc in range(4):
                        S.op("pe", lambda e, A=A, fc=fc, r=r, c0=c0, half=half, aT=aT, wd=wd: e.matmul(
                            PS[A][0:r, :], lhsT=aT[:, fc, c0:c0 + r], rhs=wd[:, fc, half * 512:(half + 1) * 512],
                            start=(fc == 0), stop=(fc == 3)), reads=[kd, ak], writes=["ps%d" % A], sig=(fc == 3))
                    S.op("dve", lambda e, A=A, r=r, yt=yt, half=half: e.tensor_tensor(
                        out=yt[0:r, half * 512:(half + 1) * 512], in0=yt[0:r, half * 512:(half + 1) * 512], in1=PS[A][0:r, :], op=ALU.add),
                         reads=["ps%d" % A, "yacc%d" % i], writes=["yacc%d" % i])
                if g == 7:
                    if i == 0:
                        S.dma("sp", "d_y", lambda e: e.dma_start(out=y_prompt[0:112, :], in_=yacc[16:128, 0, :]), reads=["yacc0"], writes=["y_prompt0"])
                    elif i < 16:
                        S.dma("sp", "d_y", lambda e, i=i: e.dma_start(out=y_prompt[128 * i - 16:128 * i + 112, :], in_=yacc[:, i, :]),
                              reads=["yacc%d" % i], writes=["y_prompt%d" % i])
                    else:
                        S.dma("sp", "d_y", lambda e: e.dma_start(out=y_prompt[2032:2048, :], in_=yacc16[0:16, :]), reads=["yacc16"], writes=["y_prompt16"])
                        S.dma("sp", "d_y", lambda e: e.dma_start(out=y_sample[:, :], in_=yacc16[16:32, :]), reads=["yacc16"], writes=["y_sample"])
            w_done(G_MLP + 2 * g); w_done(G_MLP + 2 * g + 1)

        if "attn_tok" in dbg_out:
            S.dma("sp", "d_dbg", lambda e: e.dma_start(out=dbg_out["attn_tok"][:, :, :], in_=attn_tok[:, :, :]),
                  reads=["attn_tok%d" % i for i in range(NTILE)])
        if "cT24" in dbg_out:
            S.dma("sp", "d_dbg", lambda e: e.dma_start(out=dbg_out["cT24"][:, :], in_=cT24[:, :]), reads=["c_all_T"])
        if "negc" in dbg_out:
            S.dma("sp", "d_dbg", lambda e: e.dma_start(out=dbg_out["negc"][:, :, :], in_=negc[:, :, :]), reads=["c_all_neg"])
        if "qT" in dbg_out:
            S.dma("sp", "d_dbg", lambda e: e.dma_start(out=dbg_out["qT"][:, :, :], in_=qT[:, :, :]), reads=["qT_all"])
        if "c_all" in dbg_out:
            S.dma("sp", "d_dbg", lambda e: e.dma_start(out=dbg_out["c_all"][:, :, :], in_=c_all[:, :, :]), reads=["c_all"])


    except _Stop:
        pass

    S.finish("sp")
    S.emit()
    return nc


def make_in_maps(inputs):
    f = lambda a: np.ascontiguousarray(np.asarray(a, dtype=np.float32))
    x_prompt = f(inputs["x_prompt"]); x_sample = f(inputs["x_sample"])
    ck = f(inputs["cache_k"])[0]; cv = f(inputs["cache_v"])[0]; cl = f(inputs["cache_logf"])[0]
    sc = f(inputs["state_conv"])[0]
    pk = np.zeros((P, 64), np.float32)
    pk[:, 0:8] = f(inputs["norm1_g"])[0].reshape(8, P).T
    pk[:, 8:16] = f(inputs["norm2_g"])[0].reshape(8, P).T
    pk[:, 16:24] = np.broadcast_to(f(inputs["b_f"])[0][None, :], (P, 8))
    cw = f(inputs["conv_w"])[0]
    pk[:, 24:36] = cw.reshape(3, 4, P).transpose(2, 1, 0).reshape(P, 12)
    pk[:, 36:40] = f(inputs["conv_b"])[0].reshape(4, P).T
    pk[:, 40] = np.tile(f(inputs["q_norm_g"])[0], 2)
    pk[:, 41] = np.tile(f(inputs["k_norm_g"])[0], 2)
    maps = []
    for b in range(8):
        stt = np.ascontiguousarray(sc[b].reshape(2, 4, P).transpose(2, 1, 0).reshape(P, 8))
        maps.append({
            "x_prompt": x_prompt[b], "x_sample": x_sample[b],
            "cache_k": ck[b].reshape(PAST, 512), "cache_v": cv[b].reshape(PAST, 512),
            "cache_logf": cl[b], "meta": f(inputs["meta"]),
            "w_in": f(inputs["w_in"])[0], "w_br_conv": f(inputs["w_br_conv"])[0],
            "w_br_attn": f(inputs["w_br_attn"])[0], "w_out": f(inputs["w_out"])[0],
            "w_up": f(inputs["w_up"])[0], "w_down": f(inputs["w_down"])[0],
            "ppk": pk, "state_conv_t": stt,
        })
    return maps


_NC_CACHE = {}


def kernel(**inputs):
    maps = make_in_maps(inputs)
    if "nc" not in _NC_CACHE:
        _NC_CACHE["nc"] = build()
    nc = _NC_CACHE["nc"]
    res = run_bass_kernel_spmd(nc, maps, core_ids=list(range(8)))
    R = res.results
    st = lambda name: np.stack([np.asarray(R[b][name], dtype=np.float32) for b in range(8)])
    y_prompt = st("y_prompt")
    y_sample = st("y_sample")
    nk_p = st("nk_p").reshape(1, 8, L, H, HD)
    nv_p = st("nv_p").reshape(1, 8, L, H, HD)
    nf_p = st("nf_p").reshape(1, 8, L, H)
    nc_p = st("nc_p").reshape(1, 8, 2, DC)
    nk_s = st("nk_s").reshape(1, 8, NS, H, HD)
    nv_s = st("nv_s").reshape(1, 8, NS, H, HD)
    nf_s = st("nf_s").reshape(1, 8, NS, H)
    nc_s = st("nc_s").reshape(1, 8, 2, DC)
    return (y_prompt, y_sample, nk_p, nv_p, nf_p, nc_p, nk_s, nv_s, nf_s, nc_s)
```

```python
import os
import numpy as np
import concourse.bass as bass
import concourse.mybir as mybir
from concourse.bass_utils import run_bass_kernel_spmd

F32 = mybir.dt.float32
BF16 = mybir.dt.bfloat16
AF = mybir.ActivationFunctionType
ALU = mybir.AluOpType
AX = mybir.AxisListType

P = 128
D = 1024
KD = 8
SEQ = 2048
NMETA = 16
L = SEQ + NMETA
NS = 16
NT = L + NS
NTILE = 17
PAST = 2048
H = 8
HD = 64
DC = 512
DA = 512
DFF = 4096
INC = 5128
EPS = 1e-6
C_B, C_C, C_H, C_Q, C_K, C_V, C_F, C_G = 0, 512, 1024, 1536, 2048, 2560, 3072, 3080
BLKS = [(0, 512), (512, 512), (1024, 512), (1536, 512), (2048, 32)]
BLKS_EQ = [(416 * i, 416) for i in range(5)]


def tile_rows(i):
    return 128 if i < 16 else 32


class Sched:
    ENG = ("pe", "act", "dve", "pool", "sp")

    def __init__(self, nc):
        self.nc = nc
        self.q = {e: [] for e in self.ENG}
        self.cnt = {}
        self.sems = {}
        self.lastw = {}
        self.lastr = {}
        self.seen = {e: {} for e in self.ENG}
        self.pending = {e: {} for e in self.ENG}
        for e in ("pe", "act", "dve", "pool"):
            self._sem(e)

    def _sem(self, name):
        if name not in self.sems:
            self.sems[name] = self.nc.alloc_semaphore("s_" + name)
            self.cnt[name] = 0
        return self.sems[name]

    def _deps(self, eng, reads, writes):
        deps = dict(self.pending[eng])
        self.pending[eng] = {}

        def merge(src, raw):
            for s, v in src.items():
                if s == eng and eng == "pe":
                    continue
                if deps.get(s, 0) < v:
                    deps[s] = v
        for k in reads:
            merge(self.lastw.get(k, {}), True)
        for k in writes:
            merge(self.lastw.get(k, {}), False)
            merge(self.lastr.get(k, {}), False)
        out = []
        seen = self.seen[eng]
        for s, v in deps.items():
            if seen.get(s, 0) < v:
                seen[s] = v
                out.append((s, v))
        return out

    def _record(self, s, v, reads, writes):
        for k in reads:
            d = self.lastr.setdefault(k, {})
            if d.get(s, 0) < v:
                d[s] = v
        for k in writes:
            d = self.lastw.setdefault(k, {})
            if d.get(s, 0) < v:
                d[s] = v

    def op(self, eng, fn, reads=(), writes=(), sig=True):
        waits = self._deps(eng, reads, writes)
        if sig:
            self.cnt[eng] += 1
            v = self.cnt[eng]
            inc = (eng, 1)
        else:
            v = self.cnt[eng] + 1
            inc = None
        self._record(eng, v, reads, writes)
        self.q[eng].append((waits, fn, inc))

    def dma(self, queue, sem, fn, reads=(), writes=()):
        self._sem(sem)
        waits = self._deps(queue, reads, writes)
        self.cnt[sem] += 16
        self._record(sem, self.cnt[sem], reads, writes)
        self.q[queue].append((waits, fn, (sem, 16)))

    def barrier(self, engines=("pe", "act", "dve", "sp"), exclude=()):
        snap = {s: v for s, v in self.cnt.items() if v > 0 and s not in exclude
                and not s.startswith(("d_ring", "d_kc", "d_vc", "d_wfl"))}
        for e in engines:
            for s, v in snap.items():
                if s == e and e == "pe":
                    continue
                if self.pending[e].get(s, 0) < v:
                    self.pending[e][s] = v

    def finish(self, eng="sp"):
        waits = []
        for s, v in self.cnt.items():
            if v > 0 and self.seen[eng].get(s, 0) < v:
                waits.append((s, v))
        self.q[eng].append((waits, None, None))

    def replay(self, name, e):
        for waits, fn, inc in self.q[name]:
            for s, v in waits:
                e.wait_ge(self.sems[s], v)
            if fn is None:
                continue
            ins = fn(e)
            if inc is not None:
                ins.then_inc(self.sems[inc[0]], inc[1])

    def emit(self):
        nc = self.nc
        with nc.Block() as block:
            @block.tensor
            def _(e):
                self.replay("pe", e)

            @block.scalar
            def _(e):
                self.replay("act", e)

            @block.vector
            def _(e):
                self.replay("dve", e)

            @block.gpsimd
            def _(e):
                self.replay("pool", e)

            @block.sync
            def _(e):
                self.replay("sp", e)


class Mem:
    def __init__(self, nc):
        self.nc = nc
        self.base = 16512
        self.top = 229344
        self.n = 0

    def at(self, off, shape, dtype, name):
        self.n += 1
        nb = int(np.prod(shape[1:])) * (4 if dtype == F32 else 2)
        assert off % 32 == 0, (name, off)
        assert self.base <= off and off + nb <= self.top, (name, off, nb, self.top)
        return self.nc.alloc_sbuf_tensor_at("%s_%d" % (name, self.n), list(shape), dtype, offset=off)


def build(dbg=None, stop_after=99):
    dbg = dbg or []
    nc = bass.Bass("TRN2", target_bir_lowering=False)
    S = Sched(nc)
    M = Mem(nc)

    def din(name, shape):
        return nc.dram_tensor(name, list(shape), F32, kind="ExternalInput")

    def dout(name, shape):
        return nc.dram_tensor(name, list(shape), F32, kind="ExternalOutput")

    x_prompt = din("x_prompt", (SEQ, D))
    x_sample = din("x_sample", (NS, D))
    cache_k = din("cache_k", (PAST, 512))
    cache_v = din("cache_v", (PAST, 512))
    cache_logf = din("cache_logf", (PAST, H))
    meta = din("meta", (NMETA, D))
    w_in = din("w_in", (D, INC))
    w_br_conv = din("w_br_conv", (DC, D))
    w_br_attn = din("w_br_attn", (DA, D))
    w_out = din("w_out", (D, D))
    w_up = din("w_up", (D, DFF))
    w_down = din("w_down", (DFF, D))
    NPK = 64
    ppk = din("ppk", (P, NPK))
    stc = din("state_conv_t", (P, 8))

    y_prompt = dout("y_prompt", (SEQ, D))
    y_sample = dout("y_sample", (NS, D))
    nk_p = dout("nk_p", (L, 512))
    nv_p = dout("nv_p", (L, 512))
    nf_p = dout("nf_p", (L, H))
    nc_p = dout("nc_p", (2, DC))
    nk_s = dout("nk_s", (NS, 512))
    nv_s = dout("nv_s", (NS, 512))
    nf_s = dout("nf_s", (NS, H))
    nc_s = dout("nc_s", (2, DC))
    dbg_out = {}
    for (name, shape, dt_) in dbg:
        dbg_out[name] = nc.dram_tensor("dbg_" + name, list(shape), dt_, kind="ExternalOutput")

    if os.environ.get("PAIR_EXP", "1") == "1":
        PS2 = [nc.alloc_psum_tensor("ps2_%d" % i, [P, 1024], F32) for i in range(4)]
        PS = [PS2[i // 2][:, (i % 2) * 512:(i % 2 + 1) * 512] for i in range(8)]
    else:
        PS2 = None
        PS = [nc.alloc_psum_tensor("ps%d" % i, [P, 512], F32) for i in range(8)]

    o = M.base
    RING_SLOTS = 5
    ring = [M.at(o + i * 8192, (P, 4096), BF16, "ring") for i in range(RING_SLOTS)]
    o += RING_SLOTS * 8192
    pk = M.at(o, (P, NPK), F32, "pk"); o += NPK * 4
    identb = M.at(o, (P, P), BF16, "identb"); o += 256
    identf = M.at(o, (P, P), F32, "identf"); o += 512
    small = [o]
    o += 13312
    def sm(shape, dtype, name):
        nb = int(np.prod(shape[1:])) * (4 if dtype == F32 else 2)
        nb = (nb + 31) // 32 * 32
        t = M.at(small[0], shape, dtype, name)
        small[0] += nb
        assert small[0] <= o_small_end
        return t
    o_small_end = o
    AUG = o; o += 12544
    R2 = o; o += 33280
    R1 = o; o += 33280
    R3 = o; o += 34304
    R45 = o
    R45_SIZE = M.top - o
    assert R45_SIZE >= 41472, R45_SIZE

    xnT = M.at(R1, (P, KD, NT), BF16, "xnT")
    NXT = 6
    xt = [M.at(R3 + i * 4096, (P, D), F32, "xt") for i in range(2)] + \
         [M.at(R45 + 25600 + i * 4096, (P, D), F32, "xt") for i in range(4)]
    xs = [M.at(R3 + 8192 + i * 2048, (P, D), BF16, "xs") for i in range(2)] + [M.at(R45 + 41984, (P, D), BF16, "xs")]
    junk = M.at(R3 + 12288, (P, D), BF16, "junk")
    ssq = sm((P, 32), F32, "ssq")
    rstd = sm((P, 32), F32, "rstd")

    S.dma("sp", "d_pk", lambda e: e.dma_start(out=pk[:, :], in_=ppk[:, :]), writes=["pk"])
    def mk_ident(t, key):
        S.op("pool", lambda e: e.memset(t[:, :], 1.0), writes=[key])
        S.op("pool", lambda e: e.affine_select(t[:, :], t[:, :], [[-1, P]], ALU.is_equal, 0.0,
                                               base=0, channel_multiplier=1), reads=[key], writes=[key])
    mk_ident(identb, "identb")
    mk_ident(identf, "identf")

    epsc = sm((P, 1), F32, "epsc")
    S.op("pool", lambda e: e.memset(epsc[:, :], EPS), writes=["epsc"])

    def load_xtile(i, buf, key, sem):
        if i == 0:
            S.dma("sp", sem, lambda e: e.dma_start(out=buf[0:16, :], in_=meta[:, :]), writes=[key])
            S.dma("sp", sem, lambda e: e.dma_start(out=buf[16:128, :], in_=x_prompt[0:112, :]), writes=[key])
        elif i < 16:
            S.dma("sp", sem, lambda e: e.dma_start(out=buf[:, :], in_=x_prompt[128 * i - 16:128 * i + 112, :]), writes=[key])
        else:
            S.dma("sp", sem, lambda e: e.dma_start(out=buf[0:16, :], in_=x_prompt[2032:2048, :]), writes=[key])
            S.dma("sp", sem, lambda e: e.dma_start(out=buf[16:32, :], in_=x_sample[:, :]), writes=[key])

    def rms_rstd(src, r, col, inv_n, kin, tagk, junk=junk, jkey="junk"):
        S.op("act", lambda e: e.activation(out=junk[0:r, :], in_=src, func=AF.Square,
                                           accum_out=ssq[0:r, col:col + 1]),
             reads=[kin, "epsc"], writes=[jkey, "ssq%s%d" % (tagk, col)])
        S.op("act", lambda e: e.activation(out=ssq[0:r, col:col + 1], in_=ssq[0:r, col:col + 1], func=AF.Ln,
                                           bias=epsc[0:r, :], scale=inv_n),
             reads=["ssq%s%d" % (tagk, col)], writes=["ssq%s%d" % (tagk, col)])
        S.op("act", lambda e: e.activation(out=rstd[0:r, col:col + 1], in_=ssq[0:r, col:col + 1], func=AF.Exp,
                                           scale=-0.5),
             reads=["ssq%s%d" % (tagk, col)], writes=["rstd%s%d" % (tagk, col)])

    for i in range(NTILE):
        r = tile_rows(i)
        b = i % NXT
        b3 = i % 3
        kx, ks = "xt%d" % b, "xs%d" % b3
        load_xtile(i, xt[b], kx, "d_xt%d" % b)
        rms_rstd(xt[b][0:r, :], r, i, 1.0 / D, kx, "a")
        S.op("dve", lambda e, b=b, b3=b3, r=r, i=i: e.tensor_scalar(out=xs[b3][0:r, :], in0=xt[b][0:r, :],
                                                          scalar1=rstd[0:r, i:i + 1], scalar2=None, op0=ALU.mult),
             reads=[kx, "rstda%d" % i], writes=[ks])
        pb = i % 2
        pst = PS[pb][:, :].bitcast(BF16)
        for k in range(KD):
            S.op("pe", lambda e, k=k, b3=b3, r=r, pst=pst: e.transpose(out=pst[:, k * 128:k * 128 + r],
                                                                    in_=xs[b3][0:r, k * 128:(k + 1) * 128],
                                                                    identity=identb[0:r, 0:r]),
                 reads=[ks, "identb"], writes=["ps%d" % pb], sig=(k == KD - 1))
        c0 = 128 * i
        S.op("dve", lambda e, pst=pst, r=r, c0=c0: e.tensor_tensor(
            out=xnT[:, :, c0:c0 + r],
            in0=pst.rearrange("p (k t) -> p k t", k=KD)[:, :, 0:r],
            in1=pk[:, 0:KD].unsqueeze(2).to_broadcast([P, KD, r]), op=ALU.mult),
             reads=["ps%d" % pb, "pk"], writes=["xnT%d" % i])

    if "xnT" in dbg_out:
        S.dma("sp", "d_dbg", lambda e: e.dma_start(out=dbg_out["xnT"][:, :, :], in_=xnT[:, :, :]),
              reads=["xnT%d" % i for i in range(NTILE)])

    class _Stop(Exception):
        pass

    def chk(st):
        if stop_after < st:
            raise _Stop()

    try:
        XN_ALL = ["xnT%d" % i for i in range(NTILE)]

        def xn_keys(c0, n):
            return ["xnT%d" % i for i in range(c0 // 128, (c0 + n - 1) // 128 + 1)]

        wlist = []

        def wg_cols(w, c0, kch=KD, ncol=512):
            return (lambda t: t[:, 0:kch * ncol].rearrange("p (k c) -> p k c", k=kch),
                    w[0:kch * 128, c0:c0 + ncol].rearrange("(k p) c -> p k c", p=P))

        G_Q, G_K, G_V = 0, 1, 2
        wlist.append(wg_cols(w_in, C_Q))
        wlist.append(wg_cols(w_in, C_K))
        wlist.append(wg_cols(w_in, C_V))
        G_CV = [3, 4, 5, 6]
        for cch in range(4):
            wlist.append((lambda t: t[:, 0:KD * 384].rearrange("p (k c) -> p k c", k=KD),
                          [((128 * j3, 128 * (j3 + 1)),
                            w_in[:, base + 128 * cch:base + 128 * (cch + 1)].rearrange("(k p) c -> p k c", p=P))
                           for j3, base in enumerate((C_B, C_C, C_H))]))
        G_BRC, G_BRA = 7, 8
        G_GP = [9, 10, 11, 12]
        wlist.append(wg_cols(w_br_conv, 0, kch=4, ncol=1024))
        wlist.append(wg_cols(w_br_attn, 0, kch=4, ncol=1024))
        for pr in range(4):
            wlist.append((lambda t: t[:, 0:KD * 512].rearrange("p (k c) -> p k c", k=KD),
                          [((0, 256), w_in[:, C_G + 256 * pr:C_G + 256 * (pr + 1)].rearrange("(k p) c -> p k c", p=P)),
                           ((256, 512), w_in[:, C_G + 1024 + 256 * pr:C_G + 1024 + 256 * (pr + 1)].rearrange("(k p) c -> p k c", p=P))]))
        G_OUT0 = 13
        wlist.append(wg_cols(w_out, 0))
        wlist.append(wg_cols(w_out, 512))
        G_MLP = 15
        for g in range(8):
            wlist.append(wg_cols(w_up, 512 * g))
            wlist.append((lambda t: t[:, :].rearrange("p (k c) -> p k c", k=4),
                          w_down[512 * g:512 * (g + 1), :].rearrange("(k p) c -> p k c", p=P)))
        wstate = {"issued": 0, "free": list(range(RING_SLOTS)), "slot": {}}

        def w_try_issue(limit=None, after=()):
            while wstate["issued"] < len(wlist) and wstate["free"] and (limit is None or wstate["issued"] < limit):
                g = wstate["issued"]
                slot = wstate["free"].pop(0)
                wstate["slot"][g] = slot
                vf, srcs = wlist[g]
                dst = vf(ring[slot])
                if not isinstance(srcs, list):
                    srcs = [(None, srcs)]
                for (sub, src) in srcs:
                    d = dst if sub is None else dst[:, :, sub[0]:sub[1]]
                    S.dma("pool", "d_ring%d" % slot, lambda e, d=d, src=src: e.dma_start(out=d, in_=src),
                          reads=list(after), writes=["ring%d" % slot])
                wstate["issued"] += 1

        def w_get(g):
            if g not in wstate["slot"]:
                w_try_issue(g + 1)
            slot = wstate["slot"][g]
            return wlist[g][0](ring[slot]), "ring%d" % slot

        def w_done(g):
            wstate["free"].append(wstate["slot"][g])
            w_try_issue()

        w_try_issue(1)
        w_try_issue(2, after=["xt%d" % (7 % NXT)])
        w_try_issue(3, after=["xt%d" % (12 % NXT)])

        blockones = sm((P, P), BF16, "blockones")
        S.op("pool", lambda e: e.memset(blockones[:, :], 0.0), writes=["blockones"])
        S.op("pool", lambda e: e.memset(blockones[0:64, 0:64], 1.0), writes=["blockones"])
        S.op("pool", lambda e: e.memset(blockones[64:128, 64:128], 1.0), writes=["blockones"])
        onesf = sm((P, P), F32, "onesf")
        S.op("pool", lambda e: e.memset(onesf[:, :], 1.0), writes=["onesf"])
        trif = sm((P, P), F32, "trif")
        S.op("pool", lambda e: e.memset(trif[:, :], 1.0), writes=["trif"])
        S.op("pool", lambda e: e.affine_select(trif[:, :], trif[:, :], [[1, P]], ALU.is_ge, 0.0,
                                               base=0, channel_multiplier=-1), reads=["trif"], writes=["trif"])
        maskb = sm((P, P), BF16, "maskb")
        S.op("pool", lambda e: e.memset(maskb[:, :], 1.0), writes=["maskb"])
        S.op("pool", lambda e: e.affine_select(maskb[:, :], maskb[:, :], [[1, P]], ALU.is_ge, 0.0,
                                               base=0, channel_multiplier=-1), reads=["maskb"], writes=["maskb"])
        maskneg = sm((P, P), BF16, "maskneg")
        S.op("pool", lambda e: e.memset(maskneg[:, :], 0.0), writes=["maskneg"])
        S.op("pool", lambda e: e.affine_select(maskneg[:, :], maskneg[:, :], [[1, P]], ALU.is_ge, -9984.0,
                                               base=0, channel_multiplier=-1), reads=["maskneg"], writes=["maskneg"])
        ones3 = sm((3, P), BF16, "ones3")
        S.op("pool", lambda e: e.memset(ones3[:, :], 1.0), writes=["ones3"])
        onecol = sm((P, 1), F32, "onecol")
        S.op("pool", lambda e: e.memset(onecol[:, :], 1.0), writes=["onecol"])
        wfl = sm((P, KD, 8), BF16, "wfl")
        if not os.environ.get("SKIP_WFL"):
          S.dma("pool", "d_wfl", lambda e: e.dma_start(out=wfl[:, :, :],
                                                     in_=w_in[:, C_F:C_F + 8].rearrange("(k p) c -> p k c", p=P)),
              writes=["wfl"])
        fl_all = sm((P, NTILE, 8), F32, "fl_all")
        lf_all = sm((P, NTILE, 8), F32, "lf_all")
        S.op("pool", lambda e: e.memset(fl_all[:, :, :], 0.0), writes=["fl_all"])
        S.op("pool", lambda e: e.memset(lf_all[:, :, :], 0.0), writes=["lf_all"])
        c_all = sm((P, NTILE, 8), F32, "c_all")
        S.op("pool", lambda e: e.memset(c_all[:, :, :], 0.0), writes=["c_all"])
        negc = sm((P, NTILE, 8), F32, "negc")
        csplit = sm((P, NTILE, 3, 8), BF16, "csplit")
        cres = sm((P, NTILE, 8), F32, "cres")
        cT24 = M.at(AUG, (24, NT), BF16, "cT24")
        qaug = [M.at(AUG + 4160 + i * 4160, (3, NT), BF16, "qaug") for i in range(2)]

        qT = M.at(R2, (P, 4, NT), BF16, "qT")
        kT = M.at(R2 + 16640, (P, 4, NT), BF16, "kT")
        Vp = M.at(R3 + 14336, (P, NTILE, H, 66), BF16, "Vp")
        sqb = [M.at(R45 + i * 1024, (P, 512), BF16, "sqb") for i in range(2)]
        rsb = [M.at(R45 + 2048 + i * 2048, (P, 512), F32, "rsb") for i in range(2)]
        kf = M.at(R45 + 6144, (P, 4, 512), F32, "kf")
        ktok = [M.at(R45 + 14336 + i * 2048, (P, 512), F32, "ktok") for i in range(2)]
        vtok = [M.at(R45 + 18432 + i * 2048, (P, 512), F32, "vtok") for i in range(2)]

        psn = {"n": 0}

        def next_ps(lo=0, hi=8, key="n"):
            psn[key] = psn.get(key, lo - 1) + 1
            if psn[key] >= hi or psn[key] < lo:
                psn[key] = lo
            return psn[key]

        if not os.environ.get("SKIP_VPMEM"):
            S.op("pool", lambda e: e.memset(Vp[:, :, :, 64:65], 1.0), writes=["Vp_ones"])

        chk(1)
        sqb3 = [M.at(R45 + 22528 + i * 1024, (P, 512), BF16, "sqb3") for i in range(3)]
        cnt1 = {"kt": 0}
        units1 = []
        for which, G in (("q", G_Q), ("k", G_K)):
            for (c0, n) in BLKS:
                for m in range(4):
                    units1.append(dict(which=which, G=G, c0=c0, n=n, m=m, idx=len(units1)))

        def s1A(u):
            which, G, c0, n, m = u["which"], u["G"], u["c0"], u["n"], u["m"]
            wv, wkey = w_get(G)
            A = next_ps(0, 4, "qa")
            sj = u["idx"] % 3
            u["A"], u["sj"] = A, sj
            for k in range(KD):
                S.op("pe", lambda e, k=k: e.matmul(
                    PS[A][:, 0:n], lhsT=wv[:, k, m * 128:(m + 1) * 128], rhs=xnT[:, k, c0:c0 + n],
                    start=(k == 0), stop=(k == KD - 1)),
                     reads=[wkey] + xn_keys(c0, n), writes=["ps%d" % A], sig=(k == KD - 1))
            S.op("act", lambda e: e.activation(out=sqb3[sj][:, 0:n], in_=PS[A][:, 0:n], func=AF.Square),
                 reads=["ps%d" % A], writes=["sqb%d" % sj])

        def s1B(u):
            which, G, c0, n, m = u["which"], u["G"], u["c0"], u["n"], u["m"]
            A, sj = u["A"], u["sj"]
            j = u["idx"] % 2
            B = next_ps(4, 6, "qb")
            S.op("pe", lambda e: e.matmul(PS[B][:, 0:n], lhsT=blockones[:, :], rhs=sqb3[sj][:, 0:n], start=True, stop=True),
                 reads=["sqb%d" % sj, "blockones"], writes=["ps%d" % B])
            S.op("act", lambda e: e.activation(out=rsb[j][:, 0:n], in_=PS[B][:, 0:n], func=AF.Ln,
                                               bias=epsc[:, :], scale=1.0 / HD),
                 reads=["ps%d" % B, "epsc"], writes=["rsb%d" % j])
            S.op("act", lambda e: e.activation(out=rsb[j][:, 0:n], in_=rsb[j][:, 0:n], func=AF.Exp, scale=-0.5),
                 reads=["rsb%d" % j], writes=["rsb%d" % j])
            if which == "q":
                S.op("dve", lambda e: e.scalar_tensor_tensor(
                    out=qT[:, m, c0:c0 + n], in0=PS[A][:, 0:n], scalar=pk[:, 40:41], in1=rsb[j][:, 0:n],
                    op0=ALU.mult, op1=ALU.mult),
                     reads=["ps%d" % A, "rsb%d" % j, "pk"], writes=["qT%d_%d" % (m, c0)])
            else:
                S.op("dve", lambda e: e.scalar_tensor_tensor(
                    out=kf[:, m, 0:n], in0=PS[A][:, 0:n], scalar=pk[:, 41:42], in1=rsb[j][:, 0:n],
                    op0=ALU.mult, op1=ALU.mult),
                     reads=["ps%d" % A, "rsb%d" % j, "pk"], writes=["kf%d" % m])
                S.op("dve", lambda e: e.tensor_copy(out=kT[:, m, c0:c0 + n], in_=kf[:, m, 0:n]),
                     reads=["kf%d" % m], writes=["kT%d_%d" % (m, c0)])
                if m == 3:
                    for tt in range((n + 127) // 128):
                        r = min(128, n - tt * 128)
                        jj = cnt1["kt"] % 2
                        cnt1["kt"] += 1
                        Cb = next_ps(6, 8, "kt")
                        for mm in range(4):
                            S.op("pe", lambda e, Cb=Cb, mm=mm, tt=tt, r=r: e.transpose(
                                out=PS[Cb][0:r, mm * 128:(mm + 1) * 128], in_=kf[:, mm, tt * 128:tt * 128 + r], identity=identf[:, :]),
                                 reads=["kf%d" % mm, "identf"], writes=["ps%d" % Cb], sig=(mm == 3))
                        S.op("dve", lambda e, Cb=Cb, jj=jj, r=r: e.tensor_copy(out=ktok[jj][0:r, :], in_=PS[Cb][0:r, :]),
                             reads=["ps%d" % Cb], writes=["ktok%d" % jj])
                        p0 = c0 + tt * 128
                        if p0 < 2048:
                            S.dma("sp", "d_ktok%d" % jj, lambda e, jj=jj, p0=p0: e.dma_start(out=nk_p[p0:p0 + 128, :], in_=ktok[jj][:, :]),
                                  reads=["ktok%d" % jj], writes=["nk_p_%d" % p0])
                        else:
                            S.dma("sp", "d_ktok%d" % jj, lambda e, jj=jj: e.dma_start(out=nk_p[2048:2064, :], in_=ktok[jj][0:16, :]),
                                  reads=["ktok%d" % jj], writes=["nk_p_%d" % p0])
                            S.dma("sp", "d_ktok%d" % jj, lambda e, jj=jj: e.dma_start(out=nk_s[:, :], in_=ktok[jj][16:32, :]),
                                  reads=["ktok%d" % jj], writes=["nk_s"])
            if m == 3 and c0 == 2048:
                w_done(G)

        LA1 = 2
        for i in range(LA1):
            s1A(units1[i])
        for i in range(len(units1)):
            s1B(units1[i])
            if i + LA1 < len(units1):
                s1A(units1[i + LA1])

        chk(2)
        wv, wkey = w_get(G_V)
        FB = 4
        for i in range(NTILE):
            r = tile_rows(i)
            c0 = 128 * i
            for k in range(KD):
                S.op("pe", lambda e, k=k, r=r, c0=c0, i=i: e.matmul(
                    PS[FB][0:r, 8 * i:8 * i + 8], lhsT=xnT[:, k, c0:c0 + r], rhs=wfl[:, k, :], start=(k == 0), stop=(k == KD - 1)),
                     reads=["wfl", "xnT%d" % i], writes=["ps%d" % FB], sig=(k == KD - 1))
        S.op("dve", lambda e: e.tensor_tensor(out=fl_all[:, 0:16, :], in0=PS[FB][:, 0:128].rearrange("p (i h) -> p i h", h=8),
                                              in1=pk[:, 16:24].unsqueeze(1).to_broadcast([P, 16, 8]), op=ALU.add),
             reads=["ps%d" % FB, "pk"], writes=["fl_all"])
        S.op("dve", lambda e: e.tensor_tensor(out=fl_all[0:32, 16, :], in0=PS[FB][0:32, 128:136], in1=pk[0:32, 16:24], op=ALU.add),
             reads=["ps%d" % FB, "pk"], writes=["fl_all"])

        def logsig(dst, src, r, kin, kout):
            S.op("act", lambda e: e.activation(out=dst, in_=src, func=AF.Exp, scale=-1.0), reads=[kin], writes=[kout])
            S.op("act", lambda e: e.activation(out=dst, in_=dst, func=AF.Ln, bias=onecol[0:r, :], scale=1.0),
                 reads=[kout, "onecol"], writes=[kout])
            S.op("dve", lambda e: e.tensor_scalar(out=dst, in0=dst, scalar1=-1.0, scalar2=None, op0=ALU.mult),
                 reads=[kout], writes=[kout])
        logsig(lf_all[:, 0:16, :], fl_all[:, 0:16, :], P, "fl_all", "lf_all")
        logsig(lf_all[0:32, 16, :], fl_all[0:32, 16, :], 32, "fl_all", "lf_all")
        S.dma("sp", "d_lf", lambda e: e.dma_start(out=nf_p[0:2048, :].rearrange("(i p) h -> p i h", p=P), in_=lf_all[:, 0:16, :]),
              reads=["lf_all"], writes=["nf_p_a"])
        S.dma("sp", "d_lf", lambda e: e.dma_start(out=nf_p[2048:2064, :], in_=lf_all[0:16, 16, :]), reads=["lf_all"], writes=["nf_p_b"])
        S.dma("sp", "d_lf", lambda e: e.dma_start(out=nf_s[:, :], in_=lf_all[16:32, 16, :]), reads=["lf_all"], writes=["nf_s"])

        carr = sm((P, NTILE, 8), F32, "carr")
        lfc = sm((P, 16, 8), F32, "lfc")
        S.dma("sp", "d_lfc", lambda e: e.dma_start(out=lfc[:, :, :], in_=cache_logf[:, :].rearrange("(i p) h -> p i h", p=P)),
              writes=["lfc"])
        c_s = sm((P, NTILE, 8), F32, "c_s")
        S.op("pool", lambda e: e.memset(c_s[:, :, :], 0.0), writes=["c_s"])
        negc_s = sm((P, NTILE, 8), F32, "negc_s")
        msel = sm((32, 16), F32, "msel")
        S.op("pool", lambda e: e.memset(msel[:, :], 1.0), writes=["msel"])
        S.op("pool", lambda e: e.affine_select(msel[:, :], msel[:, :], [[1, 16]], ALU.is_ge, 0.0,
                                               base=16, channel_multiplier=-1), reads=["msel"], writes=["msel"])
        S.op("pool", lambda e: e.memset(msel[0:16, :], 0.0), reads=["msel"], writes=["msel"])
        carr_s = sm((P, 17, 8), F32, "carr_s")
        Cb = 5
        S.op("pe", lambda e: e.matmul(PS[Cb][:, 0:136], lhsT=trif[:, :], rhs=lf_all[:, :, :].rearrange("p i h -> p (i h)"), start=True, stop=True),
             reads=["lf_all", "trif"], writes=["ps%d" % Cb])
        S.op("pe", lambda e: e.matmul(PS[Cb][:, 136:272], lhsT=onesf[:, :], rhs=lf_all[:, :, :].rearrange("p i h -> p (i h)"), start=True, stop=True),
             reads=["lf_all", "onesf"], writes=["ps%d" % Cb])
        Cs = 6
        S.op("pe", lambda e: e.matmul(PS[Cs][:, 0:128], lhsT=trif[:, :], rhs=lfc[:, :, :].rearrange("p i h -> p (i h)"), start=True, stop=True),
             reads=["lfc", "trif"], writes=["ps%d" % Cs])
        S.op("pe", lambda e: e.matmul(PS[Cs][:, 128:256], lhsT=onesf[:, :], rhs=lfc[:, :, :].rearrange("p i h -> p (i h)"), start=True, stop=True),
             reads=["lfc", "onesf"], writes=["ps%d" % Cs])
        S.op("pe", lambda e: e.matmul(PS[Cs][0:16, 256:264], lhsT=msel[:, :], rhs=lf_all[0:32, 16, :], start=True, stop=True),
             reads=["lf_all", "msel"], writes=["ps%d" % Cs])

        chain = []

        def DF(fn, **kw):
            chain.append(lambda: S.op("dve", fn, **kw))
        DF(lambda e: e.memset(carr[:, 0, :], 0.0), writes=["carr"])
        for i in range(1, NTILE):
            DF(lambda e, i=i: e.tensor_tensor(out=carr[:, i, :], in0=carr[:, i - 1, :], in1=PS[Cb][:, 136 + 8 * (i - 1):136 + 8 * i], op=ALU.add),
              reads=["carr", "ps%d" % Cb], writes=["carr"])
        DF(lambda e: e.tensor_tensor(out=c_all[:, :, :], in0=carr[:, :, :], in1=PS[Cb][:, 0:136].rearrange("p (i h) -> p i h", h=8), op=ALU.add),
          reads=["carr", "ps%d" % Cb], writes=["c_all"])
        key = "c_all"
        DF(lambda e: e.tensor_scalar(out=negc[:, :, :], in0=c_all[:, :, :], scalar1=-1.0, scalar2=None, op0=ALU.mult),
          reads=[key], writes=[key + "_neg"])
        DF(lambda e: e.tensor_copy(out=csplit[:, :, 0, :], in_=c_all[:, :, :]), reads=[key], writes=[key + "_s"])
        DF(lambda e: e.tensor_tensor(out=cres[:, :, :], in0=c_all[:, :, :], in1=csplit[:, :, 0, :], op=ALU.subtract),
          reads=[key, key + "_s"], writes=[key + "_r"])
        DF(lambda e: e.tensor_copy(out=csplit[:, :, 1, :], in_=cres[:, :, :]), reads=[key + "_r"], writes=[key + "_s"])
        DF(lambda e: e.tensor_tensor(out=cres[:, :, :], in0=cres[:, :, :], in1=csplit[:, :, 1, :], op=ALU.subtract),
          reads=[key + "_r", key + "_s"], writes=[key + "_r"])
        DF(lambda e: e.tensor_copy(out=csplit[:, :, 2, :], in_=cres[:, :, :]), reads=[key + "_r"], writes=[key + "_s"])
        DF(lambda e: e.memset(carr_s[:, 0, :], 0.0), writes=["carr_s"])
        for i in range(1, 17):
            DF(lambda e, i=i: e.tensor_tensor(out=carr_s[:, i, :], in0=carr_s[:, i - 1, :], in1=PS[Cs][:, 128 + 8 * (i - 1):128 + 8 * i], op=ALU.add),
              reads=["carr_s", "ps%d" % Cs], writes=["carr_s"])
        DF(lambda e: e.tensor_tensor(out=c_s[:, 0:16, :], in0=carr_s[:, 0:16, :], in1=PS[Cs][:, 0:128].rearrange("p (i h) -> p i h", h=8), op=ALU.add),
          reads=["carr_s", "ps%d" % Cs], writes=["c_s"])
        DF(lambda e: e.tensor_tensor(out=c_s[:, 0:16, :], in0=c_s[:, 0:16, :],
                                    in1=carr_s[:, 16, :].unsqueeze(1).to_broadcast([P, 16, 8]), op=ALU.subtract),
          reads=["carr_s", "c_s"], writes=["c_s"])
        DF(lambda e: e.tensor_copy(out=c_s[0:16, 16, :], in_=PS[Cs][0:16, 256:264]), reads=["ps%d" % Cs], writes=["c_s"])
        DF(lambda e: e.tensor_scalar(out=negc_s[:, :, :], in0=c_s[:, :, :], scalar1=-1.0, scalar2=None, op0=ALU.mult),
          reads=["c_s"], writes=["c_s_neg"])

        def run_chain(n):
            for _ in range(n):
                if chain:
                    chain.pop(0)()

        for i in range(NTILE):
            r = tile_rows(i)
            c0 = 128 * i
            jj = i % 2
            A = next_ps(0, 4, "vp")
            for k in range(KD):
                S.op("pe", lambda e, A=A, k=k, r=r, c0=c0, wv=wv: e.matmul(
                    PS[A][0:r, :], lhsT=xnT[:, k, c0:c0 + r], rhs=wv[:, k, :], start=(k == 0), stop=(k == KD - 1)),
                     reads=[wkey, "xnT%d" % i], writes=["ps%d" % A], sig=(k == KD - 1))
            S.op("dve", lambda e, A=A, jj=jj, r=r: e.tensor_copy(out=vtok[jj][0:r, :], in_=PS[A][0:r, :]),
                 reads=["ps%d" % A], writes=["vtok%d" % jj])
            S.op("dve", lambda e, A=A, r=r, i=i: e.tensor_copy(
                out=Vp[0:r, i, :, 0:64], in_=PS[A][0:r, :].rearrange("p (h d) -> p h d", h=H)),
                 reads=["ps%d" % A], writes=["Vp%d" % i])
            if i < 16:
                S.dma("sp", "d_vtok%d" % jj, lambda e, jj=jj, c0=c0: e.dma_start(out=nv_p[c0:c0 + 128, :], in_=vtok[jj][:, :]),
                      reads=["vtok%d" % jj], writes=["nv_p_%d" % i])
            else:
                S.dma("sp", "d_vtok%d" % jj, lambda e, jj=jj: e.dma_start(out=nv_p[2048:2064, :], in_=vtok[jj][0:16, :]),
                      reads=["vtok%d" % jj], writes=["nv_p_%d" % i])
                S.dma("sp", "d_vtok%d" % jj, lambda e, jj=jj: e.dma_start(out=nv_s[:, :], in_=vtok[jj][16:32, :]),
                      reads=["vtok%d" % jj], writes=["nv_s"])
            run_chain(4)
        run_chain(len(chain))
        Vsn = sm((16, H, 66), BF16, "Vsn")
        S.op("pool", lambda e: e.memset(Vsn[:, :, 64:65], 1.0), writes=["Vsn_ones"])
        A = next_ps(0, 4, "vp")
        for k in range(KD):
            S.op("pe", lambda e, A=A, k=k, wv=wv: e.matmul(PS[A][0:16, :], lhsT=xnT[:, k, L:NT], rhs=wv[:, k, :],
                                                          start=(k == 0), stop=(k == KD - 1)),
                 reads=[wkey, "xnT16"], writes=["ps%d" % A], sig=(k == KD - 1))
        S.op("dve", lambda e, A=A: e.tensor_copy(out=Vsn[:, :, 0:64], in_=PS[A][0:16, :].rearrange("p (h d) -> p h d", h=H)),
             reads=["ps%d" % A], writes=["Vsn"])
        w_done(G_V)

        for bnk in range(3):
            Tb = next_ps(4, 8, "ct")
            pst = PS[Tb][:, :].bitcast(BF16)
            tiles = list(range(8 * bnk, min(NTILE, 8 * bnk + 8)))
            for i in tiles:
                r = 128 if i < 16 else 16
                S.op("pe", lambda e, i=i, r=r, pst=pst: e.transpose(
                    out=pst[0:24, 128 * (i % 8):128 * (i % 8) + r], in_=csplit[0:r, i, :, :].rearrange("p j h -> p (j h)"),
                    identity=identb[0:r, 0:r]),
                     reads=["c_all_s", "identb"], writes=["ps%d" % Tb], sig=(i == tiles[-1]))
            w0 = 128 * tiles[0]
            wn = sum(128 if i < 16 else 16 for i in tiles)
            S.op("dve", lambda e, pst=pst, w0=w0, wn=wn: e.tensor_scalar(out=cT24[:, w0:w0 + wn], in0=pst[0:24, 0:wn],
                                                                  scalar1=8.0, scalar2=None, op0=ALU.mult),
                 reads=["ps%d" % Tb], writes=["c_all_T"])

        chk(6)
        S.barrier(engines=("pe", "act", "dve", "sp", "pool"))

        attnT = M.at(R45, (P, 4, NT), BF16, "attnT")
        attn_tok = M.at(R45 + 16640, (P, NTILE, 512), BF16, "attn_tok")
        NPB = 4
        Pb = [M.at(R45 + 34048 + i * 2048, (P, 1024), BF16, "Pb") for i in range(NPB)]
        Qh = [M.at(R45 + i * 4160, (P, NT), BF16, "Qh") for i in range(2)]
        Kh = [M.at(R45 + 8320 + i * 4160, (P, NT), BF16, "Kh") for i in range(2)]
        ncT24 = M.at(AUG + 4160, (24, NT), BF16, "ncT24")
        S.op("dve", lambda e: e.tensor_scalar(out=ncT24[:, 0:L], in0=cT24[:, 0:L], scalar1=-1.0, scalar2=None, op0=ALU.mult),
             reads=["c_all_T"], writes=["ncT24"])
        for i in range(2):
            S.op("pool", lambda e, i=i: e.memset(Qh[i][64:128, :], 0.0), writes=["Qh%d" % i])
            S.op("pool", lambda e, i=i: e.memset(Kh[i][64:128, :], 0.0), writes=["Kh%d" % i])
        S.op("dve", lambda e: e.memset(Qh[0][64:70, :], 1.0), writes=["Qh0"])
        S.dma("sp", "d_qh1", lambda e: e.dma_start(out=Qh[1][67:70, 0:L], in_=Qh[0][67:70, 0:L]), reads=["Qh0"], writes=["Qh1"])
        S.dma("sp", "d_kh0", lambda e: e.dma_start(out=Kh[0][64:67, 0:L], in_=Qh[0][67:70, 0:L]), reads=["Qh0"], writes=["Kh0"])
        S.dma("sp", "d_kh1", lambda e: e.dma_start(out=Kh[1][64:67, 0:L], in_=Qh[0][67:70, 0:L]), reads=["Qh0"], writes=["Kh1"])
        rsum = sm((P, 4), F32, "rsum")
        QG = [(0, 512), (512, 512), (1024, 512), (1536, 512), (2048, 16)]
        units = []
        for h in range(H):
            for gi, (q0, qn) in enumerate(QG):
                kt_last = (q0 + qn - 1) // 128
                kts = list(range(kt_last + 1))
                groups = []
                full = [kt for kt in kts if kt * 128 < q0 and qn == 512]
                rest = [kt for kt in kts if kt not in full]
                for j in range(0, len(full), 2):
                    groups.append(full[j:j + 2])
                for kt in rest:
                    groups.append([kt])
                for gj, g in enumerate(groups):
                    units.append(dict(h=h, q0=q0, qn=qn, kts=g, first_h=(gi == 0 and gj == 0), first_g=(gj == 0),
                                      last_g=(gj == len(groups) - 1), idx=len(units)))
        ostate = {}

        def emitA(u):
            h, q0, qn, kts = u["h"], u["q0"], u["qn"], u["kts"]
            hp, hoff = h // 2, (h % 2) * 64
            hb = h % 2
            nqb = (qn + 127) // 128
            if u["first_h"]:
                S.dma("sp", "d_qh%d" % hb, lambda e: e.dma_start(out=Qh[hb][0:64, :], in_=qT[hoff:hoff + 64, hp, :]),
                      reads=["qT_all"], writes=["Qh%d" % hb])
                S.dma("sp", "d_kh%d" % hb, lambda e: e.dma_start(out=Kh[hb][0:64, :], in_=kT[hoff:hoff + 64, hp, :]),
                      reads=["kT_all"], writes=["Kh%d" % hb])
                for j3 in range(3):
                    S.dma("sp", "d_qh%d" % hb, lambda e, j3=j3: e.dma_start(
                        out=Qh[hb][64 + j3:65 + j3, 0:L], in_=cT24[8 * j3 + h:8 * j3 + h + 1, 0:L]),
                          reads=["c_all_T"], writes=["Qh%d" % hb])
                    S.dma("sp", "d_kh%d" % hb, lambda e, j3=j3: e.dma_start(
                        out=Kh[hb][67 + j3:68 + j3, 0:L], in_=ncT24[8 * j3 + h:8 * j3 + h + 1, 0:L]),
                          reads=["ncT24"], writes=["Kh%d" % hb])
            if u["first_g"]:
                O = next_ps(6, 8, "o")
                ostate[(h, q0)] = O
                S.op("dve", lambda e, O=O, nqb=nqb: e.memset(PS[O][:, 0:65 * nqb], 0.0), writes=["ps%d" % O])
            pp = next_ps(0, 3, "sp2")
            pj = u["idx"] % NPB
            u["pj"] = pj
            u["geo"] = []
            for hf, kt in enumerate(kts):
                kr = 128 if kt < 16 else 16
                qs = max(q0, kt * 128)
                nn = q0 + qn - qs
                u["geo"].append((kt, kr, qs, nn))
                diag = (kt * 128 >= q0)
                dst = PS2[pp][0:kr, hf * 512:hf * 512 + nn] if PS2 is not None else PS[2 * pp + hf][0:kr, 0:nn]
                S.op("pe", lambda e, dst=dst, kt=kt, kr=kr, qs=qs, nn=nn, diag=diag: e.matmul(
                    dst, lhsT=Kh[hb][:, kt * 128:kt * 128 + kr], rhs=Qh[hb][:, qs:qs + nn], start=True, stop=not diag),
                     reads=["Kh%d" % hb, "Qh%d" % hb], writes=["ps%d" % (2 * pp), "ps%d" % (2 * pp + 1)], sig=not diag)
                if diag:
                    dn = min(128, nn)
                    S.op("pe", lambda e, kr=kr, dn=dn, hf=hf: e.matmul(
                        (PS2[pp][0:kr, hf * 512:hf * 512 + dn] if PS2 is not None else PS[2 * pp + hf][0:kr, 0:dn]),
                        lhsT=identb[0:kr, 0:kr], rhs=maskneg[0:kr, 0:dn], start=False, stop=True),
                         reads=["identb", "maskneg"], writes=["ps%d" % (2 * pp), "ps%d" % (2 * pp + 1)])
            if len(kts) == 2 and os.environ.get("PAIR_EXP", "1") == "1":
                S.op("act", lambda e: e.activation(out=Pb[pj][:, :], in_=PS2[pp][:, :], func=AF.Exp, scale=0.125),
                     reads=["ps%d" % (2 * pp), "ps%d" % (2 * pp + 1)], writes=["Pb%d" % pj])
            elif len(kts) == 2:
                for hf in range(2):
                    S.op("act", lambda e, hf=hf: e.activation(out=Pb[pj][:, hf * 512:(hf + 1) * 512],
                                                             in_=(PS2[pp][:, hf * 512:(hf + 1) * 512] if PS2 is not None else PS[2 * pp + hf][:, :]),
                                                             func=AF.Exp, scale=0.125),
                         reads=["ps%d" % (2 * pp), "ps%d" % (2 * pp + 1)], writes=["Pb%d" % pj])
            else:
                kt, kr, qs, nn = u["geo"][0]
                S.op("act", lambda e: e.activation(out=Pb[pj][0:kr, 0:nn], in_=(PS2[pp][0:kr, 0:nn] if PS2 is not None else PS[2 * pp][0:kr, 0:nn]),
                                                   func=AF.Exp, scale=0.125),
                     reads=["ps%d" % (2 * pp), "ps%d" % (2 * pp + 1)], writes=["Pb%d" % pj])

        def emitB(u):
            h, q0, qn = u["h"], u["q0"], u["qn"]
            pj = u["pj"]
            nqb = (qn + 127) // 128
            O = ostate[(h, q0)]
            nk = len(u["geo"])
            for hf, (kt, kr, qs, nn) in enumerate(u["geo"]):
                for qb in range(nqb):
                    qcol = q0 + qb * 128
                    qr = min(128, q0 + qn - qcol)
                    if qcol + qr - 1 < kt * 128:
                        continue
                    S.op("pe", lambda e, qb=qb, qr=qr, qcol=qcol, hf=hf, kt=kt, kr=kr, qs=qs: e.matmul(
                        PS[O][0:qr, 65 * qb:65 * qb + 65], lhsT=Pb[pj][0:kr, hf * 512 + qcol - qs:hf * 512 + qcol - qs + qr],
                        rhs=Vp[0:kr, kt, h, 0:65], start=False, stop=(kt == qcol // 128), skip_group_check=True),
                         reads=["Pb%d" % pj, "Vp%d" % kt, "Vp_ones"], writes=["ps%d" % O],
                         sig=(qb == nqb - 1 and hf == nk - 1))
            if u["last_g"]:
                for qb in range(nqb):
                    qcol = q0 + qb * 128
                    qr = min(128, q0 + qn - qcol)
                    S.op("dve", lambda e, qb=qb, qr=qr: e.reciprocal(out=rsum[0:qr, qb:qb + 1], in_=PS[O][0:qr, 65 * qb + 64:65 * qb + 65]),
                         reads=["ps%d" % O], writes=["rsum%d" % qb])
                    S.op("dve", lambda e, qb=qb, qr=qr, qcol=qcol: e.tensor_scalar(
                        out=attn_tok[0:qr, qcol // 128, h * 64:(h + 1) * 64], in0=PS[O][0:qr, 65 * qb:65 * qb + 64],
                        scalar1=rsum[0:qr, qb:qb + 1], scalar2=None, op0=ALU.mult),
                         reads=["ps%d" % O, "rsum%d" % qb], writes=["attn_tok%d" % (qcol // 128)])

        LA = 3
        for i in range(min(LA, len(units))):
            emitA(units[i])
        for i in range(len(units)):
            emitB(units[i])
            if i + LA < len(units):
                emitA(units[i + LA])

        chk(7)
        KC = 256
        kc_tm = [M.at(R3 + i * 2048, (P, 2, 512), BF16, "kc_tm") for i in range(2)]
        vc_tm = [M.at(R3 + 4096 + i * 2048, (P, 2, 512), BF16, "vc_tm") for i in range(2)]
        KcT = [M.at(R3 + 8192 + i * 2048, (P, 2, 4, 128), BF16, "KcT") for i in range(2)]
        Psb = [M.at(R3 + 12288 + i * 512, (P, 2, H, 16), BF16, "Psb") for i in range(2)]
        Vw = [M.at(AUG + 8320 + i * 2112, (P, 2, H, 66), BF16, "Vw") for i in range(2)]
        attn_s = sm((16, 512), BF16, "attn_s")
        rsum_s = sm((16, 8), F32, "rsum_s")
        wts = sm((P, NTILE, 8), F32, "wts")
        Qz = sm((P, H, 16), BF16, "Qz")
        S.op("act", lambda e: e.activation(out=wts[:, :, :].rearrange("p i h -> p (i h)"), in_=negc_s[:, :, :].rearrange("p i h -> p (i h)"), func=AF.Exp),
             reads=["c_s_neg"], writes=["wts"])
        S.op("dve", lambda e: e.memset(Qz[:, :, :], 0.0), writes=["Qz"])
        for h in range(H):
            hp, hoff = h // 2, (h % 2) * 64
            S.op("dve", lambda e, h=h, hp=hp, hoff=hoff: e.tensor_copy(out=Qz[hoff:hoff + 64, h, :], in_=qT[hoff:hoff + 64, hp, L:NT]),
                 reads=["qT_all"], writes=["Qz"])

        def load_cache_chunk(c):
            j = c % 2
            S.dma("pool", "d_kc%d" % j, lambda e, c=c, j=j: e.dma_start(
                out=kc_tm[j][:, :, :], in_=cache_k[KC * c:KC * (c + 1), :].rearrange("(i p) d -> p i d", p=P)),
                  writes=["kc_tm%d" % j])
            S.dma("pool", "d_vc%d" % j, lambda e, c=c, j=j: e.dma_start(
                out=vc_tm[j][:, :, :], in_=cache_v[KC * c:KC * (c + 1), :].rearrange("(i p) d -> p i d", p=P)),
                  writes=["vc_tm%d" % j])
        load_cache_chunk(0)
        load_cache_chunk(1)
        OS = (6, 7)
        for ob in OS:
            S.op("dve", lambda e, ob=ob: e.memset(PS[ob][0:16, 0:260], 0.0), writes=["ps%d" % ob])
        NCH = PAST // KC

        def sA(c):
            j = c % 2
            Tk = next_ps(0, 2, "tk")
            pstk = PS[Tk][:, :].bitcast(BF16)
            for t in range(2):
                for hp in range(4):
                    S.op("pe", lambda e, t=t, hp=hp: e.transpose(
                        out=pstk[:, (t * 4 + hp) * 128:(t * 4 + hp + 1) * 128], in_=kc_tm[j][:, t, hp * 128:(hp + 1) * 128],
                        identity=identb[:, :]),
                         reads=["kc_tm%d" % j, "identb"], writes=["ps%d" % Tk], sig=(t == 1 and hp == 3))
            S.op("act", lambda e: e.activation(out=KcT[j][:, :, :, :].rearrange("p t h k -> p (t h k)"), in_=pstk[:, :], func=AF.Copy),
                 reads=["ps%d" % Tk], writes=["KcT%d" % j])
            for t in range(2):
                kt = 2 * c + t
                S.op("dve", lambda e, t=t, kt=kt: e.tensor_tensor(
                    out=Vw[j][:, t, :, 0:64], in0=vc_tm[j][:, t, :].rearrange("p (h d) -> p h d", h=H),
                    in1=wts[:, kt, :].unsqueeze(2).to_broadcast([P, H, 64]), op=ALU.mult),
                     reads=["vc_tm%d" % j, "wts"], writes=["Vw%d" % j])
                S.op("dve", lambda e, t=t, kt=kt: e.tensor_copy(out=Vw[j][:, t, :, 64], in_=wts[:, kt, :]),
                     reads=["wts"], writes=["Vw%d" % j])
            Sb = next_ps(2, 4, "sb")
            for t in range(2):
                for h in range(H):
                    hp = h // 2
                    col = (t * H + h) * 16
                    S.op("pe", lambda e, col=col, t=t, hp=hp, h=h: e.matmul(
                        PS[Sb][:, col:col + 16], lhsT=KcT[j][:, t, hp, :], rhs=Qz[:, h, :], start=True, stop=True),
                         reads=["KcT%d" % j, "Qz"], writes=["ps%d" % Sb], sig=(t == 1 and h == H - 1))
            S.op("act", lambda e: e.activation(out=Psb[j][:, :, :, :].rearrange("p t h q -> p (t h q)"), in_=PS[Sb][:, 0:256],
                                               func=AF.Exp, scale=0.125),
                 reads=["ps%d" % Sb], writes=["Psb%d" % j])

        def sB(c):
            j = c % 2
            for t in range(2):
                for h in range(H):
                    ob = OS[h // 4]
                    oc = (h % 4) * 65
                    S.op("pe", lambda e, ob=ob, oc=oc, t=t, h=h: e.matmul(
                        PS[ob][0:16, oc:oc + 65], lhsT=Psb[j][:, t, h, :], rhs=Vw[j][:, t, h, 0:65],
                        start=False, stop=False, skip_group_check=True),
                         reads=["Psb%d" % j, "Vw%d" % j], writes=["ps%d" % ob], sig=(t == 1 and h == H - 1))
            if c + 2 < NCH:
                load_cache_chunk(c + 2)
        sA(0)
        for c in range(NCH):
            if c + 1 < NCH:
                sA(c + 1)
            sB(c)
        Sb = next_ps(2, 4, "sb")
        Pn = sm((16, H, 16), BF16, "Pn")
        Vsw = sm((16, H, 66), BF16, "Vsw")
        for h in range(H):
            hp = h // 2
            S.op("pe", lambda e, h=h, hp=hp: e.matmul(
                PS[Sb][0:16, 16 * h:16 * h + 16], lhsT=kT[:, hp, L:NT], rhs=Qz[:, h, :], start=True, stop=False),
                 reads=["Qz"], writes=["ps%d" % Sb], sig=False)
            S.op("pe", lambda e, h=h: e.matmul(
                PS[Sb][0:16, 16 * h:16 * h + 16], lhsT=identb[0:16, 0:16], rhs=maskneg[0:16, 0:16], start=False, stop=True),
                 reads=["identb", "maskneg"], writes=["ps%d" % Sb], sig=(h == H - 1))
        S.op("act", lambda e: e.activation(out=Pn[:, :, :].rearrange("p h q -> p (h q)"), in_=PS[Sb][0:16, 0:128], func=AF.Exp, scale=0.125),
             reads=["ps%d" % Sb], writes=["Pn"])
        S.op("dve", lambda e: e.tensor_tensor(out=Vsw[:, :, 0:65], in0=Vsn[:, :, 0:65],
                                              in1=wts[0:16, 16, :].unsqueeze(2).to_broadcast([16, H, 65]), op=ALU.mult),
             reads=["Vsn", "Vsn_ones", "wts"], writes=["Vsw"])
        for h in range(H):
            ob = OS[h // 4]
            oc = (h % 4) * 65
            S.op("pe", lambda e, ob=ob, oc=oc, h=h: e.matmul(
                PS[ob][0:16, oc:oc + 65], lhsT=Pn[:, h, :], rhs=Vsw[:, h, 0:65], start=False, stop=True, skip_group_check=True),
                 reads=["Pn", "Vsw"], writes=["ps%d" % ob], sig=(h % 4 == 3))
        for h in range(H):
            ob = OS[h // 4]
            oc = (h % 4) * 65
            S.op("dve", lambda e, ob=ob, oc=oc, h=h: e.reciprocal(out=rsum_s[:, h:h + 1], in_=PS[ob][0:16, oc + 64:oc + 65]),
                 reads=["ps%d" % ob], writes=["rsum_s"])
            S.op("dve", lambda e, ob=ob, oc=oc, h=h: e.tensor_scalar(
                out=attn_s[:, h * 64:(h + 1) * 64], in0=PS[ob][0:16, oc:oc + 64], scalar1=rsum_s[:, h:h + 1], scalar2=None, op0=ALU.mult),
                 reads=["ps%d" % ob, "rsum_s"], writes=["attn_s"])

        for i in range(NTILE):
            r = 128 if i < 16 else 16
            Tb = next_ps(0, 4, "ta")
            pst = PS[Tb][:, :].bitcast(BF16)
            for hp in range(4):
                S.op("pe", lambda e, i=i, r=r, hp=hp, pst=pst: e.transpose(
                    out=pst[:, hp * 128:hp * 128 + r], in_=attn_tok[0:r, i, hp * 128:(hp + 1) * 128], identity=identb[0:r, 0:r]),
                     reads=["attn_tok%d" % i, "identb"], writes=["ps%d" % Tb], sig=(hp == 3))
            S.op("dve", lambda e, i=i, r=r, pst=pst: e.tensor_copy(
                out=attnT[:, :, 128 * i:128 * i + r], in_=pst[:, 0:512].rearrange("p (h t) -> p h t", h=4)[:, :, 0:r]),
                 reads=["ps%d" % Tb], writes=["attnT%d" % i])
        Tb = next_ps(0, 4, "ta")
        pst = PS[Tb][:, :].bitcast(BF16)
        for hp in range(4):
            S.op("pe", lambda e, hp=hp, pst=pst: e.transpose(
                out=pst[:, hp * 128:hp * 128 + 16], in_=attn_s[0:16, hp * 128:(hp + 1) * 128], identity=identb[0:16, 0:16]),
                 reads=["attn_s", "identb"], writes=["ps%d" % Tb], sig=(hp == 3))
        S.op("dve", lambda e, pst=pst: e.tensor_copy(
            out=attnT[:, :, L:NT], in_=pst[:, 0:512].rearrange("p (h t) -> p h t", h=4)[:, :, 0:16]),
             reads=["ps%d" % Tb], writes=["attnT17"])
        if "attnT" in dbg_out:
            S.dma("sp", "d_dbg", lambda e: e.dma_start(out=dbg_out["attnT"][:, :, :], in_=attnT[:, :, :]),
                  reads=["attnT%d" % i for i in range(18)])

        chk(8)
        S.barrier()
        ZW = 2088
        zbuf = [M.at(R3 + i * (ZW * 4), (P, ZW), F32, "zbuf") for i in range(2)]
        convT = M.at(R3 + 16896, (P, 4, NT), BF16, "convT")
        tmpC = [M.at(R45 + 16640 + i * 2048, (P, 512), F32, "tmpC") for i in range(2)]
        ytmp = [M.at(R45 + 20736 + i * 2048, (P, 512), F32, "ytmp") for i in range(2)]
        tmpB = [M.at(AUG + 4160 + i * 2048, (P, 512), F32, "tmpB") for i in range(2)]
        stc_sb = sm((P, 8), F32, "stc_sb")
        zlast = sm((P, 4, 4), F32, "zlast")
        S.dma("sp", "d_stc", lambda e: e.dma_start(out=stc_sb[:, :], in_=stc[:, :]), writes=["stc_sb"])
        cc3 = {"n": 0}
        for c in range(4):
            wX, kX = w_get(G_CV[c])
            zb = zbuf[c % 2]
            zk = "zbuf%d" % (c % 2)
            S.op("dve", lambda e, zb=zb: e.memset(zb[:, 0:2], 0.0), writes=[zk])
            S.op("dve", lambda e, zb=zb, c=c: e.tensor_copy(out=zb[:, 2066:2068], in_=stc_sb[:, 2 * c:2 * c + 2]), reads=["stc_sb"], writes=[zk])
            for (c0, n) in BLKS:
                jj = cc3["n"] % 2
                cc3["n"] += 1
                banks = []
                for j3 in range(3):
                    A = next_ps()
                    banks.append(A)
                    for k in range(KD):
                        S.op("pe", lambda e, A=A, k=k, j3=j3, c0=c0, n=n, wX=wX: e.matmul(
                            PS[A][:, 0:n], lhsT=wX[:, k, 128 * j3:128 * (j3 + 1)], rhs=xnT[:, k, c0:c0 + n],
                            start=(k == 0), stop=(k == KD - 1)), reads=[kX], writes=["ps%d" % A], sig=(k == KD - 1))
                bB, bC, bH = banks
                S.op("act", lambda e, bC=bC, jj=jj, n=n: e.activation(out=tmpC[jj][:, 0:n], in_=PS[bC][:, 0:n], func=AF.Copy),
                     reads=["ps%d" % bC], writes=["tmpC%d" % jj])
                S.op("act", lambda e, bB=bB, jj=jj, n=n: e.activation(out=tmpB[jj][:, 0:n], in_=PS[bB][:, 0:n], func=AF.Copy),
                     reads=["ps%d" % bB], writes=["tmpB%d" % jj])
                segs = [(0, n, c0 + 2)] if n == 512 else [(0, 16, 2050), (16, 16, 2068)]
                for (so, sn, zc) in segs:
                    S.op("dve", lambda e, bH=bH, jj=jj, so=so, sn=sn, zc=zc, zb=zb: e.tensor_tensor(
                        out=zb[:, zc:zc + sn], in0=tmpC[jj][:, so:so + sn], in1=PS[bH][:, so:so + sn], op=ALU.mult),
                         reads=["tmpC%d" % jj, "ps%d" % bH], writes=[zk])
                for (so, sn, zc) in segs:
                    S.op("dve", lambda e, jj=jj, so=so, sn=sn, zc=zc, zb=zb, c=c: e.tensor_scalar(
                        out=ytmp[jj][:, so:so + sn], in0=zb[:, zc:zc + sn], scalar1=pk[:, 24 + 3 * c + 2:24 + 3 * c + 3],
                        scalar2=pk[:, 36 + c:37 + c], op0=ALU.mult, op1=ALU.add),
                         reads=[zk, "pk"], writes=["ytmp%d" % jj])
                    for tap, sh in ((1, 1), (0, 2)):
                        S.op("dve", lambda e, jj=jj, so=so, sn=sn, zc=zc, zb=zb, c=c, tap=tap, sh=sh: e.scalar_tensor_tensor(
                            out=ytmp[jj][:, so:so + sn], in0=zb[:, zc - sh:zc - sh + sn], scalar=pk[:, 24 + 3 * c + tap:24 + 3 * c + tap + 1],
                            in1=ytmp[jj][:, so:so + sn], op0=ALU.mult, op1=ALU.add),
                             reads=[zk, "pk", "ytmp%d" % jj], writes=["ytmp%d" % jj])
                S.op("dve", lambda e, jj=jj, n=n, c=c, c0=c0: e.tensor_tensor(
                    out=convT[:, c, c0:c0 + n], in0=tmpB[jj][:, 0:n], in1=ytmp[jj][:, 0:n], op=ALU.mult),
                     reads=["tmpB%d" % jj, "ytmp%d" % jj], writes=["convT"])
            S.op("dve", lambda e, zb=zb, c=c: e.tensor_copy(out=zlast[:, c, 0:2], in_=zb[:, 2064:2066]), reads=[zk], writes=["zlast"])
            S.op("dve", lambda e, zb=zb, c=c: e.tensor_copy(out=zlast[:, c, 2:4], in_=zb[:, 2082:2084]), reads=[zk], writes=["zlast"])
            w_done(G_CV[c])
        with nc.allow_non_contiguous_dma(reason="tiny transposed conv-state rows"):
            for c in range(4):
                S.dma("sp", "d_ncp", lambda e, c=c: e.dma_start(
                    out=nc_p[:, 128 * c:128 * (c + 1)].rearrange("r p -> p r"), in_=zlast[:, c, 0:2], allow_slow_non_contiguous=True),
                      reads=["zlast"], writes=["nc_p%d" % c])
                S.dma("sp", "d_ncs", lambda e, c=c: e.dma_start(
                    out=nc_s[:, 128 * c:128 * (c + 1)].rearrange("r p -> p r"), in_=zlast[:, c, 2:4], allow_slow_non_contiguous=True),
                      reads=["zlast"], writes=["nc_s%d" % c])

        chk(9)
        mergedT = M.at(R2, (P, KD, NT), BF16, "mergedT")
        gt = [[M.at(R45 + 24832 + (i * 4 + q_) * 2048, (P, 512), F32, "gt") for q_ in range(4)] for i in range(2)]
        g4 = {"n": 0}
        wbrc, kbrc = w_get(G_BRC)
        wbra, kbra = w_get(G_BRA)
        for pr in range(4):
            wgp, kgp = w_get(G_GP[pr])
            for jj2 in range(2):
                j = 2 * pr + jj2
                for (c0, n) in BLKS_EQ:
                    si = g4["n"] % 2
                    g4["n"] += 1
                    sA, sB, t1, t2 = gt[si]
                    ba = next_ps(); bb = next_ps(); bc = next_ps(); bd = next_ps()
                    for (bank, wv, wk, src, nk, col0) in ((ba, wgp, kgp, xnT, KD, jj2 * 128), (bb, wgp, kgp, xnT, KD, 256 + jj2 * 128),
                                                        (bc, wbrc, kbrc, convT, 4, j * 128), (bd, wbra, kbra, attnT, 4, j * 128)):
                        for k in range(nk):
                            S.op("pe", lambda e, bank=bank, wv=wv, src=src, k=k, nk=nk, col0=col0, c0=c0, n=n: e.matmul(
                                PS[bank][:, 0:n], lhsT=wv[:, k, col0:col0 + 128], rhs=src[:, k, c0:c0 + n],
                                start=(k == 0), stop=(k == nk - 1)),
                                 reads=[wk, "convT"] + ["attnT%d" % i for i in range(18)], writes=["ps%d" % bank], sig=(k == nk - 1))
                    S.op("act", lambda e, ba=ba, sA=sA, n=n: e.activation(out=sA[:, 0:n], in_=PS[ba][:, 0:n], func=AF.Sigmoid),
                         reads=["ps%d" % ba], writes=["gtA%d" % si])
                    S.op("act", lambda e, bb=bb, sB=sB, n=n: e.activation(out=sB[:, 0:n], in_=PS[bb][:, 0:n], func=AF.Sigmoid),
                         reads=["ps%d" % bb], writes=["gtB%d" % si])
                    S.op("dve", lambda e, bc=bc, sA=sA, t1=t1, n=n: e.tensor_tensor(out=t1[:, 0:n], in0=sA[:, 0:n], in1=PS[bc][:, 0:n], op=ALU.mult),
                         reads=["ps%d" % bc, "gtA%d" % si], writes=["gt1%d" % si])
                    S.op("dve", lambda e, bd=bd, sB=sB, t2=t2, n=n: e.tensor_tensor(out=t2[:, 0:n], in0=sB[:, 0:n], in1=PS[bd][:, 0:n], op=ALU.mult),
                         reads=["ps%d" % bd, "gtB%d" % si], writes=["gt2%d" % si])
                    S.op("dve", lambda e, t1=t1, t2=t2, j=j, c0=c0, n=n: e.tensor_tensor(out=mergedT[:, j, c0:c0 + n], in0=t1[:, 0:n], in1=t2[:, 0:n], op=ALU.add),
                         reads=["gt1%d" % si, "gt2%d" % si], writes=["mergedT"])
            w_done(G_GP[pr])
        w_done(G_BRC); w_done(G_BRA)

        chk(10)
        S.barrier()
        yacc = M.at(R1, (P, 16, D), F32, "yacc")
        yacc16 = M.at(R45 + 33280, (P, D), F32, "yacc16")
        hnT = M.at(R45, (P, KD, NT), BF16, "hnT")
        hsb = [M.at(AUG + i * 2048, (P, D), BF16, "hs") for i in range(3)]
        junk2 = M.at(R45 + 41472, (P, D), BF16, "junk2")
        wo0, ko0 = w_get(G_OUT0)
        wo1, ko1 = w_get(G_OUT0 + 1)

        def ytile(i):
            return yacc[:, i, :] if i < 16 else yacc16[:, :]

        def s5A(i):
            r = tile_rows(i)
            c0 = 128 * i
            b = i % 3
            yt = ytile(i)
            for half, (wo, ko) in enumerate(((wo0, ko0), (wo1, ko1))):
                A = next_ps(0, 6, "w5")
                for k in range(KD):
                    S.op("pe", lambda e, A=A, k=k, wo=wo: e.matmul(
                        PS[A][0:r, :], lhsT=mergedT[:, k, c0:c0 + r], rhs=wo[:, k, :], start=(k == 0), stop=(k == KD - 1)),
                         reads=[ko, "mergedT"], writes=["ps%d" % A], sig=(k == KD - 1))
                S.op("dve", lambda e, A=A, half=half: e.tensor_tensor(
                    out=yt[0:r, half * 512:(half + 1) * 512], in0=yt[0:r, half * 512:(half + 1) * 512], in1=PS[A][0:r, :], op=ALU.add),
                     reads=["ps%d" % A, "yacc%d" % i], writes=["yacc%d" % i])
            rms_rstd(yt[0:r, :], r, i, 1.0 / D, "yacc%d" % i, "b", junk=junk2, jkey="junk2")

        def s5A2(i):
            r = tile_rows(i)
            b = i % 3
            yt = ytile(i)
            S.op("dve", lambda e: e.tensor_scalar(out=hsb[b][0:r, :], in0=yt[0:r, :], scalar1=rstd[0:r, i:i + 1],
                                                  scalar2=None, op0=ALU.mult),
                 reads=["yacc%d" % i, "rstdb%d" % i], writes=["hs%d" % b])

        def s5B(i):
            r = tile_rows(i)
            c0 = 128 * i
            b = i % 3
            Tb = next_ps(6, 8, "t5")
            pst = PS[Tb][:, :].bitcast(BF16)
            for k in range(KD):
                S.op("pe", lambda e, k=k: e.transpose(out=pst[:, k * 128:k * 128 + r], in_=hsb[b][0:r, k * 128:(k + 1) * 128],
                                                     identity=identb[0:r, 0:r]),
                     reads=["hs%d" % b, "identb"], writes=["ps%d" % Tb], sig=(k == KD - 1))
            S.op("dve", lambda e: e.tensor_tensor(
                out=hnT[:, :, c0:c0 + r], in0=pst.rearrange("p (k t) -> p k t", k=KD)[:, :, 0:r],
                in1=pk[:, 8:16].unsqueeze(2).to_broadcast([P, KD, r]), op=ALU.mult),
                 reads=["ps%d" % Tb, "pk"], writes=["hnT"])
        for i in range(NTILE):
            load_xtile(i, ytile(i), "yacc%d" % i, "d_xr%d" % i)
        s5A(0)
        s5A(1)
        s5A2(0)
        for i in range(NTILE):
            if i + 2 < NTILE:
                s5A(i + 2)
            if i + 1 < NTILE:
                s5A2(i + 1)
            s5B(i)
        w_done(G_OUT0); w_done(G_OUT0 + 1)

        chk(11)
        S.barrier()
        aTb = [M.at(R2 + i * 16640, (P, 4, NT), BF16, "aT") for i in range(2)]
        rtmp = [M.at(R45 + 37376 + i * 2048, (P, 512), F32, "rtmp") for i in range(2)]
        r6 = {"n": 0}
        for g in range(8):
            wu, ku = w_get(G_MLP + 2 * g)
            wd, kd = w_get(G_MLP + 2 * g + 1)
            aT = aTb[g % 2]
            ak = "aT%d" % (g % 2)
            for fc in range(4):
                for (c0, n) in BLKS_EQ:
                    A = next_ps()
                    for k in range(KD):
                        S.op("pe", lambda e, A=A, k=k, fc=fc, c0=c0, n=n, wu=wu: e.matmul(
                            PS[A][:, 0:n], lhsT=wu[:, k, fc * 128:(fc + 1) * 128], rhs=hnT[:, k, c0:c0 + n],
                            start=(k == 0), stop=(k == KD - 1)), reads=[ku, "hnT"], writes=["ps%d" % A], sig=(k == KD - 1))
                    rj = r6["n"] % 2
                    r6["n"] += 1
                    S.op("act", lambda e, A=A, n=n, rj=rj: e.activation(out=rtmp[rj][:, 0:n], in_=PS[A][:, 0:n], func=AF.Relu),
                         reads=["ps%d" % A], writes=["rtmp%d" % rj])
                    S.op("dve", lambda e, fc=fc, c0=c0, n=n, aT=aT, rj=rj: e.tensor_tensor(
                        out=aT[:, fc, c0:c0 + n], in0=rtmp[rj][:, 0:n], in1=rtmp[rj][:, 0:n], op=ALU.mult),
                         reads=["rtmp%d" % rj], writes=[ak])
            for i in range(NTILE):
                r = tile_rows(i)
                c0 = 128 * i
                yt = ytile(i)
                for half in range(2):
                    A = next_ps()
                    for fc in range(4):
                        S.op("pe", lambda e, A=A, fc=fc, r=r, c0=c0, half=half, aT=aT, wd=wd: e.matmul(
                            PS[A][0:r, :], lhsT=aT[:, fc, c0:c0 + r], rhs=wd[:, fc, half * 512:(half + 1) * 512],
                            start=(fc == 0), stop=(fc == 3)), reads=[kd, ak], writes=["ps%d" % A], sig=(fc == 3))
                    S.op("dve", lambda e, A=A, r=r, yt=yt, half=half: e.tensor_tensor(
                        out=yt[0:r, half * 512:(half + 1) * 512], in0=yt[0:r, half * 512:(half + 1) * 512], in1=PS[A][0:r, :], op=ALU.add),
                         reads=["ps%d" % A, "yacc%d" % i], writes=["yacc%d" % i])
                if g == 7:
                    if i == 0:
                        S.dma("sp", "d_y", lambda e: e.dma_start(out=y_prompt[0:112, :], in_=yacc[16:128, 0, :]), reads=["yacc0"], writes=["y_prompt0"])
                    elif i < 16:
                        S.dma("sp", "d_y", lambda e, i=i: e.dma_start(out=y_prompt[128 * i - 16:128 * i + 112, :], in_=yacc[:, i, :]),
                              reads=["yacc%d" % i], writes=["y_prompt%d" % i])
                    else:
                        S.dma("sp", "d_y", lambda e: e.dma_start(out=y_prompt[2032:2048, :], in_=yacc16[0:16, :]), reads=["yacc16"], writes=["y_prompt16"])
                        S.dma("sp", "d_y", lambda e: e.dma_start(out=y_sample[:, :], in_=yacc16[16:32, :]), reads=["yacc16"], writes=["y_sample"])
            w_done(G_MLP + 2 * g); w_done(G_MLP + 2 * g + 1)

        if "attn_tok" in dbg_out:
            S.dma("sp", "d_dbg", lambda e: e.dma_start(out=dbg_out["attn_tok"][:, :, :], in_=attn_tok[:, :, :]),
                  reads=["attn_tok%d" % i for i in range(NTILE)])
        if "cT24" in dbg_out:
            S.dma("sp", "d_dbg", lambda e: e.dma_start(out=dbg_out["cT24"][:, :], in_=cT24[:, :]), reads=["c_all_T"])
        if "negc" in dbg_out:
            S.dma("sp", "d_dbg", lambda e: e.dma_start(out=dbg_out["negc"][:, :, :], in_=negc[:, :, :]), reads=["c_all_neg"])
        if "qT" in dbg_out:
            S.dma("sp", "d_dbg", lambda e: e.dma_start(out=dbg_out["qT"][:, :, :], in_=qT[:, :, :]), reads=["qT_all"])
        if "c_all" in dbg_out:
            S.dma("sp", "d_dbg", lambda e: e.dma_start(out=dbg_out["c_all"][:, :, :], in_=c_all[:, :, :]), reads=["c_all"])


    except _Stop:
        pass

    S.finish("sp")
    S.emit()
    return nc


def make_in_maps(inputs):
    f = lambda a: np.ascontiguousarray(np.asarray(a, dtype=np.float32))
    x_prompt = f(inputs["x_prompt"]); x_sample = f(inputs["x_sample"])
    ck = f(inputs["cache_k"])[0]; cv = f(inputs["cache_v"])[0]; cl = f(inputs["cache_logf"])[0]
    sc = f(inputs["state_conv"])[0]
    pk = np.zeros((P, 64), np.float32)
    pk[:, 0:8] = f(inputs["norm1_g"])[0].reshape(8, P).T
    pk[:, 8:16] = f(inputs["norm2_g"])[0].reshape(8, P).T
    pk[:, 16:24] = np.broadcast_to(f(inputs["b_f"])[0][None, :], (P, 8))
    cw = f(inputs["conv_w"])[0]
    pk[:, 24:36] = cw.reshape(3, 4, P).transpose(2, 1, 0).reshape(P, 12)
    pk[:, 36:40] = f(inputs["conv_b"])[0].reshape(4, P).T
    pk[:, 40] = np.tile(f(inputs["q_norm_g"])[0], 2)
    pk[:, 41] = np.tile(f(inputs["k_norm_g"])[0], 2)
    maps = []
    for b in range(8):
        stt = np.ascontiguousarray(sc[b].reshape(2, 4, P).transpose(2, 1, 0).reshape(P, 8))
        maps.append({
            "x_prompt": x_prompt[b], "x_sample": x_sample[b],
            "cache_k": ck[b].reshape(PAST, 512), "cache_v": cv[b].reshape(PAST, 512),
            "cache_logf": cl[b], "meta": f(inputs["meta"]),
            "w_in": f(inputs["w_in"])[0], "w_br_conv": f(inputs["w_br_conv"])[0],
            "w_br_attn": f(inputs["w_br_attn"])[0], "w_out": f(inputs["w_out"])[0],
            "w_up": f(inputs["w_up"])[0], "w_down": f(inputs["w_down"])[0],
            "ppk": pk, "state_conv_t": stt,
        })
    return maps


_NC_CACHE = {}


def kernel(**inputs):
    maps = make_in_maps(inputs)
    if "nc" not in _NC_CACHE:
        _NC_CACHE["nc"] = build()
    nc = _NC_CACHE["nc"]
    res = run_bass_kernel_spmd(nc, maps, core_ids=list(range(8)))
    R = res.results
    st = lambda name: np.stack([np.asarray(R[b][name], dtype=np.float32) for b in range(8)])
    y_prompt = st("y_prompt")
    y_sample = st("y_sample")
    nk_p = st("nk_p").reshape(1, 8, L, H, HD)
    nv_p = st("nv_p").reshape(1, 8, L, H, HD)
    nf_p = st("nf_p").reshape(1, 8, L, H)
    nc_p = st("nc_p").reshape(1, 8, 2, DC)
    nk_s = st("nk_s").reshape(1, 8, NS, H, HD)
    nv_s = st("nv_s").reshape(1, 8, NS, H, HD)
    nf_s = st("nf_s").reshape(1, 8, NS, H)
    nc_s = st("nc_s").reshape(1, 8, 2, DC)
    return (y_prompt, y_sample, nk_p, nv_p, nf_p, nc_p, nk_s, nv_s, nf_s, nc_s)
```

```python
import os
import numpy as np
import concourse.bass as bass
import concourse.mybir as mybir
from concourse.bass_utils import run_bass_kernel_spmd

F32 = mybir.dt.float32
BF16 = mybir.dt.bfloat16
AF = mybir.ActivationFunctionType
ALU = mybir.AluOpType
AX = mybir.AxisListType

P = 128
D = 1024
KD = 8
SEQ = 2048
NMETA = 16
L = SEQ + NMETA
NS = 16
NT = L + NS
NTILE = 17
PAST = 2048
H = 8
HD = 64
DC = 512
DA = 512
DFF = 4096
INC = 5128
EPS = 1e-6
C_B, C_C, C_H, C_Q, C_K, C_V, C_F, C_G = 0, 512, 1024, 1536, 2048, 2560, 3072, 3080
BLKS = [(0, 512), (512, 512), (1024, 512), (1536, 512), (2048, 32)]
BLKS_EQ = [(416 * i, 416) for i in range(5)]


def tile_rows(i):
    return 128 if i < 16 else 32


class Sched:
    ENG = ("pe", "act", "dve", "pool", "sp")

    def __init__(self, nc):
        self.nc = nc
        self.q = {e: [] for e in self.ENG}
        self.cnt = {}
        self.sems = {}
        self.lastw = {}
        self.lastr = {}
        self.seen = {e: {} for e in self.ENG}
        self.pending = {e: {} for e in self.ENG}
        for e in ("pe", "act", "dve", "pool"):
            self._sem(e)

    def _sem(self, name):
        if name not in self.sems:
            self.sems[name] = self.nc.alloc_semaphore("s_" + name)
            self.cnt[name] = 0
        return self.sems[name]

    def _deps(self, eng, reads, writes):
        deps = dict(self.pending[eng])
        self.pending[eng] = {}

        def merge(src, raw):
            for s, v in src.items():
                if s == eng and eng == "pe":
                    continue
                if deps.get(s, 0) < v:
                    deps[s] = v
        for k in reads:
            merge(self.lastw.get(k, {}), True)
        for k in writes:
            merge(self.lastw.get(k, {}), False)
            merge(self.lastr.get(k, {}), False)
        out = []
        seen = self.seen[eng]
        for s, v in deps.items():
            if seen.get(s, 0) < v:
                seen[s] = v
                out.append((s, v))
        return out

    def _record(self, s, v, reads, writes):
        for k in reads:
            d = self.lastr.setdefault(k, {})
            if d.get(s, 0) < v:
                d[s] = v
        for k in writes:
            d = self.lastw.setdefault(k, {})
            if d.get(s, 0) < v:
                d[s] = v

    def op(self, eng, fn, reads=(), writes=(), sig=True):
        waits = self._deps(eng, reads, writes)
        if sig:
            self.cnt[eng] += 1
            v = self.cnt[eng]
            inc = (eng, 1)
        else:
            v = self.cnt[eng] + 1
            inc = None
        self._record(eng, v, reads, writes)
        self.q[eng].append((waits, fn, inc))

    def dma(self, queue, sem, fn, reads=(), writes=()):
        self._sem(sem)
        waits = self._deps(queue, reads, writes)
        self.cnt[sem] += 16
        self._record(sem, self.cnt[sem], reads, writes)
        self.q[queue].append((waits, fn, (sem, 16)))

    def barrier(self, engines=("pe", "act", "dve", "sp"), exclude=()):
        snap = {s: v for s, v in self.cnt.items() if v > 0 and s not in exclude
                and not s.startswith(("d_ring", "d_kc", "d_vc", "d_wfl"))}
        for e in engines:
            for s, v in snap.items():
                if s == e and e == "pe":
                    continue
                if self.pending[e].get(s, 0) < v:
                    self.pending[e][s] = v

    def finish(self, eng="sp"):
        waits = []
        for s, v in self.cnt.items():
            if v > 0 and self.seen[eng].get(s, 0) < v:
                waits.append((s, v))
        self.q[eng].append((waits, None, None))

    def replay(self, name, e):
        for waits, fn, inc in self.q[name]:
            for s, v in waits:
                e.wait_ge(self.sems[s], v)
            if fn is None:
                continue
            ins = fn(e)
            if inc is not None:
                ins.then_inc(self.sems[inc[0]], inc[1])

    def emit(self):
        nc = self.nc
        with nc.Block() as block:
            @block.tensor
            def _(e):
                self.replay("pe", e)

            @block.scalar
            def _(e):
                self.replay("act", e)

            @block.vector
            def _(e):
                self.replay("dve", e)

            @block.gpsimd
            def _(e):
                self.replay("pool", e)

            @block.sync
            def _(e):
                self.replay("sp", e)


class Mem:
    def __init__(self, nc):
        self.nc = nc
        self.base = 16512
        self.top = 229344
        self.n = 0

    def at(self, off, shape, dtype, name):
        self.n += 1
        nb = int(np.prod(shape[1:])) * (4 if dtype == F32 else 2)
        assert off % 32 == 0, (name, off)
        assert self.base <= off and off + nb <= self.top, (name, off, nb, self.top)
        return self.nc.alloc_sbuf_tensor_at("%s_%d" % (name, self.n), list(shape), dtype, offset=off)


def build(dbg=None, stop_after=99):
    dbg = dbg or []
    nc = bass.Bass("TRN2", target_bir_lowering=False)
    S = Sched(nc)
    M = Mem(nc)

    def din(name, shape):
        return nc.dram_tensor(name, list(shape), F32, kind="ExternalInput")

    def dout(name, shape):
        return nc.dram_tensor(name, list(shape), F32, kind="ExternalOutput")

    x_prompt = din("x_prompt", (SEQ, D))
    x_sample = din("x_sample", (NS, D))
    cache_k = din("cache_k", (PAST, 512))
    cache_v = din("cache_v", (PAST, 512))
    cache_logf = din("cache_logf", (PAST, H))
    meta = din("meta", (NMETA, D))
    w_in = din("w_in", (D, INC))
    w_br_conv = din("w_br_conv", (DC, D))
    w_br_attn = din("w_br_attn", (DA, D))
    w_out = din("w_out", (D, D))
    w_up = din("w_up", (D, DFF))
    w_down = din("w_down", (DFF, D))
    NPK = 64
    ppk = din("ppk", (P, NPK))
    stc = din("state_conv_t", (P, 8))

    y_prompt = dout("y_prompt", (SEQ, D))
    y_sample = dout("y_sample", (NS, D))
    nk_p = dout("nk_p", (L, 512))
    nv_p = dout("nv_p", (L, 512))
    nf_p = dout("nf_p", (L, H))
    nc_p = dout("nc_p", (2, DC))
    nk_s = dout("nk_s", (NS, 512))
    nv_s = dout("nv_s", (NS, 512))
    nf_s = dout("nf_s", (NS, H))
    nc_s = dout("nc_s", (2, DC))
    dbg_out = {}
    for (name, shape, dt_) in dbg:
        dbg_out[name] = nc.dram_tensor("dbg_" + name, list(shape), dt_, kind="ExternalOutput")

    if os.environ.get("PAIR_EXP", "1") == "1":
        PS2 = [nc.alloc_psum_tensor("ps2_%d" % i, [P, 1024], F32) for i in range(4)]
        PS = [PS2[i // 2][:, (i % 2) * 512:(i % 2 + 1) * 512] for i in range(8)]
    else:
        PS2 = None
        PS = [nc.alloc_psum_tensor("ps%d" % i, [P, 512], F32) for i in range(8)]

    o = M.base
    RING_SLOTS = 5
    ring = [M.at(o + i * 8192, (P, 4096), BF16, "ring") for i in range(RING_SLOTS)]
    o += RING_SLOTS * 8192
    pk = M.at(o, (P, NPK), F32, "pk"); o += NPK * 4
    identb = M.at(o, (P, P), BF16, "identb"); o += 256
    identf = M.at(o, (P, P), F32, "identf"); o += 512
    small = [o]
    o += 13312
    def sm(shape, dtype, name):
        nb = int(np.prod(shape[1:])) * (4 if dtype == F32 else 2)
        nb = (nb + 31) // 32 * 32
        t = M.at(small[0], shape, dtype, name)
        small[0] += nb
        assert small[0] <= o_small_end
        return t
    o_small_end = o
    AUG = o; o += 12544
    R2 = o; o += 33280
    R1 = o; o += 33280
    R3 = o; o += 34304
    R45 = o
    R45_SIZE = M.top - o
    assert R45_SIZE >= 41472, R45_SIZE

    xnT = M.at(R1, (P, KD, NT), BF16, "xnT")
    NXT = 6
    xt = [M.at(R3 + i * 4096, (P, D), F32, "xt") for i in range(2)] + \
         [M.at(R45 + 25600 + i * 4096, (P, D), F32, "xt") for i in range(4)]
    xs = [M.at(R3 + 8192 + i * 2048, (P, D), BF16, "xs") for i in range(2)] + [M.at(R45 + 41984, (P, D), BF16, "xs")]
    junk = M.at(R3 + 12288, (P, D), BF16, "junk")
    ssq = sm((P, 32), F32, "ssq")
    rstd = sm((P, 32), F32, "rstd")

    S.dma("sp", "d_pk", lambda e: e.dma_start(out=pk[:, :], in_=ppk[:, :]), writes=["pk"])
    def mk_ident(t, key):
        S.op("pool", lambda e: e.memset(t[:, :], 1.0), writes=[key])
        S.op("pool", lambda e: e.affine_select(t[:, :], t[:, :], [[-1, P]], ALU.is_equal, 0.0,
                                               base=0, channel_multiplier=1), reads=[key], writes=[key])
    mk_ident(identb, "identb")
    mk_ident(identf, "identf")

    epsc = sm((P, 1), F32, "epsc")
    S.op("pool", lambda e: e.memset(epsc[:, :], EPS), writes=["epsc"])

    def load_xtile(i, buf, key, sem):
        if i == 0:
            S.dma("sp", sem, lambda e: e.dma_start(out=buf[0:16, :], in_=meta[:, :]), writes=[key])
            S.dma("sp", sem, lambda e: e.dma_start(out=buf[16:128, :], in_=x_prompt[0:112, :]), writes=[key])
        elif i < 16:
            S.dma("sp", sem, lambda e: e.dma_start(out=buf[:, :], in_=x_prompt[128 * i - 16:128 * i + 112, :]), writes=[key])
        else:
            S.dma("sp", sem, lambda e: e.dma_start(out=buf[0:16, :], in_=x_prompt[2032:2048, :]), writes=[key])
            S.dma("sp", sem, lambda e: e.dma_start(out=buf[16:32, :], in_=x_sample[:, :]), writes=[key])

    def rms_rstd(src, r, col, inv_n, kin, tagk, junk=junk, jkey="junk"):
        S.op("act", lambda e: e.activation(out=junk[0:r, :], in_=src, func=AF.Square,
                                           accum_out=ssq[0:r, col:col + 1]),
             reads=[kin, "epsc"], writes=[jkey, "ssq%s%d" % (tagk, col)])
        S.op("act", lambda e: e.activation(out=ssq[0:r, col:col + 1], in_=ssq[0:r, col:col + 1], func=AF.Ln,
                                           bias=epsc[0:r, :], scale=inv_n),
             reads=["ssq%s%d" % (tagk, col)], writes=["ssq%s%d" % (tagk, col)])
        S.op("act", lambda e: e.activation(out=rstd[0:r, col:col + 1], in_=ssq[0:r, col:col + 1], func=AF.Exp,
                                           scale=-0.5),
             reads=["ssq%s%d" % (tagk, col)], writes=["rstd%s%d" % (tagk, col)])

    for i in range(NTILE):
        r = tile_rows(i)
        b = i % NXT
        b3 = i % 3
        kx, ks = "xt%d" % b, "xs%d" % b3
        load_xtile(i, xt[b], kx, "d_xt%d" % b)
        rms_rstd(xt[b][0:r, :], r, i, 1.0 / D, kx, "a")
        S.op("dve", lambda e, b=b, b3=b3, r=r, i=i: e.tensor_scalar(out=xs[b3][0:r, :], in0=xt[b][0:r, :],
                                                          scalar1=rstd[0:r, i:i + 1], scalar2=None, op0=ALU.mult),
             reads=[kx, "rstda%d" % i], writes=[ks])
        pb = i % 2
        pst = PS[pb][:, :].bitcast(BF16)
        for k in range(KD):
            S.op("pe", lambda e, k=k, b3=b3, r=r, pst=pst: e.transpose(out=pst[:, k * 128:k * 128 + r],
                                                                    in_=xs[b3][0:r, k * 128:(k + 1) * 128],
                                                                    identity=identb[0:r, 0:r]),
                 reads=[ks, "identb"], writes=["ps%d" % pb], sig=(k == KD - 1))
        c0 = 128 * i
        S.op("dve", lambda e, pst=pst, r=r, c0=c0: e.tensor_tensor(
            out=xnT[:, :, c0:c0 + r],
            in0=pst.rearrange("p (k t) -> p k t", k=KD)[:, :, 0:r],
            in1=pk[:, 0:KD].unsqueeze(2).to_broadcast([P, KD, r]), op=ALU.mult),
             reads=["ps%d" % pb, "pk"], writes=["xnT%d" % i])

    if "xnT" in dbg_out:
        S.dma("sp", "d_dbg", lambda e: e.dma_start(out=dbg_out["xnT"][:, :, :], in_=xnT[:, :, :]),
              reads=["xnT%d" % i for i in range(NTILE)])

    class _Stop(Exception):
        pass

    def chk(st):
        if stop_after < st:
            raise _Stop()

    try:
        XN_ALL = ["xnT%d" % i for i in range(NTILE)]

        def xn_keys(c0, n):
            return ["xnT%d" % i for i in range(c0 // 128, (c0 + n - 1) // 128 + 1)]

        wlist = []

        def wg_cols(w, c0, kch=KD, ncol=512):
            return (lambda t: t[:, 0:kch * ncol].rearrange("p (k c) -> p k c", k=kch),
                    w[0:kch * 128, c0:c0 + ncol].rearrange("(k p) c -> p k c", p=P))

        G_Q, G_K, G_V = 0, 1, 2
        wlist.append(wg_cols(w_in, C_Q))
        wlist.append(wg_cols(w_in, C_K))
        wlist.append(wg_cols(w_in, C_V))
        G_CV = [3, 4, 5, 6]
        for cch in range(4):
            wlist.append((lambda t: t[:, 0:KD * 384].rearrange("p (k c) -> p k c", k=KD),
                          [((128 * j3, 128 * (j3 + 1)),
                            w_in[:, base + 128 * cch:base + 128 * (cch + 1)].rearrange("(k p) c -> p k c", p=P))
                           for j3, base in enumerate((C_B, C_C, C_H))]))
        G_BRC, G_BRA = 7, 8
        G_GP = [9, 10, 11, 12]
        wlist.append(wg_cols(w_br_conv, 0, kch=4, ncol=1024))
        wlist.append(wg_cols(w_br_attn, 0, kch=4, ncol=1024))
        for pr in range(4):
            wlist.append((lambda t: t[:, 0:KD * 512].rearrange("p (k c) -> p k c", k=KD),
                          [((0, 256), w_in[:, C_G + 256 * pr:C_G + 256 * (pr + 1)].rearrange("(k p) c -> p k c", p=P)),
                           ((256, 512), w_in[:, C_G + 1024 + 256 * pr:C_G + 1024 + 256 * (pr + 1)].rearrange("(k p) c -> p k c", p=P))]))
        G_OUT0 = 13
        wlist.append(wg_cols(w_out, 0))
        wlist.append(wg_cols(w_out, 512))
        G_MLP = 15
        for g in range(8):
            wlist.append(wg_cols(w_up, 512 * g))
            wlist.append((lambda t: t[:, :].rearrange("p (k c) -> p k c", k=4),
                          w_down[512 * g:512 * (g + 1), :].rearrange("(k p) c -> p k c", p=P)))
        wstate = {"issued": 0, "free": list(range(RING_SLOTS)), "slot": {}}

        def w_try_issue(limit=None, after=()):
            while wstate["issued"] < len(wlist) and wstate["free"] and (limit is None or wstate["issued"] < limit):
                g = wstate["issued"]
                slot = wstate["free"].pop(0)
                wstate["slot"][g] = slot
                vf, srcs = wlist[g]
                dst = vf(ring[slot])
                if not isinstance(srcs, list):
                    srcs = [(None, srcs)]
                for (sub, src) in srcs:
                    d = dst if sub is None else dst[:, :, sub[0]:sub[1]]
                    S.dma("pool", "d_ring%d" % slot, lambda e, d=d, src=src: e.dma_start(out=d, in_=src),
                          reads=list(after), writes=["ring%d" % slot])
                wstate["issued"] += 1

        def w_get(g):
            if g not in wstate["slot"]:
                w_try_issue(g + 1)
            slot = wstate["slot"][g]
            return wlist[g][0](ring[slot]), "ring%d" % slot

        def w_done(g):
            wstate["free"].append(wstate["slot"][g])
            w_try_issue()

        w_try_issue(1)
        w_try_issue(2, after=["xt%d" % (7 % NXT)])
        w_try_issue(3, after=["xt%d" % (12 % NXT)])

        blockones = sm((P, P), BF16, "blockones")
        S.op("pool", lambda e: e.memset(blockones[:, :], 0.0), writes=["blockones"])
        S.op("pool", lambda e: e.memset(blockones[0:64, 0:64], 1.0), writes=["blockones"])
        S.op("pool", lambda e: e.memset(blockones[64:128, 64:128], 1.0), writes=["blockones"])
        onesf = sm((P, P), F32, "onesf")
        S.op("pool", lambda e: e.memset(onesf[:, :], 1.0), writes=["onesf"])
        trif = sm((P, P), F32, "trif")
        S.op("pool", lambda e: e.memset(trif[:, :], 1.0), writes=["trif"])
        S.op("pool", lambda e: e.affine_select(trif[:, :], trif[:, :], [[1, P]], ALU.is_ge, 0.0,
                                               base=0, channel_multiplier=-1), reads=["trif"], writes=["trif"])
        maskb = sm((P, P), BF16, "maskb")
        S.op("pool", lambda e: e.memset(maskb[:, :], 1.0), writes=["maskb"])
        S.op("pool", lambda e: e.affine_select(maskb[:, :], maskb[:, :], [[1, P]], ALU.is_ge, 0.0,
                                               base=0, channel_multiplier=-1), reads=["maskb"], writes=["maskb"])
        maskneg = sm((P, P), BF16, "maskneg")
        S.op("pool", lambda e: e.memset(maskneg[:, :], 0.0), writes=["maskneg"])
        S.op("pool", lambda e: e.affine_select(maskneg[:, :], maskneg[:, :], [[1, P]], ALU.is_ge, -9984.0,
                                               base=0, channel_multiplier=-1), reads=["maskneg"], writes=["maskneg"])
        ones3 = sm((3, P), BF16, "ones3")
        S.op("pool", lambda e: e.memset(ones3[:, :], 1.0), writes=["ones3"])
        onecol = sm((P, 1), F32, "onecol")
        S.op("pool", lambda e: e.memset(onecol[:, :], 1.0), writes=["onecol"])
        wfl = sm((P, KD, 8), BF16, "wfl")
        if not os.environ.get("SKIP_WFL"):
          S.dma("pool", "d_wfl", lambda e: e.dma_start(out=wfl[:, :, :],
                                                     in_=w_in[:, C_F:C_F + 8].rearrange("(k p) c -> p k c", p=P)),
              writes=["wfl"])
        fl_all = sm((P, NTILE, 8), F32, "fl_all")
        lf_all = sm((P, NTILE, 8), F32, "lf_all")
        S.op("pool", lambda e: e.memset(fl_all[:, :, :], 0.0), writes=["fl_all"])
        S.op("pool", lambda e: e.memset(lf_all[:, :, :], 0.0), writes=["lf_all"])
        c_all = sm((P, NTILE, 8), F32, "c_all")
        S.op("pool", lambda e: e.memset(c_all[:, :, :], 0.0), writes=["c_all"])
        negc = sm((P, NTILE, 8), F32, "negc")
        csplit = sm((P, NTILE, 3, 8), BF16, "csplit")
        cres = sm((P, NTILE, 8), F32, "cres")
        cT24 = M.at(AUG, (24, NT), BF16, "cT24")
        qaug = [M.at(AUG + 4160 + i * 4160, (3, NT), BF16, "qaug") for i in range(2)]

        qT = M.at(R2, (P, 4, NT), BF16, "qT")
        kT = M.at(R2 + 16640, (P, 4, NT), BF16, "kT")
        Vp = M.at(R3 + 14336, (P, NTILE, H, 66), BF16, "Vp")
        sqb = [M.at(R45 + i * 1024, (P, 512), BF16, "sqb") for i in range(2)]
        rsb = [M.at(R45 + 2048 + i * 2048, (P, 512), F32, "rsb") for i in range(2)]
        kf = M.at(R45 + 6144, (P, 4, 512), F32, "kf")
        ktok = [M.at(R45 + 14336 + i * 2048, (P, 512), F32, "ktok") for i in range(2)]
        vtok = [M.at(R45 + 18432 + i * 2048, (P, 512), F32, "vtok") for i in range(2)]

        psn = {"n": 0}

        def next_ps(lo=0, hi=8, key="n"):
            psn[key] = psn.get(key, lo - 1) + 1
            if psn[key] >= hi or psn[key] < lo:
                psn[key] = lo
            return psn[key]

        if not os.environ.get("SKIP_VPMEM"):
            S.op("pool", lambda e: e.memset(Vp[:, :, :, 64:65], 1.0), writes=["Vp_ones"])

        chk(1)
        sqb3 = [M.at(R45 + 22528 + i * 1024, (P, 512), BF16, "sqb3") for i in range(3)]
        cnt1 = {"kt": 0}
        units1 = []
        for which, G in (("q", G_Q), ("k", G_K)):
            for (c0, n) in BLKS:
                for m in range(4):
                    units1.append(dict(which=which, G=G, c0=c0, n=n, m=m, idx=len(units1)))

        def s1A(u):
            which, G, c0, n, m = u["which"], u["G"], u["c0"], u["n"], u["m"]
            wv, wkey = w_get(G)
            A = next_ps(0, 4, "qa")
            sj = u["idx"] % 3
            u["A"], u["sj"] = A, sj
            for k in range(KD):
                S.op("pe", lambda e, k=k: e.matmul(
                    PS[A][:, 0:n], lhsT=wv[:, k, m * 128:(m + 1) * 128], rhs=xnT[:, k, c0:c0 + n],
                    start=(k == 0), stop=(k == KD - 1)),
                     reads=[wkey] + xn_keys(c0, n), writes=["ps%d" % A], sig=(k == KD - 1))
            S.op("act", lambda e: e.activation(out=sqb3[sj][:, 0:n], in_=PS[A][:, 0:n], func=AF.Square),
                 reads=["ps%d" % A], writes=["sqb%d" % sj])

        def s1B(u):
            which, G, c0, n, m = u["which"], u["G"], u["c0"], u["n"], u["m"]
            A, sj = u["A"], u["sj"]
            j = u["idx"] % 2
            B = next_ps(4, 6, "qb")
            S.op("pe", lambda e: e.matmul(PS[B][:, 0:n], lhsT=blockones[:, :], rhs=sqb3[sj][:, 0:n], start=True, stop=True),
                 reads=["sqb%d" % sj, "blockones"], writes=["ps%d" % B])
            S.op("act", lambda e: e.activation(out=rsb[j][:, 0:n], in_=PS[B][:, 0:n], func=AF.Ln,
                                               bias=epsc[:, :], scale=1.0 / HD),
                 reads=["ps%d" % B, "epsc"], writes=["rsb%d" % j])
            S.op("act", lambda e: e.activation(out=rsb[j][:, 0:n], in_=rsb[j][:, 0:n], func=AF.Exp, scale=-0.5),
                 reads=["rsb%d" % j], writes=["rsb%d" % j])
            if which == "q":
                S.op("dve", lambda e: e.scalar_tensor_tensor(
                    out=qT[:, m, c0:c0 + n], in0=PS[A][:, 0:n], scalar=pk[:, 40:41], in1=rsb[j][:, 0:n],
                    op0=ALU.mult, op1=ALU.mult),
                     reads=["ps%d" % A, "rsb%d" % j, "pk"], writes=["qT%d_%d" % (m, c0)])
            else:
                S.op("dve", lambda e: e.scalar_tensor_tensor(
                    out=kf[:, m, 0:n], in0=PS[A][:, 0:n], scalar=pk[:, 41:42], in1=rsb[j][:, 0:n],
                    op0=ALU.mult, op1=ALU.mult),
                     reads=["ps%d" % A, "rsb%d" % j, "pk"], writes=["kf%d" % m])
                S.op("dve", lambda e: e.tensor_copy(out=kT[:, m, c0:c0 + n], in_=kf[:, m, 0:n]),
                     reads=["kf%d" % m], writes=["kT%d_%d" % (m, c0)])
                if m == 3:
                    for tt in range((n + 127) // 128):
                        r = min(128, n - tt * 128)
                        jj = cnt1["kt"] % 2
                        cnt1["kt"] += 1
                        Cb = next_ps(6, 8, "kt")
                        for mm in range(4):
                            S.op("pe", lambda e, Cb=Cb, mm=mm, tt=tt, r=r: e.transpose(
                                out=PS[Cb][0:r, mm * 128:(mm + 1) * 128], in_=kf[:, mm, tt * 128:tt * 128 + r], identity=identf[:, :]),
                                 reads=["kf%d" % mm, "identf"], writes=["ps%d" % Cb], sig=(mm == 3))
                        S.op("dve", lambda e, Cb=Cb, jj=jj, r=r: e.tensor_copy(out=ktok[jj][0:r, :], in_=PS[Cb][0:r, :]),
                             reads=["ps%d" % Cb], writes=["ktok%d" % jj])
                        p0 = c0 + tt * 128
                        if p0 < 2048:
                            S.dma("sp", "d_ktok%d" % jj, lambda e, jj=jj, p0=p0: e.dma_start(out=nk_p[p0:p0 + 128, :], in_=ktok[jj][:, :]),
                                  reads=["ktok%d" % jj], writes=["nk_p_%d" % p0])
                        else:
                            S.dma("sp", "d_ktok%d" % jj, lambda e, jj=jj: e.dma_start(out=nk_p[2048:2064, :], in_=ktok[jj][0:16, :]),
                                  reads=["ktok%d" % jj], writes=["nk_p_%d" % p0])
                            S.dma("sp", "d_ktok%d" % jj, lambda e, jj=jj: e.dma_start(out=nk_s[:, :], in_=ktok[jj][16:32, :]),
                                  reads=["ktok%d" % jj], writes=["nk_s"])
            if m == 3 and c0 == 2048:
                w_done(G)

        LA1 = 2
        for i in range(LA1):
            s1A(units1[i])
        for i in range(len(units1)):
            s1B(units1[i])
            if i + LA1 < len(units1):
                s1A(units1[i + LA1])

        chk(2)
        wv, wkey = w_get(G_V)
        FB = 4
        for i in range(NTILE):
            r = tile_rows(i)
            c0 = 128 * i
            for k in range(KD):
                S.op("pe", lambda e, k=k, r=r, c0=c0, i=i: e.matmul(
                    PS[FB][0:r, 8 * i:8 * i + 8], lhsT=xnT[:, k, c0:c0 + r], rhs=wfl[:, k, :], start=(k == 0), stop=(k == KD - 1)),
                     reads=["wfl", "xnT%d" % i], writes=["ps%d" % FB], sig=(k == KD - 1))
        S.op("dve", lambda e: e.tensor_tensor(out=fl_all[:, 0:16, :], in0=PS[FB][:, 0:128].rearrange("p (i h) -> p i h", h=8),
                                              in1=pk[:, 16:24].unsqueeze(1).to_broadcast([P, 16, 8]), op=ALU.add),
             reads=["ps%d" % FB, "pk"], writes=["fl_all"])
        S.op("dve", lambda e: e.tensor_tensor(out=fl_all[0:32, 16, :], in0=PS[FB][0:32, 128:136], in1=pk[0:32, 16:24], op=ALU.add),
             reads=["ps%d" % FB, "pk"], writes=["fl_all"])

        def logsig(dst, src, r, kin, kout):
            S.op("act", lambda e: e.activation(out=dst, in_=src, func=AF.Exp, scale=-1.0), reads=[kin], writes=[kout])
            S.op("act", lambda e: e.activation(out=dst, in_=dst, func=AF.Ln, bias=onecol[0:r, :], scale=1.0),
                 reads=[kout, "onecol"], writes=[kout])
            S.op("dve", lambda e: e.tensor_scalar(out=dst, in0=dst, scalar1=-1.0, scalar2=None, op0=ALU.mult),
                 reads=[kout], writes=[kout])
        logsig(lf_all[:, 0:16, :], fl_all[:, 0:16, :], P, "fl_all", "lf_all")
        logsig(lf_all[0:32, 16, :], fl_all[0:32, 16, :], 32, "fl_all", "lf_all")
        S.dma("sp", "d_lf", lambda e: e.dma_start(out=nf_p[0:2048, :].rearrange("(i p) h -> p i h", p=P), in_=lf_all[:, 0:16, :]),
              reads=["lf_all"], writes=["nf_p_a"])
        S.dma("sp", "d_lf", lambda e: e.dma_start(out=nf_p[2048:2064, :], in_=lf_all[0:16, 16, :]), reads=["lf_all"], writes=["nf_p_b"])
        S.dma("sp", "d_lf", lambda e: e.dma_start(out=nf_s[:, :], in_=lf_all[16:32, 16, :]), reads=["lf_all"], writes=["nf_s"])

        carr = sm((P, NTILE, 8), F32, "carr")
        lfc = sm((P, 16, 8), F32, "lfc")
        S.dma("sp", "d_lfc", lambda e: e.dma_start(out=lfc[:, :, :], in_=cache_logf[:, :].rearrange("(i p) h -> p i h", p=P)),
              writes=["lfc"])
        c_s = sm((P, NTILE, 8), F32, "c_s")
        S.op("pool", lambda e: e.memset(c_s[:, :, :], 0.0), writes=["c_s"])
        negc_s = sm((P, NTILE, 8), F32, "negc_s")
        msel = sm((32, 16), F32, "msel")
        S.op("pool", lambda e: e.memset(msel[:, :], 1.0), writes=["msel"])
        S.op("pool", lambda e: e.affine_select(msel[:, :], msel[:, :], [[1, 16]], ALU.is_ge, 0.0,
                                               base=16, channel_multiplier=-1), reads=["msel"], writes=["msel"])
        S.op("pool", lambda e: e.memset(msel[0:16, :], 0.0), reads=["msel"], writes=["msel"])
        carr_s = sm((P, 17, 8), F32, "carr_s")
        Cb = 5
        S.op("pe", lambda e: e.matmul(PS[Cb][:, 0:136], lhsT=trif[:, :], rhs=lf_all[:, :, :].rearrange("p i h -> p (i h)"), start=True, stop=True),
             reads=["lf_all", "trif"], writes=["ps%d" % Cb])
        S.op("pe", lambda e: e.matmul(PS[Cb][:, 136:272], lhsT=onesf[:, :], rhs=lf_all[:, :, :].rearrange("p i h -> p (i h)"), start=True, stop=True),
             reads=["lf_all", "onesf"], writes=["ps%d" % Cb])
        Cs = 6
        S.op("pe", lambda e: e.matmul(PS[Cs][:, 0:128], lhsT=trif[:, :], rhs=lfc[:, :, :].rearrange("p i h -> p (i h)"), start=True, stop=True),
             reads=["lfc", "trif"], writes=["ps%d" % Cs])
        S.op("pe", lambda e: e.matmul(PS[Cs][:, 128:256], lhsT=onesf[:, :], rhs=lfc[:, :, :].rearrange("p i h -> p (i h)"), start=True, stop=True),
             reads=["lfc", "onesf"], writes=["ps%d" % Cs])
        S.op("pe", lambda e: e.matmul(PS[Cs][0:16, 256:264], lhsT=msel[:, :], rhs=lf_all[0:32, 16, :], start=True, stop=True),
             reads=["lf_all", "msel"], writes=["ps%d" % Cs])

        chain = []

        def DF(fn, **kw):
            chain.append(lambda: S.op("dve", fn, **kw))
        DF(lambda e: e.memset(carr[:, 0, :], 0.0), writes=["carr"])
        for i in range(1, NTILE):
            DF(lambda e, i=i: e.tensor_tensor(out=carr[:, i, :], in0=carr[:, i - 1, :], in1=PS[Cb][:, 136 + 8 * (i - 1):136 + 8 * i], op=ALU.add),
              reads=["carr", "ps%d" % Cb], writes=["carr"])
        DF(lambda e: e.tensor_tensor(out=c_all[:, :, :], in0=carr[:, :, :], in1=PS[Cb][:, 0:136].rearrange("p (i h) -> p i h", h=8), op=ALU.add),
          reads=["carr", "ps%d" % Cb], writes=["c_all"])
        key = "c_all"
        DF(lambda e: e.tensor_scalar(out=negc[:, :, :], in0=c_all[:, :, :], scalar1=-1.0, scalar2=None, op0=ALU.mult),
          reads=[key], writes=[key + "_neg"])
        DF(lambda e: e.tensor_copy(out=csplit[:, :, 0, :], in_=c_all[:, :, :]), reads=[key], writes=[key + "_s"])
        DF(lambda e: e.tensor_tensor(out=cres[:, :, :], in0=c_all[:, :, :], in1=csplit[:, :, 0, :], op=ALU.subtract),
          reads=[key, key + "_s"], writes=[key + "_r"])
        DF(lambda e: e.tensor_copy(out=csplit[:, :, 1, :], in_=cres[:, :, :]), reads=[key + "_r"], writes=[key + "_s"])
        DF(lambda e: e.tensor_tensor(out=cres[:, :, :], in0=cres[:, :, :], in1=csplit[:, :, 1, :], op=ALU.subtract),
          reads=[key + "_r", key + "_s"], writes=[key + "_r"])
        DF(lambda e: e.tensor_copy(out=csplit[:, :, 2, :], in_=cres[:, :, :]), reads=[key + "_r"], writes=[key + "_s"])
        DF(lambda e: e.memset(carr_s[:, 0, :], 0.0), writes=["carr_s"])
        for i in range(1, 17):
            DF(lambda e, i=i: e.tensor_tensor(out=carr_s[:, i, :], in0=carr_s[:, i - 1, :], in1=PS[Cs][:, 128 + 8 * (i - 1):128 + 8 * i], op=ALU.add),
              reads=["carr_s", "ps%d" % Cs], writes=["carr_s"])
        DF(lambda e: e.tensor_tensor(out=c_s[:, 0:16, :], in0=carr_s[:, 0:16, :], in1=PS[Cs][:, 0:128].rearrange("p (i h) -> p i h", h=8), op=ALU.add),
          reads=["carr_s", "ps%d" % Cs], writes=["c_s"])
        DF(lambda e: e.tensor_tensor(out=c_s[:, 0:16, :], in0=c_s[:, 0:16, :],
                                    in1=carr_s[:, 16, :].unsqueeze(1).to_broadcast([P, 16, 8]), op=ALU.subtract),
          reads=["carr_s", "c_s"], writes=["c_s"])
        DF(lambda e: e.tensor_copy(out=c_s[0:16, 16, :], in_=PS[Cs][0:16, 256:264]), reads=["ps%d" % Cs], writes=["c_s"])
        DF(lambda e: e.tensor_scalar(out=negc_s[:, :, :], in0=c_s[:, :, :], scalar1=-1.0, scalar2=None, op0=ALU.mult),
          reads=["c_s"], writes=["c_s_neg"])

        def run_chain(n):
            for _ in range(n):
                if chain:
                    chain.pop(0)()

        for i in range(NTILE):
            r = tile_rows(i)
            c0 = 128 * i
            jj = i % 2
            A = next_ps(0, 4, "vp")
            for k in range(KD):
                S.op("pe", lambda e, A=A, k=k, r=r, c0=c0, wv=wv: e.matmul(
                    PS[A][0:r, :], lhsT=xnT[:, k, c0:c0 + r], rhs=wv[:, k, :], start=(k == 0), stop=(k == KD - 1)),
                     reads=[wkey, "xnT%d" % i], writes=["ps%d" % A], sig=(k == KD - 1))
            S.op("dve", lambda e, A=A, jj=jj, r=r: e.tensor_copy(out=vtok[jj][0:r, :], in_=PS[A][0:r, :]),
                 reads=["ps%d" % A], writes=["vtok%d" % jj])
            S.op("dve", lambda e, A=A, r=r, i=i: e.tensor_copy(
                out=Vp[0:r, i, :, 0:64], in_=PS[A][0:r, :].rearrange("p (h d) -> p h d", h=H)),
                 reads=["ps%d" % A], writes=["Vp%d" % i])
            if i < 16:
                S.dma("sp", "d_vtok%d" % jj, lambda e, jj=jj, c0=c0: e.dma_start(out=nv_p[c0:c0 + 128, :], in_=vtok[jj][:, :]),
                      reads=["vtok%d" % jj], writes=["nv_p_%d" % i])
            else:
                S.dma("sp", "d_vtok%d" % jj, lambda e, jj=jj: e.dma_start(out=nv_p[2048:2064, :], in_=vtok[jj][0:16, :]),
                      reads=["vtok%d" % jj], writes=["nv_p_%d" % i])
                S.dma("sp", "d_vtok%d" % jj, lambda e, jj=jj: e.dma_start(out=nv_s[:, :], in_=vtok[jj][16:32, :]),
                      reads=["vtok%d" % jj], writes=["nv_s"])
            run_chain(4)
        run_chain(len(chain))
        Vsn = sm((16, H, 66), BF16, "Vsn")
        S.op("pool", lambda e: e.memset(Vsn[:, :, 64:65], 1.0), writes=["Vsn_ones"])
        A = next_ps(0, 4, "vp")
        for k in range(KD):
            S.op("pe", lambda e, A=A, k=k, wv=wv: e.matmul(PS[A][0:16, :], lhsT=xnT[:, k, L:NT], rhs=wv[:, k, :],
                                                          start=(k == 0), stop=(k == KD - 1)),
                 reads=[wkey, "xnT16"], writes=["ps%d" % A], sig=(k == KD - 1))
        S.op("dve", lambda e, A=A: e.tensor_copy(out=Vsn[:, :, 0:64], in_=PS[A][0:16, :].rearrange("p (h d) -> p h d", h=H)),
             reads=["ps%d" % A], writes=["Vsn"])
        w_done(G_V)

        for bnk in range(3):
            Tb = next_ps(4, 8, "ct")
            pst = PS[Tb][:, :].bitcast(BF16)
            tiles = list(range(8 * bnk, min(NTILE, 8 * bnk + 8)))
            for i in tiles:
                r = 128 if i < 16 else 16
                S.op("pe", lambda e, i=i, r=r, pst=pst: e.transpose(
                    out=pst[0:24, 128 * (i % 8):128 * (i % 8) + r], in_=csplit[0:r, i, :, :].rearrange("p j h -> p (j h)"),
                    identity=identb[0:r, 0:r]),
                     reads=["c_all_s", "identb"], writes=["ps%d" % Tb], sig=(i == tiles[-1]))
            w0 = 128 * tiles[0]
            wn = sum(128 if i < 16 else 16 for i in tiles)
            S.op("dve", lambda e, pst=pst, w0=w0, wn=wn: e.tensor_scalar(out=cT24[:, w0:w0 + wn], in0=pst[0:24, 0:wn],
                                                                  scalar1=8.0, scalar2=None, op0=ALU.mult),
                 reads=["ps%d" % Tb], writes=["c_all_T"])

        chk(6)
        S.barrier(engines=("pe", "act", "dve", "sp", "pool"))

        attnT = M.at(R45, (P, 4, NT), BF16, "attnT")
        attn_tok = M.at(R45 + 16640, (P, NTILE, 512), BF16, "attn_tok")
        NPB = 4
        Pb = [M.at(R45 + 34048 + i * 2048, (P, 1024), BF16, "Pb") for i in range(NPB)]
        Qh = [M.at(R45 + i * 4160, (P, NT), BF16, "Qh") for i in range(2)]
        Kh = [M.at(R45 + 8320 + i * 4160, (P, NT), BF16, "Kh") for i in range(2)]
        ncT24 = M.at(AUG + 4160, (24, NT), BF16, "ncT24")
        S.op("dve", lambda e: e.tensor_scalar(out=ncT24[:, 0:L], in0=cT24[:, 0:L], scalar1=-1.0, scalar2=None, op0=ALU.mult),
             reads=["c_all_T"], writes=["ncT24"])
        for i in range(2):
            S.op("pool", lambda e, i=i: e.memset(Qh[i][64:128, :], 0.0), writes=["Qh%d" % i])
            S.op("pool", lambda e, i=i: e.memset(Kh[i][64:128, :], 0.0), writes=["Kh%d" % i])
        S.op("dve", lambda e: e.memset(Qh[0][64:70, :], 1.0), writes=["Qh0"])
        S.dma("sp", "d_qh1", lambda e: e.dma_start(out=Qh[1][67:70, 0:L], in_=Qh[0][67:70, 0:L]), reads=["Qh0"], writes=["Qh1"])
        S.dma("sp", "d_kh0", lambda e: e.dma_start(out=Kh[0][64:67, 0:L], in_=Qh[0][67:70, 0:L]), reads=["Qh0"], writes=["Kh0"])
        S.dma("sp", "d_kh1", lambda e: e.dma_start(out=Kh[1][64:67, 0:L], in_=Qh[0][67:70, 0:L]), reads=["Qh0"], writes=["Kh1"])
        rsum = sm((P, 4), F32, "rsum")
        QG = [(0, 512), (512, 512), (1024, 512), (1536, 512), (2048, 16)]
        units = []
        for h in range(H):
            for gi, (q0, qn) in enumerate(QG):
                kt_last = (q0 + qn - 1) // 128
                kts = list(range(kt_last + 1))
                groups = []
                full = [kt for kt in kts if kt * 128 < q0 and qn == 512]
                rest = [kt for kt in kts if kt not in full]
                for j in range(0, len(full), 2):
                    groups.append(full[j:j + 2])
                for kt in rest:
                    groups.append([kt])
                for gj, g in enumerate(groups):
                    units.append(dict(h=h, q0=q0, qn=qn, kts=g, first_h=(gi == 0 and gj == 0), first_g=(gj == 0),
                                      last_g=(gj == len(groups) - 1), idx=len(units)))
        ostate = {}

        def emitA(u):
            h, q0, qn, kts = u["h"], u["q0"], u["qn"], u["kts"]
            hp, hoff = h // 2, (h % 2) * 64
            hb = h % 2
            nqb = (qn + 127) // 128
            if u["first_h"]:
                S.dma("sp", "d_qh%d" % hb, lambda e: e.dma_start(out=Qh[hb][0:64, :], in_=qT[hoff:hoff + 64, hp, :]),
                      reads=["qT_all"], writes=["Qh%d" % hb])
                S.dma("sp", "d_kh%d" % hb, lambda e: e.dma_start(out=Kh[hb][0:64, :], in_=kT[hoff:hoff + 64, hp, :]),
                      reads=["kT_all"], writes=["Kh%d" % hb])
                for j3 in range(3):
                    S.dma("sp", "d_qh%d" % hb, lambda e, j3=j3: e.dma_start(
                        out=Qh[hb][64 + j3:65 + j3, 0:L], in_=cT24[8 * j3 + h:8 * j3 + h + 1, 0:L]),
                          reads=["c_all_T"], writes=["Qh%d" % hb])
                    S.dma("sp", "d_kh%d" % hb, lambda e, j3=j3: e.dma_start(
                        out=Kh[hb][67 + j3:68 + j3, 0:L], in_=ncT24[8 * j3 + h:8 * j3 + h + 1, 0:L]),
                          reads=["ncT24"], writes=["Kh%d" % hb])
            if u["first_g"]:
                O = next_ps(6, 8, "o")
                ostate[(h, q0)] = O
                S.op("dve", lambda e, O=O, nqb=nqb: e.memset(PS[O][:, 0:65 * nqb], 0.0), writes=["ps%d" % O])
            pp = next_ps(0, 3, "sp2")
            pj = u["idx"] % NPB
            u["pj"] = pj
            u["geo"] = []
            for hf, kt in enumerate(kts):
                kr = 128 if kt < 16 else 16
                qs = max(q0, kt * 128)
                nn = q0 + qn - qs
                u["geo"].append((kt, kr, qs, nn))
                diag = (kt * 128 >= q0)
                dst = PS2[pp][0:kr, hf * 512:hf * 512 + nn] if PS2 is not None else PS[2 * pp + hf][0:kr, 0:nn]
                S.op("pe", lambda e, dst=dst, kt=kt, kr=kr, qs=qs, nn=nn, diag=diag: e.matmul(
                    dst, lhsT=Kh[hb][:, kt * 128:kt * 128 + kr], rhs=Qh[hb][:, qs:qs + nn], start=True, stop=not diag),
                     reads=["Kh%d" % hb, "Qh%d" % hb], writes=["ps%d" % (2 * pp), "ps%d" % (2 * pp + 1)], sig=not diag)
                if diag:
                    dn = min(128, nn)
                    S.op("pe", lambda e, kr=kr, dn=dn, hf=hf: e.matmul(
                        (PS2[pp][0:kr, hf * 512:hf * 512 + dn] if PS2 is not None else PS[2 * pp + hf][0:kr, 0:dn]),
                        lhsT=identb[0:kr, 0:kr], rhs=maskneg[0:kr, 0:dn], start=False, stop=True),
                         reads=["identb", "maskneg"], writes=["ps%d" % (2 * pp), "ps%d" % (2 * pp + 1)])
            if len(kts) == 2 and os.environ.get("PAIR_EXP", "1") == "1":
                S.op("act", lambda e: e.activation(out=Pb[pj][:, :], in_=PS2[pp][:, :], func=AF.Exp, scale=0.125),
                     reads=["ps%d" % (2 * pp), "ps%d" % (2 * pp + 1)], writes=["Pb%d" % pj])
            elif len(kts) == 2:
                for hf in range(2):
                    S.op("act", lambda e, hf=hf: e.activation(out=Pb[pj][:, hf * 512:(hf + 1) * 512],
                                                             in_=(PS2[pp][:, hf * 512:(hf + 1) * 512] if PS2 is not None else PS[2 * pp + hf][:, :]),
                                                             func=AF.Exp, scale=0.125),
                         reads=["ps%d" % (2 * pp), "ps%d" % (2 * pp + 1)], writes=["Pb%d" % pj])
            else:
                kt, kr, qs, nn = u["geo"][0]
                S.op("act", lambda e: e.activation(out=Pb[pj][0:kr, 0:nn], in_=(PS2[pp][0:kr, 0:nn] if PS2 is not None else PS[2 * pp][0:kr, 0:nn]),
                                                   func=AF.Exp, scale=0.125),
                     reads=["ps%d" % (2 * pp), "ps%d" % (2 * pp + 1)], writes=["Pb%d" % pj])

        def emitB(u):
            h, q0, qn = u["h"], u["q0"], u["qn"]
            pj = u["pj"]
            nqb = (qn + 127) // 128
            O = ostate[(h, q0)]
            nk = len(u["geo"])
            for hf, (kt, kr, qs, nn) in enumerate(u["geo"]):
                for qb in range(nqb):
                    qcol = q0 + qb * 128
                    qr = min(128, q0 + qn - qcol)
                    if qcol + qr - 1 < kt * 128:
                        continue
                    S.op("pe", lambda e, qb=qb, qr=qr, qcol=qcol, hf=hf, kt=kt, kr=kr, qs=qs: e.matmul(
                        PS[O][0:qr, 65 * qb:65 * qb + 65], lhsT=Pb[pj][0:kr, hf * 512 + qcol - qs:hf * 512 + qcol - qs + qr],
                        rhs=Vp[0:kr, kt, h, 0:65], start=False, stop=(kt == qcol // 128), skip_group_check=True),
                         reads=["Pb%d" % pj, "Vp%d" % kt, "Vp_ones"], writes=["ps%d" % O],
                         sig=(qb == nqb - 1 and hf == nk - 1))
            if u["last_g"]:
                for qb in range(nqb):
                    qcol = q0 + qb * 128
                    qr = min(128, q0 + qn - qcol)
                    S.op("dve", lambda e, qb=qb, qr=qr: e.reciprocal(out=rsum[0:qr, qb:qb + 1], in_=PS[O][0:qr, 65 * qb + 64:65 * qb + 65]),
                         reads=["ps%d" % O], writes=["rsum%d" % qb])
                    S.op("dve", lambda e, qb=qb, qr=qr, qcol=qcol: e.tensor_scalar(
                        out=attn_tok[0:qr, qcol // 128, h * 64:(h + 1) * 64], in0=PS[O][0:qr, 65 * qb:65 * qb + 64],
                        scalar1=rsum[0:qr, qb:qb + 1], scalar2=None, op0=ALU.mult),
                         reads=["ps%d" % O, "rsum%d" % qb], writes=["attn_tok%d" % (qcol // 128)])

        LA = 3
        for i in range(min(LA, len(units))):
            emitA(units[i])
        for i in range(len(units)):
            emitB(units[i])
            if i + LA < len(units):
                emitA(units[i + LA])

        chk(7)
        KC = 256
        kc_tm = [M.at(R3 + i * 2048, (P, 2, 512), BF16, "kc_tm") for i in range(2)]
        vc_tm = [M.at(R3 + 4096 + i * 2048, (P, 2, 512), BF16, "vc_tm") for i in range(2)]
        KcT = [M.at(R3 + 8192 + i * 2048, (P, 2, 4, 128), BF16, "KcT") for i in range(2)]
        Psb = [M.at(R3 + 12288 + i * 512, (P, 2, H, 16), BF16, "Psb") for i in range(2)]
        Vw = [M.at(AUG + 8320 + i * 2112, (P, 2, H, 66), BF16, "Vw") for i in range(2)]
        attn_s = sm((16, 512), BF16, "attn_s")
        rsum_s = sm((16, 8), F32, "rsum_s")
        wts = sm((P, NTILE, 8), F32, "wts")
        Qz = sm((P, H, 16), BF16, "Qz")
        S.op("act", lambda e: e.activation(out=wts[:, :, :].rearrange("p i h -> p (i h)"), in_=negc_s[:, :, :].rearrange("p i h -> p (i h)"), func=AF.Exp),
             reads=["c_s_neg"], writes=["wts"])
        S.op("dve", lambda e: e.memset(Qz[:, :, :], 0.0), writes=["Qz"])
        for h in range(H):
            hp, hoff = h // 2, (h % 2) * 64
            S.op("dve", lambda e, h=h, hp=hp, hoff=hoff: e.tensor_copy(out=Qz[hoff:hoff + 64, h, :], in_=qT[hoff:hoff + 64, hp, L:NT]),
                 reads=["qT_all"], writes=["Qz"])

        def load_cache_chunk(c):
            j = c % 2
            S.dma("pool", "d_kc%d" % j, lambda e, c=c, j=j: e.dma_start(
                out=kc_tm[j][:, :, :], in_=cache_k[KC * c:KC * (c + 1), :].rearrange("(i p) d -> p i d", p=P)),
                  writes=["kc_tm%d" % j])
            S.dma("pool", "d_vc%d" % j, lambda e, c=c, j=j: e.dma_start(
                out=vc_tm[j][:, :, :], in_=cache_v[KC * c:KC * (c + 1), :].rearrange("(i p) d -> p i d", p=P)),
                  writes=["vc_tm%d" % j])
        load_cache_chunk(0)
        load_cache_chunk(1)
        OS = (6, 7)
        for ob in OS:
            S.op("dve", lambda e, ob=ob: e.memset(PS[ob][0:16, 0:260], 0.0), writes=["ps%d" % ob])
        NCH = PAST // KC

        def sA(c):
            j = c % 2
            Tk = next_ps(0, 2, "tk")
            pstk = PS[Tk][:, :].bitcast(BF16)
            for t in range(2):
                for hp in range(4):
                    S.op("pe", lambda e, t=t, hp=hp: e.transpose(
                        out=pstk[:, (t * 4 + hp) * 128:(t * 4 + hp + 1) * 128], in_=kc_tm[j][:, t, hp * 128:(hp + 1) * 128],
                        identity=identb[:, :]),
                         reads=["kc_tm%d" % j, "identb"], writes=["ps%d" % Tk], sig=(t == 1 and hp == 3))
            S.op("act", lambda e: e.activation(out=KcT[j][:, :, :, :].rearrange("p t h k -> p (t h k)"), in_=pstk[:, :], func=AF.Copy),
                 reads=["ps%d" % Tk], writes=["KcT%d" % j])
            for t in range(2):
                kt = 2 * c + t
                S.op("dve", lambda e, t=t, kt=kt: e.tensor_tensor(
                    out=Vw[j][:, t, :, 0:64], in0=vc_tm[j][:, t, :].rearrange("p (h d) -> p h d", h=H),
                    in1=wts[:, kt, :].unsqueeze(2).to_broadcast([P, H, 64]), op=ALU.mult),
                     reads=["vc_tm%d" % j, "wts"], writes=["Vw%d" % j])
                S.op("dve", lambda e, t=t, kt=kt: e.tensor_copy(out=Vw[j][:, t, :, 64], in_=wts[:, kt, :]),
                     reads=["wts"], writes=["Vw%d" % j])
            Sb = next_ps(2, 4, "sb")
            for t in range(2):
                for h in range(H):
                    hp = h // 2
                    col = (t * H + h) * 16
                    S.op("pe", lambda e, col=col, t=t, hp=hp, h=h: e.matmul(
                        PS[Sb][:, col:col + 16], lhsT=KcT[j][:, t, hp, :], rhs=Qz[:, h, :], start=True, stop=True),
                         reads=["KcT%d" % j, "Qz"], writes=["ps%d" % Sb], sig=(t == 1 and h == H - 1))
            S.op("act", lambda e: e.activation(out=Psb[j][:, :, :, :].rearrange("p t h q -> p (t h q)"), in_=PS[Sb][:, 0:256],
                                               func=AF.Exp, scale=0.125),
                 reads=["ps%d" % Sb], writes=["Psb%d" % j])

        def sB(c):
            j = c % 2
            for t in range(2):
                for h in range(H):
                    ob = OS[h // 4]
                    oc = (h % 4) * 65
                    S.op("pe", lambda e, ob=ob, oc=oc, t=t, h=h: e.matmul(
                        PS[ob][0:16, oc:oc + 65], lhsT=Psb[j][:, t, h, :], rhs=Vw[j][:, t, h, 0:65],
                        start=False, stop=False, skip_group_check=True),
                         reads=["Psb%d" % j, "Vw%d" % j], writes=["ps%d" % ob], sig=(t == 1 and h == H - 1))
            if c + 2 < NCH:
                load_cache_chunk(c + 2)
        sA(0)
        for c in range(NCH):
            if c + 1 < NCH:
                sA(c + 1)
            sB(c)
        Sb = next_ps(2, 4, "sb")
        Pn = sm((16, H, 16), BF16, "Pn")
        Vsw = sm((16, H, 66), BF16, "Vsw")
        for h in range(H):
            hp = h // 2
            S.op("pe", lambda e, h=h, hp=hp: e.matmul(
                PS[Sb][0:16, 16 * h:16 * h + 16], lhsT=kT[:, hp, L:NT], rhs=Qz[:, h, :], start=True, stop=False),
                 reads=["Qz"], writes=["ps%d" % Sb], sig=False)
            S.op("pe", lambda e, h=h: e.matmul(
                PS[Sb][0:16, 16 * h:16 * h + 16], lhsT=identb[0:16, 0:16], rhs=maskneg[0:16, 0:16], start=False, stop=True),
                 reads=["identb", "maskneg"], writes=["ps%d" % Sb], sig=(h == H - 1))
        S.op("act", lambda e: e.activation(out=Pn[:, :, :].rearrange("p h q -> p (h q)"), in_=PS[Sb][0:16, 0:128], func=AF.Exp, scale=0.125),
             reads=["ps%d" % Sb], writes=["Pn"])
        S.op("dve", lambda e: e.tensor_tensor(out=Vsw[:, :, 0:65], in0=Vsn[:, :, 0:65],
                                              in1=wts[0:16, 16, :].unsqueeze(2).to_broadcast([16, H, 65]), op=ALU.mult),
             reads=["Vsn", "Vsn_ones", "wts"], writes=["Vsw"])
        for h in range(H):
            ob = OS[h // 4]
            oc = (h % 4) * 65
            S.op("pe", lambda e, ob=ob, oc=oc, h=h: e.matmul(
                PS[ob][0:16, oc:oc + 65], lhsT=Pn[:, h, :], rhs=Vsw[:, h, 0:65], start=False, stop=True, skip_group_check=True),
                 reads=["Pn", "Vsw"], writes=["ps%d" % ob], sig=(h % 4 == 3))
        for h in range(H):
            ob = OS[h // 4]
            oc = (h % 4) * 65
            S.op("dve", lambda e, ob=ob, oc=oc, h=h: e.reciprocal(out=rsum_s[:, h:h + 1], in_=PS[ob][0:16, oc + 64:oc + 65]),
                 reads=["ps%d" % ob], writes=["rsum_s"])
            S.op("dve", lambda e, ob=ob, oc=oc, h=h: e.tensor_scalar(
                out=attn_s[:, h * 64:(h + 1) * 64], in0=PS[ob][0:16, oc:oc + 64], scalar1=rsum_s[:, h:h + 1], scalar2=None, op0=ALU.mult),
                 reads=["ps%d" % ob, "rsum_s"], writes=["attn_s"])

        for i in range(NTILE):
            r = 128 if i < 16 else 16
            Tb = next_ps(0, 4, "ta")
            pst = PS[Tb][:, :].bitcast(BF16)
            for hp in range(4):
                S.op("pe", lambda e, i=i, r=r, hp=hp, pst=pst: e.transpose(
                    out=pst[:, hp * 128:hp * 128 + r], in_=attn_tok[0:r, i, hp * 128:(hp + 1) * 128], identity=identb[0:r, 0:r]),
                     reads=["attn_tok%d" % i, "identb"], writes=["ps%d" % Tb], sig=(hp == 3))
            S.op("dve", lambda e, i=i, r=r, pst=pst: e.tensor_copy(
                out=attnT[:, :, 128 * i:128 * i + r], in_=pst[:, 0:512].rearrange("p (h t) -> p h t", h=4)[:, :, 0:r]),
                 reads=["ps%d" % Tb], writes=["attnT%d" % i])
        Tb = next_ps(0, 4, "ta")
        pst = PS[Tb][:, :].bitcast(BF16)
        for hp in range(4):
            S.op("pe", lambda e, hp=hp, pst=pst: e.transpose(
                out=pst[:, hp * 128:hp * 128 + 16], in_=attn_s[0:16, hp * 128:(hp + 1) * 128], identity=identb[0:16, 0:16]),
                 reads=["attn_s", "identb"], writes=["ps%d" % Tb], sig=(hp == 3))
        S.op("dve", lambda e, pst=pst: e.tensor_copy(
            out=attnT[:, :, L:NT], in_=pst[:, 0:512].rearrange("p (h t) -> p h t", h=4)[:, :, 0:16]),
             reads=["ps%d" % Tb], writes=["attnT17"])
        if "attnT" in dbg_out:
            S.dma("sp", "d_dbg", lambda e: e.dma_start(out=dbg_out["attnT"][:, :, :], in_=attnT[:, :, :]),
                  reads=["attnT%d" % i for i in range(18)])

        chk(8)
        S.barrier()
        ZW = 2088
        zbuf = [M.at(R3 + i * (ZW * 4), (P, ZW), F32, "zbuf") for i in range(2)]
        convT = M.at(R3 + 16896, (P, 4, NT), BF16, "convT")
        tmpC = [M.at(R45 + 16640 + i * 2048, (P, 512), F32, "tmpC") for i in range(2)]
        ytmp = [M.at(R45 + 20736 + i * 2048, (P, 512), F32, "ytmp") for i in range(2)]
        tmpB = [M.at(AUG + 4160 + i * 2048, (P, 512), F32, "tmpB") for i in range(2)]
        stc_sb = sm((P, 8), F32, "stc_sb")
        zlast = sm((P, 4, 4), F32, "zlast")
        S.dma("sp", "d_stc", lambda e: e.dma_start(out=stc_sb[:, :], in_=stc[:, :]), writes=["stc_sb"])
        cc3 = {"n": 0}
        for c in range(4):
            wX, kX = w_get(G_CV[c])
            zb = zbuf[c % 2]
            zk = "zbuf%d" % (c % 2)
            S.op("dve", lambda e, zb=zb: e.memset(zb[:, 0:2], 0.0), writes=[zk])
            S.op("dve", lambda e, zb=zb, c=c: e.tensor_copy(out=zb[:, 2066:2068], in_=stc_sb[:, 2 * c:2 * c + 2]), reads=["stc_sb"], writes=[zk])
            for (c0, n) in BLKS_EQ:
                jj = cc3["n"] % 2
                cc3["n"] += 1
                banks = []
                for j3 in range(3):
                    A = next_ps()
                    banks.append(A)
                    for k in range(KD):
                        S.op("pe", lambda e, A=A, k=k, j3=j3, c0=c0, n=n, wX=wX: e.matmul(
                            PS[A][:, 0:n], lhsT=wX[:, k, 128 * j3:128 * (j3 + 1)], rhs=xnT[:, k, c0:c0 + n],
                            start=(k == 0), stop=(k == KD - 1)), reads=[kX], writes=["ps%d" % A], sig=(k == KD - 1))
                bB, bC, bH = banks
                S.op("act", lambda e, bC=bC, jj=jj, n=n: e.activation(out=tmpC[jj][:, 0:n], in_=PS[bC][:, 0:n], func=AF.Copy),
                     reads=["ps%d" % bC], writes=["tmpC%d" % jj])
                S.op("act", lambda e, bB=bB, jj=jj, n=n: e.activation(out=tmpB[jj][:, 0:n], in_=PS[bB][:, 0:n], func=AF.Copy),
                     reads=["ps%d" % bB], writes=["tmpB%d" % jj])
                segs = []
                if c0 < L:
                    segs.append((0, min(c0 + n, L) - c0, c0 + 2))
                if c0 + n > L:
                    s0 = max(c0, L)
                    segs.append((s0 - c0, c0 + n - s0, s0 + 4))
                for (so, sn, zc) in segs:
                    S.op("dve", lambda e, bH=bH, jj=jj, so=so, sn=sn, zc=zc, zb=zb: e.tensor_tensor(
                        out=zb[:, zc:zc + sn], in0=tmpC[jj][:, so:so + sn], in1=PS[bH][:, so:so + sn], op=ALU.mult),
                         reads=["tmpC%d" % jj, "ps%d" % bH], writes=[zk])
                for (so, sn, zc) in segs:
                    S.op("dve", lambda e, jj=jj, so=so, sn=sn, zc=zc, zb=zb, c=c: e.tensor_scalar(
                        out=ytmp[jj][:, so:so + sn], in0=zb[:, zc:zc + sn], scalar1=pk[:, 24 + 3 * c + 2:24 + 3 * c + 3],
                        scalar2=pk[:, 36 + c:37 + c], op0=ALU.mult, op1=ALU.add),
                         reads=[zk, "pk"], writes=["ytmp%d" % jj])
                    for tap, sh in ((1, 1), (0, 2)):
                        S.op("dve", lambda e, jj=jj, so=so, sn=sn, zc=zc, zb=zb, c=c, tap=tap, sh=sh: e.scalar_tensor_tensor(
                            out=ytmp[jj][:, so:so + sn], in0=zb[:, zc - sh:zc - sh + sn], scalar=pk[:, 24 + 3 * c + tap:24 + 3 * c + tap + 1],
                            in1=ytmp[jj][:, so:so + sn], op0=ALU.mult, op1=ALU.add),
                             reads=[zk, "pk", "ytmp%d" % jj], writes=["ytmp%d" % jj])
                S.op("dve", lambda e, jj=jj, n=n, c=c, c0=c0: e.tensor_tensor(
                    out=convT[:, c, c0:c0 + n], in0=tmpB[jj][:, 0:n], in1=ytmp[jj][:, 0:n], op=ALU.mult),
                     reads=["tmpB%d" % jj, "ytmp%d" % jj], writes=["convT"])
            S.op("dve", lambda e, zb=zb, c=c: e.tensor_copy(out=zlast[:, c, 0:2], in_=zb[:, 2064:2066]), reads=[zk], writes=["zlast"])
            S.op("dve", lambda e, zb=zb, c=c: e.tensor_copy(out=zlast[:, c, 2:4], in_=zb[:, 2082:2084]), reads=[zk], writes=["zlast"])
            w_done(G_CV[c])
        with nc.allow_non_contiguous_dma(reason="tiny transposed conv-state rows"):
            for c in range(4):
                S.dma("sp", "d_ncp", lambda e, c=c: e.dma_start(
                    out=nc_p[:, 128 * c:128 * (c + 1)].rearrange("r p -> p r"), in_=zlast[:, c, 0:2], allow_slow_non_contiguous=True),
                      reads=["zlast"], writes=["nc_p%d" % c])
                S.dma("sp", "d_ncs", lambda e, c=c: e.dma_start(
                    out=nc_s[:, 128 * c:128 * (c + 1)].rearrange("r p -> p r"), in_=zlast[:, c, 2:4], allow_slow_non_contiguous=True),
                      reads=["zlast"], writes=["nc_s%d" % c])

        chk(9)
        mergedT = M.at(R2, (P, KD, NT), BF16, "mergedT")
        gt = [[M.at(R45 + 24832 + (i * 4 + q_) * 2048, (P, 512), F32, "gt") for q_ in range(4)] for i in range(2)]
        g4 = {"n": 0}
        wbrc, kbrc = w_get(G_BRC)
        wbra, kbra = w_get(G_BRA)
        for pr in range(4):
            wgp, kgp = w_get(G_GP[pr])
            for jj2 in range(2):
                j = 2 * pr + jj2
                for (c0, n) in BLKS_EQ:
                    si = g4["n"] % 2
                    g4["n"] += 1
                    sA, sB, t1, t2 = gt[si]
                    ba = next_ps(); bb = next_ps(); bc = next_ps(); bd = next_ps()
                    for (bank, wv, wk, src, nk, col0) in ((ba, wgp, kgp, xnT, KD, jj2 * 128), (bb, wgp, kgp, xnT, KD, 256 + jj2 * 128),
                                                        (bc, wbrc, kbrc, convT, 4, j * 128), (bd, wbra, kbra, attnT, 4, j * 128)):
                        for k in range(nk):
                            S.op("pe", lambda e, bank=bank, wv=wv, src=src, k=k, nk=nk, col0=col0, c0=c0, n=n: e.matmul(
                                PS[bank][:, 0:n], lhsT=wv[:, k, col0:col0 + 128], rhs=src[:, k, c0:c0 + n],
                                start=(k == 0), stop=(k == nk - 1)),
                                 reads=[wk, "convT"] + ["attnT%d" % i for i in range(18)], writes=["ps%d" % bank], sig=(k == nk - 1))
                    S.op("act", lambda e, ba=ba, sA=sA, n=n: e.activation(out=sA[:, 0:n], in_=PS[ba][:, 0:n], func=AF.Sigmoid),
                         reads=["ps%d" % ba], writes=["gtA%d" % si])
                    S.op("act", lambda e, bb=bb, sB=sB, n=n: e.activation(out=sB[:, 0:n], in_=PS[bb][:, 0:n], func=AF.Sigmoid),
                         reads=["ps%d" % bb], writes=["gtB%d" % si])
                    S.op("dve", lambda e, bc=bc, sA=sA, t1=t1, n=n: e.tensor_tensor(out=t1[:, 0:n], in0=sA[:, 0:n], in1=PS[bc][:, 0:n], op=ALU.mult),
                         reads=["ps%d" % bc, "gtA%d" % si], writes=["gt1%d" % si])
                    S.op("dve", lambda e, bd=bd, sB=sB, t2=t2, n=n: e.tensor_tensor(out=t2[:, 0:n], in0=sB[:, 0:n], in1=PS[bd][:, 0:n], op=ALU.mult),
                         reads=["ps%d" % bd, "gtB%d" % si], writes=["gt2%d" % si])
                    S.op("dve", lambda e, t1=t1, t2=t2, j=j, c0=c0, n=n: e.tensor_tensor(out=mergedT[:, j, c0:c0 + n], in0=t1[:, 0:n], in1=t2[:, 0:n], op=ALU.add),
                         reads=["gt1%d" % si, "gt2%d" % si], writes=["mergedT"])
            w_done(G_GP[pr])
        w_done(G_BRC); w_done(G_BRA)

        chk(10)
        S.barrier()
        yacc = M.at(R1, (P, 16, D), F32, "yacc")
        yacc16 = M.at(R45 + 33280, (P, D), F32, "yacc16")
        hnT = M.at(R45, (P, KD, NT), BF16, "hnT")
        hsb = [M.at(AUG + i * 2048, (P, D), BF16, "hs") for i in range(3)]
        junk2 = M.at(R45 + 41472, (P, D), BF16, "junk2")
        wo0, ko0 = w_get(G_OUT0)
        wo1, ko1 = w_get(G_OUT0 + 1)

        def ytile(i):
            return yacc[:, i, :] if i < 16 else yacc16[:, :]

        def s5A(i):
            r = tile_rows(i)
            c0 = 128 * i
            b = i % 3
            yt = ytile(i)
            for half, (wo, ko) in enumerate(((wo0, ko0), (wo1, ko1))):
                A = next_ps(0, 6, "w5")
                for k in range(KD):
                    S.op("pe", lambda e, A=A, k=k, wo=wo: e.matmul(
                        PS[A][0:r, :], lhsT=mergedT[:, k, c0:c0 + r], rhs=wo[:, k, :], start=(k == 0), stop=(k == KD - 1)),
                         reads=[ko, "mergedT"], writes=["ps%d" % A], sig=(k == KD - 1))
                S.op("dve", lambda e, A=A, half=half: e.tensor_tensor(
                    out=yt[0:r, half * 512:(half + 1) * 512], in0=yt[0:r, half * 512:(half + 1) * 512], in1=PS[A][0:r, :], op=ALU.add),
                     reads=["ps%d" % A, "yacc%d" % i], writes=["yacc%d" % i])
            rms_rstd(yt[0:r, :], r, i, 1.0 / D, "yacc%d" % i, "b", junk=junk2, jkey="junk2")

        def s5A2(i):
            r = tile_rows(i)
            b = i % 3
            yt = ytile(i)
            S.op("dve", lambda e: e.tensor_scalar(out=hsb[b][0:r, :], in0=yt[0:r, :], scalar1=rstd[0:r, i:i + 1],
                                                  scalar2=None, op0=ALU.mult),
                 reads=["yacc%d" % i, "rstdb%d" % i], writes=["hs%d" % b])

        def s5B(i):
            r = tile_rows(i)
            c0 = 128 * i
            b = i % 3
            Tb = next_ps(6, 8, "t5")
            pst = PS[Tb][:, :].bitcast(BF16)
            for k in range(KD):
                S.op("pe", lambda e, k=k: e.transpose(out=pst[:, k * 128:k * 128 + r], in_=hsb[b][0:r, k * 128:(k + 1) * 128],
                                                     identity=identb[0:r, 0:r]),
                     reads=["hs%d" % b, "identb"], writes=["ps%d" % Tb], sig=(k == KD - 1))
            S.op("dve", lambda e: e.tensor_tensor(
                out=hnT[:, :, c0:c0 + r], in0=pst.rearrange("p (k t) -> p k t", k=KD)[:, :, 0:r],
                in1=pk[:, 8:16].unsqueeze(2).to_broadcast([P, KD, r]), op=ALU.mult),
                 reads=["ps%d" % Tb, "pk"], writes=["hnT"])
        for i in range(NTILE):
            load_xtile(i, ytile(i), "yacc%d" % i, "d_xr%d" % i)
        s5A(0)
        s5A(1)
        s5A2(0)
        for i in range(NTILE):
            if i + 2 < NTILE:
                s5A(i + 2)
            if i + 1 < NTILE:
                s5A2(i + 1)
            s5B(i)
        w_done(G_OUT0); w_done(G_OUT0 + 1)

        chk(11)
        S.barrier()
        aTb = [M.at(R2 + i * 16640, (P, 4, NT), BF16, "aT") for i in range(2)]
        rtmp = [M.at(R45 + 37376 + i * 2048, (P, 512), F32, "rtmp") for i in range(2)]
        r6 = {"n": 0}
        for g in range(8):
            wu, ku = w_get(G_MLP + 2 * g)
            wd, kd = w_get(G_MLP + 2 * g + 1)
            aT = aTb[g % 2]
            ak = "aT%d" % (g % 2)
            for fc in range(4):
                for (c0, n) in BLKS_EQ:
                    A = next_ps()
                    for k in range(KD):
                        S.op("pe", lambda e, A=A, k=k, fc=fc, c0=c0, n=n, wu=wu: e.matmul(
                            PS[A][:, 0:n], lhsT=wu[:, k, fc * 128:(fc + 1) * 128], rhs=hnT[:, k, c0:c0 + n],
                            start=(k == 0), stop=(k == KD - 1)), reads=[ku, "hnT"], writes=["ps%d" % A], sig=(k == KD - 1))
                    rj = r6["n"] % 2
                    r6["n"] += 1
                    S.op("act", lambda e, A=A, n=n, rj=rj: e.activation(out=rtmp[rj][:, 0:n], in_=PS[A][:, 0:n], func=AF.Relu),
                         reads=["ps%d" % A], writes=["rtmp%d" % rj])
                    S.op("dve", lambda e, fc=fc, c0=c0, n=n, aT=aT, rj=rj: e.tensor_tensor(
                        out=aT[:, fc, c0:c0 + n], in0=rtmp[rj][:, 0:n], in1=rtmp[rj][:, 0:n], op=ALU.mult),
                         reads=["rtmp%d" % rj], writes=[ak])
            for i in range(NTILE):
                r = tile_rows(i)
                c0 = 128 * i
                yt = ytile(i)
                for half in range(2):
                    A = next_ps()
                    for fc in range(4):
                        S.op("pe", lambda e, A=A, fc=fc, r=r, c0=c0, half=half, aT=aT, wd=wd: e.matmul(
                            PS[A][0:r, :], lhsT=aT[:, fc, c0:c0 + r], rhs=wd[:, fc, half * 512:(half + 1) * 512],
                            start=(fc == 0), stop=(fc == 3)), reads=[kd, ak], writes=["ps%d" % A], sig=(fc == 3))
                    S.op("dve", lambda e, A=A, r=r, yt=yt, half=half: e.tensor_tensor(
                        out=yt[0:r, half * 512:(half + 1) * 512], in0=yt[0:r, half * 512:(half + 1) * 512], in1=PS[A][0:r, :], op=ALU.add),
                         reads=["ps%d" % A, "yacc%d" % i], writes=["yacc%d" % i])
                if g == 7:
                    if i == 0:
                        S.dma("sp", "d_y", lambda e: e.dma_start(out=y_prompt[0:112, :], in_=yacc[16:128, 0, :]), reads=["yacc0"], writes=["y_prompt0"])
                    elif i < 16:
                        S.dma("sp", "d_y", lambda e, i=i: e.dma_start(out=y_prompt[128 * i - 16:128 * i + 112, :], in_=yacc[:, i, :]),
                              reads=["yacc%d" % i], writes=["y_prompt%d" % i])
                    else:
                        S.dma("sp", "d_y", lambda e: e.dma_start(out=y_prompt[2032:2048, :], in_=yacc16[0:16, :]), reads=["yacc16"], writes=["y_prompt16"])
                        S.dma("sp", "d_y", lambda e: e.dma_start(out=y_sample[:, :], in_=yacc16[16:32, :]), reads=["yacc16"], writes=["y_sample"])
            w_done(G_MLP + 2 * g); w_done(G_MLP + 2 * g + 1)

        if "attn_tok" in dbg_out:
            S.dma("sp", "d_dbg", lambda e: e.dma_start(out=dbg_out["attn_tok"][:, :, :], in_=attn_tok[:, :, :]),
                  reads=["attn_tok%d" % i for i in range(NTILE)])
        if "cT24" in dbg_out:
            S.dma("sp", "d_dbg", lambda e: e.dma_start(out=dbg_out["cT24"][:, :], in_=cT24[:, :]), reads=["c_all_T"])
        if "negc" in dbg_out:
            S.dma("sp", "d_dbg", lambda e: e.dma_start(out=dbg_out["negc"][:, :, :], in_=negc[:, :, :]), reads=["c_all_neg"])
        if "qT" in dbg_out:
            S.dma("sp", "d_dbg", lambda e: e.dma_start(out=dbg_out["qT"][:, :, :], in_=qT[:, :, :]), reads=["qT_all"])
        if "c_all" in dbg_out:
            S.dma("sp", "d_dbg", lambda e: e.dma_start(out=dbg_out["c_all"][:, :, :], in_=c_all[:, :, :]), reads=["c_all"])


    except _Stop:
        pass

    S.finish("sp")
    S.emit()
    return nc


def make_in_maps(inputs):
    f = lambda a: np.ascontiguousarray(np.asarray(a, dtype=np.float32))
    x_prompt = f(inputs["x_prompt"]); x_sample = f(inputs["x_sample"])
    ck = f(inputs["cache_k"])[0]; cv = f(inputs["cache_v"])[0]; cl = f(inputs["cache_logf"])[0]
    sc = f(inputs["state_conv"])[0]
    pk = np.zeros((P, 64), np.float32)
    pk[:, 0:8] = f(inputs["norm1_g"])[0].reshape(8, P).T
    pk[:, 8:16] = f(inputs["norm2_g"])[0].reshape(8, P).T
    pk[:, 16:24] = np.broadcast_to(f(inputs["b_f"])[0][None, :], (P, 8))
    cw = f(inputs["conv_w"])[0]
    pk[:, 24:36] = cw.reshape(3, 4, P).transpose(2, 1, 0).reshape(P, 12)
    pk[:, 36:40] = f(inputs["conv_b"])[0].reshape(4, P).T
    pk[:, 40] = np.tile(f(inputs["q_norm_g"])[0], 2)
    pk[:, 41] = np.tile(f(inputs["k_norm_g"])[0], 2)
    maps = []
    for b in range(8):
        stt = np.ascontiguousarray(sc[b].reshape(2, 4, P).transpose(2, 1, 0).reshape(P, 8))
        maps.append({
            "x_prompt": x_prompt[b], "x_sample": x_sample[b],
            "cache_k": ck[b].reshape(PAST, 512), "cache_v": cv[b].reshape(PAST, 512),
            "cache_logf": cl[b], "meta": f(inputs["meta"]),
            "w_in": f(inputs["w_in"])[0], "w_br_conv": f(inputs["w_br_conv"])[0],
            "w_br_attn": f(inputs["w_br_attn"])[0], "w_out": f(inputs["w_out"])[0],
            "w_up": f(inputs["w_up"])[0], "w_down": f(inputs["w_down"])[0],
            "ppk": pk, "state_conv_t": stt,
        })
    return maps


_NC_CACHE = {}


def kernel(**inputs):
    maps = make_in_maps(inputs)
    if "nc" not in _NC_CACHE:
        _NC_CACHE["nc"] = build()
    nc = _NC_CACHE["nc"]
    res = run_bass_kernel_spmd(nc, maps, core_ids=list(range(8)))
    R = res.results
    st = lambda name: np.stack([np.asarray(R[b][name], dtype=np.float32) for b in range(8)])
    y_prompt = st("y_prompt")
    y_sample = st("y_sample")
    nk_p = st("nk_p").reshape(1, 8, L, H, HD)
    nv_p = st("nv_p").reshape(1, 8, L, H, HD)
    nf_p = st("nf_p").reshape(1, 8, L, H)
    nc_p = st("nc_p").reshape(1, 8, 2, DC)
    nk_s = st("nk_s").reshape(1, 8, NS, H, HD)
    nv_s = st("nv_s").reshape(1, 8, NS, H, HD)
    nf_s = st("nf_s").reshape(1, 8, NS, H)
    nc_s = st("nc_s").reshape(1, 8, 2, DC)
    return (y_prompt, y_sample, nk_p, nv_p, nf_p, nc_p, nk_s, nv_s, nf_s, nc_s)
```

```python
import os
import numpy as np
import concourse.bass as bass
import concourse.mybir as mybir
from concourse.bass_utils import run_bass_kernel_spmd

F32 = mybir.dt.float32
BF16 = mybir.dt.bfloat16
AF = mybir.ActivationFunctionType
ALU = mybir.AluOpType
AX = mybir.AxisListType

P = 128
D = 1024
KD = 8
SEQ = 2048
NMETA = 16
L = SEQ + NMETA
NS = 16
NT = L + NS
NTILE = 17
PAST = 2048
H = 8
HD = 64
DC = 512
DA = 512
DFF = 4096
INC = 5128
EPS = 1e-6
C_B, C_C, C_H, C_Q, C_K, C_V, C_F, C_G = 0, 512, 1024, 1536, 2048, 2560, 3072, 3080
BLKS = [(0, 512), (512, 512), (1024, 512), (1536, 512), (2048, 32)]
BLKS_EQ = [(416 * i, 416) for i in range(5)]


def tile_rows(i):
    return 128 if i < 16 else 32


class Sched:
    ENG = ("pe", "act", "dve", "pool", "sp")

    def __init__(self, nc):
        self.nc = nc
        self.q = {e: [] for e in self.ENG}
        self.cnt = {}
        self.sems = {}
        self.lastw = {}
        self.lastr = {}
        self.seen = {e: {} for e in self.ENG}
        self.pending = {e: {} for e in self.ENG}
        for e in ("pe", "act", "dve", "pool"):
            self._sem(e)

    def _sem(self, name):
        if name not in self.sems:
            self.sems[name] = self.nc.alloc_semaphore("s_" + name)
            self.cnt[name] = 0
        return self.sems[name]

    def _deps(self, eng, reads, writes):
        deps = dict(self.pending[eng])
        self.pending[eng] = {}

        def merge(src, raw):
            for s, v in src.items():
                if s == eng and eng == "pe":
                    continue
                if deps.get(s, 0) < v:
                    deps[s] = v
        for k in reads:
            merge(self.lastw.get(k, {}), True)
        for k in writes:
            merge(self.lastw.get(k, {}), False)
            merge(self.lastr.get(k, {}), False)
        out = []
        seen = self.seen[eng]
        for s, v in deps.items():
            if seen.get(s, 0) < v:
                seen[s] = v
                out.append((s, v))
        return out

    def _record(self, s, v, reads, writes):
        for k in reads:
            d = self.lastr.setdefault(k, {})
            if d.get(s, 0) < v:
                d[s] = v
        for k in writes:
            d = self.lastw.setdefault(k, {})
            if d.get(s, 0) < v:
                d[s] = v

    def op(self, eng, fn, reads=(), writes=(), sig=True):
        waits = self._deps(eng, reads, writes)
        if sig:
            self.cnt[eng] += 1
            v = self.cnt[eng]
            inc = (eng, 1)
        else:
            v = self.cnt[eng] + 1
            inc = None
        self._record(eng, v, reads, writes)
        self.q[eng].append((waits, fn, inc))

    def dma(self, queue, sem, fn, reads=(), writes=()):
        self._sem(sem)
        waits = self._deps(queue, reads, writes)
        self.cnt[sem] += 16
        self._record(sem, self.cnt[sem], reads, writes)
        self.q[queue].append((waits, fn, (sem, 16)))

    def barrier(self, engines=("pe", "act", "dve", "sp"), exclude=()):
        snap = {s: v for s, v in self.cnt.items() if v > 0 and s not in exclude
                and not s.startswith(("d_ring", "d_kc", "d_vc", "d_wfl"))}
        for e in engines:
            for s, v in snap.items():
                if s == e and e == "pe":
                    continue
                if self.pending[e].get(s, 0) < v:
                    self.pending[e][s] = v

    def finish(self, eng="sp"):
        waits = []
        for s, v in self.cnt.items():
            if v > 0 and self.seen[eng].get(s, 0) < v:
                waits.append((s, v))
        self.q[eng].append((waits, None, None))

    def replay(self, name, e):
        for waits, fn, inc in self.q[name]:
            for s, v in waits:
                e.wait_ge(self.sems[s], v)
            if fn is None:
                continue
            ins = fn(e)
            if inc is not None:
                ins.then_inc(self.sems[inc[0]], inc[1])

    def emit(self):
        nc = self.nc
        with nc.Block() as block:
            @block.tensor
            def _(e):
                self.replay("pe", e)

            @block.scalar
            def _(e):
                self.replay("act", e)

            @block.vector
            def _(e):
                self.replay("dve", e)

            @block.gpsimd
            def _(e):
                self.replay("pool", e)

            @block.sync
            def _(e):
                self.replay("sp", e)


class Mem:
    def __init__(self, nc):
        self.nc = nc
        self.base = 16512
        self.top = 229344
        self.n = 0

    def at(self, off, shape, dtype, name):
        self.n += 1
        nb = int(np.prod(shape[1:])) * (4 if dtype == F32 else 2)
        assert off % 32 == 0, (name, off)
        assert self.base <= off and off + nb <= self.top, (name, off, nb, self.top)
        return self.nc.alloc_sbuf_tensor_at("%s_%d" % (name, self.n), list(shape), dtype, offset=off)


def build(dbg=None, stop_after=99):
    dbg = dbg or []
    nc = bass.Bass("TRN2", target_bir_lowering=False)
    S = Sched(nc)
    M = Mem(nc)

    def din(name, shape):
        return nc.dram_tensor(name, list(shape), F32, kind="ExternalInput")

    def dout(name, shape):
        return nc.dram_tensor(name, list(shape), F32, kind="ExternalOutput")

    x_prompt = din("x_prompt", (SEQ, D))
    x_sample = din("x_sample", (NS, D))
    cache_k = din("cache_k", (PAST, 512))
    cache_v = din("cache_v", (PAST, 512))
    cache_logf = din("cache_logf", (PAST, H))
    meta = din("meta", (NMETA, D))
    w_in = din("w_in", (D, INC))
    w_br_conv = din("w_br_conv", (DC, D))
    w_br_attn = din("w_br_attn", (DA, D))
    w_out = din("w_out", (D, D))
    w_up = din("w_up", (D, DFF))
    w_down = din("w_down", (DFF, D))
    NPK = 64
    ppk = din("ppk", (P, NPK))
    stc = din("state_conv_t", (P, 8))

    y_prompt = dout("y_prompt", (SEQ, D))
    y_sample = dout("y_sample", (NS, D))
    nk_p = dout("nk_p", (L, 512))
    nv_p = dout("nv_p", (L, 512))
    nf_p = dout("nf_p", (L, H))
    nc_p = dout("nc_p", (2, DC))
    nk_s = dout("nk_s", (NS, 512))
    nv_s = dout("nv_s", (NS, 512))
    nf_s = dout("nf_s", (NS, H))
    nc_s = dout("nc_s", (2, DC))
    dbg_out = {}
    for (name, shape, dt_) in dbg:
        dbg_out[name] = nc.dram_tensor("dbg_" + name, list(shape), dt_, kind="ExternalOutput")

    if os.environ.get("PAIR_EXP", "1") == "1":
        PS2 = [nc.alloc_psum_tensor("ps2_%d" % i, [P, 1024], F32) for i in range(4)]
        PS = [PS2[i // 2][:, (i % 2) * 512:(i % 2 + 1) * 512] for i in range(8)]
    else:
        PS2 = None
        PS = [nc.alloc_psum_tensor("ps%d" % i, [P, 512], F32) for i in range(8)]

    o = M.base
    RING_SLOTS = 5
    ring = [M.at(o + i * 8192, (P, 4096), BF16, "ring") for i in range(RING_SLOTS)]
    o += RING_SLOTS * 8192
    pk = M.at(o, (P, NPK), F32, "pk"); o += NPK * 4
    identb = M.at(o, (P, P), BF16, "identb"); o += 256
    identf = M.at(o, (P, P), F32, "identf"); o += 512
    small = [o]
    o += 13312
    def sm(shape, dtype, name):
        nb = int(np.prod(shape[1:])) * (4 if dtype == F32 else 2)
        nb = (nb + 31) // 32 * 32
        t = M.at(small[0], shape, dtype, name)
        small[0] += nb
        assert small[0] <= o_small_end
        return t
    o_small_end = o
    AUG = o; o += 12544
    R2 = o; o += 33280
    R1 = o; o += 33280
    R3 = o; o += 34304
    R45 = o
    R45_SIZE = M.top - o
    assert R45_SIZE >= 41472, R45_SIZE

    xnT = M.at(R1, (P, KD, NT), BF16, "xnT")
    NXT = 6
    xt = [M.at(R3 + i * 4096, (P, D), F32, "xt") for i in range(2)] + \
         [M.at(R45 + 25600 + i * 4096, (P, D), F32, "xt") for i in range(4)]
    xs = [M.at(R3 + 8192 + i * 2048, (P, D), BF16, "xs") for i in range(2)] + [M.at(R45 + 41984, (P, D), BF16, "xs")]
    junk = M.at(R3 + 12288, (P, D), BF16, "junk")
    ssq = sm((P, 32), F32, "ssq")
    rstd = sm((P, 32), F32, "rstd")

    S.dma("sp", "d_pk", lambda e: e.dma_start(out=pk[:, :], in_=ppk[:, :]), writes=["pk"])
    def mk_ident(t, key):
        S.op("pool", lambda e: e.memset(t[:, :], 1.0), writes=[key])
        S.op("pool", lambda e: e.affine_select(t[:, :], t[:, :], [[-1, P]], ALU.is_equal, 0.0,
                                               base=0, channel_multiplier=1), reads=[key], writes=[key])
    mk_ident(identb, "identb")
    mk_ident(identf, "identf")

    epsc = sm((P, 1), F32, "epsc")
    S.op("pool", lambda e: e.memset(epsc[:, :], EPS), writes=["epsc"])

    def load_xtile(i, buf, key, sem):
        if i == 0:
            S.dma("sp", sem, lambda e: e.dma_start(out=buf[0:16, :], in_=meta[:, :]), writes=[key])
            S.dma("sp", sem, lambda e: e.dma_start(out=buf[16:128, :], in_=x_prompt[0:112, :]), writes=[key])
        elif i < 16:
            S.dma("sp", sem, lambda e: e.dma_start(out=buf[:, :], in_=x_prompt[128 * i - 16:128 * i + 112, :]), writes=[key])
        else:
            S.dma("sp", sem, lambda e: e.dma_start(out=buf[0:16, :], in_=x_prompt[2032:2048, :]), writes=[key])
            S.dma("sp", sem, lambda e: e.dma_start(out=buf[16:32, :], in_=x_sample[:, :]), writes=[key])

    def rms_rstd(src, r, col, inv_n, kin, tagk, junk=junk, jkey="junk"):
        S.op("act", lambda e: e.activation(out=junk[0:r, :], in_=src, func=AF.Square,
                                           accum_out=ssq[0:r, col:col + 1]),
             reads=[kin, "epsc"], writes=[jkey, "ssq%s%d" % (tagk, col)])
        S.op("act", lambda e: e.activation(out=ssq[0:r, col:col + 1], in_=ssq[0:r, col:col + 1], func=AF.Ln,
                                           bias=epsc[0:r, :], scale=inv_n),
             reads=["ssq%s%d" % (tagk, col)], writes=["ssq%s%d" % (tagk, col)])
        S.op("act", lambda e: e.activation(out=rstd[0:r, col:col + 1], in_=ssq[0:r, col:col + 1], func=AF.Exp,
                                           scale=-0.5),
             reads=["ssq%s%d" % (tagk, col)], writes=["rstd%s%d" % (tagk, col)])

    for i in range(NTILE):
        r = tile_rows(i)
        b = i % NXT
        b3 = i % 3
        kx, ks = "xt%d" % b, "xs%d" % b3
        load_xtile(i, xt[b], kx, "d_xt%d" % b)
        rms_rstd(xt[b][0:r, :], r, i, 1.0 / D, kx, "a")
        S.op("dve", lambda e, b=b, b3=b3, r=r, i=i: e.tensor_scalar(out=xs[b3][0:r, :], in0=xt[b][0:r, :],
                                                          scalar1=rstd[0:r, i:i + 1], scalar2=None, op0=ALU.mult),
             reads=[kx, "rstda%d" % i], writes=[ks])
        pb = i % 2
        pst = PS[pb][:, :].bitcast(BF16)
        for k in range(KD):
            S.op("pe", lambda e, k=k, b3=b3, r=r, pst=pst: e.transpose(out=pst[:, k * 128:k * 128 + r],
                                                                    in_=xs[b3][0:r, k * 128:(k + 1) * 128],
                                                                    identity=identb[0:r, 0:r]),
                 reads=[ks, "identb"], writes=["ps%d" % pb], sig=(k == KD - 1))
        c0 = 128 * i
        S.op("dve", lambda e, pst=pst, r=r, c0=c0: e.tensor_tensor(
            out=xnT[:, :, c0:c0 + r],
            in0=pst.rearrange("p (k t) -> p k t", k=KD)[:, :, 0:r],
            in1=pk[:, 0:KD].unsqueeze(2).to_broadcast([P, KD, r]), op=ALU.mult),
             reads=["ps%d" % pb, "pk"], writes=["xnT%d" % i])

    if "xnT" in dbg_out:
        S.dma("sp", "d_dbg", lambda e: e.dma_start(out=dbg_out["xnT"][:, :, :], in_=xnT[:, :, :]),
              reads=["xnT%d" % i for i in range(NTILE)])

    class _Stop(Exception):
        pass

    def chk(st):
        if stop_after < st:
            raise _Stop()

    try:
        XN_ALL = ["xnT%d" % i for i in range(NTILE)]

        def xn_keys(c0, n):
            return ["xnT%d" % i for i in range(c0 // 128, (c0 + n - 1) // 128 + 1)]

        wlist = []

        def wg_cols(w, c0, kch=KD, ncol=512):
            return (lambda t: t[:, 0:kch * ncol].rearrange("p (k c) -> p k c", k=kch),
                    w[0:kch * 128, c0:c0 + ncol].rearrange("(k p) c -> p k c", p=P))

        G_Q, G_K, G_V = 0, 1, 2
        wlist.append(wg_cols(w_in, C_Q))
        wlist.append(wg_cols(w_in, C_K))
        wlist.append(wg_cols(w_in, C_V))
        G_CV = [3, 4, 5, 6]
        for cch in range(4):
            wlist.append((lambda t: t[:, 0:KD * 384].rearrange("p (k c) -> p k c", k=KD),
                          [((128 * j3, 128 * (j3 + 1)),
                            w_in[:, base + 128 * cch:base + 128 * (cch + 1)].rearrange("(k p) c -> p k c", p=P))
                           for j3, base in enumerate((C_B, C_C, C_H))]))
        G_BRC, G_BRA = 7, 8
        G_GP = [9, 10, 11, 12]
        wlist.append(wg_cols(w_br_conv, 0, kch=4, ncol=1024))
        wlist.append(wg_cols(w_br_attn, 0, kch=4, ncol=1024))
        for pr in range(4):
            wlist.append((lambda t: t[:, 0:KD * 512].rearrange("p (k c) -> p k c", k=KD),
                          [((0, 256), w_in[:, C_G + 256 * pr:C_G + 256 * (pr + 1)].rearrange("(k p) c -> p k c", p=P)),
                           ((256, 512), w_in[:, C_G + 1024 + 256 * pr:C_G + 1024 + 256 * (pr + 1)].rearrange("(k p) c -> p k c", p=P))]))
        G_OUT0 = 13
        wlist.append(wg_cols(w_out, 0))
        wlist.append(wg_cols(w_out, 512))
        G_MLP = 15
        for g in range(8):
            wlist.append(wg_cols(w_up, 512 * g))
            wlist.append((lambda t: t[:, :].rearrange("p (k c) -> p k c", k=4),
                          w_down[512 * g:512 * (g + 1), :].rearrange("(k p) c -> p k c", p=P)))
        wstate = {"issued": 0, "free": list(range(RING_SLOTS)), "slot": {}}

        def w_try_issue(limit=None, after=()):
            while wstate["issued"] < len(wlist) and wstate["free"] and (limit is None or wstate["issued"] < limit):
                g = wstate["issued"]
                slot = wstate["free"].pop(0)
                wstate["slot"][g] = slot
                vf, srcs = wlist[g]
                dst = vf(ring[slot])
                if not isinstance(srcs, list):
                    srcs = [(None, srcs)]
                for (sub, src) in srcs:
                    d = dst if sub is None else dst[:, :, sub[0]:sub[1]]
                    S.dma("pool", "d_ring%d" % slot, lambda e, d=d, src=src: e.dma_start(out=d, in_=src),
                          reads=list(after), writes=["ring%d" % slot])
                wstate["issued"] += 1

        def w_get(g):
            if g not in wstate["slot"]:
                w_try_issue(g + 1)
            slot = wstate["slot"][g]
            return wlist[g][0](ring[slot]), "ring%d" % slot

        def w_done(g):
            wstate["free"].append(wstate["slot"][g])
            w_try_issue()

        w_try_issue(1)
        w_try_issue(2, after=["xt%d" % (7 % NXT)])
        w_try_issue(3, after=["xt%d" % (12 % NXT)])

        blockones = sm((P, P), BF16, "blockones")
        S.op("pool", lambda e: e.memset(blockones[:, :], 0.0), writes=["blockones"])
        S.op("pool", lambda e: e.memset(blockones[0:64, 0:64], 1.0), writes=["blockones"])
        S.op("pool", lambda e: e.memset(blockones[64:128, 64:128], 1.0), writes=["blockones"])
        onesf = sm((P, P), F32, "onesf")
        S.op("pool", lambda e: e.memset(onesf[:, :], 1.0), writes=["onesf"])
        trif = sm((P, P), F32, "trif")
        S.op("pool", lambda e: e.memset(trif[:, :], 1.0), writes=["trif"])
        S.op("pool", lambda e: e.affine_select(trif[:, :], trif[:, :], [[1, P]], ALU.is_ge, 0.0,
                                               base=0, channel_multiplier=-1), reads=["trif"], writes=["trif"])
        maskb = sm((P, P), BF16, "maskb")
        S.op("pool", lambda e: e.memset(maskb[:, :], 1.0), writes=["maskb"])
        S.op("pool", lambda e: e.affine_select(maskb[:, :], maskb[:, :], [[1, P]], ALU.is_ge, 0.0,
                                               base=0, channel_multiplier=-1), reads=["maskb"], writes=["maskb"])
        maskneg = sm((P, P), BF16, "maskneg")
        S.op("pool", lambda e: e.memset(maskneg[:, :], 0.0), writes=["maskneg"])
        S.op("pool", lambda e: e.affine_select(maskneg[:, :], maskneg[:, :], [[1, P]], ALU.is_ge, -9984.0,
                                               base=0, channel_multiplier=-1), reads=["maskneg"], writes=["maskneg"])
        ones3 = sm((3, P), BF16, "ones3")
        S.op("pool", lambda e: e.memset(ones3[:, :], 1.0), writes=["ones3"])
        onecol = sm((P, 1), F32, "onecol")
        S.op("pool", lambda e: e.memset(onecol[:, :], 1.0), writes=["onecol"])
        wfl = sm((P, KD, 8), BF16, "wfl")
        if not os.environ.get("SKIP_WFL"):
          S.dma("pool", "d_wfl", lambda e: e.dma_start(out=wfl[:, :, :],
                                                     in_=w_in[:, C_F:C_F + 8].rearrange("(k p) c -> p k c", p=P)),
              writes=["wfl"])
        fl_all = sm((P, NTILE, 8), F32, "fl_all")
        lf_all = sm((P, NTILE, 8), F32, "lf_all")
        S.op("pool", lambda e: e.memset(fl_all[:, :, :], 0.0), writes=["fl_all"])
        S.op("pool", lambda e: e.memset(lf_all[:, :, :], 0.0), writes=["lf_all"])
        c_all = sm((P, NTILE, 8), F32, "c_all")
        S.op("pool", lambda e: e.memset(c_all[:, :, :], 0.0), writes=["c_all"])
        negc = sm((P, NTILE, 8), F32, "negc")
        csplit = sm((P, NTILE, 3, 8), BF16, "csplit")
        cres = sm((P, NTILE, 8), F32, "cres")
        cT24 = M.at(AUG, (24, NT), BF16, "cT24")
        qaug = [M.at(AUG + 4160 + i * 4160, (3, NT), BF16, "qaug") for i in range(2)]

        qT = M.at(R2, (P, 4, NT), BF16, "qT")
        kT = M.at(R2 + 16640, (P, 4, NT), BF16, "kT")
        Vp = M.at(R3 + 14336, (P, NTILE, H, 66), BF16, "Vp")
        sqb = [M.at(R45 + i * 1024, (P, 512), BF16, "sqb") for i in range(2)]
        rsb = [M.at(R45 + 2048 + i * 2048, (P, 512), F32, "rsb") for i in range(2)]
        kf = M.at(R45 + 6144, (P, 4, 512), F32, "kf")
        ktok = [M.at(R45 + 14336 + i * 2048, (P, 512), F32, "ktok") for i in range(2)]
        vtok = [M.at(R45 + 18432 + i * 2048, (P, 512), F32, "vtok") for i in range(2)]

        psn = {"n": 0}

        def next_ps(lo=0, hi=8, key="n"):
            psn[key] = psn.get(key, lo - 1) + 1
            if psn[key] >= hi or psn[key] < lo:
                psn[key] = lo
            return psn[key]

        if not os.environ.get("SKIP_VPMEM"):
            S.op("pool", lambda e: e.memset(Vp[:, :, :, 64:65], 1.0), writes=["Vp_ones"])

        chk(1)
        sqb3 = [M.at(R45 + 22528 + i * 1024, (P, 512), BF16, "sqb3") for i in range(3)]
        cnt1 = {"kt": 0}
        units1 = []
        for which, G in (("q", G_Q), ("k", G_K)):
            for (c0, n) in BLKS:
                for m in range(4):
                    units1.append(dict(which=which, G=G, c0=c0, n=n, m=m, idx=len(units1)))

        def s1A(u):
            which, G, c0, n, m = u["which"], u["G"], u["c0"], u["n"], u["m"]
            wv, wkey = w_get(G)
            A = next_ps(0, 4, "qa")
            sj = u["idx"] % 3
            u["A"], u["sj"] = A, sj
            for k in range(KD):
                S.op("pe", lambda e, k=k: e.matmul(
                    PS[A][:, 0:n], lhsT=wv[:, k, m * 128:(m + 1) * 128], rhs=xnT[:, k, c0:c0 + n],
                    start=(k == 0), stop=(k == KD - 1)),
                     reads=[wkey] + xn_keys(c0, n), writes=["ps%d" % A], sig=(k == KD - 1))
            S.op("act", lambda e: e.activation(out=sqb3[sj][:, 0:n], in_=PS[A][:, 0:n], func=AF.Square),
                 reads=["ps%d" % A], writes=["sqb%d" % sj])

        def s1B(u):
            which, G, c0, n, m = u["which"], u["G"], u["c0"], u["n"], u["m"]
            A, sj = u["A"], u["sj"]
            j = u["idx"] % 2
            B = next_ps(4, 6, "qb")
            S.op("pe", lambda e: e.matmul(PS[B][:, 0:n], lhsT=blockones[:, :], rhs=sqb3[sj][:, 0:n], start=True, stop=True),
                 reads=["sqb%d" % sj, "blockones"], writes=["ps%d" % B])
            S.op("act", lambda e: e.activation(out=rsb[j][:, 0:n], in_=PS[B][:, 0:n], func=AF.Ln,
                                               bias=epsc[:, :], scale=1.0 / HD),
                 reads=["ps%d" % B, "epsc"], writes=["rsb%d" % j])
            S.op("act", lambda e: e.activation(out=rsb[j][:, 0:n], in_=rsb[j][:, 0:n], func=AF.Exp, scale=-0.5),
                 reads=["rsb%d" % j], writes=["rsb%d" % j])
            if which == "q":
                S.op("dve", lambda e: e.scalar_tensor_tensor(
                    out=qT[:, m, c0:c0 + n], in0=PS[A][:, 0:n], scalar=pk[:, 40:41], in1=rsb[j][:, 0:n],
                    op0=ALU.mult, op1=ALU.mult),
                     reads=["ps%d" % A, "rsb%d" % j, "pk"], writes=["qT%d_%d" % (m, c0)])
            else:
                S.op("dve", lambda e: e.scalar_tensor_tensor(
                    out=kf[:, m, 0:n], in0=PS[A][:, 0:n], scalar=pk[:, 41:42], in1=rsb[j][:, 0:n],
                    op0=ALU.mult, op1=ALU.mult),
                     reads=["ps%d" % A, "rsb%d" % j, "pk"], writes=["kf%d" % m])
                S.op("dve", lambda e: e.tensor_copy(out=kT[:, m, c0:c0 + n], in_=kf[:, m, 0:n]),
                     reads=["kf%d" % m], writes=["kT%d_%d" % (m, c0)])
                if m == 3:
                    for tt in range((n + 127) // 128):
                        r = min(128, n - tt * 128)
                        jj = cnt1["kt"] % 2
                        cnt1["kt"] += 1
                        Cb = next_ps(6, 8, "kt")
                        for mm in range(4):
                            S.op("pe", lambda e, Cb=Cb, mm=mm, tt=tt, r=r: e.transpose(
                                out=PS[Cb][0:r, mm * 128:(mm + 1) * 128], in_=kf[:, mm, tt * 128:tt * 128 + r], identity=identf[:, :]),
                                 reads=["kf%d" % mm, "identf"], writes=["ps%d" % Cb], sig=(mm == 3))
                        S.op("dve", lambda e, Cb=Cb, jj=jj, r=r: e.tensor_copy(out=ktok[jj][0:r, :], in_=PS[Cb][0:r, :]),
                             reads=["ps%d" % Cb], writes=["ktok%d" % jj])
                        p0 = c0 + tt * 128
                        if p0 < 2048:
                            S.dma("sp", "d_ktok%d" % jj, lambda e, jj=jj, p0=p0: e.dma_start(out=nk_p[p0:p0 + 128, :], in_=ktok[jj][:, :]),
                                  reads=["ktok%d" % jj], writes=["nk_p_%d" % p0])
                        else:
                            S.dma("sp", "d_ktok%d" % jj, lambda e, jj=jj: e.dma_start(out=nk_p[2048:2064, :], in_=ktok[jj][0:16, :]),
                                  reads=["ktok%d" % jj], writes=["nk_p_%d" % p0])
                            S.dma("sp", "d_ktok%d" % jj, lambda e, jj=jj: e.dma_start(out=nk_s[:, :], in_=ktok[jj][16:32, :]),
                                  reads=["ktok%d" % jj], writes=["nk_s"])
            if m == 3 and c0 == 2048:
                w_done(G)

        LA1 = 2
        for i in range(LA1):
            s1A(units1[i])
        for i in range(len(units1)):
            s1B(units1[i])
            if i + LA1 < len(units1):
                s1A(units1[i + LA1])

        chk(2)
        wv, wkey = w_get(G_V)
        FB = 4
        for i in range(NTILE):
            r = tile_rows(i)
            c0 = 128 * i
            for k in range(KD):
                S.op("pe", lambda e, k=k, r=r, c0=c0, i=i: e.matmul(
                    PS[FB][0:r, 8 * i:8 * i + 8], lhsT=xnT[:, k, c0:c0 + r], rhs=wfl[:, k, :], start=(k == 0), stop=(k == KD - 1)),
                     reads=["wfl", "xnT%d" % i], writes=["ps%d" % FB], sig=(k == KD - 1))
        S.op("dve", lambda e: e.tensor_tensor(out=fl_all[:, 0:16, :], in0=PS[FB][:, 0:128].rearrange("p (i h) -> p i h", h=8),
                                              in1=pk[:, 16:24].unsqueeze(1).to_broadcast([P, 16, 8]), op=ALU.add),
             reads=["ps%d" % FB, "pk"], writes=["fl_all"])
        S.op("dve", lambda e: e.tensor_tensor(out=fl_all[0:32, 16, :], in0=PS[FB][0:32, 128:136], in1=pk[0:32, 16:24], op=ALU.add),
             reads=["ps%d" % FB, "pk"], writes=["fl_all"])

        def logsig(dst, src, r, kin, kout):
            S.op("act", lambda e: e.activation(out=dst, in_=src, func=AF.Exp, scale=-1.0), reads=[kin], writes=[kout])
            S.op("act", lambda e: e.activation(out=dst, in_=dst, func=AF.Ln, bias=onecol[0:r, :], scale=1.0),
                 reads=[kout, "onecol"], writes=[kout])
            S.op("dve", lambda e: e.tensor_scalar(out=dst, in0=dst, scalar1=-1.0, scalar2=None, op0=ALU.mult),
                 reads=[kout], writes=[kout])
        logsig(lf_all[:, 0:16, :], fl_all[:, 0:16, :], P, "fl_all", "lf_all")
        logsig(lf_all[0:32, 16, :], fl_all[0:32, 16, :], 32, "fl_all", "lf_all")
        S.dma("sp", "d_lf", lambda e: e.dma_start(out=nf_p[0:2048, :].rearrange("(i p) h -> p i h", p=P), in_=lf_all[:, 0:16, :]),
              reads=["lf_all"], writes=["nf_p_a"])
        S.dma("sp", "d_lf", lambda e: e.dma_start(out=nf_p[2048:2064, :], in_=lf_all[0:16, 16, :]), reads=["lf_all"], writes=["nf_p_b"])
        S.dma("sp", "d_lf", lambda e: e.dma_start(out=nf_s[:, :], in_=lf_all[16:32, 16, :]), reads=["lf_all"], writes=["nf_s"])

        carr = sm((P, NTILE, 8), F32, "carr")
        lfc = sm((P, 16, 8), F32, "lfc")
        S.dma("sp", "d_lfc", lambda e: e.dma_start(out=lfc[:, :, :], in_=cache_logf[:, :].rearrange("(i p) h -> p i h", p=P)),
              writes=["lfc"])
        c_s = sm((P, NTILE, 8), F32, "c_s")
        S.op("pool", lambda e: e.memset(c_s[:, :, :], 0.0), writes=["c_s"])
        negc_s = sm((P, NTILE, 8), F32, "negc_s")
        msel = sm((32, 16), F32, "msel")
        S.op("pool", lambda e: e.memset(msel[:, :], 1.0), writes=["msel"])
        S.op("pool", lambda e: e.affine_select(msel[:, :], msel[:, :], [[1, 16]], ALU.is_ge, 0.0,
                                               base=16, channel_multiplier=-1), reads=["msel"], writes=["msel"])
        S.op("pool", lambda e: e.memset(msel[0:16, :], 0.0), reads=["msel"], writes=["msel"])
        carr_s = sm((P, 17, 8), F32, "carr_s")
        Cb = 5
        S.op("pe", lambda e: e.matmul(PS[Cb][:, 0:136], lhsT=trif[:, :], rhs=lf_all[:, :, :].rearrange("p i h -> p (i h)"), start=True, stop=True),
             reads=["lf_all", "trif"], writes=["ps%d" % Cb])
        S.op("pe", lambda e: e.matmul(PS[Cb][:, 136:272], lhsT=onesf[:, :], rhs=lf_all[:, :, :].rearrange("p i h -> p (i h)"), start=True, stop=True),
             reads=["lf_all", "onesf"], writes=["ps%d" % Cb])
        Cs = 6
        S.op("pe", lambda e: e.matmul(PS[Cs][:, 0:128], lhsT=trif[:, :], rhs=lfc[:, :, :].rearrange("p i h -> p (i h)"), start=True, stop=True),
             reads=["lfc", "trif"], writes=["ps%d" % Cs])
        S.op("pe", lambda e: e.matmul(PS[Cs][:, 128:256], lhsT=onesf[:, :], rhs=lfc[:, :, :].rearrange("p i h -> p (i h)"), start=True, stop=True),
             reads=["lfc", "onesf"], writes=["ps%d" % Cs])
        S.op("pe", lambda e: e.matmul(PS[Cs][0:16, 256:264], lhsT=msel[:, :], rhs=lf_all[0:32, 16, :], start=True, stop=True),
             reads=["lf_all", "msel"], writes=["ps%d" % Cs])

        chain = []

        def DF(fn, **kw):
            chain.append(lambda: S.op("dve", fn, **kw))
        DF(lambda e: e.memset(carr[:, 0, :], 0.0), writes=["carr"])
        for i in range(1, NTILE):
            DF(lambda e, i=i: e.tensor_tensor(out=carr[:, i, :], in0=carr[:, i - 1, :], in1=PS[Cb][:, 136 + 8 * (i - 1):136 + 8 * i], op=ALU.add),
              reads=["carr", "ps%d" % Cb], writes=["carr"])
        DF(lambda e: e.tensor_tensor(out=c_all[:, :, :], in0=carr[:, :, :], in1=PS[Cb][:, 0:136].rearrange("p (i h) -> p i h", h=8), op=ALU.add),
          reads=["carr", "ps%d" % Cb], writes=["c_all"])
        key = "c_all"
        DF(lambda e: e.tensor_scalar(out=negc[:, :, :], in0=c_all[:, :, :], scalar1=-1.0, scalar2=None, op0=ALU.mult),
          reads=[key], writes=[key + "_neg"])
        DF(lambda e: e.tensor_copy(out=csplit[:, :, 0, :], in_=c_all[:, :, :]), reads=[key], writes=[key + "_s"])
        DF(lambda e: e.tensor_tensor(out=cres[:, :, :], in0=c_all[:, :, :], in1=csplit[:, :, 0, :], op=ALU.subtract),
          reads=[key, key + "_s"], writes=[key + "_r"])
        DF(lambda e: e.tensor_copy(out=csplit[:, :, 1, :], in_=cres[:, :, :]), reads=[key + "_r"], writes=[key + "_s"])
        DF(lambda e: e.tensor_tensor(out=cres[:, :, :], in0=cres[:, :, :], in1=csplit[:, :, 1, :], op=ALU.subtract),
          reads=[key + "_r", key + "_s"], writes=[key + "_r"])
        DF(lambda e: e.tensor_copy(out=csplit[:, :, 2, :], in_=cres[:, :, :]), reads=[key + "_r"], writes=[key + "_s"])
        DF(lambda e: e.memset(carr_s[:, 0, :], 0.0), writes=["carr_s"])
        for i in range(1, 17):
            DF(lambda e, i=i: e.tensor_tensor(out=carr_s[:, i, :], in0=carr_s[:, i - 1, :], in1=PS[Cs][:, 128 + 8 * (i - 1):128 + 8 * i], op=ALU.add),
              reads=["carr_s", "ps%d" % Cs], writes=["carr_s"])
        DF(lambda e: e.tensor_tensor(out=c_s[:, 0:16, :], in0=carr_s[:, 0:16, :], in1=PS[Cs][:, 0:128].rearrange("p (i h) -> p i h", h=8), op=ALU.add),
          reads=["carr_s", "ps%d" % Cs], writes=["c_s"])
        DF(lambda e: e.tensor_tensor(out=c_s[:, 0:16, :], in0=c_s[:, 0:16, :],
                                    in1=carr_s[:, 16, :].unsqueeze(1).to_broadcast([P, 16, 8]), op=ALU.subtract),
          reads=["carr_s", "c_s"], writes=["c_s"])
        DF(lambda e: e.tensor_copy(out=c_s[0:16, 16, :], in_=PS[Cs][0:16, 256:264]), reads=["ps%d" % Cs], writes=["c_s"])
        DF(lambda e: e.tensor_scalar(out=negc_s[:, :, :], in0=c_s[:, :, :], scalar1=-1.0, scalar2=None, op0=ALU.mult),
          reads=["c_s"], writes=["c_s_neg"])

        def run_chain(n):
            for _ in range(n):
                if chain:
                    chain.pop(0)()

        for i in range(NTILE):
            r = tile_rows(i)
            c0 = 128 * i
            jj = i % 2
            A = next_ps(0, 4, "vp")
            for k in range(KD):
                S.op("pe", lambda e, A=A, k=k, r=r, c0=c0, wv=wv: e.matmul(
                    PS[A][0:r, :], lhsT=xnT[:, k, c0:c0 + r], rhs=wv[:, k, :], start=(k == 0), stop=(k == KD - 1)),
                     reads=[wkey, "xnT%d" % i], writes=["ps%d" % A], sig=(k == KD - 1))
            S.op("dve", lambda e, A=A, jj=jj, r=r: e.tensor_copy(out=vtok[jj][0:r, :], in_=PS[A][0:r, :]),
                 reads=["ps%d" % A], writes=["vtok%d" % jj])
            S.op("dve", lambda e, A=A, r=r, i=i: e.tensor_copy(
                out=Vp[0:r, i, :, 0:64], in_=PS[A][0:r, :].rearrange("p (h d) -> p h d", h=H)),
                 reads=["ps%d" % A], writes=["Vp%d" % i])
            if i < 16:
                S.dma("sp", "d_vtok%d" % jj, lambda e, jj=jj, c0=c0: e.dma_start(out=nv_p[c0:c0 + 128, :], in_=vtok[jj][:, :]),
                      reads=["vtok%d" % jj], writes=["nv_p_%d" % i])
            else:
                S.dma("sp", "d_vtok%d" % jj, lambda e, jj=jj: e.dma_start(out=nv_p[2048:2064, :], in_=vtok[jj][0:16, :]),
                      reads=["vtok%d" % jj], writes=["nv_p_%d" % i])
                S.dma("sp", "d_vtok%d" % jj, lambda e, jj=jj: e.dma_start(out=nv_s[:, :], in_=vtok[jj][16:32, :]),
                      reads=["vtok%d" % jj], writes=["nv_s"])
            run_chain(4)
        run_chain(len(chain))
        Vsn = sm((16, H, 66), BF16, "Vsn")
        S.op("pool", lambda e: e.memset(Vsn[:, :, 64:65], 1.0), writes=["Vsn_ones"])
        A = next_ps(0, 4, "vp")
        for k in range(KD):
            S.op("pe", lambda e, A=A, k=k, wv=wv: e.matmul(PS[A][0:16, :], lhsT=xnT[:, k, L:NT], rhs=wv[:, k, :],
                                                          start=(k == 0), stop=(k == KD - 1)),
                 reads=[wkey, "xnT16"], writes=["ps%d" % A], sig=(k == KD - 1))
        S.op("dve", lambda e, A=A: e.tensor_copy(out=Vsn[:, :, 0:64], in_=PS[A][0:16, :].rearrange("p (h d) -> p h d", h=H)),
             reads=["ps%d" % A], writes=["Vsn"])
        w_done(G_V)

        for bnk in range(3):
            Tb = next_ps(4, 8, "ct")
            pst = PS[Tb][:, :].bitcast(BF16)
            tiles = list(range(8 * bnk, min(NTILE, 8 * bnk + 8)))
            for i in tiles:
                r = 128 if i < 16 else 16
                S.op("pe", lambda e, i=i, r=r, pst=pst: e.transpose(
                    out=pst[0:24, 128 * (i % 8):128 * (i % 8) + r], in_=csplit[0:r, i, :, :].rearrange("p j h -> p (j h)"),
                    identity=identb[0:r, 0:r]),
                     reads=["c_all_s", "identb"], writes=["ps%d" % Tb], sig=(i == tiles[-1]))
            w0 = 128 * tiles[0]
            wn = sum(128 if i < 16 else 16 for i in tiles)
            S.op("dve", lambda e, pst=pst, w0=w0, wn=wn: e.tensor_scalar(out=cT24[:, w0:w0 + wn], in0=pst[0:24, 0:wn],
                                                                  scalar1=8.0, scalar2=None, op0=ALU.mult),
                 reads=["ps%d" % Tb], writes=["c_all_T"])

        chk(6)
        S.barrier(engines=("pe", "act", "dve", "sp", "pool"))

        attnT = M.at(R45, (P, 4, NT), BF16, "attnT")
        attn_tok = M.at(R45 + 16640, (P, NTILE, 512), BF16, "attn_tok")
        NPB = 4
        Pb = [M.at(R45 + 34048 + i * 2048, (P, 1024), BF16, "Pb") for i in range(NPB)]
        Qh = [M.at(R45 + i * 4160, (P, NT), BF16, "Qh") for i in range(2)]
        Kh = [M.at(R45 + 8320 + i * 4160, (P, NT), BF16, "Kh") for i in range(2)]
        ncT24 = M.at(AUG + 4160, (24, NT), BF16, "ncT24")
        S.op("dve", lambda e: e.tensor_scalar(out=ncT24[:, 0:L], in0=cT24[:, 0:L], scalar1=-1.0, scalar2=None, op0=ALU.mult),
             reads=["c_all_T"], writes=["ncT24"])
        for i in range(2):
            S.op("pool", lambda e, i=i: e.memset(Qh[i][64:128, :], 0.0), writes=["Qh%d" % i])
            S.op("pool", lambda e, i=i: e.memset(Kh[i][64:128, :], 0.0), writes=["Kh%d" % i])
        S.op("dve", lambda e: e.memset(Qh[0][64:70, :], 1.0), writes=["Qh0"])
        S.dma("sp", "d_qh1", lambda e: e.dma_start(out=Qh[1][67:70, 0:L], in_=Qh[0][67:70, 0:L]), reads=["Qh0"], writes=["Qh1"])
        S.dma("sp", "d_kh0", lambda e: e.dma_start(out=Kh[0][64:67, 0:L], in_=Qh[0][67:70, 0:L]), reads=["Qh0"], writes=["Kh0"])
        S.dma("sp", "d_kh1", lambda e: e.dma_start(out=Kh[1][64:67, 0:L], in_=Qh[0][67:70, 0:L]), reads=["Qh0"], writes=["Kh1"])
        rsum = sm((P, 4), F32, "rsum")
        QG = [(0, 512), (512, 512), (1024, 512), (1536, 512), (2048, 16)]
        units = []
        for h in range(H):
            for gi, (q0, qn) in enumerate(QG):
                kt_last = (q0 + qn - 1) // 128
                kts = list(range(kt_last + 1))
                groups = []
                full = [kt for kt in kts if kt * 128 < q0 and qn == 512]
                rest = [kt for kt in kts if kt not in full]
                for j in range(0, len(full), 2):
                    groups.append(full[j:j + 2])
                for kt in rest:
                    groups.append([kt])
                for gj, g in enumerate(groups):
                    units.append(dict(h=h, q0=q0, qn=qn, kts=g, first_h=(gi == 0 and gj == 0), first_g=(gj == 0),
                                      last_g=(gj == len(groups) - 1), idx=len(units)))
        ostate = {}

        def emitA(u):
            h, q0, qn, kts = u["h"], u["q0"], u["qn"], u["kts"]
            hp, hoff = h // 2, (h % 2) * 64
            hb = h % 2
            nqb = (qn + 127) // 128
            if u["first_h"]:
                S.dma("sp", "d_qh%d" % hb, lambda e: e.dma_start(out=Qh[hb][0:64, :], in_=qT[hoff:hoff + 64, hp, :]),
                      reads=["qT_all"], writes=["Qh%d" % hb])
                S.dma("sp", "d_kh%d" % hb, lambda e: e.dma_start(out=Kh[hb][0:64, :], in_=kT[hoff:hoff + 64, hp, :]),
                      reads=["kT_all"], writes=["Kh%d" % hb])
                for j3 in range(3):
                    S.dma("sp", "d_qh%d" % hb, lambda e, j3=j3: e.dma_start(
                        out=Qh[hb][64 + j3:65 + j3, 0:L], in_=cT24[8 * j3 + h:8 * j3 + h + 1, 0:L]),
                          reads=["c_all_T"], writes=["Qh%d" % hb])
                    S.dma("sp", "d_kh%d" % hb, lambda e, j3=j3: e.dma_start(
                        out=Kh[hb][67 + j3:68 + j3, 0:L], in_=ncT24[8 * j3 + h:8 * j3 + h + 1, 0:L]),
                          reads=["ncT24"], writes=["Kh%d" % hb])
            if u["first_g"]:
                O = next_ps(6, 8, "o")
                ostate[(h, q0)] = O
                S.op("dve", lambda e, O=O, nqb=nqb: e.memset(PS[O][:, 0:65 * nqb], 0.0), writes=["ps%d" % O])
            pp = next_ps(0, 3, "sp2")
            pj = u["idx"] % NPB
            u["pj"] = pj
            u["geo"] = []
            for hf, kt in enumerate(kts):
                kr = 128 if kt < 16 else 16
                qs = max(q0, kt * 128)
                nn = q0 + qn - qs
                u["geo"].append((kt, kr, qs, nn))
                diag = (kt * 128 >= q0)
                dst = PS2[pp][0:kr, hf * 512:hf * 512 + nn] if PS2 is not None else PS[2 * pp + hf][0:kr, 0:nn]
                S.op("pe", lambda e, dst=dst, kt=kt, kr=kr, qs=qs, nn=nn, diag=diag: e.matmul(
                    dst, lhsT=Kh[hb][:, kt * 128:kt * 128 + kr], rhs=Qh[hb][:, qs:qs + nn], start=True, stop=not diag),
                     reads=["Kh%d" % hb, "Qh%d" % hb], writes=["ps%d" % (2 * pp), "ps%d" % (2 * pp + 1)], sig=not diag)
                if diag:
                    dn = min(128, nn)
                    S.op("pe", lambda e, kr=kr, dn=dn, hf=hf: e.matmul(
                        (PS2[pp][0:kr, hf * 512:hf * 512 + dn] if PS2 is not None else PS[2 * pp + hf][0:kr, 0:dn]),
                        lhsT=identb[0:kr, 0:kr], rhs=maskneg[0:kr, 0:dn], start=False, stop=True),
                         reads=["identb", "maskneg"], writes=["ps%d" % (2 * pp), "ps%d" % (2 * pp + 1)])
            if len(kts) == 2 and os.environ.get("PAIR_EXP", "1") == "1":
                S.op("act", lambda e: e.activation(out=Pb[pj][:, :], in_=PS2[pp][:, :], func=AF.Exp, scale=0.125),
                     reads=["ps%d" % (2 * pp), "ps%d" % (2 * pp + 1)], writes=["Pb%d" % pj])
            elif len(kts) == 2:
                for hf in range(2):
                    S.op("act", lambda e, hf=hf: e.activation(out=Pb[pj][:, hf * 512:(hf + 1) * 512],
                                                             in_=(PS2[pp][:, hf * 512:(hf + 1) * 512] if PS2 is not None else PS[2 * pp + hf][:, :]),
                                                             func=AF.Exp, scale=0.125),
                         reads=["ps%d" % (2 * pp), "ps%d" % (2 * pp + 1)], writes=["Pb%d" % pj])
            else:
                kt, kr, qs, nn = u["geo"][0]
                S.op("act", lambda e: e.activation(out=Pb[pj][0:kr, 0:nn], in_=(PS2[pp][0:kr, 0:nn] if PS2 is not None else PS[2 * pp][0:kr, 0:nn]),
                                                   func=AF.Exp, scale=0.125),
                     reads=["ps%d" % (2 * pp), "ps%d" % (2 * pp + 1)], writes=["Pb%d" % pj])

        def emitB(u):
            h, q0, qn = u["h"], u["q0"], u["qn"]
            pj = u["pj"]
            nqb = (qn + 127) // 128
            O = ostate[(h, q0)]
            nk = len(u["geo"])
            for hf, (kt, kr, qs, nn) in enumerate(u["geo"]):
                for qb in range(nqb):
                    qcol = q0 + qb * 128
                    qr = min(128, q0 + qn - qcol)
                    if qcol + qr - 1 < kt * 128:
                        continue
                    S.op("pe", lambda e, qb=qb, qr=qr, qcol=qcol, hf=hf, kt=kt, kr=kr, qs=qs: e.matmul(
                        PS[O][0:qr, 65 * qb:65 * qb + 65], lhsT=Pb[pj][0:kr, hf * 512 + qcol - qs:hf * 512 + qcol - qs + qr],
                        rhs=Vp[0:kr, kt, h, 0:65], start=False, stop=(kt == qcol // 128), skip_group_check=True),
                         reads=["Pb%d" % pj, "Vp%d" % kt, "Vp_ones"], writes=["ps%d" % O],
                         sig=(qb == nqb - 1 and hf == nk - 1))
            if u["last_g"]:
                for qb in range(nqb):
                    qcol = q0 + qb * 128
                    qr = min(128, q0 + qn - qcol)
                    S.op("dve", lambda e, qb=qb, qr=qr: e.reciprocal(out=rsum[0:qr, qb:qb + 1], in_=PS[O][0:qr, 65 * qb + 64:65 * qb + 65]),
                         reads=["ps%d" % O], writes=["rsum%d" % qb])
                    S.op("dve", lambda e, qb=qb, qr=qr, qcol=qcol: e.tensor_scalar(
                        out=attn_tok[0:qr, qcol // 128, h * 64:(h + 1) * 64], in0=PS[O][0:qr, 65 * qb:65 * qb + 64],
                        scalar1=rsum[0:qr, qb:qb + 1], scalar2=None, op0=ALU.mult),
                         reads=["ps%d" % O, "rsum%d" % qb], writes=["attn_tok%d" % (qcol // 128)])

        LA = 3
        for i in range(min(LA, len(units))):
            emitA(units[i])
        for i in range(len(units)):
            emitB(units[i])
            if i + LA < len(units):
                emitA(units[i + LA])

        chk(7)
        KC = 256
        kc_tm = [M.at(R3 + i * 2048, (P, 2, 512), BF16, "kc_tm") for i in range(2)]
        vc_tm = [M.at(R3 + 4096 + i * 2048, (P, 2, 512), BF16, "vc_tm") for i in range(2)]
        KcT = [M.at(R3 + 8192 + i * 2048, (P, 2, 4, 128), BF16, "KcT") for i in range(2)]
        Psb = [M.at(R3 + 12288 + i * 512, (P, 2, H, 16), BF16, "Psb") for i in range(2)]
        Vw = [M.at(AUG + 8320 + i * 2112, (P, 2, H, 66), BF16, "Vw") for i in range(2)]
        attn_s = sm((16, 512), BF16, "attn_s")
        rsum_s = sm((16, 8), F32, "rsum_s")
        wts = sm((P, NTILE, 8), F32, "wts")
        Qz = sm((P, H, 16), BF16, "Qz")
        S.op("act", lambda e: e.activation(out=wts[:, :, :].rearrange("p i h -> p (i h)"), in_=negc_s[:, :, :].rearrange("p i h -> p (i h)"), func=AF.Exp),
             reads=["c_s_neg"], writes=["wts"])
        S.op("dve", lambda e: e.memset(Qz[:, :, :], 0.0), writes=["Qz"])
        for h in range(H):
            hp, hoff = h // 2, (h % 2) * 64
            S.op("dve", lambda e, h=h, hp=hp, hoff=hoff: e.tensor_copy(out=Qz[hoff:hoff + 64, h, :], in_=qT[hoff:hoff + 64, hp, L:NT]),
                 reads=["qT_all"], writes=["Qz"])

        def load_cache_chunk(c):
            j = c % 2
            S.dma("pool", "d_kc%d" % j, lambda e, c=c, j=j: e.dma_start(
                out=kc_tm[j][:, :, :], in_=cache_k[KC * c:KC * (c + 1), :].rearrange("(i p) d -> p i d", p=P)),
                  writes=["kc_tm%d" % j])
            S.dma("pool", "d_vc%d" % j, lambda e, c=c, j=j: e.dma_start(
                out=vc_tm[j][:, :, :], in_=cache_v[KC * c:KC * (c + 1), :].rearrange("(i p) d -> p i d", p=P)),
                  writes=["vc_tm%d" % j])
        load_cache_chunk(0)
        load_cache_chunk(1)
        OS = (6, 7)
        for ob in OS:
            S.op("dve", lambda e, ob=ob: e.memset(PS[ob][0:16, 0:260], 0.0), writes=["ps%d" % ob])
        NCH = PAST // KC

        def sA(c):
            j = c % 2
            Tk = next_ps(0, 2, "tk")
            pstk = PS[Tk][:, :].bitcast(BF16)
            for t in range(2):
                for hp in range(4):
                    S.op("pe", lambda e, t=t, hp=hp: e.transpose(
                        out=pstk[:, (t * 4 + hp) * 128:(t * 4 + hp + 1) * 128], in_=kc_tm[j][:, t, hp * 128:(hp + 1) * 128],
                        identity=identb[:, :]),
                         reads=["kc_tm%d" % j, "identb"], writes=["ps%d" % Tk], sig=(t == 1 and hp == 3))
            S.op("act", lambda e: e.activation(out=KcT[j][:, :, :, :].rearrange("p t h k -> p (t h k)"), in_=pstk[:, :], func=AF.Copy),
                 reads=["ps%d" % Tk], writes=["KcT%d" % j])
            for t in range(2):
                kt = 2 * c + t
                S.op("dve", lambda e, t=t, kt=kt: e.tensor_tensor(
                    out=Vw[j][:, t, :, 0:64], in0=vc_tm[j][:, t, :].rearrange("p (h d) -> p h d", h=H),
                    in1=wts[:, kt, :].unsqueeze(2).to_broadcast([P, H, 64]), op=ALU.mult),
                     reads=["vc_tm%d" % j, "wts"], writes=["Vw%d" % j])
                S.op("dve", lambda e, t=t, kt=kt: e.tensor_copy(out=Vw[j][:, t, :, 64], in_=wts[:, kt, :]),
                     reads=["wts"], writes=["Vw%d" % j])
            Sb = next_ps(2, 4, "sb")
            for t in range(2):
                for h in range(H):
                    hp = h // 2
                    col = (t * H + h) * 16
                    S.op("pe", lambda e, col=col, t=t, hp=hp, h=h: e.matmul(
                        PS[Sb][:, col:col + 16], lhsT=KcT[j][:, t, hp, :], rhs=Qz[:, h, :], start=True, stop=True),
                         reads=["KcT%d" % j, "Qz"], writes=["ps%d" % Sb], sig=(t == 1 and h == H - 1))
            S.op("act", lambda e: e.activation(out=Psb[j][:, :, :, :].rearrange("p t h q -> p (t h q)"), in_=PS[Sb][:, 0:256],
                                               func=AF.Exp, scale=0.125),
                 reads=["ps%d" % Sb], writes=["Psb%d" % j])

        def sB(c):
            j = c % 2
            for t in range(2):
                for h in range(H):
                    ob = OS[h // 4]
                    oc = (h % 4) * 65
                    S.op("pe", lambda e, ob=ob, oc=oc, t=t, h=h: e.matmul(
                        PS[ob][0:16, oc:oc + 65], lhsT=Psb[j][:, t, h, :], rhs=Vw[j][:, t, h, 0:65],
                        start=False, stop=False, skip_group_check=True),
                         reads=["Psb%d" % j, "Vw%d" % j], writes=["ps%d" % ob], sig=(t == 1 and h == H - 1))
            if c + 2 < NCH:
                load_cache_chunk(c + 2)
        sA(0)
        for c in range(NCH):
            if c + 1 < NCH:
                sA(c + 1)
            sB(c)
        Sb = next_ps(2, 4, "sb")
        Pn = sm((16, H, 16), BF16, "Pn")
        Vsw = sm((16, H, 66), BF16, "Vsw")
        for h in range(H):
            hp = h // 2
            S.op("pe", lambda e, h=h, hp=hp: e.matmul(
                PS[Sb][0:16, 16 * h:16 * h + 16], lhsT=kT[:, hp, L:NT], rhs=Qz[:, h, :], start=True, stop=False),
                 reads=["Qz"], writes=["ps%d" % Sb], sig=False)
            S.op("pe", lambda e, h=h: e.matmul(
                PS[Sb][0:16, 16 * h:16 * h + 16], lhsT=identb[0:16, 0:16], rhs=maskneg[0:16, 0:16], start=False, stop=True),
                 reads=["identb", "maskneg"], writes=["ps%d" % Sb], sig=(h == H - 1))
        S.op("act", lambda e: e.activation(out=Pn[:, :, :].rearrange("p h q -> p (h q)"), in_=PS[Sb][0:16, 0:128], func=AF.Exp, scale=0.125),
             reads=["ps%d" % Sb], writes=["Pn"])
        S.op("dve", lambda e: e.tensor_tensor(out=Vsw[:, :, 0:65], in0=Vsn[:, :, 0:65],
                                              in1=wts[0:16, 16, :].unsqueeze(2).to_broadcast([16, H, 65]), op=ALU.mult),
             reads=["Vsn", "Vsn_ones", "wts"], writes=["Vsw"])
        for h in range(H):
            ob = OS[h // 4]
            oc = (h % 4) * 65
            S.op("pe", lambda e, ob=ob, oc=oc, h=h: e.matmul(
                PS[ob][0:16, oc:oc + 65], lhsT=Pn[:, h, :], rhs=Vsw[:, h, 0:65], start=False, stop=True, skip_group_check=True),
                 reads=["Pn", "Vsw"], writes=["ps%d" % ob], sig=(h % 4 == 3))
        for h in range(H):
            ob = OS[h // 4]
            oc = (h % 4) * 65
            S.op("dve", lambda e, ob=ob, oc=oc, h=h: e.reciprocal(out=rsum_s[:, h:h + 1], in_=PS[ob][0:16, oc + 64:oc + 65]),
                 reads=["ps%d" % ob], writes=["rsum_s"])
            S.op("dve", lambda e, ob=ob, oc=oc, h=h: e.tensor_scalar(
                out=attn_s[:, h * 64:(h + 1) * 64], in0=PS[ob][0:16, oc:oc + 64], scalar1=rsum_s[:, h:h + 1], scalar2=None, op0=ALU.mult),
                 reads=["ps%d" % ob, "rsum_s"], writes=["attn_s"])

        for i in range(NTILE):
            r = 128 if i < 16 else 16
            Tb = next_ps(0, 4, "ta")
            pst = PS[Tb][:, :].bitcast(BF16)
            for hp in range(4):
                S.op("pe", lambda e, i=i, r=r, hp=hp, pst=pst: e.transpose(
                    out=pst[:, hp * 128:hp * 128 + r], in_=attn_tok[0:r, i, hp * 128:(hp + 1) * 128], identity=identb[0:r, 0:r]),
                     reads=["attn_tok%d" % i, "identb"], writes=["ps%d" % Tb], sig=(hp == 3))
            S.op("dve", lambda e, i=i, r=r, pst=pst: e.tensor_copy(
                out=attnT[:, :, 128 * i:128 * i + r], in_=pst[:, 0:512].rearrange("p (h t) -> p h t", h=4)[:, :, 0:r]),
                 reads=["ps%d" % Tb], writes=["attnT%d" % i])
        Tb = next_ps(0, 4, "ta")
        pst = PS[Tb][:, :].bitcast(BF16)
        for hp in range(4):
            S.op("pe", lambda e, hp=hp, pst=pst: e.transpose(
                out=pst[:, hp * 128:hp * 128 + 16], in_=attn_s[0:16, hp * 128:(hp + 1) * 128], identity=identb[0:16, 0:16]),
                 reads=["attn_s", "identb"], writes=["ps%d" % Tb], sig=(hp == 3))
        S.op("dve", lambda e, pst=pst: e.tensor_copy(
            out=attnT[:, :, L:NT], in_=pst[:, 0:512].rearrange("p (h t) -> p h t", h=4)[:, :, 0:16]),
             reads=["ps%d" % Tb], writes=["attnT17"])
        if "attnT" in dbg_out:
            S.dma("sp", "d_dbg", lambda e: e.dma_start(out=dbg_out["attnT"][:, :, :], in_=attnT[:, :, :]),
                  reads=["attnT%d" % i for i in range(18)])

        chk(8)
        S.barrier()
        ZW = 2088
        zbuf = [M.at(R3 + i * (ZW * 4), (P, ZW), F32, "zbuf") for i in range(2)]
        convT = M.at(R3 + 16896, (P, 4, NT), BF16, "convT")
        tmpC = [M.at(R45 + 16640 + i * 2048, (P, 512), F32, "tmpC") for i in range(2)]
        ytmp = [M.at(R45 + 20736 + i * 2048, (P, 512), F32, "ytmp") for i in range(2)]
        tmpB = [M.at(AUG + 4160 + i * 2048, (P, 512), F32, "tmpB") for i in range(2)]
        stc_sb = sm((P, 8), F32, "stc_sb")
        zlast = sm((P, 4, 4), F32, "zlast")
        S.dma("sp", "d_stc", lambda e: e.dma_start(out=stc_sb[:, :], in_=stc[:, :]), writes=["stc_sb"])
        cc3 = {"n": 0}
        for c in range(4):
            wX, kX = w_get(G_CV[c])
            zb = zbuf[c % 2]
            zk = "zbuf%d" % (c % 2)
            S.op("dve", lambda e, zb=zb: e.memset(zb[:, 0:2], 0.0), writes=[zk])
            S.op("dve", lambda e, zb=zb, c=c: e.tensor_copy(out=zb[:, 2066:2068], in_=stc_sb[:, 2 * c:2 * c + 2]), reads=["stc_sb"], writes=[zk])
            for (c0, n) in BLKS_EQ:
                jj = cc3["n"] % 2
                cc3["n"] += 1
                banks = []
                for j3 in range(3):
                    A = next_ps()
                    banks.append(A)
                    for k in range(KD):
                        S.op("pe", lambda e, A=A, k=k, j3=j3, c0=c0, n=n, wX=wX: e.matmul(
                            PS[A][:, 0:n], lhsT=wX[:, k, 128 * j3:128 * (j3 + 1)], rhs=xnT[:, k, c0:c0 + n],
                            start=(k == 0), stop=(k == KD - 1)), reads=[kX], writes=["ps%d" % A], sig=(k == KD - 1))
                bB, bC, bH = banks
                S.op("act", lambda e, bC=bC, jj=jj, n=n: e.activation(out=tmpC[jj][:, 0:n], in_=PS[bC][:, 0:n], func=AF.Copy),
                     reads=["ps%d" % bC], writes=["tmpC%d" % jj])
                S.op("act", lambda e, bB=bB, jj=jj, n=n: e.activation(out=tmpB[jj][:, 0:n], in_=PS[bB][:, 0:n], func=AF.Copy),
                     reads=["ps%d" % bB], writes=["tmpB%d" % jj])
                segs = []
                if c0 < L:
                    segs.append((0, min(c0 + n, L) - c0, c0 + 2))
                if c0 + n > L:
                    s0 = max(c0, L)
                    segs.append((s0 - c0, c0 + n - s0, s0 + 4))
                for (so, sn, zc) in segs:
                    S.op("dve", lambda e, bH=bH, jj=jj, so=so, sn=sn, zc=zc, zb=zb: e.tensor_tensor(
                        out=zb[:, zc:zc + sn], in0=tmpC[jj][:, so:so + sn], in1=PS[bH][:, so:so + sn], op=ALU.mult),
                         reads=["tmpC%d" % jj, "ps%d" % bH], writes=[zk])
                for (so, sn, zc) in segs:
                    S.op("dve", lambda e, jj=jj, so=so, sn=sn, zc=zc, zb=zb, c=c: e.tensor_scalar(
                        out=ytmp[jj][:, so:so + sn], in0=zb[:, zc:zc + sn], scalar1=pk[:, 24 + 3 * c + 2:24 + 3 * c + 3],
                        scalar2=pk[:, 36 + c:37 + c], op0=ALU.mult, op1=ALU.add),
                         reads=[zk, "pk"], writes=["ytmp%d" % jj])
                    for tap, sh in ((1, 1), (0, 2)):
                        S.op("dve", lambda e, jj=jj, so=so, sn=sn, zc=zc, zb=zb, c=c, tap=tap, sh=sh: e.scalar_tensor_tensor(
                            out=ytmp[jj][:, so:so + sn], in0=zb[:, zc - sh:zc - sh + sn], scalar=pk[:, 24 + 3 * c + tap:24 + 3 * c + tap + 1],
                            in1=ytmp[jj][:, so:so + sn], op0=ALU.mult, op1=ALU.add),
                             reads=[zk, "pk", "ytmp%d" % jj], writes=["ytmp%d" % jj])
                S.op("dve", lambda e, jj=jj, n=n, c=c, c0=c0: e.tensor_tensor(
                    out=convT[:, c, c0:c0 + n], in0=tmpB[jj][:, 0:n], in1=ytmp[jj][:, 0:n], op=ALU.mult),
                     reads=["tmpB%d" % jj, "ytmp%d" % jj], writes=["convT"])
            S.op("dve", lambda e, zb=zb, c=c: e.tensor_copy(out=zlast[:, c, 0:2], in_=zb[:, 2064:2066]), reads=[zk], writes=["zlast"])
            S.op("dve", lambda e, zb=zb, c=c: e.tensor_copy(out=zlast[:, c, 2:4], in_=zb[:, 2082:2084]), reads=[zk], writes=["zlast"])
            w_done(G_CV[c])
        with nc.allow_non_contiguous_dma(reason="tiny transposed conv-state rows"):
            for c in range(4):
                S.dma("sp", "d_ncp", lambda e, c=c: e.dma_start(
                    out=nc_p[:, 128 * c:128 * (c + 1)].rearrange("r p -> p r"), in_=zlast[:, c, 0:2], allow_slow_non_contiguous=True),
                      reads=["zlast"], writes=["nc_p%d" % c])
                S.dma("sp", "d_ncs", lambda e, c=c: e.dma_start(
                    out=nc_s[:, 128 * c:128 * (c + 1)].rearrange("r p -> p r"), in_=zlast[:, c, 2:4], allow_slow_non_contiguous=True),
                      reads=["zlast"], writes=["nc_s%d" % c])

        chk(9)
        mergedT = M.at(R2, (P, KD, NT), BF16, "mergedT")
        gt = [[M.at(R45 + 24832 + (i * 4 + q_) * 2048, (P, 512), F32, "gt") for q_ in range(4)] for i in range(2)]
        g4 = {"n": 0}
        wbrc, kbrc = w_get(G_BRC)
        wbra, kbra = w_get(G_BRA)
        for pr in range(4):
            wgp, kgp = w_get(G_GP[pr])
            for jj2 in range(2):
                j = 2 * pr + jj2
                for (c0, n) in BLKS_EQ:
                    si = g4["n"] % 2
                    g4["n"] += 1
                    sA, sB, t1, t2 = gt[si]
                    ba = next_ps(); bb = next_ps(); bc = next_ps(); bd = next_ps()
                    for (bank, wv, wk, src, nk, col0) in ((ba, wgp, kgp, xnT, KD, jj2 * 128), (bb, wgp, kgp, xnT, KD, 256 + jj2 * 128),
                                                        (bc, wbrc, kbrc, convT, 4, j * 128), (bd, wbra, kbra, attnT, 4, j * 128)):
                        for k in range(nk):
                            S.op("pe", lambda e, bank=bank, wv=wv, src=src, k=k, nk=nk, col0=col0, c0=c0, n=n: e.matmul(
                                PS[bank][:, 0:n], lhsT=wv[:, k, col0:col0 + 128], rhs=src[:, k, c0:c0 + n],
                                start=(k == 0), stop=(k == nk - 1)),
                                 reads=[wk, "convT"] + ["attnT%d" % i for i in range(18)], writes=["ps%d" % bank], sig=(k == nk - 1))
                    S.op("act", lambda e, ba=ba, sA=sA, n=n: e.activation(out=sA[:, 0:n], in_=PS[ba][:, 0:n], func=AF.Sigmoid),
                         reads=["ps%d" % ba], writes=["gtA%d" % si])
                    S.op("act", lambda e, bb=bb, sB=sB, n=n: e.activation(out=sB[:, 0:n], in_=PS[bb][:, 0:n], func=AF.Sigmoid),
                         reads=["ps%d" % bb], writes=["gtB%d" % si])
                    S.op("dve", lambda e, bc=bc, sA=sA, t1=t1, n=n: e.tensor_tensor(out=t1[:, 0:n], in0=sA[:, 0:n], in1=PS[bc][:, 0:n], op=ALU.mult),
                         reads=["ps%d" % bc, "gtA%d" % si], writes=["gt1%d" % si])
                    S.op("dve", lambda e, bd=bd, sB=sB, t2=t2, n=n: e.tensor_tensor(out=t2[:, 0:n], in0=sB[:, 0:n], in1=PS[bd][:, 0:n], op=ALU.mult),
                         reads=["ps%d" % bd, "gtB%d" % si], writes=["gt2%d" % si])
                    S.op("dve", lambda e, t1=t1, t2=t2, j=j, c0=c0, n=n: e.tensor_tensor(out=mergedT[:, j, c0:c0 + n], in0=t1[:, 0:n], in1=t2[:, 0:n], op=ALU.add),
                         reads=["gt1%d" % si, "gt2%d" % si], writes=["mergedT"])
            w_done(G_GP[pr])
        w_done(G_BRC); w_done(G_BRA)

        chk(10)
        S.barrier()
        yacc = M.at(R1, (P, 16, D), F32, "yacc")
        yacc16 = M.at(R45 + 33280, (P, D), F32, "yacc16")
        hnT = M.at(R45, (P, KD, NT), BF16, "hnT")
        hsb = [M.at(AUG + i * 2048, (P, D), BF16, "hs") for i in range(3)]
        junk2 = M.at(R45 + 41472, (P, D), BF16, "junk2")
        wo0, ko0 = w_get(G_OUT0)
        wo1, ko1 = w_get(G_OUT0 + 1)

        def ytile(i):
            return yacc[:, i, :] if i < 16 else yacc16[:, :]

        def s5A(i):
            r = tile_rows(i)
            c0 = 128 * i
            b = i % 3
            yt = ytile(i)
            for half, (wo, ko) in enumerate(((wo0, ko0), (wo1, ko1))):
                A = next_ps(0, 6, "w5")
                for k in range(KD):
                    S.op("pe", lambda e, A=A, k=k, wo=wo: e.matmul(
                        PS[A][0:r, :], lhsT=mergedT[:, k, c0:c0 + r], rhs=wo[:, k, :], start=(k == 0), stop=(k == KD - 1)),
                         reads=[ko, "mergedT"], writes=["ps%d" % A], sig=(k == KD - 1))
                S.op("dve", lambda e, A=A, half=half: e.tensor_tensor(
                    out=yt[0:r, half * 512:(half + 1) * 512], in0=yt[0:r, half * 512:(half + 1) * 512], in1=PS[A][0:r, :], op=ALU.add),
                     reads=["ps%d" % A, "yacc%d" % i], writes=["yacc%d" % i])
            rms_rstd(yt[0:r, :], r, i, 1.0 / D, "yacc%d" % i, "b", junk=junk2, jkey="junk2")

        def s5A2(i):
            r = tile_rows(i)
            b = i % 3
            yt = ytile(i)
            S.op("dve", lambda e: e.tensor_scalar(out=hsb[b][0:r, :], in0=yt[0:r, :], scalar1=rstd[0:r, i:i + 1],
                                                  scalar2=None, op0=ALU.mult),
                 reads=["yacc%d" % i, "rstdb%d" % i], writes=["hs%d" % b])

        def s5B(i):
            r = tile_rows(i)
            c0 = 128 * i
            b = i % 3
            Tb = next_ps(6, 8, "t5")
            pst = PS[Tb][:, :].bitcast(BF16)
            for k in range(KD):
                S.op("pe", lambda e, k=k: e.transpose(out=pst[:, k * 128:k * 128 + r], in_=hsb[b][0:r, k * 128:(k + 1) * 128],
                                                     identity=identb[0:r, 0:r]),
                     reads=["hs%d" % b, "identb"], writes=["ps%d" % Tb], sig=(k == KD - 1))
            S.op("dve", lambda e: e.tensor_tensor(
                out=hnT[:, :, c0:c0 + r], in0=pst.rearrange("p (k t) -> p k t", k=KD)[:, :, 0:r],
                in1=pk[:, 8:16].unsqueeze(2).to_broadcast([P, KD, r]), op=ALU.mult),
                 reads=["ps%d" % Tb, "pk"], writes=["hnT"])
        for i in range(NTILE):
            load_xtile(i, ytile(i), "yacc%d" % i, "d_xr%d" % i)
        s5A(0)
        s5A(1)
        s5A2(0)
        for i in range(NTILE):
            if i + 2 < NTILE:
                s5A(i + 2)
            if i + 1 < NTILE:
                s5A2(i + 1)
            s5B(i)
        w_done(G_OUT0); w_done(G_OUT0 + 1)

        chk(11)
        S.barrier()
        aTb = [M.at(R2 + i * 16640, (P, 4, NT), BF16, "aT") for i in range(2)]
        rtmp = [M.at(R45 + 37376 + i * 2048, (P, 512), F32, "rtmp") for i in range(2)]
        r6 = {"n": 0}
        for g in range(8):
            wu, ku = w_get(G_MLP + 2 * g)
            wd, kd = w_get(G_MLP + 2 * g + 1)
            aT = aTb[g % 2]
            ak = "aT%d" % (g % 2)
            for fc in range(4):
                for (c0, n) in BLKS_EQ:
                    A = next_ps()
                    for k in range(KD):
                        S.op("pe", lambda e, A=A, k=k, fc=fc, c0=c0, n=n, wu=wu: e.matmul(
                            PS[A][:, 0:n], lhsT=wu[:, k, fc * 128:(fc + 1) * 128], rhs=hnT[:, k, c0:c0 + n],
                            start=(k == 0), stop=(k == KD - 1)), reads=[ku, "hnT"], writes=["ps%d" % A], sig=(k == KD - 1))
                    rj = r6["n"] % 2
                    r6["n"] += 1
                    S.op("act", lambda e, A=A, n=n, rj=rj: e.activation(out=rtmp[rj][:, 0:n], in_=PS[A][:, 0:n], func=AF.Relu),
                         reads=["ps%d" % A], writes=["rtmp%d" % rj])
                    S.op("act", lambda e, fc=fc, c0=c0, n=n, aT=aT, rj=rj: e.activation(
                        out=aT[:, fc, c0:c0 + n], in_=rtmp[rj][:, 0:n], func=AF.Square),
                         reads=["rtmp%d" % rj], writes=[ak])
            for i in range(NTILE):
                r = tile_rows(i)
                c0 = 128 * i
                yt = ytile(i)
                for half in range(2):
                    A = next_ps()
                    for fc in range(4):
                        S.op("pe", lambda e, A=A, fc=fc, r=r, c0=c0, half=half, aT=aT, wd=wd: e.matmul(
                            PS[A][0:r, :], lhsT=aT[:, fc, c0:c0 + r], rhs=wd[:, fc, half * 512:(half + 1) * 512],
                            start=(fc == 0), stop=(fc == 3)), reads=[kd, ak], writes=["ps%d" % A], sig=(fc == 3))
                    S.op("dve", lambda e, A=A, r=r, yt=yt, half=half: e.tensor_tensor(
                        out=yt[0:r, half * 512:(half + 1) * 512], in0=yt[0:r, half * 512:(half + 1) * 512], in1=PS[A][0:r, :], op=ALU.add),
                         reads=["ps%d" % A, "yacc%d" % i], writes=["yacc%d" % i])
                if g == 7:
                    if i == 0:
                        S.dma("sp", "d_y", lambda e: e.dma_start(out=y_prompt[0:112, :], in_=yacc[16:128, 0, :]), reads=["yacc0"], writes=["y_prompt0"])
                    elif i < 16:
                        S.dma("sp", "d_y", lambda e, i=i: e.dma_start(out=y_prompt[128 * i - 16:128 * i + 112, :], in_=yacc[:, i, :]),
                              reads=["yacc%d" % i], writes=["y_prompt%d" % i])
                    else:
                        S.dma("sp", "d_y", lambda e: e.dma_start(out=y_prompt[2032:2048, :], in_=yacc16[0:16, :]), reads=["yacc16"], writes=["y_prompt16"])
                        S.dma("sp", "d_y", lambda e: e.dma_start(out=y_sample[:, :], in_=yacc16[16:32, :]), reads=["yacc16"], writes=["y_sample"])
            w_done(G_MLP + 2 * g); w_done(G_MLP + 2 * g + 1)

        if "attn_tok" in dbg_out:
            S.dma("sp", "d_dbg", lambda e: e.dma_start(out=dbg_out["attn_tok"][:, :, :], in_=attn_tok[:, :, :]),
                  reads=["attn_tok%d" % i for i in range(NTILE)])
        if "cT24" in dbg_out:
            S.dma("sp", "d_dbg", lambda e: e.dma_start(out=dbg_out["cT24"][:, :], in_=cT24[:, :]), reads=["c_all_T"])
        if "negc" in dbg_out:
            S.dma("sp", "d_dbg", lambda e: e.dma_start(out=dbg_out["negc"][:, :, :], in_=negc[:, :, :]), reads=["c_all_neg"])
        if "qT" in dbg_out:
            S.dma("sp", "d_dbg", lambda e: e.dma_start(out=dbg_out["qT"][:, :, :], in_=qT[:, :, :]), reads=["qT_all"])
        if "c_all" in dbg_out:
            S.dma("sp", "d_dbg", lambda e: e.dma_start(out=dbg_out["c_all"][:, :, :], in_=c_all[:, :, :]), reads=["c_all"])


    except _Stop:
        pass

    S.finish("sp")
    S.emit()
    return nc


def make_in_maps(inputs):
    f = lambda a: np.ascontiguousarray(np.asarray(a, dtype=np.float32))
    x_prompt = f(inputs["x_prompt"]); x_sample = f(inputs["x_sample"])
    ck = f(inputs["cache_k"])[0]; cv = f(inputs["cache_v"])[0]; cl = f(inputs["cache_logf"])[0]
    sc = f(inputs["state_conv"])[0]
    pk = np.zeros((P, 64), np.float32)
    pk[:, 0:8] = f(inputs["norm1_g"])[0].reshape(8, P).T
    pk[:, 8:16] = f(inputs["norm2_g"])[0].reshape(8, P).T
    pk[:, 16:24] = np.broadcast_to(f(inputs["b_f"])[0][None, :], (P, 8))
    cw = f(inputs["conv_w"])[0]
    pk[:, 24:36] = cw.reshape(3, 4, P).transpose(2, 1, 0).reshape(P, 12)
    pk[:, 36:40] = f(inputs["conv_b"])[0].reshape(4, P).T
    pk[:, 40] = np.tile(f(inputs["q_norm_g"])[0], 2)
    pk[:, 41] = np.tile(f(inputs["k_norm_g"])[0], 2)
    maps = []
    for b in range(8):
        stt = np.ascontiguousarray(sc[b].reshape(2, 4, P).transpose(2, 1, 0).reshape(P, 8))
        maps.append({
            "x_prompt": x_prompt[b], "x_sample": x_sample[b],
            "cache_k": ck[b].reshape(PAST, 512), "cache_v": cv[b].reshape(PAST, 512),
            "cache_logf": cl[b], "meta": f(inputs["meta"]),
            "w_in": f(inputs["w_in"])[0], "w_br_conv": f(inputs["w_br_conv"])[0],
            "w_br_attn": f(inputs["w_br_attn"])[0], "w_out": f(inputs["w_out"])[0],
            "w_up": f(inputs["w_up"])[0], "w_down": f(inputs["w_down"])[0],
            "ppk": pk, "state_conv_t": stt,
        })
    return maps


_NC_CACHE = {}


def kernel(**inputs):
    maps = make_in_maps(inputs)
    if "nc" not in _NC_CACHE:
        _NC_CACHE["nc"] = build()
    nc = _NC_CACHE["nc"]
    res = run_bass_kernel_spmd(nc, maps, core_ids=list(range(8)))
    R = res.results
    st = lambda name: np.stack([np.asarray(R[b][name], dtype=np.float32) for b in range(8)])
    y_prompt = st("y_prompt")
    y_sample = st("y_sample")
    nk_p = st("nk_p").reshape(1, 8, L, H, HD)
    nv_p = st("nv_p").reshape(1, 8, L, H, HD)
    nf_p = st("nf_p").reshape(1, 8, L, H)
    nc_p = st("nc_p").reshape(1, 8, 2, DC)
    nk_s = st("nk_s").reshape(1, 8, NS, H, HD)
    nv_s = st("nv_s").reshape(1, 8, NS, H, HD)
    nf_s = st("nf_s").reshape(1, 8, NS, H)
    nc_s = st("nc_s").reshape(1, 8, 2, DC)
    return (y_prompt, y_sample, nk_p, nv_p, nf_p, nc_p, nk_s, nv_s, nf_s, nc_s)
```

```python
import os
import numpy as np
import concourse.bass as bass
import concourse.mybir as mybir
from concourse.bass_utils import run_bass_kernel_spmd

F32 = mybir.dt.float32
BF16 = mybir.dt.bfloat16
AF = mybir.ActivationFunctionType
ALU = mybir.AluOpType
AX = mybir.AxisListType

P = 128
D = 1024
KD = 8
SEQ = 2048
NMETA = 16
L = SEQ + NMETA
NS = 16
NT = L + NS
NTILE = 17
PAST = 2048
H = 8
HD = 64
DC = 512
DA = 512
DFF = 4096
INC = 5128
EPS = 1e-6
C_B, C_C, C_H, C_Q, C_K, C_V, C_F, C_G = 0, 512, 1024, 1536, 2048, 2560, 3072, 3080
BLKS = [(0, 512), (512, 512), (1024, 512), (1536, 512), (2048, 32)]
BLKS_EQ = [(416 * i, 416) for i in range(5)]


def tile_rows(i):
    return 128 if i < 16 else 32


class Sched:
    ENG = ("pe", "act", "dve", "pool", "sp")

    def __init__(self, nc):
        self.nc = nc
        self.q = {e: [] for e in self.ENG}
        self.cnt = {}
        self.sems = {}
        self.lastw = {}
        self.lastr = {}
        self.seen = {e: {} for e in self.ENG}
        self.pending = {e: {} for e in self.ENG}
        for e in ("pe", "act", "dve", "pool"):
            self._sem(e)

    def _sem(self, name):
        if name not in self.sems:
            self.sems[name] = self.nc.alloc_semaphore("s_" + name)
            self.cnt[name] = 0
        return self.sems[name]

    def _deps(self, eng, reads, writes):
        deps = dict(self.pending[eng])
        self.pending[eng] = {}

        def merge(src, raw):
            for s, v in src.items():
                if s == eng and eng == "pe":
                    continue
                if deps.get(s, 0) < v:
                    deps[s] = v
        for k in reads:
            merge(self.lastw.get(k, {}), True)
        for k in writes:
            merge(self.lastw.get(k, {}), False)
            merge(self.lastr.get(k, {}), False)
        out = []
        seen = self.seen[eng]
        for s, v in deps.items():
            if seen.get(s, 0) < v:
                seen[s] = v
                out.append((s, v))
        return out

    def _record(self, s, v, reads, writes):
        for k in reads:
            d = self.lastr.setdefault(k, {})
            if d.get(s, 0) < v:
                d[s] = v
        for k in writes:
            d = self.lastw.setdefault(k, {})
            if d.get(s, 0) < v:
                d[s] = v

    def op(self, eng, fn, reads=(), writes=(), sig=True):
        waits = self._deps(eng, reads, writes)
        if sig:
            self.cnt[eng] += 1
            v = self.cnt[eng]
            inc = (eng, 1)
        else:
            v = self.cnt[eng] + 1
            inc = None
        self._record(eng, v, reads, writes)
        self.q[eng].append((waits, fn, inc))

    def dma(self, queue, sem, fn, reads=(), writes=()):
        self._sem(sem)
        waits = self._deps(queue, reads, writes)
        self.cnt[sem] += 16
        self._record(sem, self.cnt[sem], reads, writes)
        self.q[queue].append((waits, fn, (sem, 16)))

    def barrier(self, engines=("pe", "act", "dve", "sp"), exclude=()):
        snap = {s: v for s, v in self.cnt.items() if v > 0 and s not in exclude
                and not s.startswith(("d_ring", "d_kc", "d_vc", "d_wfl"))}
        for e in engines:
            for s, v in snap.items():
                if s == e and e == "pe":
                    continue
                if self.pending[e].get(s, 0) < v:
                    self.pending[e][s] = v

    def finish(self, eng="sp"):
        waits = []
        for s, v in self.cnt.items():
            if v > 0 and self.seen[eng].get(s, 0) < v:
                waits.append((s, v))
        self.q[eng].append((waits, None, None))

    def replay(self, name, e):
        for waits, fn, inc in self.q[name]:
            for s, v in waits:
                e.wait_ge(self.sems[s], v)
            if fn is None:
                continue
            ins = fn(e)
            if inc is not None:
                ins.then_inc(self.sems[inc[0]], inc[1])

    def emit(self):
        nc = self.nc
        with nc.Block() as block:
            @block.tensor
            def _(e):
                self.replay("pe", e)

            @block.scalar
            def _(e):
                self.replay("act", e)

            @block.vector
            def _(e):
                self.replay("dve", e)

            @block.gpsimd
            def _(e):
                self.replay("pool", e)

            @block.sync
            def _(e):
                self.replay("sp", e)


class Mem:
    def __init__(self, nc):
        self.nc = nc
        self.base = 16512
        self.top = 229344
        self.n = 0

    def at(self, off, shape, dtype, name):
        self.n += 1
        nb = int(np.prod(shape[1:])) * (4 if dtype == F32 else 2)
        assert off % 32 == 0, (name, off)
        assert self.base <= off and off + nb <= self.top, (name, off, nb, self.top)
        return self.nc.alloc_sbuf_tensor_at("%s_%d" % (name, self.n), list(shape), dtype, offset=off)


def build(dbg=None, stop_after=99):
    dbg = dbg or []
    nc = bass.Bass("TRN2", target_bir_lowering=False)
    S = Sched(nc)
    M = Mem(nc)

    def din(name, shape):
        return nc.dram_tensor(name, list(shape), F32, kind="ExternalInput")

    def dout(name, shape):
        return nc.dram_tensor(name, list(shape), F32, kind="ExternalOutput")

    x_prompt = din("x_prompt", (SEQ, D))
    x_sample = din("x_sample", (NS, D))
    cache_k = din("cache_k", (PAST, 512))
    cache_v = din("cache_v", (PAST, 512))
    cache_logf = din("cache_logf", (PAST, H))
    meta = din("meta", (NMETA, D))
    w_in = din("w_in", (D, INC))
    w_br_conv = din("w_br_conv", (DC, D))
    w_br_attn = din("w_br_attn", (DA, D))
    w_out = din("w_out", (D, D))
    w_up = din("w_up", (D, DFF))
    w_down = din("w_down", (DFF, D))
    NPK = 64
    ppk = din("ppk", (P, NPK))
    stc = din("state_conv_t", (P, 8))

    y_prompt = dout("y_prompt", (SEQ, D))
    y_sample = dout("y_sample", (NS, D))
    nk_p = dout("nk_p", (L, 512))
    nv_p = dout("nv_p", (L, 512))
    nf_p = dout("nf_p", (L, H))
    nc_p = dout("nc_p", (2, DC))
    nk_s = dout("nk_s", (NS, 512))
    nv_s = dout("nv_s", (NS, 512))
    nf_s = dout("nf_s", (NS, H))
    nc_s = dout("nc_s", (2, DC))
    dbg_out = {}
    for (name, shape, dt_) in dbg:
        dbg_out[name] = nc.dram_tensor("dbg_" + name, list(shape), dt_, kind="ExternalOutput")

    if os.environ.get("PAIR_EXP", "1") == "1":
        PS2 = [nc.alloc_psum_tensor("ps2_%d" % i, [P, 1024], F32) for i in range(4)]
        PS = [PS2[i // 2][:, (i % 2) * 512:(i % 2 + 1) * 512] for i in range(8)]
    else:
        PS2 = None
        PS = [nc.alloc_psum_tensor("ps%d" % i, [P, 512], F32) for i in range(8)]

    o = M.base
    RING_SLOTS = 5
    ring = [M.at(o + i * 8192, (P, 4096), BF16, "ring") for i in range(RING_SLOTS)]
    o += RING_SLOTS * 8192
    pk = M.at(o, (P, NPK), F32, "pk"); o += NPK * 4
    identb = M.at(o, (P, P), BF16, "identb"); o += 256
    identf = M.at(o, (P, P), F32, "identf"); o += 512
    small = [o]
    o += 13312
    def sm(shape, dtype, name):
        nb = int(np.prod(shape[1:])) * (4 if dtype == F32 else 2)
        nb = (nb + 31) // 32 * 32
        t = M.at(small[0], shape, dtype, name)
        small[0] += nb
        assert small[0] <= o_small_end
        return t
    o_small_end = o
    AUG = o; o += 12544
    R2 = o; o += 33280
    R1 = o; o += 33280
    R3 = o; o += 34304
    R45 = o
    R45_SIZE = M.top - o
    assert R45_SIZE >= 41472, R45_SIZE

    xnT = M.at(R1, (P, KD, NT), BF16, "xnT")
    NXT = 6
    xt = [M.at(R3 + i * 4096, (P, D), F32, "xt") for i in range(2)] + \
         [M.at(R45 + 25600 + i * 4096, (P, D), F32, "xt") for i in range(4)]
    xs = [M.at(R3 + 8192 + i * 2048, (P, D), BF16, "xs") for i in range(2)] + [M.at(R45 + 41984, (P, D), BF16, "xs")]
    junk = M.at(R3 + 12288, (P, D), BF16, "junk")
    ssq = sm((P, 32), F32, "ssq")
    rstd = sm((P, 32), F32, "rstd")

    S.dma("sp", "d_pk", lambda e: e.dma_start(out=pk[:, :], in_=ppk[:, :]), writes=["pk"])
    def mk_ident(t, key):
        S.op("pool", lambda e: e.memset(t[:, :], 1.0), writes=[key])
        S.op("pool", lambda e: e.affine_select(t[:, :], t[:, :], [[-1, P]], ALU.is_equal, 0.0,
                                               base=0, channel_multiplier=1), reads=[key], writes=[key])
    mk_ident(identb, "identb")
    mk_ident(identf, "identf")

    epsc = sm((P, 1), F32, "epsc")
    S.op("pool", lambda e: e.memset(epsc[:, :], EPS), writes=["epsc"])

    def load_xtile(i, buf, key, sem):
        if i == 0:
            S.dma("sp", sem, lambda e: e.dma_start(out=buf[0:16, :], in_=meta[:, :]), writes=[key])
            S.dma("sp", sem, lambda e: e.dma_start(out=buf[16:128, :], in_=x_prompt[0:112, :]), writes=[key])
        elif i < 16:
            S.dma("sp", sem, lambda e: e.dma_start(out=buf[:, :], in_=x_prompt[128 * i - 16:128 * i + 112, :]), writes=[key])
        else:
            S.dma("sp", sem, lambda e: e.dma_start(out=buf[0:16, :], in_=x_prompt[2032:2048, :]), writes=[key])
            S.dma("sp", sem, lambda e: e.dma_start(out=buf[16:32, :], in_=x_sample[:, :]), writes=[key])

    def rms_rstd(src, r, col, inv_n, kin, tagk, junk=junk, jkey="junk"):
        S.op("act", lambda e: e.activation(out=junk[0:r, :], in_=src, func=AF.Square,
                                           accum_out=ssq[0:r, col:col + 1]),
             reads=[kin, "epsc"], writes=[jkey, "ssq%s%d" % (tagk, col)])
        S.op("act", lambda e: e.activation(out=ssq[0:r, col:col + 1], in_=ssq[0:r, col:col + 1], func=AF.Ln,
                                           bias=epsc[0:r, :], scale=inv_n),
             reads=["ssq%s%d" % (tagk, col)], writes=["ssq%s%d" % (tagk, col)])
        S.op("act", lambda e: e.activation(out=rstd[0:r, col:col + 1], in_=ssq[0:r, col:col + 1], func=AF.Exp,
                                           scale=-0.5),
             reads=["ssq%s%d" % (tagk, col)], writes=["rstd%s%d" % (tagk, col)])

    for i in range(NTILE):
        r = tile_rows(i)
        b = i % NXT
        b3 = i % 3
        kx, ks = "xt%d" % b, "xs%d" % b3
        load_xtile(i, xt[b], kx, "d_xt%d" % b)
        rms_rstd(xt[b][0:r, :], r, i, 1.0 / D, kx, "a")
        S.op("dve", lambda e, b=b, b3=b3, r=r, i=i: e.tensor_scalar(out=xs[b3][0:r, :], in0=xt[b][0:r, :],
                                                          scalar1=rstd[0:r, i:i + 1], scalar2=None, op0=ALU.mult),
             reads=[kx, "rstda%d" % i], writes=[ks])
        pb = i % 2
        pst = PS[pb][:, :].bitcast(BF16)
        for k in range(KD):
            S.op("pe", lambda e, k=k, b3=b3, r=r, pst=pst: e.transpose(out=pst[:, k * 128:k * 128 + r],
                                                                    in_=xs[b3][0:r, k * 128:(k + 1) * 128],
                                                                    identity=identb[0:r, 0:r]),
                 reads=[ks, "identb"], writes=["ps%d" % pb], sig=(k == KD - 1))
        c0 = 128 * i
        S.op("dve", lambda e, pst=pst, r=r, c0=c0: e.tensor_tensor(
            out=xnT[:, :, c0:c0 + r],
            in0=pst.rearrange("p (k t) -> p k t", k=KD)[:, :, 0:r],
            in1=pk[:, 0:KD].unsqueeze(2).to_broadcast([P, KD, r]), op=ALU.mult),
             reads=["ps%d" % pb, "pk"], writes=["xnT%d" % i])

    if "xnT" in dbg_out:
        S.dma("sp", "d_dbg", lambda e: e.dma_start(out=dbg_out["xnT"][:, :, :], in_=xnT[:, :, :]),
              reads=["xnT%d" % i for i in range(NTILE)])

    class _Stop(Exception):
        pass

    def chk(st):
        if stop_after < st:
            raise _Stop()

    try:
        XN_ALL = ["xnT%d" % i for i in range(NTILE)]

        def xn_keys(c0, n):
            return ["xnT%d" % i for i in range(c0 // 128, (c0 + n - 1) // 128 + 1)]

        wlist = []

        def wg_cols(w, c0, kch=KD, ncol=512):
            return (lambda t: t[:, 0:kch * ncol].rearrange("p (k c) -> p k c", k=kch),
                    w[0:kch * 128, c0:c0 + ncol].rearrange("(k p) c -> p k c", p=P))

        G_Q, G_K, G_V = 0, 1, 2
        wlist.append(wg_cols(w_in, C_Q))
        wlist.append(wg_cols(w_in, C_K))
        wlist.append(wg_cols(w_in, C_V))
        G_CV = [3, 4, 5, 6]
        for cch in range(4):
            wlist.append((lambda t: t[:, 0:KD * 384].rearrange("p (k c) -> p k c", k=KD),
                          [((128 * j3, 128 * (j3 + 1)),
                            w_in[:, base + 128 * cch:base + 128 * (cch + 1)].rearrange("(k p) c -> p k c", p=P))
                           for j3, base in enumerate((C_B, C_C, C_H))]))
        G_BRC, G_BRA = 7, 8
        G_GP = [9, 10, 11, 12]
        wlist.append(wg_cols(w_br_conv, 0, kch=4, ncol=1024))
        wlist.append(wg_cols(w_br_attn, 0, kch=4, ncol=1024))
        for pr in range(4):
            wlist.append((lambda t: t[:, 0:KD * 512].rearrange("p (k c) -> p k c", k=KD),
                          [((0, 256), w_in[:, C_G + 256 * pr:C_G + 256 * (pr + 1)].rearrange("(k p) c -> p k c", p=P)),
                           ((256, 512), w_in[:, C_G + 1024 + 256 * pr:C_G + 1024 + 256 * (pr + 1)].rearrange("(k p) c -> p k c", p=P))]))
        G_OUT0 = 13
        wlist.append(wg_cols(w_out, 0))
        wlist.append(wg_cols(w_out, 512))
        G_MLP = 15
        for g in range(8):
            wlist.append(wg_cols(w_up, 512 * g))
            wlist.append((lambda t: t[:, :].rearrange("p (k c) -> p k c", k=4),
                          w_down[512 * g:512 * (g + 1), :].rearrange("(k p) c -> p k c", p=P)))
        wstate = {"issued": 0, "free": list(range(RING_SLOTS)), "slot": {}}

        def w_try_issue(limit=None, after=()):
            while wstate["issued"] < len(wlist) and wstate["free"] and (limit is None or wstate["issued"] < limit):
                g = wstate["issued"]
                slot = wstate["free"].pop(0)
                wstate["slot"][g] = slot
                vf, srcs = wlist[g]
                dst = vf(ring[slot])
                if not isinstance(srcs, list):
                    srcs = [(None, srcs)]
                for (sub, src) in srcs:
                    d = dst if sub is None else dst[:, :, sub[0]:sub[1]]
                    S.dma("pool", "d_ring%d" % slot, lambda e, d=d, src=src: e.dma_start(out=d, in_=src),
                          reads=list(after), writes=["ring%d" % slot])
                wstate["issued"] += 1

        def w_get(g):
            if g not in wstate["slot"]:
                w_try_issue(g + 1)
            slot = wstate["slot"][g]
            return wlist[g][0](ring[slot]), "ring%d" % slot

        def w_done(g):
            wstate["free"].append(wstate["slot"][g])
            w_try_issue()

        w_try_issue(1)
        w_try_issue(2, after=["xt%d" % (7 % NXT)])
        w_try_issue(3, after=["xt%d" % (12 % NXT)])

        blockones = sm((P, P), BF16, "blockones")
        S.op("pool", lambda e: e.memset(blockones[:, :], 0.0), writes=["blockones"])
        S.op("pool", lambda e: e.memset(blockones[0:64, 0:64], 1.0), writes=["blockones"])
        S.op("pool", lambda e: e.memset(blockones[64:128, 64:128], 1.0), writes=["blockones"])
        onesf = sm((P, P), F32, "onesf")
        S.op("pool", lambda e: e.memset(onesf[:, :], 1.0), writes=["onesf"])
        trif = sm((P, P), F32, "trif")
        S.op("pool", lambda e: e.memset(trif[:, :], 1.0), writes=["trif"])
        S.op("pool", lambda e: e.affine_select(trif[:, :], trif[:, :], [[1, P]], ALU.is_ge, 0.0,
                                               base=0, channel_multiplier=-1), reads=["trif"], writes=["trif"])
        maskb = sm((P, P), BF16, "maskb")
        S.op("pool", lambda e: e.memset(maskb[:, :], 1.0), writes=["maskb"])
        S.op("pool", lambda e: e.affine_select(maskb[:, :], maskb[:, :], [[1, P]], ALU.is_ge, 0.0,
                                               base=0, channel_multiplier=-1), reads=["maskb"], writes=["maskb"])
        maskneg = sm((P, P), BF16, "maskneg")
        S.op("pool", lambda e: e.memset(maskneg[:, :], 0.0), writes=["maskneg"])
        S.op("pool", lambda e: e.affine_select(maskneg[:, :], maskneg[:, :], [[1, P]], ALU.is_ge, -9984.0,
                                               base=0, channel_multiplier=-1), reads=["maskneg"], writes=["maskneg"])
        ones3 = sm((3, P), BF16, "ones3")
        S.op("pool", lambda e: e.memset(ones3[:, :], 1.0), writes=["ones3"])
        onecol = sm((P, 1), F32, "onecol")
        S.op("pool", lambda e: e.memset(onecol[:, :], 1.0), writes=["onecol"])
        wfl = sm((P, KD, 8), BF16, "wfl")
        if not os.environ.get("SKIP_WFL"):
          S.dma("pool", "d_wfl", lambda e: e.dma_start(out=wfl[:, :, :],
                                                     in_=w_in[:, C_F:C_F + 8].rearrange("(k p) c -> p k c", p=P)),
              writes=["wfl"])
        fl_all = sm((P, NTILE, 8), F32, "fl_all")
        lf_all = sm((P, NTILE, 8), F32, "lf_all")
        S.op("pool", lambda e: e.memset(fl_all[:, :, :], 0.0), writes=["fl_all"])
        S.op("pool", lambda e: e.memset(lf_all[:, :, :], 0.0), writes=["lf_all"])
        c_all = sm((P, NTILE, 8), F32, "c_all")
        S.op("pool", lambda e: e.memset(c_all[:, :, :], 0.0), writes=["c_all"])
        negc = sm((P, NTILE, 8), F32, "negc")
        csplit = sm((P, NTILE, 3, 8), BF16, "csplit")
        cres = sm((P, NTILE, 8), F32, "cres")
        cT24 = M.at(AUG, (24, NT), BF16, "cT24")
        qaug = [M.at(AUG + 4160 + i * 4160, (3, NT), BF16, "qaug") for i in range(2)]

        qT = M.at(R2, (P, 4, NT), BF16, "qT")
        kT = M.at(R2 + 16640, (P, 4, NT), BF16, "kT")
        Vp = M.at(R3 + 14336, (P, NTILE, H, 66), BF16, "Vp")
        sqb = [M.at(R45 + i * 1024, (P, 512), BF16, "sqb") for i in range(2)]
        rsb = [M.at(R45 + 2048 + i * 2048, (P, 512), F32, "rsb") for i in range(2)]
        kf = M.at(R45 + 6144, (P, 4, 512), F32, "kf")
        ktok = [M.at(R45 + 14336 + i * 2048, (P, 512), F32, "ktok") for i in range(2)]
        vtok = [M.at(R45 + 18432 + i * 2048, (P, 512), F32, "vtok") for i in range(2)]

        psn = {"n": 0}

        def next_ps(lo=0, hi=8, key="n"):
            psn[key] = psn.get(key, lo - 1) + 1
            if psn[key] >= hi or psn[key] < lo:
                psn[key] = lo
            return psn[key]

        if not os.environ.get("SKIP_VPMEM"):
            S.op("pool", lambda e: e.memset(Vp[:, :, :, 64:65], 1.0), writes=["Vp_ones"])

        chk(1)
        sqb3 = [M.at(R45 + 22528 + i * 1024, (P, 512), BF16, "sqb3") for i in range(3)]
        cnt1 = {"kt": 0}
        units1 = []
        for which, G in (("q", G_Q), ("k", G_K)):
            for (c0, n) in BLKS:
                for m in range(4):
                    units1.append(dict(which=which, G=G, c0=c0, n=n, m=m, idx=len(units1)))

        def s1A(u):
            which, G, c0, n, m = u["which"], u["G"], u["c0"], u["n"], u["m"]
            wv, wkey = w_get(G)
            A = next_ps(0, 4, "qa")
            sj = u["idx"] % 3
            u["A"], u["sj"] = A, sj
            for k in range(KD):
                S.op("pe", lambda e, k=k: e.matmul(
                    PS[A][:, 0:n], lhsT=wv[:, k, m * 128:(m + 1) * 128], rhs=xnT[:, k, c0:c0 + n],
                    start=(k == 0), stop=(k == KD - 1)),
                     reads=[wkey] + xn_keys(c0, n), writes=["ps%d" % A], sig=(k == KD - 1))
            S.op("act", lambda e: e.activation(out=sqb3[sj][:, 0:n], in_=PS[A][:, 0:n], func=AF.Square),
                 reads=["ps%d" % A], writes=["sqb%d" % sj])

        def s1B(u):
            which, G, c0, n, m = u["which"], u["G"], u["c0"], u["n"], u["m"]
            A, sj = u["A"], u["sj"]
            j = u["idx"] % 2
            B = next_ps(4, 6, "qb")
            S.op("pe", lambda e: e.matmul(PS[B][:, 0:n], lhsT=blockones[:, :], rhs=sqb3[sj][:, 0:n], start=True, stop=True),
                 reads=["sqb%d" % sj, "blockones"], writes=["ps%d" % B])
            S.op("act", lambda e: e.activation(out=rsb[j][:, 0:n], in_=PS[B][:, 0:n], func=AF.Ln,
                                               bias=epsc[:, :], scale=1.0 / HD),
                 reads=["ps%d" % B, "epsc"], writes=["rsb%d" % j])
            S.op("act", lambda e: e.activation(out=rsb[j][:, 0:n], in_=rsb[j][:, 0:n], func=AF.Exp, scale=-0.5),
                 reads=["rsb%d" % j], writes=["rsb%d" % j])
            if which == "q":
                S.op("dve", lambda e: e.scalar_tensor_tensor(
                    out=qT[:, m, c0:c0 + n], in0=PS[A][:, 0:n], scalar=pk[:, 40:41], in1=rsb[j][:, 0:n],
                    op0=ALU.mult, op1=ALU.mult),
                     reads=["ps%d" % A, "rsb%d" % j, "pk"], writes=["qT%d_%d" % (m, c0)])
            else:
                S.op("dve", lambda e: e.scalar_tensor_tensor(
                    out=kf[:, m, 0:n], in0=PS[A][:, 0:n], scalar=pk[:, 41:42], in1=rsb[j][:, 0:n],
                    op0=ALU.mult, op1=ALU.mult),
                     reads=["ps%d" % A, "rsb%d" % j, "pk"], writes=["kf%d" % m])
                S.op("dve", lambda e: e.tensor_copy(out=kT[:, m, c0:c0 + n], in_=kf[:, m, 0:n]),
                     reads=["kf%d" % m], writes=["kT%d_%d" % (m, c0)])
                if m == 3:
                    for tt in range((n + 127) // 128):
                        r = min(128, n - tt * 128)
                        jj = cnt1["kt"] % 2
                        cnt1["kt"] += 1
                        Cb = next_ps(6, 8, "kt")
                        for mm in range(4):
                            S.op("pe", lambda e, Cb=Cb, mm=mm, tt=tt, r=r: e.transpose(
                                out=PS[Cb][0:r, mm * 128:(mm + 1) * 128], in_=kf[:, mm, tt * 128:tt * 128 + r], identity=identf[:, :]),
                                 reads=["kf%d" % mm, "identf"], writes=["ps%d" % Cb], sig=(mm == 3))
                        S.op("dve", lambda e, Cb=Cb, jj=jj, r=r: e.tensor_copy(out=ktok[jj][0:r, :], in_=PS[Cb][0:r, :]),
                             reads=["ps%d" % Cb], writes=["ktok%d" % jj])
                        p0 = c0 + tt * 128
                        if p0 < 2048:
                            S.dma("sp", "d_ktok%d" % jj, lambda e, jj=jj, p0=p0: e.dma_start(out=nk_p[p0:p0 + 128, :], in_=ktok[jj][:, :]),
                                  reads=["ktok%d" % jj], writes=["nk_p_%d" % p0])
                        else:
                            S.dma("sp", "d_ktok%d" % jj, lambda e, jj=jj: e.dma_start(out=nk_p[2048:2064, :], in_=ktok[jj][0:16, :]),
                                  reads=["ktok%d" % jj], writes=["nk_p_%d" % p0])
                            S.dma("sp", "d_ktok%d" % jj, lambda e, jj=jj: e.dma_start(out=nk_s[:, :], in_=ktok[jj][16:32, :]),
                                  reads=["ktok%d" % jj], writes=["nk_s"])
            if m == 3 and c0 == 2048:
                w_done(G)

        LA1 = 2
        for i in range(LA1):
            s1A(units1[i])
        for i in range(len(units1)):
            s1B(units1[i])
            if i + LA1 < len(units1):
                s1A(units1[i + LA1])

        chk(2)
        wv, wkey = w_get(G_V)
        FB = 4
        for i in range(NTILE):
            r = tile_rows(i)
            c0 = 128 * i
            for k in range(KD):
                S.op("pe", lambda e, k=k, r=r, c0=c0, i=i: e.matmul(
                    PS[FB][0:r, 8 * i:8 * i + 8], lhsT=xnT[:, k, c0:c0 + r], rhs=wfl[:, k, :], start=(k == 0), stop=(k == KD - 1)),
                     reads=["wfl", "xnT%d" % i], writes=["ps%d" % FB], sig=(k == KD - 1))
        S.op("dve", lambda e: e.tensor_tensor(out=fl_all[:, 0:16, :], in0=PS[FB][:, 0:128].rearrange("p (i h) -> p i h", h=8),
                                              in1=pk[:, 16:24].unsqueeze(1).to_broadcast([P, 16, 8]), op=ALU.add),
             reads=["ps%d" % FB, "pk"], writes=["fl_all"])
        S.op("dve", lambda e: e.tensor_tensor(out=fl_all[0:32, 16, :], in0=PS[FB][0:32, 128:136], in1=pk[0:32, 16:24], op=ALU.add),
             reads=["ps%d" % FB, "pk"], writes=["fl_all"])

        def logsig(dst, src, r, kin, kout):
            S.op("act", lambda e: e.activation(out=dst, in_=src, func=AF.Exp, scale=-1.0), reads=[kin], writes=[kout])
            S.op("act", lambda e: e.activation(out=dst, in_=dst, func=AF.Ln, bias=onecol[0:r, :], scale=1.0),
                 reads=[kout, "onecol"], writes=[kout])
            S.op("dve", lambda e: e.tensor_scalar(out=dst, in0=dst, scalar1=-1.0, scalar2=None, op0=ALU.mult),
                 reads=[kout], writes=[kout])
        logsig(lf_all[:, 0:16, :], fl_all[:, 0:16, :], P, "fl_all", "lf_all")
        logsig(lf_all[0:32, 16, :], fl_all[0:32, 16, :], 32, "fl_all", "lf_all")
        S.dma("sp", "d_lf", lambda e: e.dma_start(out=nf_p[0:2048, :].rearrange("(i p) h -> p i h", p=P), in_=lf_all[:, 0:16, :]),
              reads=["lf_all"], writes=["nf_p_a"])
        S.dma("sp", "d_lf", lambda e: e.dma_start(out=nf_p[2048:2064, :], in_=lf_all[0:16, 16, :]), reads=["lf_all"], writes=["nf_p_b"])
        S.dma("sp", "d_lf", lambda e: e.dma_start(out=nf_s[:, :], in_=lf_all[16:32, 16, :]), reads=["lf_all"], writes=["nf_s"])

        carr = sm((P, NTILE, 8), F32, "carr")
        lfc = sm((P, 16, 8), F32, "lfc")
        S.dma("sp", "d_lfc", lambda e: e.dma_start(out=lfc[:, :, :], in_=cache_logf[:, :].rearrange("(i p) h -> p i h", p=P)),
              writes=["lfc"])
        c_s = sm((P, NTILE, 8), F32, "c_s")
        S.op("pool", lambda e: e.memset(c_s[:, :, :], 0.0), writes=["c_s"])
        negc_s = sm((P, NTILE, 8), F32, "negc_s")
        msel = sm((32, 16), F32, "msel")
        S.op("pool", lambda e: e.memset(msel[:, :], 1.0), writes=["msel"])
        S.op("pool", lambda e: e.affine_select(msel[:, :], msel[:, :], [[1, 16]], ALU.is_ge, 0.0,
                                               base=16, channel_multiplier=-1), reads=["msel"], writes=["msel"])
        S.op("pool", lambda e: e.memset(msel[0:16, :], 0.0), reads=["msel"], writes=["msel"])
        carr_s = sm((P, 17, 8), F32, "carr_s")
        Cb = 5
        S.op("pe", lambda e: e.matmul(PS[Cb][:, 0:136], lhsT=trif[:, :], rhs=lf_all[:, :, :].rearrange("p i h -> p (i h)"), start=True, stop=True),
             reads=["lf_all", "trif"], writes=["ps%d" % Cb])
        S.op("pe", lambda e: e.matmul(PS[Cb][:, 136:272], lhsT=onesf[:, :], rhs=lf_all[:, :, :].rearrange("p i h -> p (i h)"), start=True, stop=True),
             reads=["lf_all", "onesf"], writes=["ps%d" % Cb])
        Cs = 6
        S.op("pe", lambda e: e.matmul(PS[Cs][:, 0:128], lhsT=trif[:, :], rhs=lfc[:, :, :].rearrange("p i h -> p (i h)"), start=True, stop=True),
             reads=["lfc", "trif"], writes=["ps%d" % Cs])
        S.op("pe", lambda e: e.matmul(PS[Cs][:, 128:256], lhsT=onesf[:, :], rhs=lfc[:, :, :].rearrange("p i h -> p (i h)"), start=True, stop=True),
             reads=["lfc", "onesf"], writes=["ps%d" % Cs])
        S.op("pe", lambda e: e.matmul(PS[Cs][0:16, 256:264], lhsT=msel[:, :], rhs=lf_all[0:32, 16, :], start=True, stop=True),
             reads=["lf_all", "msel"], writes=["ps%d" % Cs])

        chain = []

        def DF(fn, **kw):
            chain.append(lambda: S.op("dve", fn, **kw))
        DF(lambda e: e.memset(carr[:, 0, :], 0.0), writes=["carr"])
        for i in range(1, NTILE):
            DF(lambda e, i=i: e.tensor_tensor(out=carr[:, i, :], in0=carr[:, i - 1, :], in1=PS[Cb][:, 136 + 8 * (i - 1):136 + 8 * i], op=ALU.add),
              reads=["carr", "ps%d" % Cb], writes=["carr"])
        DF(lambda e: e.tensor_tensor(out=c_all[:, :, :], in0=carr[:, :, :], in1=PS[Cb][:, 0:136].rearrange("p (i h) -> p i h", h=8), op=ALU.add),
          reads=["carr", "ps%d" % Cb], writes=["c_all"])
        key = "c_all"
        DF(lambda e: e.tensor_scalar(out=negc[:, :, :], in0=c_all[:, :, :], scalar1=-1.0, scalar2=None, op0=ALU.mult),
          reads=[key], writes=[key + "_neg"])
        DF(lambda e: e.tensor_copy(out=csplit[:, :, 0, :], in_=c_all[:, :, :]), reads=[key], writes=[key + "_s"])
        DF(lambda e: e.tensor_tensor(out=cres[:, :, :], in0=c_all[:, :, :], in1=csplit[:, :, 0, :], op=ALU.subtract),
          reads=[key, key + "_s"], writes=[key + "_r"])
        DF(lambda e: e.tensor_copy(out=csplit[:, :, 1, :], in_=cres[:, :, :]), reads=[key + "_r"], writes=[key + "_s"])
        DF(lambda e: e.tensor_tensor(out=cres[:, :, :], in0=cres[:, :, :], in1=csplit[:, :, 1, :], op=ALU.subtract),
          reads=[key + "_r", key + "_s"], writes=[key + "_r"])
        DF(lambda e: e.tensor_copy(out=csplit[:, :, 2, :], in_=cres[:, :, :]), reads=[key + "_r"], writes=[key + "_s"])
        DF(lambda e: e.memset(carr_s[:, 0, :], 0.0), writes=["carr_s"])
        for i in range(1, 17):
            DF(lambda e, i=i: e.tensor_tensor(out=carr_s[:, i, :], in0=carr_s[:, i - 1, :], in1=PS[Cs][:, 128 + 8 * (i - 1):128 + 8 * i], op=ALU.add),
              reads=["carr_s", "ps%d" % Cs], writes=["carr_s"])
        DF(lambda e: e.tensor_tensor(out=c_s[:, 0:16, :], in0=carr_s[:, 0:16, :], in1=PS[Cs][:, 0:128].rearrange("p (i h) -> p i h", h=8), op=ALU.add),
          reads=["carr_s", "ps%d" % Cs], writes=["c_s"])
        DF(lambda e: e.tensor_tensor(out=c_s[:, 0:16, :], in0=c_s[:, 0:16, :],
                                    in1=carr_s[:, 16, :].unsqueeze(1).to_broadcast([P, 16, 8]), op=ALU.subtract),
          reads=["carr_s", "c_s"], writes=["c_s"])
        DF(lambda e: e.tensor_copy(out=c_s[0:16, 16, :], in_=PS[Cs][0:16, 256:264]), reads=["ps%d" % Cs], writes=["c_s"])
        DF(lambda e: e.tensor_scalar(out=negc_s[:, :, :], in0=c_s[:, :, :], scalar1=-1.0, scalar2=None, op0=ALU.mult),
          reads=["c_s"], writes=["c_s_neg"])

        def run_chain(n):
            for _ in range(n):
                if chain:
                    chain.pop(0)()

        for i in range(NTILE):
            r = tile_rows(i)
            c0 = 128 * i
            jj = i % 2
            A = next_ps(0, 4, "vp")
            for k in range(KD):
                S.op("pe", lambda e, A=A, k=k, r=r, c0=c0, wv=wv: e.matmul(
                    PS[A][0:r, :], lhsT=xnT[:, k, c0:c0 + r], rhs=wv[:, k, :], start=(k == 0), stop=(k == KD - 1)),
                     reads=[wkey, "xnT%d" % i], writes=["ps%d" % A], sig=(k == KD - 1))
            S.op("dve", lambda e, A=A, jj=jj, r=r: e.tensor_copy(out=vtok[jj][0:r, :], in_=PS[A][0:r, :]),
                 reads=["ps%d" % A], writes=["vtok%d" % jj])
            S.op("dve", lambda e, A=A, r=r, i=i: e.tensor_copy(
                out=Vp[0:r, i, :, 0:64], in_=PS[A][0:r, :].rearrange("p (h d) -> p h d", h=H)),
                 reads=["ps%d" % A], writes=["Vp%d" % i])
            if i < 16:
                S.dma("sp", "d_vtok%d" % jj, lambda e, jj=jj, c0=c0: e.dma_start(out=nv_p[c0:c0 + 128, :], in_=vtok[jj][:, :]),
                      reads=["vtok%d" % jj], writes=["nv_p_%d" % i])
            else:
                S.dma("sp", "d_vtok%d" % jj, lambda e, jj=jj: e.dma_start(out=nv_p[2048:2064, :], in_=vtok[jj][0:16, :]),
                      reads=["vtok%d" % jj], writes=["nv_p_%d" % i])
                S.dma("sp", "d_vtok%d" % jj, lambda e, jj=jj: e.dma_start(out=nv_s[:, :], in_=vtok[jj][16:32, :]),
                      reads=["vtok%d" % jj], writes=["nv_s"])
            run_chain(4)
        run_chain(len(chain))
        Vsn = sm((16, H, 66), BF16, "Vsn")
        S.op("pool", lambda e: e.memset(Vsn[:, :, 64:65], 1.0), writes=["Vsn_ones"])
        A = next_ps(0, 4, "vp")
        for k in range(KD):
            S.op("pe", lambda e, A=A, k=k, wv=wv: e.matmul(PS[A][0:16, :], lhsT=xnT[:, k, L:NT], rhs=wv[:, k, :],
                                                          start=(k == 0), stop=(k == KD - 1)),
                 reads=[wkey, "xnT16"], writes=["ps%d" % A], sig=(k == KD - 1))
        S.op("dve", lambda e, A=A: e.tensor_copy(out=Vsn[:, :, 0:64], in_=PS[A][0:16, :].rearrange("p (h d) -> p h d", h=H)),
             reads=["ps%d" % A], writes=["Vsn"])
        w_done(G_V)

        for bnk in range(3):
            Tb = next_ps(4, 8, "ct")
            pst = PS[Tb][:, :].bitcast(BF16)
            tiles = list(range(8 * bnk, min(NTILE, 8 * bnk + 8)))
            for i in tiles:
                r = 128 if i < 16 else 16
                S.op("pe", lambda e, i=i, r=r, pst=pst: e.transpose(
                    out=pst[0:24, 128 * (i % 8):128 * (i % 8) + r], in_=csplit[0:r, i, :, :].rearrange("p j h -> p (j h)"),
                    identity=identb[0:r, 0:r]),
                     reads=["c_all_s", "identb"], writes=["ps%d" % Tb], sig=(i == tiles[-1]))
            w0 = 128 * tiles[0]
            wn = sum(128 if i < 16 else 16 for i in tiles)
            S.op("dve", lambda e, pst=pst, w0=w0, wn=wn: e.tensor_scalar(out=cT24[:, w0:w0 + wn], in0=pst[0:24, 0:wn],
                                                                  scalar1=8.0, scalar2=None, op0=ALU.mult),
                 reads=["ps%d" % Tb], writes=["c_all_T"])

        chk(6)
        S.barrier(engines=("pe", "act", "dve", "sp", "pool"))

        attnT = M.at(R45, (P, 4, NT), BF16, "attnT")
        attn_tok = M.at(R45 + 16640, (P, NTILE, 512), BF16, "attn_tok")
        NPB = 4
        Pb = [M.at(R45 + 34048 + i * 2048, (P, 1024), BF16, "Pb") for i in range(NPB)]
        Qh = [M.at(R45 + i * 4160, (P, NT), BF16, "Qh") for i in range(2)]
        Kh = [M.at(R45 + 8320 + i * 4160, (P, NT), BF16, "Kh") for i in range(2)]
        ncT24 = M.at(AUG + 4160, (24, NT), BF16, "ncT24")
        S.op("dve", lambda e: e.tensor_scalar(out=ncT24[:, 0:L], in0=cT24[:, 0:L], scalar1=-1.0, scalar2=None, op0=ALU.mult),
             reads=["c_all_T"], writes=["ncT24"])
        for i in range(2):
            S.op("pool", lambda e, i=i: e.memset(Qh[i][64:128, :], 0.0), writes=["Qh%d" % i])
            S.op("pool", lambda e, i=i: e.memset(Kh[i][64:128, :], 0.0), writes=["Kh%d" % i])
        S.op("dve", lambda e: e.memset(Qh[0][64:70, :], 1.0), writes=["Qh0"])
        S.dma("sp", "d_qh1", lambda e: e.dma_start(out=Qh[1][67:70, 0:L], in_=Qh[0][67:70, 0:L]), reads=["Qh0"], writes=["Qh1"])
        S.dma("sp", "d_kh0", lambda e: e.dma_start(out=Kh[0][64:67, 0:L], in_=Qh[0][67:70, 0:L]), reads=["Qh0"], writes=["Kh0"])
        S.dma("sp", "d_kh1", lambda e: e.dma_start(out=Kh[1][64:67, 0:L], in_=Qh[0][67:70, 0:L]), reads=["Qh0"], writes=["Kh1"])
        rsum = sm((P, 4), F32, "rsum")
        QG = [(0, 512), (512, 512), (1024, 512), (1536, 512), (2048, 16)]
        units = []
        for h in range(H):
            for gi, (q0, qn) in enumerate(QG):
                kt_last = (q0 + qn - 1) // 128
                kts = list(range(kt_last + 1))
                groups = []
                full = [kt for kt in kts if kt * 128 < q0 and qn == 512]
                rest = [kt for kt in kts if kt not in full]
                for j in range(0, len(full), 2):
                    groups.append(full[j:j + 2])
                if qn < 128:
                    groups.append(rest)
                else:
                    for kt in rest:
                        groups.append([kt])
                for gj, g in enumerate(groups):
                    units.append(dict(h=h, q0=q0, qn=qn, kts=g, first_h=(gi == 0 and gj == 0), first_g=(gj == 0),
                                      last_g=(gj == len(groups) - 1), idx=len(units)))
        ostate = {}

        def emitA(u):
            h, q0, qn, kts = u["h"], u["q0"], u["qn"], u["kts"]
            hp, hoff = h // 2, (h % 2) * 64
            hb = h % 2
            nqb = (qn + 127) // 128
            if u["first_h"]:
                S.dma("sp", "d_qh%d" % hb, lambda e: e.dma_start(out=Qh[hb][0:64, :], in_=qT[hoff:hoff + 64, hp, :]),
                      reads=["qT_all"], writes=["Qh%d" % hb])
                S.dma("sp", "d_kh%d" % hb, lambda e: e.dma_start(out=Kh[hb][0:64, :], in_=kT[hoff:hoff + 64, hp, :]),
                      reads=["kT_all"], writes=["Kh%d" % hb])
                for j3 in range(3):
                    S.dma("sp", "d_qh%d" % hb, lambda e, j3=j3: e.dma_start(
                        out=Qh[hb][64 + j3:65 + j3, 0:L], in_=cT24[8 * j3 + h:8 * j3 + h + 1, 0:L]),
                          reads=["c_all_T"], writes=["Qh%d" % hb])
                    S.dma("sp", "d_kh%d" % hb, lambda e, j3=j3: e.dma_start(
                        out=Kh[hb][67 + j3:68 + j3, 0:L], in_=ncT24[8 * j3 + h:8 * j3 + h + 1, 0:L]),
                          reads=["ncT24"], writes=["Kh%d" % hb])
            if u["first_g"]:
                if qn < 128:
                    ostate[(h, q0)] = (ostate[(h, QG[3][0])][0], 260)
                else:
                    O = next_ps(6, 8, "o")
                    ostate[(h, q0)] = (O, 0)
                    zc = 325 if q0 == QG[3][0] else 65 * nqb
                    S.op("dve", lambda e, O=O, zc=zc: e.memset(PS[O][:, 0:zc], 0.0), writes=["ps%d" % O])
            pp = next_ps(0, 3, "sp2")
            pj = u["idx"] % NPB
            u["pj"] = pj
            u["geo"] = []
            multi = len(kts) > 2
            for hf, kt in enumerate(kts):
                kr = 128 if kt < 16 else 16
                qs = max(q0, kt * 128)
                nn = q0 + qn - qs
                coff = hf * nn if multi else hf * 512
                u["geo"].append((kt, kr, qs, nn, coff))
                diag = (kt * 128 >= q0)
                dst = PS2[pp][0:kr, coff:coff + nn]
                S.op("pe", lambda e, dst=dst, kt=kt, kr=kr, qs=qs, nn=nn, diag=diag: e.matmul(
                    dst, lhsT=Kh[hb][:, kt * 128:kt * 128 + kr], rhs=Qh[hb][:, qs:qs + nn], start=True, stop=not diag),
                     reads=["Kh%d" % hb, "Qh%d" % hb], writes=["ps%d" % (2 * pp), "ps%d" % (2 * pp + 1)], sig=not diag)
                if diag:
                    dn = min(128, nn)
                    S.op("pe", lambda e, kr=kr, dn=dn, coff=coff: e.matmul(
                        PS2[pp][0:kr, coff:coff + dn],
                        lhsT=identb[0:kr, 0:kr], rhs=maskneg[0:kr, 0:dn], start=False, stop=True),
                         reads=["identb", "maskneg"], writes=["ps%d" % (2 * pp), "ps%d" % (2 * pp + 1)])
            if len(kts) == 2 and os.environ.get("PAIR_EXP", "1") == "1":
                S.op("act", lambda e: e.activation(out=Pb[pj][:, :], in_=PS2[pp][:, :], func=AF.Exp, scale=0.125),
                     reads=["ps%d" % (2 * pp), "ps%d" % (2 * pp + 1)], writes=["Pb%d" % pj])
            elif len(kts) == 2:
                for hf in range(2):
                    S.op("act", lambda e, hf=hf: e.activation(out=Pb[pj][:, hf * 512:(hf + 1) * 512],
                                                             in_=(PS2[pp][:, hf * 512:(hf + 1) * 512] if PS2 is not None else PS[2 * pp + hf][:, :]),
                                                             func=AF.Exp, scale=0.125),
                         reads=["ps%d" % (2 * pp), "ps%d" % (2 * pp + 1)], writes=["Pb%d" % pj])
            elif multi:
                wtot = u["geo"][-1][4] + u["geo"][-1][3]
                S.op("act", lambda e: e.activation(out=Pb[pj][:, 0:wtot], in_=PS2[pp][:, 0:wtot], func=AF.Exp, scale=0.125),
                     reads=["ps%d" % (2 * pp), "ps%d" % (2 * pp + 1)], writes=["Pb%d" % pj])
            else:
                kt, kr, qs, nn, coff = u["geo"][0]
                S.op("act", lambda e: e.activation(out=Pb[pj][0:kr, 0:nn], in_=PS2[pp][0:kr, 0:nn],
                                                   func=AF.Exp, scale=0.125),
                     reads=["ps%d" % (2 * pp), "ps%d" % (2 * pp + 1)], writes=["Pb%d" % pj])

        def emitB(u):
            h, q0, qn = u["h"], u["q0"], u["qn"]
            pj = u["pj"]
            nqb = (qn + 127) // 128
            O, oc0 = ostate[(h, q0)]
            nk = len(u["geo"])
            for hf, (kt, kr, qs, nn, coff) in enumerate(u["geo"]):
                for qb in range(nqb):
                    qcol = q0 + qb * 128
                    qr = min(128, q0 + qn - qcol)
                    if qcol + qr - 1 < kt * 128:
                        continue
                    S.op("pe", lambda e, qb=qb, qr=qr, qcol=qcol, coff=coff, kt=kt, kr=kr, qs=qs: e.matmul(
                        PS[O][0:qr, oc0 + 65 * qb:oc0 + 65 * qb + 65], lhsT=Pb[pj][0:kr, coff + qcol - qs:coff + qcol - qs + qr],
                        rhs=Vp[0:kr, kt, h, 0:65], start=False, stop=(kt == qcol // 128), skip_group_check=True),
                         reads=["Pb%d" % pj, "Vp%d" % kt, "Vp_ones"], writes=["ps%d" % O],
                         sig=(qb == nqb - 1 and hf == nk - 1))
            if u["last_g"]:
                for qb in range(nqb):
                    qcol = q0 + qb * 128
                    qr = min(128, q0 + qn - qcol)
                    S.op("dve", lambda e, qb=qb, qr=qr: e.reciprocal(out=rsum[0:qr, qb:qb + 1], in_=PS[O][0:qr, oc0 + 65 * qb + 64:oc0 + 65 * qb + 65]),
                         reads=["ps%d" % O], writes=["rsum%d" % qb])
                    S.op("dve", lambda e, qb=qb, qr=qr, qcol=qcol: e.tensor_scalar(
                        out=attn_tok[0:qr, qcol // 128, h * 64:(h + 1) * 64], in0=PS[O][0:qr, oc0 + 65 * qb:oc0 + 65 * qb + 64],
                        scalar1=rsum[0:qr, qb:qb + 1], scalar2=None, op0=ALU.mult),
                         reads=["ps%d" % O, "rsum%d" % qb], writes=["attn_tok%d" % (qcol // 128)])

        LA = 3
        for i in range(min(LA, len(units))):
            emitA(units[i])
        for i in range(len(units)):
            emitB(units[i])
            if i + LA < len(units):
                emitA(units[i + LA])

        chk(7)
        KC = 256
        kc_tm = [M.at(R3 + i * 2048, (P, 2, 512), BF16, "kc_tm") for i in range(2)]
        vc_tm = [M.at(R3 + 4096 + i * 2048, (P, 2, 512), BF16, "vc_tm") for i in range(2)]
        KcT = [M.at(R3 + 8192 + i * 2048, (P, 2, 4, 128), BF16, "KcT") for i in range(2)]
        Psb = [M.at(R3 + 12288 + i * 512, (P, 2, H, 16), BF16, "Psb") for i in range(2)]
        Vw = [M.at(AUG + 8320 + i * 2112, (P, 2, H, 66), BF16, "Vw") for i in range(2)]
        attn_s = sm((16, 512), BF16, "attn_s")
        rsum_s = sm((16, 8), F32, "rsum_s")
        wts = sm((P, NTILE, 8), F32, "wts")
        Qz = sm((P, H, 16), BF16, "Qz")
        S.op("act", lambda e: e.activation(out=wts[:, :, :].rearrange("p i h -> p (i h)"), in_=negc_s[:, :, :].rearrange("p i h -> p (i h)"), func=AF.Exp),
             reads=["c_s_neg"], writes=["wts"])
        S.op("dve", lambda e: e.memset(Qz[:, :, :], 0.0), writes=["Qz"])
        for h in range(H):
            hp, hoff = h // 2, (h % 2) * 64
            S.op("dve", lambda e, h=h, hp=hp, hoff=hoff: e.tensor_copy(out=Qz[hoff:hoff + 64, h, :], in_=qT[hoff:hoff + 64, hp, L:NT]),
                 reads=["qT_all"], writes=["Qz"])

        def load_cache_chunk(c):
            j = c % 2
            S.dma("pool", "d_kc%d" % j, lambda e, c=c, j=j: e.dma_start(
                out=kc_tm[j][:, :, :], in_=cache_k[KC * c:KC * (c + 1), :].rearrange("(i p) d -> p i d", p=P)),
                  writes=["kc_tm%d" % j])
            S.dma("pool", "d_vc%d" % j, lambda e, c=c, j=j: e.dma_start(
                out=vc_tm[j][:, :, :], in_=cache_v[KC * c:KC * (c + 1), :].rearrange("(i p) d -> p i d", p=P)),
                  writes=["vc_tm%d" % j])
        load_cache_chunk(0)
        load_cache_chunk(1)
        OS = (6, 7)
        for ob in OS:
            S.op("dve", lambda e, ob=ob: e.memset(PS[ob][0:16, 0:260], 0.0), writes=["ps%d" % ob])
        NCH = PAST // KC

        def sA(c):
            j = c % 2
            Tk = next_ps(0, 2, "tk")
            pstk = PS[Tk][:, :].bitcast(BF16)
            for t in range(2):
                for hp in range(4):
                    S.op("pe", lambda e, t=t, hp=hp: e.transpose(
                        out=pstk[:, (t * 4 + hp) * 128:(t * 4 + hp + 1) * 128], in_=kc_tm[j][:, t, hp * 128:(hp + 1) * 128],
                        identity=identb[:, :]),
                         reads=["kc_tm%d" % j, "identb"], writes=["ps%d" % Tk], sig=(t == 1 and hp == 3))
            S.op("act", lambda e: e.activation(out=KcT[j][:, :, :, :].rearrange("p t h k -> p (t h k)"), in_=pstk[:, :], func=AF.Copy),
                 reads=["ps%d" % Tk], writes=["KcT%d" % j])
            for t in range(2):
                kt = 2 * c + t
                S.op("dve", lambda e, t=t, kt=kt: e.tensor_tensor(
                    out=Vw[j][:, t, :, 0:64], in0=vc_tm[j][:, t, :].rearrange("p (h d) -> p h d", h=H),
                    in1=wts[:, kt, :].unsqueeze(2).to_broadcast([P, H, 64]), op=ALU.mult),
                     reads=["vc_tm%d" % j, "wts"], writes=["Vw%d" % j])
                S.op("dve", lambda e, t=t, kt=kt: e.tensor_copy(out=Vw[j][:, t, :, 64], in_=wts[:, kt, :]),
                     reads=["wts"], writes=["Vw%d" % j])
            Sb = next_ps(2, 4, "sb")
            for t in range(2):
                for h in range(H):
                    hp = h // 2
                    col = (t * H + h) * 16
                    S.op("pe", lambda e, col=col, t=t, hp=hp, h=h: e.matmul(
                        PS[Sb][:, col:col + 16], lhsT=KcT[j][:, t, hp, :], rhs=Qz[:, h, :], start=True, stop=True),
                         reads=["KcT%d" % j, "Qz"], writes=["ps%d" % Sb], sig=(t == 1 and h == H - 1))
            S.op("act", lambda e: e.activation(out=Psb[j][:, :, :, :].rearrange("p t h q -> p (t h q)"), in_=PS[Sb][:, 0:256],
                                               func=AF.Exp, scale=0.125),
                 reads=["ps%d" % Sb], writes=["Psb%d" % j])

        def sB(c):
            j = c % 2
            for t in range(2):
                for h in range(H):
                    ob = OS[h // 4]
                    oc = (h % 4) * 65
                    S.op("pe", lambda e, ob=ob, oc=oc, t=t, h=h: e.matmul(
                        PS[ob][0:16, oc:oc + 65], lhsT=Psb[j][:, t, h, :], rhs=Vw[j][:, t, h, 0:65],
                        start=False, stop=False, skip_group_check=True),
                         reads=["Psb%d" % j, "Vw%d" % j], writes=["ps%d" % ob], sig=(t == 1 and h == H - 1))
            if c + 2 < NCH:
                load_cache_chunk(c + 2)
        sA(0)
        for c in range(NCH):
            if c + 1 < NCH:
                sA(c + 1)
            sB(c)
        Sb = next_ps(2, 4, "sb")
        Pn = sm((16, H, 16), BF16, "Pn")
        Vsw = sm((16, H, 66), BF16, "Vsw")
        for h in range(H):
            hp = h // 2
            S.op("pe", lambda e, h=h, hp=hp: e.matmul(
                PS[Sb][0:16, 16 * h:16 * h + 16], lhsT=kT[:, hp, L:NT], rhs=Qz[:, h, :], start=True, stop=False),
                 reads=["Qz"], writes=["ps%d" % Sb], sig=False)
            S.op("pe", lambda e, h=h: e.matmul(
                PS[Sb][0:16, 16 * h:16 * h + 16], lhsT=identb[0:16, 0:16], rhs=maskneg[0:16, 0:16], start=False, stop=True),
                 reads=["identb", "maskneg"], writes=["ps%d" % Sb], sig=(h == H - 1))
        S.op("act", lambda e: e.activation(out=Pn[:, :, :].rearrange("p h q -> p (h q)"), in_=PS[Sb][0:16, 0:128], func=AF.Exp, scale=0.125),
             reads=["ps%d" % Sb], writes=["Pn"])
        S.op("dve", lambda e: e.tensor_tensor(out=Vsw[:, :, 0:65], in0=Vsn[:, :, 0:65],
                                              in1=wts[0:16, 16, :].unsqueeze(2).to_broadcast([16, H, 65]), op=ALU.mult),
             reads=["Vsn", "Vsn_ones", "wts"], writes=["Vsw"])
        for h in range(H):
            ob = OS[h // 4]
            oc = (h % 4) * 65
            S.op("pe", lambda e, ob=ob, oc=oc, h=h: e.matmul(
                PS[ob][0:16, oc:oc + 65], lhsT=Pn[:, h, :], rhs=Vsw[:, h, 0:65], start=False, stop=True, skip_group_check=True),
                 reads=["Pn", "Vsw"], writes=["ps%d" % ob], sig=(h % 4 == 3))
        for h in range(H):
            ob = OS[h // 4]
            oc = (h % 4) * 65
            S.op("dve", lambda e, ob=ob, oc=oc, h=h: e.reciprocal(out=rsum_s[:, h:h + 1], in_=PS[ob][0:16, oc + 64:oc + 65]),
                 reads=["ps%d" % ob], writes=["rsum_s"])
            S.op("dve", lambda e, ob=ob, oc=oc, h=h: e.tensor_scalar(
                out=attn_s[:, h * 64:(h + 1) * 64], in0=PS[ob][0:16, oc:oc + 64], scalar1=rsum_s[:, h:h + 1], scalar2=None, op0=ALU.mult),
                 reads=["ps%d" % ob, "rsum_s"], writes=["attn_s"])

        for i in range(NTILE):
            r = 128 if i < 16 else 16
            Tb = next_ps(0, 4, "ta")
            pst = PS[Tb][:, :].bitcast(BF16)
            for hp in range(4):
                S.op("pe", lambda e, i=i, r=r, hp=hp, pst=pst: e.transpose(
                    out=pst[:, hp * 128:hp * 128 + r], in_=attn_tok[0:r, i, hp * 128:(hp + 1) * 128], identity=identb[0:r, 0:r]),
                     reads=["attn_tok%d" % i, "identb"], writes=["ps%d" % Tb], sig=(hp == 3))
            S.op("dve", lambda e, i=i, r=r, pst=pst: e.tensor_copy(
                out=attnT[:, :, 128 * i:128 * i + r], in_=pst[:, 0:512].rearrange("p (h t) -> p h t", h=4)[:, :, 0:r]),
                 reads=["ps%d" % Tb], writes=["attnT%d" % i])
        Tb = next_ps(0, 4, "ta")
        pst = PS[Tb][:, :].bitcast(BF16)
        for hp in range(4):
            S.op("pe", lambda e, hp=hp, pst=pst: e.transpose(
                out=pst[:, hp * 128:hp * 128 + 16], in_=attn_s[0:16, hp * 128:(hp + 1) * 128], identity=identb[0:16, 0:16]),
                 reads=["attn_s", "identb"], writes=["ps%d" % Tb], sig=(hp == 3))
        S.op("dve", lambda e, pst=pst: e.tensor_copy(
            out=attnT[:, :, L:NT], in_=pst[:, 0:512].rearrange("p (h t) -> p h t", h=4)[:, :, 0:16]),
             reads=["ps%d" % Tb], writes=["attnT17"])
        if "attnT" in dbg_out:
            S.dma("sp", "d_dbg", lambda e: e.dma_start(out=dbg_out["attnT"][:, :, :], in_=attnT[:, :, :]),
                  reads=["attnT%d" % i for i in range(18)])

        chk(8)
        S.barrier()
        ZW = 2088
        zbuf = [M.at(R3 + i * (ZW * 4), (P, ZW), F32, "zbuf") for i in range(2)]
        convT = M.at(R3 + 16896, (P, 4, NT), BF16, "convT")
        tmpC = [M.at(R45 + 16640 + i * 2048, (P, 512), F32, "tmpC") for i in range(2)]
        ytmp = [M.at(R45 + 20736 + i * 2048, (P, 512), F32, "ytmp") for i in range(2)]
        tmpB = [M.at(AUG + 4160 + i * 2048, (P, 512), F32, "tmpB") for i in range(2)]
        stc_sb = sm((P, 8), F32, "stc_sb")
        zlast = sm((P, 4, 4), F32, "zlast")
        S.dma("sp", "d_stc", lambda e: e.dma_start(out=stc_sb[:, :], in_=stc[:, :]), writes=["stc_sb"])
        cc3 = {"n": 0}
        for c in range(4):
            wX, kX = w_get(G_CV[c])
            zb = zbuf[c % 2]
            zk = "zbuf%d" % (c % 2)
            S.op("dve", lambda e, zb=zb: e.memset(zb[:, 0:2], 0.0), writes=[zk])
            S.op("dve", lambda e, zb=zb, c=c: e.tensor_copy(out=zb[:, 2066:2068], in_=stc_sb[:, 2 * c:2 * c + 2]), reads=["stc_sb"], writes=[zk])
            for (c0, n) in BLKS_EQ:
                jj = cc3["n"] % 2
                cc3["n"] += 1
                banks = []
                for j3 in range(3):
                    A = next_ps()
                    banks.append(A)
                    for k in range(KD):
                        S.op("pe", lambda e, A=A, k=k, j3=j3, c0=c0, n=n, wX=wX: e.matmul(
                            PS[A][:, 0:n], lhsT=wX[:, k, 128 * j3:128 * (j3 + 1)], rhs=xnT[:, k, c0:c0 + n],
                            start=(k == 0), stop=(k == KD - 1)), reads=[kX], writes=["ps%d" % A], sig=(k == KD - 1))
                bB, bC, bH = banks
                S.op("act", lambda e, bC=bC, jj=jj, n=n: e.activation(out=tmpC[jj][:, 0:n], in_=PS[bC][:, 0:n], func=AF.Copy),
                     reads=["ps%d" % bC], writes=["tmpC%d" % jj])
                S.op("act", lambda e, bB=bB, jj=jj, n=n: e.activation(out=tmpB[jj][:, 0:n], in_=PS[bB][:, 0:n], func=AF.Copy),
                     reads=["ps%d" % bB], writes=["tmpB%d" % jj])
                segs = []
                if c0 < L:
                    segs.append((0, min(c0 + n, L) - c0, c0 + 2))
                if c0 + n > L:
                    s0 = max(c0, L)
                    segs.append((s0 - c0, c0 + n - s0, s0 + 4))
                for (so, sn, zc) in segs:
                    S.op("dve", lambda e, bH=bH, jj=jj, so=so, sn=sn, zc=zc, zb=zb: e.tensor_tensor(
                        out=zb[:, zc:zc + sn], in0=tmpC[jj][:, so:so + sn], in1=PS[bH][:, so:so + sn], op=ALU.mult),
                         reads=["tmpC%d" % jj, "ps%d" % bH], writes=[zk])
                for (so, sn, zc) in segs:
                    S.op("dve", lambda e, jj=jj, so=so, sn=sn, zc=zc, zb=zb, c=c: e.tensor_scalar(
                        out=ytmp[jj][:, so:so + sn], in0=zb[:, zc:zc + sn], scalar1=pk[:, 24 + 3 * c + 2:24 + 3 * c + 3],
                        scalar2=pk[:, 36 + c:37 + c], op0=ALU.mult, op1=ALU.add),
                         reads=[zk, "pk"], writes=["ytmp%d" % jj])
                    for tap, sh in ((1, 1), (0, 2)):
                        S.op("dve", lambda e, jj=jj, so=so, sn=sn, zc=zc, zb=zb, c=c, tap=tap, sh=sh: e.scalar_tensor_tensor(
                            out=ytmp[jj][:, so:so + sn], in0=zb[:, zc - sh:zc - sh + sn], scalar=pk[:, 24 + 3 * c + tap:24 + 3 * c + tap + 1],
                            in1=ytmp[jj][:, so:so + sn], op0=ALU.mult, op1=ALU.add),
                             reads=[zk, "pk", "ytmp%d" % jj], writes=["ytmp%d" % jj])
                S.op("dve", lambda e, jj=jj, n=n, c=c, c0=c0: e.tensor_tensor(
                    out=convT[:, c, c0:c0 + n], in0=tmpB[jj][:, 0:n], in1=ytmp[jj][:, 0:n], op=ALU.mult),
                     reads=["tmpB%d" % jj, "ytmp%d" % jj], writes=["convT"])
            S.op("dve", lambda e, zb=zb, c=c: e.tensor_copy(out=zlast[:, c, 0:2], in_=zb[:, 2064:2066]), reads=[zk], writes=["zlast"])
            S.op("dve", lambda e, zb=zb, c=c: e.tensor_copy(out=zlast[:, c, 2:4], in_=zb[:, 2082:2084]), reads=[zk], writes=["zlast"])
            w_done(G_CV[c])
        with nc.allow_non_contiguous_dma(reason="tiny transposed conv-state rows"):
            for c in range(4):
                S.dma("sp", "d_ncp", lambda e, c=c: e.dma_start(
                    out=nc_p[:, 128 * c:128 * (c + 1)].rearrange("r p -> p r"), in_=zlast[:, c, 0:2], allow_slow_non_contiguous=True),
                      reads=["zlast"], writes=["nc_p%d" % c])
                S.dma("sp", "d_ncs", lambda e, c=c: e.dma_start(
                    out=nc_s[:, 128 * c:128 * (c + 1)].rearrange("r p -> p r"), in_=zlast[:, c, 2:4], allow_slow_non_contiguous=True),
                      reads=["zlast"], writes=["nc_s%d" % c])

        chk(9)
        mergedT = M.at(R2, (P, KD, NT), BF16, "mergedT")
        gt = [[M.at(R45 + 24832 + (i * 4 + q_) * 2048, (P, 512), F32, "gt") for q_ in range(4)] for i in range(2)]
        g4 = {"n": 0}
        wbrc, kbrc = w_get(G_BRC)
        wbra, kbra = w_get(G_BRA)
        for pr in range(4):
            wgp, kgp = w_get(G_GP[pr])
            for jj2 in range(2):
                j = 2 * pr + jj2
                for (c0, n) in BLKS_EQ:
                    si = g4["n"] % 2
                    g4["n"] += 1
                    sA, sB, t1, t2 = gt[si]
                    ba = next_ps(); bb = next_ps(); bc = next_ps(); bd = next_ps()
                    for (bank, wv, wk, src, nk, col0) in ((ba, wgp, kgp, xnT, KD, jj2 * 128), (bb, wgp, kgp, xnT, KD, 256 + jj2 * 128),
                                                        (bc, wbrc, kbrc, convT, 4, j * 128), (bd, wbra, kbra, attnT, 4, j * 128)):
                        for k in range(nk):
                            S.op("pe", lambda e, bank=bank, wv=wv, src=src, k=k, nk=nk, col0=col0, c0=c0, n=n: e.matmul(
                                PS[bank][:, 0:n], lhsT=wv[:, k, col0:col0 + 128], rhs=src[:, k, c0:c0 + n],
                                start=(k == 0), stop=(k == nk - 1)),
                                 reads=[wk, "convT"] + ["attnT%d" % i for i in range(18)], writes=["ps%d" % bank], sig=(k == nk - 1))
                    S.op("act", lambda e, ba=ba, sA=sA, n=n: e.activation(out=sA[:, 0:n], in_=PS[ba][:, 0:n], func=AF.Sigmoid),
                         reads=["ps%d" % ba], writes=["gtA%d" % si])
                    S.op("act", lambda e, bb=bb, sB=sB, n=n: e.activation(out=sB[:, 0:n], in_=PS[bb][:, 0:n], func=AF.Sigmoid),
                         reads=["ps%d" % bb], writes=["gtB%d" % si])
                    S.op("dve", lambda e, bc=bc, sA=sA, t1=t1, n=n: e.tensor_tensor(out=t1[:, 0:n], in0=sA[:, 0:n], in1=PS[bc][:, 0:n], op=ALU.mult),
                         reads=["ps%d" % bc, "gtA%d" % si], writes=["gt1%d" % si])
                    S.op("dve", lambda e, bd=bd, sB=sB, t2=t2, n=n: e.tensor_tensor(out=t2[:, 0:n], in0=sB[:, 0:n], in1=PS[bd][:, 0:n], op=ALU.mult),
                         reads=["ps%d" % bd, "gtB%d" % si], writes=["gt2%d" % si])
                    S.op("dve", lambda e, t1=t1, t2=t2, j=j, c0=c0, n=n: e.tensor_tensor(out=mergedT[:, j, c0:c0 + n], in0=t1[:, 0:n], in1=t2[:, 0:n], op=ALU.add),
                         reads=["gt1%d" % si, "gt2%d" % si], writes=["mergedT"])
            w_done(G_GP[pr])
        w_done(G_BRC); w_done(G_BRA)

        chk(10)
        S.barrier()
        yacc = M.at(R1, (P, 16, D), F32, "yacc")
        yacc16 = M.at(R45 + 33280, (P, D), F32, "yacc16")
        hnT = M.at(R45, (P, KD, NT), BF16, "hnT")
        hsb = [M.at(AUG + i * 2048, (P, D), BF16, "hs") for i in range(3)]
        junk2 = M.at(R45 + 41472, (P, D), BF16, "junk2")
        wo0, ko0 = w_get(G_OUT0)
        wo1, ko1 = w_get(G_OUT0 + 1)

        def ytile(i):
            return yacc[:, i, :] if i < 16 else yacc16[:, :]

        def s5A(i):
            r = tile_rows(i)
            c0 = 128 * i
            b = i % 3
            yt = ytile(i)
            for half, (wo, ko) in enumerate(((wo0, ko0), (wo1, ko1))):
                A = next_ps(0, 6, "w5")
                for k in range(KD):
                    S.op("pe", lambda e, A=A, k=k, wo=wo: e.matmul(
                        PS[A][0:r, :], lhsT=mergedT[:, k, c0:c0 + r], rhs=wo[:, k, :], start=(k == 0), stop=(k == KD - 1)),
                         reads=[ko, "mergedT"], writes=["ps%d" % A], sig=(k == KD - 1))
                S.op("dve", lambda e, A=A, half=half: e.tensor_tensor(
                    out=yt[0:r, half * 512:(half + 1) * 512], in0=yt[0:r, half * 512:(half + 1) * 512], in1=PS[A][0:r, :], op=ALU.add),
                     reads=["ps%d" % A, "yacc%d" % i], writes=["yacc%d" % i])
            rms_rstd(yt[0:r, :], r, i, 1.0 / D, "yacc%d" % i, "b", junk=junk2, jkey="junk2")

        def s5A2(i):
            r = tile_rows(i)
            b = i % 3
            yt = ytile(i)
            S.op("dve", lambda e: e.tensor_scalar(out=hsb[b][0:r, :], in0=yt[0:r, :], scalar1=rstd[0:r, i:i + 1],
                                                  scalar2=None, op0=ALU.mult),
                 reads=["yacc%d" % i, "rstdb%d" % i], writes=["hs%d" % b])

        def s5B(i):
            r = tile_rows(i)
            c0 = 128 * i
            b = i % 3
            Tb = next_ps(6, 8, "t5")
            pst = PS[Tb][:, :].bitcast(BF16)
            for k in range(KD):
                S.op("pe", lambda e, k=k: e.transpose(out=pst[:, k * 128:k * 128 + r], in_=hsb[b][0:r, k * 128:(k + 1) * 128],
                                                     identity=identb[0:r, 0:r]),
                     reads=["hs%d" % b, "identb"], writes=["ps%d" % Tb], sig=(k == KD - 1))
            S.op("dve", lambda e: e.tensor_tensor(
                out=hnT[:, :, c0:c0 + r], in0=pst.rearrange("p (k t) -> p k t", k=KD)[:, :, 0:r],
                in1=pk[:, 8:16].unsqueeze(2).to_broadcast([P, KD, r]), op=ALU.mult),
                 reads=["ps%d" % Tb, "pk"], writes=["hnT"])
        for i in range(NTILE):
            load_xtile(i, ytile(i), "yacc%d" % i, "d_xr%d" % i)
        s5A(0)
        s5A(1)
        s5A2(0)
        for i in range(NTILE):
            if i + 2 < NTILE:
                s5A(i + 2)
            if i + 1 < NTILE:
                s5A2(i + 1)
            s5B(i)
        w_done(G_OUT0); w_done(G_OUT0 + 1)

        chk(11)
        S.barrier()
        aTb = [M.at(R2 + i * 16640, (P, 4, NT), BF16, "aT") for i in range(2)]
        rtmp = [M.at(R45 + 37376 + i * 2048, (P, 512), F32, "rtmp") for i in range(2)]
        r6 = {"n": 0}
        for g in range(8):
            wu, ku = w_get(G_MLP + 2 * g)
            wd, kd = w_get(G_MLP + 2 * g + 1)
            aT = aTb[g % 2]
            ak = "aT%d" % (g % 2)
            for fc in range(4):
                for (c0, n) in BLKS_EQ:
                    A = next_ps()
                    for k in range(KD):
                        S.op("pe", lambda e, A=A, k=k, fc=fc, c0=c0, n=n, wu=wu: e.matmul(
                            PS[A][:, 0:n], lhsT=wu[:, k, fc * 128:(fc + 1) * 128], rhs=hnT[:, k, c0:c0 + n],
                            start=(k == 0), stop=(k == KD - 1)), reads=[ku, "hnT"], writes=["ps%d" % A], sig=(k == KD - 1))
                    rj = r6["n"] % 2
                    r6["n"] += 1
                    S.op("act", lambda e, A=A, n=n, rj=rj: e.activation(out=rtmp[rj][:, 0:n], in_=PS[A][:, 0:n], func=AF.Relu),
                         reads=["ps%d" % A], writes=["rtmp%d" % rj])
                    S.op("dve", lambda e, fc=fc, c0=c0, n=n, aT=aT, rj=rj: e.tensor_tensor(
                        out=aT[:, fc, c0:c0 + n], in0=rtmp[rj][:, 0:n], in1=rtmp[rj][:, 0:n], op=ALU.mult),
                         reads=["rtmp%d" % rj], writes=[ak])
            for i in range(NTILE):
                r = tile_rows(i)
                c0 = 128 * i
                yt = ytile(i)
                for half in range(2):
                    A = next_ps()
                    for fc in range(4):
                        S.op("pe", lambda e, A=A, fc=fc, r=r, c0=c0, half=half, aT=aT, wd=wd: e.matmul(
                            PS[A][0:r, :], lhsT=aT[:, fc, c0:c0 + r], rhs=wd[:, fc, half * 512:(half + 1) * 512],
                            start=(fc == 0), stop=(fc == 3)), reads=[kd, ak], writes=["ps%d" % A], sig=(fc == 3))
                    S.op("dve", lambda e, A=A, r=r, yt=yt, half=half: e.tensor_tensor(
                        out=yt[0:r, half * 512:(half + 1) * 512], in0=yt[0:r, half * 512:(half + 1) * 512], in1=PS[A][0:r, :], op=ALU.add),
                         reads=["ps%d" % A, "yacc%d" % i], writes=["yacc%d" % i])
                if g == 7:
                    if i == 0:
                        S.dma("sp", "d_y", lambda e: e.dma_start(out=y_prompt[0:112, :], in_=yacc[16:128, 0, :]), reads=["yacc0"], writes=["y_prompt0"])
                    elif i < 16:
                        S.dma("sp", "d_y", lambda e, i=i: e.dma_start(out=y_prompt[128 * i - 16:128 * i + 112, :], in_=yacc[:, i, :]),
                              reads=["yacc%d" % i], writes=["y_prompt%d" % i])
                    else:
                        S.dma("sp", "d_y", lambda e: e.dma_start(out=y_prompt[2032:2048, :], in_=yacc16[0:16, :]), reads=["yacc16"], writes=["y_prompt16"])
                        S.dma("sp", "d_y", lambda e: e.dma_start(out=y_sample[:, :], in_=yacc16[16:32, :]), reads=["yacc16"], writes=["y_sample"])
            w_done(G_MLP + 2 * g); w_done(G_MLP + 2 * g + 1)

        if "attn_tok" in dbg_out:
            S.dma("sp", "d_dbg", lambda e: e.dma_start(out=dbg_out["attn_tok"][:, :, :], in_=attn_tok[:, :, :]),
                  reads=["attn_tok%d" % i for i in range(NTILE)])
        if "cT24" in dbg_out:
            S.dma("sp", "d_dbg", lambda e: e.dma_start(out=dbg_out["cT24"][:, :], in_=cT24[:, :]), reads=["c_all_T"])
        if "negc" in dbg_out:
            S.dma("sp", "d_dbg", lambda e: e.dma_start(out=dbg_out["negc"][:, :, :], in_=negc[:, :, :]), reads=["c_all_neg"])
        if "qT" in dbg_out:
            S.dma("sp", "d_dbg", lambda e: e.dma_start(out=dbg_out["qT"][:, :, :], in_=qT[:, :, :]), reads=["qT_all"])
        if "c_all" in dbg_out:
            S.dma("sp", "d_dbg", lambda e: e.dma_start(out=dbg_out["c_all"][:, :, :], in_=c_all[:, :, :]), reads=["c_all"])


    except _Stop:
        pass

    S.finish("sp")
    S.emit()
    return nc


def make_in_maps(inputs):
    f = lambda a: np.ascontiguousarray(np.asarray(a, dtype=np.float32))
    x_prompt = f(inputs["x_prompt"]); x_sample = f(inputs["x_sample"])
    ck = f(inputs["cache_k"])[0]; cv = f(inputs["cache_v"])[0]; cl = f(inputs["cache_logf"])[0]
    sc = f(inputs["state_conv"])[0]
    pk = np.zeros((P, 64), np.float32)
    pk[:, 0:8] = f(inputs["norm1_g"])[0].reshape(8, P).T
    pk[:, 8:16] = f(inputs["norm2_g"])[0].reshape(8, P).T
    pk[:, 16:24] = np.broadcast_to(f(inputs["b_f"])[0][None, :], (P, 8))
    cw = f(inputs["conv_w"])[0]
    pk[:, 24:36] = cw.reshape(3, 4, P).transpose(2, 1, 0).reshape(P, 12)
    pk[:, 36:40] = f(inputs["conv_b"])[0].reshape(4, P).T
    pk[:, 40] = np.tile(f(inputs["q_norm_g"])[0], 2)
    pk[:, 41] = np.tile(f(inputs["k_norm_g"])[0], 2)
    maps = []
    for b in range(8):
        stt = np.ascontiguousarray(sc[b].reshape(2, 4, P).transpose(2, 1, 0).reshape(P, 8))
        maps.append({
            "x_prompt": x_prompt[b], "x_sample": x_sample[b],
            "cache_k": ck[b].reshape(PAST, 512), "cache_v": cv[b].reshape(PAST, 512),
            "cache_logf": cl[b], "meta": f(inputs["meta"]),
            "w_in": f(inputs["w_in"])[0], "w_br_conv": f(inputs["w_br_conv"])[0],
            "w_br_attn": f(inputs["w_br_attn"])[0], "w_out": f(inputs["w_out"])[0],
            "w_up": f(inputs["w_up"])[0], "w_down": f(inputs["w_down"])[0],
            "ppk": pk, "state_conv_t": stt,
        })
    return maps


_NC_CACHE = {}


def kernel(**inputs):
    maps = make_in_maps(inputs)
    if "nc" not in _NC_CACHE:
        _NC_CACHE["nc"] = build()
    nc = _NC_CACHE["nc"]
    res = run_bass_kernel_spmd(nc, maps, core_ids=list(range(8)))
    R = res.results
    st = lambda name: np.stack([np.asarray(R[b][name], dtype=np.float32) for b in range(8)])
    y_prompt = st("y_prompt")
    y_sample = st("y_sample")
    nk_p = st("nk_p").reshape(1, 8, L, H, HD)
    nv_p = st("nv_p").reshape(1, 8, L, H, HD)
    nf_p = st("nf_p").reshape(1, 8, L, H)
    nc_p = st("nc_p").reshape(1, 8, 2, DC)
    nk_s = st("nk_s").reshape(1, 8, NS, H, HD)
    nv_s = st("nv_s").reshape(1, 8, NS, H, HD)
    nf_s = st("nf_s").reshape(1, 8, NS, H)
    nc_s = st("nc_s").reshape(1, 8, 2, DC)
    return (y_prompt, y_sample, nk_p, nv_p, nf_p, nc_p, nk_s, nv_s, nf_s, nc_s)
```

```python
import os
import numpy as np
import concourse.bass as bass
import concourse.mybir as mybir
from concourse.bass_utils import run_bass_kernel_spmd

F32 = mybir.dt.float32
BF16 = mybir.dt.bfloat16
AF = mybir.ActivationFunctionType
ALU = mybir.AluOpType
AX = mybir.AxisListType

P = 128
D = 1024
KD = 8
SEQ = 2048
NMETA = 16
L = SEQ + NMETA
NS = 16
NT = L + NS
NTILE = 17
PAST = 2048
H = 8
HD = 64
DC = 512
DA = 512
DFF = 4096
INC = 5128
EPS = 1e-6
C_B, C_C, C_H, C_Q, C_K, C_V, C_F, C_G = 0, 512, 1024, 1536, 2048, 2560, 3072, 3080
BLKS = [(0, 512), (512, 512), (1024, 512), (1536, 512), (2048, 32)]
BLKS_EQ = [(416 * i, 416) for i in range(5)]


def tile_rows(i):
    return 128 if i < 16 else 32


class Sched:
    ENG = ("pe", "act", "dve", "pool", "sp")

    def __init__(self, nc):
        self.nc = nc
        self.q = {e: [] for e in self.ENG}
        self.cnt = {}
        self.sems = {}
        self.lastw = {}
        self.lastr = {}
        self.seen = {e: {} for e in self.ENG}
        self.pending = {e: {} for e in self.ENG}
        for e in ("pe", "act", "dve", "pool"):
            self._sem(e)

    def _sem(self, name):
        if name not in self.sems:
            self.sems[name] = self.nc.alloc_semaphore("s_" + name)
            self.cnt[name] = 0
        return self.sems[name]

    def _deps(self, eng, reads, writes):
        deps = dict(self.pending[eng])
        self.pending[eng] = {}

        def merge(src, raw):
            for s, v in src.items():
                if s == eng and eng == "pe":
                    continue
                if deps.get(s, 0) < v:
                    deps[s] = v
        for k in reads:
            merge(self.lastw.get(k, {}), True)
        for k in writes:
            merge(self.lastw.get(k, {}), False)
            merge(self.lastr.get(k, {}), False)
        out = []
        seen = self.seen[eng]
        for s, v in deps.items():
            if seen.get(s, 0) < v:
                seen[s] = v
                out.append((s, v))
        return out

    def _record(self, s, v, reads, writes):
        for k in reads:
            d = self.lastr.setdefault(k, {})
            if d.get(s, 0) < v:
                d[s] = v
        for k in writes:
            d = self.lastw.setdefault(k, {})
            if d.get(s, 0) < v:
                d[s] = v

    def op(self, eng, fn, reads=(), writes=(), sig=True):
        waits = self._deps(eng, reads, writes)
        if sig:
            self.cnt[eng] += 1
            v = self.cnt[eng]
            inc = (eng, 1)
        else:
            v = self.cnt[eng] + 1
            inc = None
        self._record(eng, v, reads, writes)
        self.q[eng].append((waits, fn, inc))

    def dma(self, queue, sem, fn, reads=(), writes=()):
        self._sem(sem)
        waits = self._deps(queue, reads, writes)
        self.cnt[sem] += 16
        self._record(sem, self.cnt[sem], reads, writes)
        self.q[queue].append((waits, fn, (sem, 16)))

    def barrier(self, engines=("pe", "act", "dve", "sp"), exclude=()):
        snap = {s: v for s, v in self.cnt.items() if v > 0 and s not in exclude
                and not s.startswith(("d_ring", "d_kc", "d_vc", "d_wfl"))}
        for e in engines:
            for s, v in snap.items():
                if s == e and e == "pe":
                    continue
                if self.pending[e].get(s, 0) < v:
                    self.pending[e][s] = v

    def finish(self, eng="sp"):
        waits = []
        for s, v in self.cnt.items():
            if v > 0 and self.seen[eng].get(s, 0) < v:
                waits.append((s, v))
        self.q[eng].append((waits, None, None))

    def replay(self, name, e):
        for waits, fn, inc in self.q[name]:
            for s, v in waits:
                e.wait_ge(self.sems[s], v)
            if fn is None:
                continue
            ins = fn(e)
            if inc is not None:
                ins.then_inc(self.sems[inc[0]], inc[1])

    def emit(self):
        nc = self.nc
        with nc.Block() as block:
            @block.tensor
            def _(e):
                self.replay("pe", e)

            @block.scalar
            def _(e):
                self.replay("act", e)

            @block.vector
            def _(e):
                self.replay("dve", e)

            @block.gpsimd
            def _(e):
                self.replay("pool", e)

            @block.sync
            def _(e):
                self.replay("sp", e)


class Mem:
    def __init__(self, nc):
        self.nc = nc
        self.base = 16512
        self.top = 229344
        self.n = 0

    def at(self, off, shape, dtype, name):
        self.n += 1
        nb = int(np.prod(shape[1:])) * (4 if dtype == F32 else 2)
        assert off % 32 == 0, (name, off)
        assert self.base <= off and off + nb <= self.top, (name, off, nb, self.top)
        return self.nc.alloc_sbuf_tensor_at("%s_%d" % (name, self.n), list(shape), dtype, offset=off)


def build(dbg=None, stop_after=99):
    dbg = dbg or []
    nc = bass.Bass("TRN2", target_bir_lowering=False)
    S = Sched(nc)
    M = Mem(nc)

    def din(name, shape):
        return nc.dram_tensor(name, list(shape), F32, kind="ExternalInput")

    def dout(name, shape):
        return nc.dram_tensor(name, list(shape), F32, kind="ExternalOutput")

    x_prompt = din("x_prompt", (SEQ, D))
    x_sample = din("x_sample", (NS, D))
    cache_k = din("cache_k", (PAST, 512))
    cache_v = din("cache_v", (PAST, 512))
    cache_logf = din("cache_logf", (PAST, H))
    meta = din("meta", (NMETA, D))
    w_in = din("w_in", (D, INC))
    w_br_conv = din("w_br_conv", (DC, D))
    w_br_attn = din("w_br_attn", (DA, D))
    w_out = din("w_out", (D, D))
    w_up = din("w_up", (D, DFF))
    w_down = din("w_down", (DFF, D))
    NPK = 64
    ppk = din("ppk", (P, NPK))
    stc = din("state_conv_t", (P, 8))

    y_prompt = dout("y_prompt", (SEQ, D))
    y_sample = dout("y_sample", (NS, D))
    nk_p = dout("nk_p", (L, 512))
    nv_p = dout("nv_p", (L, 512))
    nf_p = dout("nf_p", (L, H))
    nc_p = dout("nc_p", (2, DC))
    nk_s = dout("nk_s", (NS, 512))
    nv_s = dout("nv_s", (NS, 512))
    nf_s = dout("nf_s", (NS, H))
    nc_s = dout("nc_s", (2, DC))
    dbg_out = {}
    for (name, shape, dt_) in dbg:
        dbg_out[name] = nc.dram_tensor("dbg_" + name, list(shape), dt_, kind="ExternalOutput")

    if os.environ.get("PAIR_EXP", "1") == "1":
        PS2 = [nc.alloc_psum_tensor("ps2_%d" % i, [P, 1024], F32) for i in range(4)]
        PS = [PS2[i // 2][:, (i % 2) * 512:(i % 2 + 1) * 512] for i in range(8)]
    else:
        PS2 = None
        PS = [nc.alloc_psum_tensor("ps%d" % i, [P, 512], F32) for i in range(8)]

    o = M.base
    RING_SLOTS = 5
    ring = [M.at(o + i * 8192, (P, 4096), BF16, "ring") for i in range(RING_SLOTS)]
    o += RING_SLOTS * 8192
    pk = M.at(o, (P, NPK), F32, "pk"); o += NPK * 4
    identb = M.at(o, (P, P), BF16, "identb"); o += 256
    identf = M.at(o, (P, P), F32, "identf"); o += 512
    small = [o]
    o += 13312
    def sm(shape, dtype, name):
        nb = int(np.prod(shape[1:])) * (4 if dtype == F32 else 2)
        nb = (nb + 31) // 32 * 32
        t = M.at(small[0], shape, dtype, name)
        small[0] += nb
        assert small[0] <= o_small_end
        return t
    o_small_end = o
    AUG = o; o += 12544
    R2 = o; o += 33280
    R1 = o; o += 33280
    R3 = o; o += 34304
    R45 = o
    R45_SIZE = M.top - o
    assert R45_SIZE >= 41472, R45_SIZE

    xnT = M.at(R1, (P, KD, NT), BF16, "xnT")
    NXT = 6
    xt = [M.at(R3 + i * 4096, (P, D), F32, "xt") for i in range(2)] + \
         [M.at(R45 + 25600 + i * 4096, (P, D), F32, "xt") for i in range(4)]
    xs = [M.at(R3 + 8192 + i * 2048, (P, D), BF16, "xs") for i in range(2)] + [M.at(R45 + 41984, (P, D), BF16, "xs")]
    junk = M.at(R3 + 12288, (P, D), BF16, "junk")
    ssq = sm((P, 32), F32, "ssq")
    rstd = sm((P, 32), F32, "rstd")

    S.dma("sp", "d_pk", lambda e: e.dma_start(out=pk[:, :], in_=ppk[:, :]), writes=["pk"])
    def mk_ident(t, key):
        S.op("pool", lambda e: e.memset(t[:, :], 1.0), writes=[key])
        S.op("pool", lambda e: e.affine_select(t[:, :], t[:, :], [[-1, P]], ALU.is_equal, 0.0,
                                               base=0, channel_multiplier=1), reads=[key], writes=[key])
    mk_ident(identb, "identb")
    mk_ident(identf, "identf")

    epsc = sm((P, 1), F32, "epsc")
    S.op("pool", lambda e: e.memset(epsc[:, :], EPS), writes=["epsc"])

    def load_xtile(i, buf, key, sem):
        if i == 0:
            S.dma("sp", sem, lambda e: e.dma_start(out=buf[0:16, :], in_=meta[:, :]), writes=[key])
            S.dma("sp", sem, lambda e: e.dma_start(out=buf[16:128, :], in_=x_prompt[0:112, :]), writes=[key])
        elif i < 16:
            S.dma("sp", sem, lambda e: e.dma_start(out=buf[:, :], in_=x_prompt[128 * i - 16:128 * i + 112, :]), writes=[key])
        else:
            S.dma("sp", sem, lambda e: e.dma_start(out=buf[0:16, :], in_=x_prompt[2032:2048, :]), writes=[key])
            S.dma("sp", sem, lambda e: e.dma_start(out=buf[16:32, :], in_=x_sample[:, :]), writes=[key])

    def rms_rstd(src, r, col, inv_n, kin, tagk, junk=junk, jkey="junk"):
        S.op("act", lambda e: e.activation(out=junk[0:r, :], in_=src, func=AF.Square,
                                           accum_out=ssq[0:r, col:col + 1]),
             reads=[kin, "epsc"], writes=[jkey, "ssq%s%d" % (tagk, col)])
        S.op("act", lambda e: e.activation(out=ssq[0:r, col:col + 1], in_=ssq[0:r, col:col + 1], func=AF.Ln,
                                           bias=epsc[0:r, :], scale=inv_n),
             reads=["ssq%s%d" % (tagk, col)], writes=["ssq%s%d" % (tagk, col)])
        S.op("act", lambda e: e.activation(out=rstd[0:r, col:col + 1], in_=ssq[0:r, col:col + 1], func=AF.Exp,
                                           scale=-0.5),
             reads=["ssq%s%d" % (tagk, col)], writes=["rstd%s%d" % (tagk, col)])

    for i in range(NTILE):
        r = tile_rows(i)
        b = i % NXT
        b3 = i % 3
        kx, ks = "xt%d" % b, "xs%d" % b3
        load_xtile(i, xt[b], kx, "d_xt%d" % b)
        rms_rstd(xt[b][0:r, :], r, i, 1.0 / D, kx, "a")
        S.op("dve", lambda e, b=b, b3=b3, r=r, i=i: e.tensor_scalar(out=xs[b3][0:r, :], in0=xt[b][0:r, :],
                                                          scalar1=rstd[0:r, i:i + 1], scalar2=None, op0=ALU.mult),
             reads=[kx, "rstda%d" % i], writes=[ks])
        pb = i % 2
        pst = PS[pb][:, :].bitcast(BF16)
        for k in range(KD):
            S.op("pe", lambda e, k=k, b3=b3, r=r, pst=pst: e.transpose(out=pst[:, k * 128:k * 128 + r],
                                                                    in_=xs[b3][0:r, k * 128:(k + 1) * 128],
                                                                    identity=identb[0:r, 0:r]),
                 reads=[ks, "identb"], writes=["ps%d" % pb], sig=(k == KD - 1))
        c0 = 128 * i
        S.op("dve", lambda e, pst=pst, r=r, c0=c0: e.tensor_tensor(
            out=xnT[:, :, c0:c0 + r],
            in0=pst.rearrange("p (k t) -> p k t", k=KD)[:, :, 0:r],
            in1=pk[:, 0:KD].unsqueeze(2).to_broadcast([P, KD, r]), op=ALU.mult),
             reads=["ps%d" % pb, "pk"], writes=["xnT%d" % i])

    if "xnT" in dbg_out:
        S.dma("sp", "d_dbg", lambda e: e.dma_start(out=dbg_out["xnT"][:, :, :], in_=xnT[:, :, :]),
              reads=["xnT%d" % i for i in range(NTILE)])

    class _Stop(Exception):
        pass

    def chk(st):
        if stop_after < st:
            raise _Stop()

    try:
        XN_ALL = ["xnT%d" % i for i in range(NTILE)]

        def xn_keys(c0, n):
            return ["xnT%d" % i for i in range(c0 // 128, (c0 + n - 1) // 128 + 1)]

        wlist = []

        def wg_cols(w, c0, kch=KD, ncol=512):
            return (lambda t: t[:, 0:kch * ncol].rearrange("p (k c) -> p k c", k=kch),
                    w[0:kch * 128, c0:c0 + ncol].rearrange("(k p) c -> p k c", p=P))

        G_Q, G_K, G_V = 0, 1, 2
        wlist.append(wg_cols(w_in, C_Q))
        wlist.append(wg_cols(w_in, C_K))
        wlist.append(wg_cols(w_in, C_V))
        G_CV = [3, 4, 5, 6]
        for cch in range(4):
            wlist.append((lambda t: t[:, 0:KD * 384].rearrange("p (k c) -> p k c", k=KD),
                          [((128 * j3, 128 * (j3 + 1)),
                            w_in[:, base + 128 * cch:base + 128 * (cch + 1)].rearrange("(k p) c -> p k c", p=P))
                           for j3, base in enumerate((C_B, C_C, C_H))]))
        G_BRC, G_BRA = 7, 8
        G_GP = [9, 10, 11, 12]
        wlist.append(wg_cols(w_br_conv, 0, kch=4, ncol=1024))
        wlist.append(wg_cols(w_br_attn, 0, kch=4, ncol=1024))
        for pr in range(4):
            wlist.append((lambda t: t[:, 0:KD * 512].rearrange("p (k c) -> p k c", k=KD),
                          [((0, 256), w_in[:, C_G + 256 * pr:C_G + 256 * (pr + 1)].rearrange("(k p) c -> p k c", p=P)),
                           ((256, 512), w_in[:, C_G + 1024 + 256 * pr:C_G + 1024 + 256 * (pr + 1)].rearrange("(k p) c -> p k c", p=P))]))
        G_OUT0 = 13
        wlist.append(wg_cols(w_out, 0))
        wlist.append(wg_cols(w_out, 512))
        G_MLP = 15
        for g in range(8):
            wlist.append(wg_cols(w_up, 512 * g))
            wlist.append((lambda t: t[:, :].rearrange("p (k c) -> p k c", k=4),
                          w_down[512 * g:512 * (g + 1), :].rearrange("(k p) c -> p k c", p=P)))
        wstate = {"issued": 0, "free": list(range(RING_SLOTS)), "slot": {}}

        def w_try_issue(limit=None, after=()):
            while wstate["issued"] < len(wlist) and wstate["free"] and (limit is None or wstate["issued"] < limit):
                g = wstate["issued"]
                slot = wstate["free"].pop(0)
                wstate["slot"][g] = slot
                vf, srcs = wlist[g]
                dst = vf(ring[slot])
                if not isinstance(srcs, list):
                    srcs = [(None, srcs)]
                for (sub, src) in srcs:
                    d = dst if sub is None else dst[:, :, sub[0]:sub[1]]
                    S.dma("pool", "d_ring%d" % slot, lambda e, d=d, src=src: e.dma_start(out=d, in_=src),
                          reads=list(after), writes=["ring%d" % slot])
                wstate["issued"] += 1

        def w_get(g):
            if g not in wstate["slot"]:
                w_try_issue(g + 1)
            slot = wstate["slot"][g]
            return wlist[g][0](ring[slot]), "ring%d" % slot

        def w_done(g):
            wstate["free"].append(wstate["slot"][g])
            w_try_issue()

        w_try_issue(1)
        w_try_issue(2, after=["xt%d" % (7 % NXT)])
        w_try_issue(3, after=["xt%d" % (12 % NXT)])

        blockones = sm((P, P), BF16, "blockones")
        S.op("pool", lambda e: e.memset(blockones[:, :], 0.0), writes=["blockones"])
        S.op("pool", lambda e: e.memset(blockones[0:64, 0:64], 1.0), writes=["blockones"])
        S.op("pool", lambda e: e.memset(blockones[64:128, 64:128], 1.0), writes=["blockones"])
        onesf = sm((P, P), F32, "onesf")
        S.op("pool", lambda e: e.memset(onesf[:, :], 1.0), writes=["onesf"])
        trif = sm((P, P), F32, "trif")
        S.op("pool", lambda e: e.memset(trif[:, :], 1.0), writes=["trif"])
        S.op("pool", lambda e: e.affine_select(trif[:, :], trif[:, :], [[1, P]], ALU.is_ge, 0.0,
                                               base=0, channel_multiplier=-1), reads=["trif"], writes=["trif"])
        maskb = sm((P, P), BF16, "maskb")
        S.op("pool", lambda e: e.memset(maskb[:, :], 1.0), writes=["maskb"])
        S.op("pool", lambda e: e.affine_select(maskb[:, :], maskb[:, :], [[1, P]], ALU.is_ge, 0.0,
                                               base=0, channel_multiplier=-1), reads=["maskb"], writes=["maskb"])
        maskneg = sm((P, P), BF16, "maskneg")
        S.op("pool", lambda e: e.memset(maskneg[:, :], 0.0), writes=["maskneg"])
        S.op("pool", lambda e: e.affine_select(maskneg[:, :], maskneg[:, :], [[1, P]], ALU.is_ge, -9984.0,
                                               base=0, channel_multiplier=-1), reads=["maskneg"], writes=["maskneg"])
        ones3 = sm((3, P), BF16, "ones3")
        S.op("pool", lambda e: e.memset(ones3[:, :], 1.0), writes=["ones3"])
        onecol = sm((P, 1), F32, "onecol")
        S.op("pool", lambda e: e.memset(onecol[:, :], 1.0), writes=["onecol"])
        wfl = sm((P, KD, 8), BF16, "wfl")
        if not os.environ.get("SKIP_WFL"):
          S.dma("pool", "d_wfl", lambda e: e.dma_start(out=wfl[:, :, :],
                                                     in_=w_in[:, C_F:C_F + 8].rearrange("(k p) c -> p k c", p=P)),
              writes=["wfl"])
        fl_all = sm((P, NTILE, 8), F32, "fl_all")
        lf_all = sm((P, NTILE, 8), F32, "lf_all")
        S.op("pool", lambda e: e.memset(fl_all[:, :, :], 0.0), writes=["fl_all"])
        S.op("pool", lambda e: e.memset(lf_all[:, :, :], 0.0), writes=["lf_all"])
        c_all = sm((P, NTILE, 8), F32, "c_all")
        S.op("pool", lambda e: e.memset(c_all[:, :, :], 0.0), writes=["c_all"])
        negc = sm((P, NTILE, 8), F32, "negc")
        csplit = sm((P, NTILE, 3, 8), BF16, "csplit")
        cres = sm((P, NTILE, 8), F32, "cres")
        cT24 = M.at(AUG, (24, NT), BF16, "cT24")
        qaug = [M.at(AUG + 4160 + i * 4160, (3, NT), BF16, "qaug") for i in range(2)]

        qT = M.at(R2, (P, 4, NT), BF16, "qT")
        kT = M.at(R2 + 16640, (P, 4, NT), BF16, "kT")
        Vp = M.at(R3 + 14336, (P, NTILE, H, 66), BF16, "Vp")
        sqb = [M.at(R45 + i * 1024, (P, 512), BF16, "sqb") for i in range(2)]
        rsb = [M.at(R45 + 2048 + i * 2048, (P, 512), F32, "rsb") for i in range(2)]
        kf = M.at(R45 + 6144, (P, 4, 512), F32, "kf")
        ktok = [M.at(R45 + 14336 + i * 2048, (P, 512), F32, "ktok") for i in range(2)]
        vtok = [M.at(R45 + 18432 + i * 2048, (P, 512), F32, "vtok") for i in range(2)]

        psn = {"n": 0}

        def next_ps(lo=0, hi=8, key="n"):
            psn[key] = psn.get(key, lo - 1) + 1
            if psn[key] >= hi or psn[key] < lo:
                psn[key] = lo
            return psn[key]

        if not os.environ.get("SKIP_VPMEM"):
            S.op("pool", lambda e: e.memset(Vp[:, :, :, 64:65], 1.0), writes=["Vp_ones"])

        chk(1)
        sqb3 = [M.at(R45 + 22528 + i * 1024, (P, 512), BF16, "sqb3") for i in range(3)]
        cnt1 = {"kt": 0}
        units1 = []
        for which, G in (("q", G_Q), ("k", G_K)):
            for (c0, n) in BLKS:
                for m in range(4):
                    units1.append(dict(which=which, G=G, c0=c0, n=n, m=m, idx=len(units1)))

        def s1A(u):
            which, G, c0, n, m = u["which"], u["G"], u["c0"], u["n"], u["m"]
            wv, wkey = w_get(G)
            A = next_ps(0, 4, "qa")
            sj = u["idx"] % 3
            u["A"], u["sj"] = A, sj
            for k in range(KD):
                S.op("pe", lambda e, k=k: e.matmul(
                    PS[A][:, 0:n], lhsT=wv[:, k, m * 128:(m + 1) * 128], rhs=xnT[:, k, c0:c0 + n],
                    start=(k == 0), stop=(k == KD - 1)),
                     reads=[wkey] + xn_keys(c0, n), writes=["ps%d" % A], sig=(k == KD - 1))
            S.op("act", lambda e: e.activation(out=sqb3[sj][:, 0:n], in_=PS[A][:, 0:n], func=AF.Square),
                 reads=["ps%d" % A], writes=["sqb%d" % sj])

        def s1B(u):
            which, G, c0, n, m = u["which"], u["G"], u["c0"], u["n"], u["m"]
            A, sj = u["A"], u["sj"]
            j = u["idx"] % 2
            B = next_ps(4, 6, "qb")
            S.op("pe", lambda e: e.matmul(PS[B][:, 0:n], lhsT=blockones[:, :], rhs=sqb3[sj][:, 0:n], start=True, stop=True),
                 reads=["sqb%d" % sj, "blockones"], writes=["ps%d" % B])
            S.op("act", lambda e: e.activation(out=rsb[j][:, 0:n], in_=PS[B][:, 0:n], func=AF.Ln,
                                               bias=epsc[:, :], scale=1.0 / HD),
                 reads=["ps%d" % B, "epsc"], writes=["rsb%d" % j])
            S.op("act", lambda e: e.activation(out=rsb[j][:, 0:n], in_=rsb[j][:, 0:n], func=AF.Exp, scale=-0.5),
                 reads=["rsb%d" % j], writes=["rsb%d" % j])
            if which == "q":
                S.op("dve", lambda e: e.scalar_tensor_tensor(
                    out=qT[:, m, c0:c0 + n], in0=PS[A][:, 0:n], scalar=pk[:, 40:41], in1=rsb[j][:, 0:n],
                    op0=ALU.mult, op1=ALU.mult),
                     reads=["ps%d" % A, "rsb%d" % j, "pk"], writes=["qT%d_%d" % (m, c0)])
            else:
                S.op("dve", lambda e: e.scalar_tensor_tensor(
                    out=kf[:, m, 0:n], in0=PS[A][:, 0:n], scalar=pk[:, 41:42], in1=rsb[j][:, 0:n],
                    op0=ALU.mult, op1=ALU.mult),
                     reads=["ps%d" % A, "rsb%d" % j, "pk"], writes=["kf%d" % m])
                S.op("dve", lambda e: e.tensor_copy(out=kT[:, m, c0:c0 + n], in_=kf[:, m, 0:n]),
                     reads=["kf%d" % m], writes=["kT%d_%d" % (m, c0)])
                if m == 3:
                    for tt in range((n + 127) // 128):
                        r = min(128, n - tt * 128)
                        jj = cnt1["kt"] % 2
                        cnt1["kt"] += 1
                        Cb = next_ps(6, 8, "kt")
                        for mm in range(4):
                            S.op("pe", lambda e, Cb=Cb, mm=mm, tt=tt, r=r: e.transpose(
                                out=PS[Cb][0:r, mm * 128:(mm + 1) * 128], in_=kf[:, mm, tt * 128:tt * 128 + r], identity=identf[:, :]),
                                 reads=["kf%d" % mm, "identf"], writes=["ps%d" % Cb], sig=(mm == 3))
                        S.op("dve", lambda e, Cb=Cb, jj=jj, r=r: e.tensor_copy(out=ktok[jj][0:r, :], in_=PS[Cb][0:r, :]),
                             reads=["ps%d" % Cb], writes=["ktok%d" % jj])
                        p0 = c0 + tt * 128
                        if p0 < 2048:
                            S.dma("sp", "d_ktok%d" % jj, lambda e, jj=jj, p0=p0: e.dma_start(out=nk_p[p0:p0 + 128, :], in_=ktok[jj][:, :]),
                                  reads=["ktok%d" % jj], writes=["nk_p_%d" % p0])
                        else:
                            S.dma("sp", "d_ktok%d" % jj, lambda e, jj=jj: e.dma_start(out=nk_p[2048:2064, :], in_=ktok[jj][0:16, :]),
                                  reads=["ktok%d" % jj], writes=["nk_p_%d" % p0])
                            S.dma("sp", "d_ktok%d" % jj, lambda e, jj=jj: e.dma_start(out=nk_s[:, :], in_=ktok[jj][16:32, :]),
                                  reads=["ktok%d" % jj], writes=["nk_s"])
            if m == 3 and c0 == 2048:
                w_done(G)

        LA1 = 2
        for i in range(LA1):
            s1A(units1[i])
        for i in range(len(units1)):
            s1B(units1[i])
            if i + LA1 < len(units1):
                s1A(units1[i + LA1])

        chk(2)
        wv, wkey = w_get(G_V)
        FB = 4
        for i in range(NTILE):
            r = tile_rows(i)
            c0 = 128 * i
            for k in range(KD):
                S.op("pe", lambda e, k=k, r=r, c0=c0, i=i: e.matmul(
                    PS[FB][0:r, 8 * i:8 * i + 8], lhsT=xnT[:, k, c0:c0 + r], rhs=wfl[:, k, :], start=(k == 0), stop=(k == KD - 1)),
                     reads=["wfl", "xnT%d" % i], writes=["ps%d" % FB], sig=(k == KD - 1))
        S.op("dve", lambda e: e.tensor_tensor(out=fl_all[:, 0:16, :], in0=PS[FB][:, 0:128].rearrange("p (i h) -> p i h", h=8),
                                              in1=pk[:, 16:24].unsqueeze(1).to_broadcast([P, 16, 8]), op=ALU.add),
             reads=["ps%d" % FB, "pk"], writes=["fl_all"])
        S.op("dve", lambda e: e.tensor_tensor(out=fl_all[0:32, 16, :], in0=PS[FB][0:32, 128:136], in1=pk[0:32, 16:24], op=ALU.add),
             reads=["ps%d" % FB, "pk"], writes=["fl_all"])

        def logsig(dst, src, r, kin, kout):
            S.op("act", lambda e: e.activation(out=dst, in_=src, func=AF.Exp, scale=-1.0), reads=[kin], writes=[kout])
            S.op("act", lambda e: e.activation(out=dst, in_=dst, func=AF.Ln, bias=onecol[0:r, :], scale=1.0),
                 reads=[kout, "onecol"], writes=[kout])
            S.op("dve", lambda e: e.tensor_scalar(out=dst, in0=dst, scalar1=-1.0, scalar2=None, op0=ALU.mult),
                 reads=[kout], writes=[kout])
        logsig(lf_all[:, 0:16, :], fl_all[:, 0:16, :], P, "fl_all", "lf_all")
        logsig(lf_all[0:32, 16, :], fl_all[0:32, 16, :], 32, "fl_all", "lf_all")
        S.dma("sp", "d_lf", lambda e: e.dma_start(out=nf_p[0:2048, :].rearrange("(i p) h -> p i h", p=P), in_=lf_all[:, 0:16, :]),
              reads=["lf_all"], writes=["nf_p_a"])
        S.dma("sp", "d_lf", lambda e: e.dma_start(out=nf_p[2048:2064, :], in_=lf_all[0:16, 16, :]), reads=["lf_all"], writes=["nf_p_b"])
        S.dma("sp", "d_lf", lambda e: e.dma_start(out=nf_s[:, :], in_=lf_all[16:32, 16, :]), reads=["lf_all"], writes=["nf_s"])

        carr = sm((P, NTILE, 8), F32, "carr")
        lfc = sm((P, 16, 8), F32, "lfc")
        S.dma("sp", "d_lfc", lambda e: e.dma_start(out=lfc[:, :, :], in_=cache_logf[:, :].rearrange("(i p) h -> p i h", p=P)),
              writes=["lfc"])
        c_s = sm((P, NTILE, 8), F32, "c_s")
        S.op("pool", lambda e: e.memset(c_s[:, :, :], 0.0), writes=["c_s"])
        negc_s = sm((P, NTILE, 8), F32, "negc_s")
        msel = sm((32, 16), F32, "msel")
        S.op("pool", lambda e: e.memset(msel[:, :], 1.0), writes=["msel"])
        S.op("pool", lambda e: e.affine_select(msel[:, :], msel[:, :], [[1, 16]], ALU.is_ge, 0.0,
                                               base=16, channel_multiplier=-1), reads=["msel"], writes=["msel"])
        S.op("pool", lambda e: e.memset(msel[0:16, :], 0.0), reads=["msel"], writes=["msel"])
        carr_s = sm((P, 17, 8), F32, "carr_s")
        Cb = 5
        S.op("pe", lambda e: e.matmul(PS[Cb][:, 0:136], lhsT=trif[:, :], rhs=lf_all[:, :, :].rearrange("p i h -> p (i h)"), start=True, stop=True),
             reads=["lf_all", "trif"], writes=["ps%d" % Cb])
        S.op("pe", lambda e: e.matmul(PS[Cb][:, 136:272], lhsT=onesf[:, :], rhs=lf_all[:, :, :].rearrange("p i h -> p (i h)"), start=True, stop=True),
             reads=["lf_all", "onesf"], writes=["ps%d" % Cb])
        Cs = 6
        S.op("pe", lambda e: e.matmul(PS[Cs][:, 0:128], lhsT=trif[:, :], rhs=lfc[:, :, :].rearrange("p i h -> p (i h)"), start=True, stop=True),
             reads=["lfc", "trif"], writes=["ps%d" % Cs])
        S.op("pe", lambda e: e.matmul(PS[Cs][:, 128:256], lhsT=onesf[:, :], rhs=lfc[:, :, :].rearrange("p i h -> p (i h)"), start=True, stop=True),
             reads=["lfc", "onesf"], writes=["ps%d" % Cs])
        S.op("pe", lambda e: e.matmul(PS[Cs][0:16, 256:264], lhsT=msel[:, :], rhs=lf_all[0:32, 16, :], start=True, stop=True),
             reads=["lf_all", "msel"], writes=["ps%d" % Cs])

        chain = []

        def DF(fn, **kw):
            chain.append(lambda: S.op("dve", fn, **kw))
        DF(lambda e: e.memset(carr[:, 0, :], 0.0), writes=["carr"])
        for i in range(1, NTILE):
            DF(lambda e, i=i: e.tensor_tensor(out=carr[:, i, :], in0=carr[:, i - 1, :], in1=PS[Cb][:, 136 + 8 * (i - 1):136 + 8 * i], op=ALU.add),
              reads=["carr", "ps%d" % Cb], writes=["carr"])
        DF(lambda e: e.tensor_tensor(out=c_all[:, :, :], in0=carr[:, :, :], in1=PS[Cb][:, 0:136].rearrange("p (i h) -> p i h", h=8), op=ALU.add),
          reads=["carr", "ps%d" % Cb], writes=["c_all"])
        key = "c_all"
        DF(lambda e: e.tensor_scalar(out=negc[:, :, :], in0=c_all[:, :, :], scalar1=-1.0, scalar2=None, op0=ALU.mult),
          reads=[key], writes=[key + "_neg"])
        DF(lambda e: e.tensor_copy(out=csplit[:, :, 0, :], in_=c_all[:, :, :]), reads=[key], writes=[key + "_s"])
        DF(lambda e: e.tensor_tensor(out=cres[:, :, :], in0=c_all[:, :, :], in1=csplit[:, :, 0, :], op=ALU.subtract),
          reads=[key, key + "_s"], writes=[key + "_r"])
        DF(lambda e: e.tensor_copy(out=csplit[:, :, 1, :], in_=cres[:, :, :]), reads=[key + "_r"], writes=[key + "_s"])
        DF(lambda e: e.tensor_tensor(out=cres[:, :, :], in0=cres[:, :, :], in1=csplit[:, :, 1, :], op=ALU.subtract),
          reads=[key + "_r", key + "_s"], writes=[key + "_r"])
        DF(lambda e: e.tensor_copy(out=csplit[:, :, 2, :], in_=cres[:, :, :]), reads=[key + "_r"], writes=[key + "_s"])
        DF(lambda e: e.memset(carr_s[:, 0, :], 0.0), writes=["carr_s"])
        for i in range(1, 17):
            DF(lambda e, i=i: e.tensor_tensor(out=carr_s[:, i, :], in0=carr_s[:, i - 1, :], in1=PS[Cs][:, 128 + 8 * (i - 1):128 + 8 * i], op=ALU.add),
              reads=["carr_s", "ps%d" % Cs], writes=["carr_s"])
        DF(lambda e: e.tensor_tensor(out=c_s[:, 0:16, :], in0=carr_s[:, 0:16, :], in1=PS[Cs][:, 0:128].rearrange("p (i h) -> p i h", h=8), op=ALU.add),
          reads=["carr_s", "ps%d" % Cs], writes=["c_s"])
        DF(lambda e: e.tensor_tensor(out=c_s[:, 0:16, :], in0=c_s[:, 0:16, :],
                                    in1=carr_s[:, 16, :].unsqueeze(1).to_broadcast([P, 16, 8]), op=ALU.subtract),
          reads=["carr_s", "c_s"], writes=["c_s"])
        DF(lambda e: e.tensor_copy(out=c_s[0:16, 16, :], in_=PS[Cs][0:16, 256:264]), reads=["ps%d" % Cs], writes=["c_s"])
        DF(lambda e: e.tensor_scalar(out=negc_s[:, :, :], in0=c_s[:, :, :], scalar1=-1.0, scalar2=None, op0=ALU.mult),
          reads=["c_s"], writes=["c_s_neg"])

        def run_chain(n):
            for _ in range(n):
                if chain:
                    chain.pop(0)()

        for i in range(NTILE):
            r = tile_rows(i)
            c0 = 128 * i
            jj = i % 2
            A = next_ps(0, 4, "vp")
            for k in range(KD):
                S.op("pe", lambda e, A=A, k=k, r=r, c0=c0, wv=wv: e.matmul(
                    PS[A][0:r, :], lhsT=xnT[:, k, c0:c0 + r], rhs=wv[:, k, :], start=(k == 0), stop=(k == KD - 1)),
                     reads=[wkey, "xnT%d" % i], writes=["ps%d" % A], sig=(k == KD - 1))
            S.op("dve", lambda e, A=A, jj=jj, r=r: e.tensor_copy(out=vtok[jj][0:r, :], in_=PS[A][0:r, :]),
                 reads=["ps%d" % A], writes=["vtok%d" % jj])
            S.op("dve", lambda e, A=A, r=r, i=i: e.tensor_copy(
                out=Vp[0:r, i, :, 0:64], in_=PS[A][0:r, :].rearrange("p (h d) -> p h d", h=H)),
                 reads=["ps%d" % A], writes=["Vp%d" % i])
            if i < 16:
                S.dma("sp", "d_vtok%d" % jj, lambda e, jj=jj, c0=c0: e.dma_start(out=nv_p[c0:c0 + 128, :], in_=vtok[jj][:, :]),
                      reads=["vtok%d" % jj], writes=["nv_p_%d" % i])
            else:
                S.dma("sp", "d_vtok%d" % jj, lambda e, jj=jj: e.dma_start(out=nv_p[2048:2064, :], in_=vtok[jj][0:16, :]),
                      reads=["vtok%d" % jj], writes=["nv_p_%d" % i])
                S.dma("sp", "d_vtok%d" % jj, lambda e, jj=jj: e.dma_start(out=nv_s[:, :], in_=vtok[jj][16:32, :]),
                      reads=["vtok%d" % jj], writes=["nv_s"])
            run_chain(4)
        run_chain(len(chain))
        Vsn = sm((16, H, 66), BF16, "Vsn")
        S.op("pool", lambda e: e.memset(Vsn[:, :, 64:65], 1.0), writes=["Vsn_ones"])
        A = next_ps(0, 4, "vp")
        for k in range(KD):
            S.op("pe", lambda e, A=A, k=k, wv=wv: e.matmul(PS[A][0:16, :], lhsT=xnT[:, k, L:NT], rhs=wv[:, k, :],
                                                          start=(k == 0), stop=(k == KD - 1)),
                 reads=[wkey, "xnT16"], writes=["ps%d" % A], sig=(k == KD - 1))
        S.op("dve", lambda e, A=A: e.tensor_copy(out=Vsn[:, :, 0:64], in_=PS[A][0:16, :].rearrange("p (h d) -> p h d", h=H)),
             reads=["ps%d" % A], writes=["Vsn"])
        w_done(G_V)

        for bnk in range(3):
            Tb = next_ps(4, 8, "ct")
            pst = PS[Tb][:, :].bitcast(BF16)
            tiles = list(range(8 * bnk, min(NTILE, 8 * bnk + 8)))
            for i in tiles:
                r = 128 if i < 16 else 16
                S.op("pe", lambda e, i=i, r=r, pst=pst: e.transpose(
                    out=pst[0:24, 128 * (i % 8):128 * (i % 8) + r], in_=csplit[0:r, i, :, :].rearrange("p j h -> p (j h)"),
                    identity=identb[0:r, 0:r]),
                     reads=["c_all_s", "identb"], writes=["ps%d" % Tb], sig=(i == tiles[-1]))
            w0 = 128 * tiles[0]
            wn = sum(128 if i < 16 else 16 for i in tiles)
            S.op("dve", lambda e, pst=pst, w0=w0, wn=wn: e.tensor_scalar(out=cT24[:, w0:w0 + wn], in0=pst[0:24, 0:wn],
                                                                  scalar1=8.0, scalar2=None, op0=ALU.mult),
                 reads=["ps%d" % Tb], writes=["c_all_T"])

        chk(6)
        S.barrier(engines=("pe", "act", "dve", "sp", "pool"))

        attnT = M.at(R45, (P, 4, NT), BF16, "attnT")
        attn_tok = M.at(R45 + 16640, (P, NTILE, 512), BF16, "attn_tok")
        NPB = 4
        Pb = [M.at(R45 + 34048 + i * 2048, (P, 1024), BF16, "Pb") for i in range(NPB)]
        Qh = [M.at(R45 + i * 4160, (P, NT), BF16, "Qh") for i in range(2)]
        Kh = [M.at(R45 + 8320 + i * 4160, (P, NT), BF16, "Kh") for i in range(2)]
        ncT24 = M.at(AUG + 4160, (24, NT), BF16, "ncT24")
        S.op("dve", lambda e: e.tensor_scalar(out=ncT24[:, 0:L], in0=cT24[:, 0:L], scalar1=-1.0, scalar2=None, op0=ALU.mult),
             reads=["c_all_T"], writes=["ncT24"])
        for i in range(2):
            S.op("pool", lambda e, i=i: e.memset(Qh[i][64:128, :], 0.0), writes=["Qh%d" % i])
            S.op("pool", lambda e, i=i: e.memset(Kh[i][64:128, :], 0.0), writes=["Kh%d" % i])
        S.op("dve", lambda e: e.memset(Qh[0][64:70, :], 1.0), writes=["Qh0"])
        S.dma("sp", "d_qh1", lambda e: e.dma_start(out=Qh[1][67:70, 0:L], in_=Qh[0][67:70, 0:L]), reads=["Qh0"], writes=["Qh1"])
        S.dma("sp", "d_kh0", lambda e: e.dma_start(out=Kh[0][64:67, 0:L], in_=Qh[0][67:70, 0:L]), reads=["Qh0"], writes=["Kh0"])
        S.dma("sp", "d_kh1", lambda e: e.dma_start(out=Kh[1][64:67, 0:L], in_=Qh[0][67:70, 0:L]), reads=["Qh0"], writes=["Kh1"])
        rsum = sm((P, 4), F32, "rsum")
        QG = [(0, 512), (512, 512), (1024, 512), (1536, 512), (2048, 16)]
        units = []
        for h in range(H):
            for gi, (q0, qn) in enumerate(QG):
                kt_last = (q0 + qn - 1) // 128
                kts = list(range(kt_last + 1))
                groups = []
                full = [kt for kt in kts if kt * 128 < q0 and qn == 512]
                rest = [kt for kt in kts if kt not in full]
                for j in range(0, len(full), 2):
                    groups.append(full[j:j + 2])
                packed_from = len(groups)
                if qn < 128:
                    groups.append(rest)
                else:
                    for kt in rest[:-2]:
                        groups.append([kt])
                    packed_from = len(groups)
                    groups.append(rest[-2:])
                for gj, g in enumerate(groups):
                    units.append(dict(h=h, q0=q0, qn=qn, kts=g, packed=(gj >= packed_from), first_h=(gi == 0 and gj == 0), first_g=(gj == 0),
                                      last_g=(gj == len(groups) - 1), idx=len(units)))
        ostate = {}

        def emitA(u):
            h, q0, qn, kts = u["h"], u["q0"], u["qn"], u["kts"]
            hp, hoff = h // 2, (h % 2) * 64
            hb = h % 2
            nqb = (qn + 127) // 128
            if u["first_h"]:
                S.dma("sp", "d_qh%d" % hb, lambda e: e.dma_start(out=Qh[hb][0:64, :], in_=qT[hoff:hoff + 64, hp, :]),
                      reads=["qT_all"], writes=["Qh%d" % hb])
                S.dma("sp", "d_kh%d" % hb, lambda e: e.dma_start(out=Kh[hb][0:64, :], in_=kT[hoff:hoff + 64, hp, :]),
                      reads=["kT_all"], writes=["Kh%d" % hb])
                for j3 in range(3):
                    S.dma("sp", "d_qh%d" % hb, lambda e, j3=j3: e.dma_start(
                        out=Qh[hb][64 + j3:65 + j3, 0:L], in_=cT24[8 * j3 + h:8 * j3 + h + 1, 0:L]),
                          reads=["c_all_T"], writes=["Qh%d" % hb])
                    S.dma("sp", "d_kh%d" % hb, lambda e, j3=j3: e.dma_start(
                        out=Kh[hb][67 + j3:68 + j3, 0:L], in_=ncT24[8 * j3 + h:8 * j3 + h + 1, 0:L]),
                          reads=["ncT24"], writes=["Kh%d" % hb])
            if u["first_g"]:
                if qn < 128:
                    ostate[(h, q0)] = (ostate[(h, QG[3][0])][0], 260)
                else:
                    O = next_ps(6, 8, "o")
                    ostate[(h, q0)] = (O, 0)
                    zc = 325 if q0 == QG[3][0] else 65 * nqb
                    S.op("dve", lambda e, O=O, zc=zc: e.memset(PS[O][:, 0:zc], 0.0), writes=["ps%d" % O])
            pp = next_ps(0, 3, "sp2")
            pj = u["idx"] % NPB
            u["pj"] = pj
            u["geo"] = []
            multi = u["packed"]
            cum = 0
            for hf, kt in enumerate(kts):
                kr = 128 if kt < 16 else 16
                qs = max(q0, kt * 128)
                nn = q0 + qn - qs
                coff = cum if multi else hf * 512
                cum += nn
                u["geo"].append((kt, kr, qs, nn, coff))
                diag = (kt * 128 >= q0)
                dst = PS2[pp][0:kr, coff:coff + nn]
                S.op("pe", lambda e, dst=dst, kt=kt, kr=kr, qs=qs, nn=nn, diag=diag: e.matmul(
                    dst, lhsT=Kh[hb][:, kt * 128:kt * 128 + kr], rhs=Qh[hb][:, qs:qs + nn], start=True, stop=not diag),
                     reads=["Kh%d" % hb, "Qh%d" % hb], writes=["ps%d" % (2 * pp), "ps%d" % (2 * pp + 1)], sig=not diag)
                if diag:
                    dn = min(128, nn)
                    S.op("pe", lambda e, kr=kr, dn=dn, coff=coff: e.matmul(
                        PS2[pp][0:kr, coff:coff + dn],
                        lhsT=identb[0:kr, 0:kr], rhs=maskneg[0:kr, 0:dn], start=False, stop=True),
                         reads=["identb", "maskneg"], writes=["ps%d" % (2 * pp), "ps%d" % (2 * pp + 1)])
            if len(kts) == 2 and not multi:
                S.op("act", lambda e: e.activation(out=Pb[pj][:, :], in_=PS2[pp][:, :], func=AF.Exp, scale=0.125),
                     reads=["ps%d" % (2 * pp), "ps%d" % (2 * pp + 1)], writes=["Pb%d" % pj])
            elif False:
                for hf in range(2):
                    S.op("act", lambda e, hf=hf: e.activation(out=Pb[pj][:, hf * 512:(hf + 1) * 512],
                                                             in_=(PS2[pp][:, hf * 512:(hf + 1) * 512] if PS2 is not None else PS[2 * pp + hf][:, :]),
                                                             func=AF.Exp, scale=0.125),
                         reads=["ps%d" % (2 * pp), "ps%d" % (2 * pp + 1)], writes=["Pb%d" % pj])
            elif multi:
                wtot = u["geo"][-1][4] + u["geo"][-1][3]
                S.op("act", lambda e: e.activation(out=Pb[pj][:, 0:wtot], in_=PS2[pp][:, 0:wtot], func=AF.Exp, scale=0.125),
                     reads=["ps%d" % (2 * pp), "ps%d" % (2 * pp + 1)], writes=["Pb%d" % pj])
            else:
                kt, kr, qs, nn, coff = u["geo"][0]
                S.op("act", lambda e: e.activation(out=Pb[pj][0:kr, 0:nn], in_=PS2[pp][0:kr, 0:nn],
                                                   func=AF.Exp, scale=0.125),
                     reads=["ps%d" % (2 * pp), "ps%d" % (2 * pp + 1)], writes=["Pb%d" % pj])

        def emitB(u):
            h, q0, qn = u["h"], u["q0"], u["qn"]
            pj = u["pj"]
            nqb = (qn + 127) // 128
            O, oc0 = ostate[(h, q0)]
            nk = len(u["geo"])
            for hf, (kt, kr, qs, nn, coff) in enumerate(u["geo"]):
                for qb in range(nqb):
                    qcol = q0 + qb * 128
                    qr = min(128, q0 + qn - qcol)
                    if qcol + qr - 1 < kt * 128:
                        continue
                    S.op("pe", lambda e, qb=qb, qr=qr, qcol=qcol, coff=coff, kt=kt, kr=kr, qs=qs: e.matmul(
                        PS[O][0:qr, oc0 + 65 * qb:oc0 + 65 * qb + 65], lhsT=Pb[pj][0:kr, coff + qcol - qs:coff + qcol - qs + qr],
                        rhs=Vp[0:kr, kt, h, 0:65], start=False, stop=(kt == qcol // 128), skip_group_check=True),
                         reads=["Pb%d" % pj, "Vp%d" % kt, "Vp_ones"], writes=["ps%d" % O],
                         sig=(qb == nqb - 1 and hf == nk - 1))
            if u["last_g"]:
                for qb in range(nqb):
                    qcol = q0 + qb * 128
                    qr = min(128, q0 + qn - qcol)
                    S.op("dve", lambda e, qb=qb, qr=qr: e.reciprocal(out=rsum[0:qr, qb:qb + 1], in_=PS[O][0:qr, oc0 + 65 * qb + 64:oc0 + 65 * qb + 65]),
                         reads=["ps%d" % O], writes=["rsum%d" % qb])
                    S.op("dve", lambda e, qb=qb, qr=qr, qcol=qcol: e.tensor_scalar(
                        out=attn_tok[0:qr, qcol // 128, h * 64:(h + 1) * 64], in0=PS[O][0:qr, oc0 + 65 * qb:oc0 + 65 * qb + 64],
                        scalar1=rsum[0:qr, qb:qb + 1], scalar2=None, op0=ALU.mult),
                         reads=["ps%d" % O, "rsum%d" % qb], writes=["attn_tok%d" % (qcol // 128)])

        LA = 3
        for i in range(min(LA, len(units))):
            emitA(units[i])
        for i in range(len(units)):
            emitB(units[i])
            if i + LA < len(units):
                emitA(units[i + LA])

        chk(7)
        KC = 256
        kc_tm = [M.at(R3 + i * 2048, (P, 2, 512), BF16, "kc_tm") for i in range(2)]
        vc_tm = [M.at(R3 + 4096 + i * 2048, (P, 2, 512), BF16, "vc_tm") for i in range(2)]
        KcT = [M.at(R3 + 8192 + i * 2048, (P, 2, 4, 128), BF16, "KcT") for i in range(2)]
        Psb = [M.at(R3 + 12288 + i * 512, (P, 2, H, 16), BF16, "Psb") for i in range(2)]
        Vw = [M.at(AUG + 8320 + i * 2112, (P, 2, H, 66), BF16, "Vw") for i in range(2)]
        attn_s = sm((16, 512), BF16, "attn_s")
        rsum_s = sm((16, 8), F32, "rsum_s")
        wts = sm((P, NTILE, 8), F32, "wts")
        Qz = sm((P, H, 16), BF16, "Qz")
        S.op("act", lambda e: e.activation(out=wts[:, :, :].rearrange("p i h -> p (i h)"), in_=negc_s[:, :, :].rearrange("p i h -> p (i h)"), func=AF.Exp),
             reads=["c_s_neg"], writes=["wts"])
        S.op("dve", lambda e: e.memset(Qz[:, :, :], 0.0), writes=["Qz"])
        for h in range(H):
            hp, hoff = h // 2, (h % 2) * 64
            S.op("dve", lambda e, h=h, hp=hp, hoff=hoff: e.tensor_copy(out=Qz[hoff:hoff + 64, h, :], in_=qT[hoff:hoff + 64, hp, L:NT]),
                 reads=["qT_all"], writes=["Qz"])

        def load_cache_chunk(c):
            j = c % 2
            S.dma("pool", "d_kc%d" % j, lambda e, c=c, j=j: e.dma_start(
                out=kc_tm[j][:, :, :], in_=cache_k[KC * c:KC * (c + 1), :].rearrange("(i p) d -> p i d", p=P)),
                  writes=["kc_tm%d" % j])
            S.dma("pool", "d_vc%d" % j, lambda e, c=c, j=j: e.dma_start(
                out=vc_tm[j][:, :, :], in_=cache_v[KC * c:KC * (c + 1), :].rearrange("(i p) d -> p i d", p=P)),
                  writes=["vc_tm%d" % j])
        load_cache_chunk(0)
        load_cache_chunk(1)
        OS = (6, 7)
        for ob in OS:
            S.op("dve", lambda e, ob=ob: e.memset(PS[ob][0:16, 0:260], 0.0), writes=["ps%d" % ob])
        NCH = PAST // KC

        def sA(c):
            j = c % 2
            Tk = next_ps(0, 2, "tk")
            pstk = PS[Tk][:, :].bitcast(BF16)
            for t in range(2):
                for hp in range(4):
                    S.op("pe", lambda e, t=t, hp=hp: e.transpose(
                        out=pstk[:, (t * 4 + hp) * 128:(t * 4 + hp + 1) * 128], in_=kc_tm[j][:, t, hp * 128:(hp + 1) * 128],
                        identity=identb[:, :]),
                         reads=["kc_tm%d" % j, "identb"], writes=["ps%d" % Tk], sig=(t == 1 and hp == 3))
            S.op("act", lambda e: e.activation(out=KcT[j][:, :, :, :].rearrange("p t h k -> p (t h k)"), in_=pstk[:, :], func=AF.Copy),
                 reads=["ps%d" % Tk], writes=["KcT%d" % j])
            for t in range(2):
                kt = 2 * c + t
                S.op("dve", lambda e, t=t, kt=kt: e.tensor_tensor(
                    out=Vw[j][:, t, :, 0:64], in0=vc_tm[j][:, t, :].rearrange("p (h d) -> p h d", h=H),
                    in1=wts[:, kt, :].unsqueeze(2).to_broadcast([P, H, 64]), op=ALU.mult),
                     reads=["vc_tm%d" % j, "wts"], writes=["Vw%d" % j])
                S.op("dve", lambda e, t=t, kt=kt: e.tensor_copy(out=Vw[j][:, t, :, 64], in_=wts[:, kt, :]),
                     reads=["wts"], writes=["Vw%d" % j])
            Sb = next_ps(2, 4, "sb")
            for t in range(2):
                for h in range(H):
                    hp = h // 2
                    col = (t * H + h) * 16
                    S.op("pe", lambda e, col=col, t=t, hp=hp, h=h: e.matmul(
                        PS[Sb][:, col:col + 16], lhsT=KcT[j][:, t, hp, :], rhs=Qz[:, h, :], start=True, stop=True),
                         reads=["KcT%d" % j, "Qz"], writes=["ps%d" % Sb], sig=(t == 1 and h == H - 1))
            S.op("act", lambda e: e.activation(out=Psb[j][:, :, :, :].rearrange("p t h q -> p (t h q)"), in_=PS[Sb][:, 0:256],
                                               func=AF.Exp, scale=0.125),
                 reads=["ps%d" % Sb], writes=["Psb%d" % j])

        def sB(c):
            j = c % 2
            for t in range(2):
                for h in range(H):
                    ob = OS[h // 4]
                    oc = (h % 4) * 65
                    S.op("pe", lambda e, ob=ob, oc=oc, t=t, h=h: e.matmul(
                        PS[ob][0:16, oc:oc + 65], lhsT=Psb[j][:, t, h, :], rhs=Vw[j][:, t, h, 0:65],
                        start=False, stop=False, skip_group_check=True),
                         reads=["Psb%d" % j, "Vw%d" % j], writes=["ps%d" % ob], sig=(t == 1 and h == H - 1))
            if c + 2 < NCH:
                load_cache_chunk(c + 2)
        sA(0)
        for c in range(NCH):
            if c + 1 < NCH:
                sA(c + 1)
            sB(c)
        Sb = next_ps(2, 4, "sb")
        Pn = sm((16, H, 16), BF16, "Pn")
        Vsw = sm((16, H, 66), BF16, "Vsw")
        for h in range(H):
            hp = h // 2
            S.op("pe", lambda e, h=h, hp=hp: e.matmul(
                PS[Sb][0:16, 16 * h:16 * h + 16], lhsT=kT[:, hp, L:NT], rhs=Qz[:, h, :], start=True, stop=False),
                 reads=["Qz"], writes=["ps%d" % Sb], sig=False)
            S.op("pe", lambda e, h=h: e.matmul(
                PS[Sb][0:16, 16 * h:16 * h + 16], lhsT=identb[0:16, 0:16], rhs=maskneg[0:16, 0:16], start=False, stop=True),
                 reads=["identb", "maskneg"], writes=["ps%d" % Sb], sig=(h == H - 1))
        S.op("act", lambda e: e.activation(out=Pn[:, :, :].rearrange("p h q -> p (h q)"), in_=PS[Sb][0:16, 0:128], func=AF.Exp, scale=0.125),
             reads=["ps%d" % Sb], writes=["Pn"])
        S.op("dve", lambda e: e.tensor_tensor(out=Vsw[:, :, 0:65], in0=Vsn[:, :, 0:65],
                                              in1=wts[0:16, 16, :].unsqueeze(2).to_broadcast([16, H, 65]), op=ALU.mult),
             reads=["Vsn", "Vsn_ones", "wts"], writes=["Vsw"])
        for h in range(H):
            ob = OS[h // 4]
            oc = (h % 4) * 65
            S.op("pe", lambda e, ob=ob, oc=oc, h=h: e.matmul(
                PS[ob][0:16, oc:oc + 65], lhsT=Pn[:, h, :], rhs=Vsw[:, h, 0:65], start=False, stop=True, skip_group_check=True),
                 reads=["Pn", "Vsw"], writes=["ps%d" % ob], sig=(h % 4 == 3))
        for h in range(H):
            ob = OS[h // 4]
            oc = (h % 4) * 65
            S.op("dve", lambda e, ob=ob, oc=oc, h=h: e.reciprocal(out=rsum_s[:, h:h + 1], in_=PS[ob][0:16, oc + 64:oc + 65]),
                 reads=["ps%d" % ob], writes=["rsum_s"])
            S.op("dve", lambda e, ob=ob, oc=oc, h=h: e.tensor_scalar(
                out=attn_s[:, h * 64:(h + 1) * 64], in0=PS[ob][0:16, oc:oc + 64], scalar1=rsum_s[:, h:h + 1], scalar2=None, op0=ALU.mult),
                 reads=["ps%d" % ob, "rsum_s"], writes=["attn_s"])

        for i in range(NTILE):
            r = 128 if i < 16 else 16
            Tb = next_ps(0, 4, "ta")
            pst = PS[Tb][:, :].bitcast(BF16)
            for hp in range(4):
                S.op("pe", lambda e, i=i, r=r, hp=hp, pst=pst: e.transpose(
                    out=pst[:, hp * 128:hp * 128 + r], in_=attn_tok[0:r, i, hp * 128:(hp + 1) * 128], identity=identb[0:r, 0:r]),
                     reads=["attn_tok%d" % i, "identb"], writes=["ps%d" % Tb], sig=(hp == 3))
            S.op("dve", lambda e, i=i, r=r, pst=pst: e.tensor_copy(
                out=attnT[:, :, 128 * i:128 * i + r], in_=pst[:, 0:512].rearrange("p (h t) -> p h t", h=4)[:, :, 0:r]),
                 reads=["ps%d" % Tb], writes=["attnT%d" % i])
        Tb = next_ps(0, 4, "ta")
        pst = PS[Tb][:, :].bitcast(BF16)
        for hp in range(4):
            S.op("pe", lambda e, hp=hp, pst=pst: e.transpose(
                out=pst[:, hp * 128:hp * 128 + 16], in_=attn_s[0:16, hp * 128:(hp + 1) * 128], identity=identb[0:16, 0:16]),
                 reads=["attn_s", "identb"], writes=["ps%d" % Tb], sig=(hp == 3))
        S.op("dve", lambda e, pst=pst: e.tensor_copy(
            out=attnT[:, :, L:NT], in_=pst[:, 0:512].rearrange("p (h t) -> p h t", h=4)[:, :, 0:16]),
             reads=["ps%d" % Tb], writes=["attnT17"])
        if "attnT" in dbg_out:
            S.dma("sp", "d_dbg", lambda e: e.dma_start(out=dbg_out["attnT"][:, :, :], in_=attnT[:, :, :]),
                  reads=["attnT%d" % i for i in range(18)])

        chk(8)
        S.barrier()
        ZW = 2088
        zbuf = [M.at(R3 + i * (ZW * 4), (P, ZW), F32, "zbuf") for i in range(2)]
        convT = M.at(R3 + 16896, (P, 4, NT), BF16, "convT")
        tmpC = [M.at(R45 + 16640 + i * 2048, (P, 512), F32, "tmpC") for i in range(2)]
        ytmp = [M.at(R45 + 20736 + i * 2048, (P, 512), F32, "ytmp") for i in range(2)]
        tmpB = [M.at(AUG + 4160 + i * 2048, (P, 512), F32, "tmpB") for i in range(2)]
        stc_sb = sm((P, 8), F32, "stc_sb")
        zlast = sm((P, 4, 4), F32, "zlast")
        S.dma("sp", "d_stc", lambda e: e.dma_start(out=stc_sb[:, :], in_=stc[:, :]), writes=["stc_sb"])
        cc3 = {"n": 0}
        for c in range(4):
            wX, kX = w_get(G_CV[c])
            zb = zbuf[c % 2]
            zk = "zbuf%d" % (c % 2)
            S.op("dve", lambda e, zb=zb: e.memset(zb[:, 0:2], 0.0), writes=[zk])
            S.op("dve", lambda e, zb=zb, c=c: e.tensor_copy(out=zb[:, 2066:2068], in_=stc_sb[:, 2 * c:2 * c + 2]), reads=["stc_sb"], writes=[zk])
            for (c0, n) in BLKS_EQ:
                jj = cc3["n"] % 2
                cc3["n"] += 1
                banks = []
                for j3 in range(3):
                    A = next_ps()
                    banks.append(A)
                    for k in range(KD):
                        S.op("pe", lambda e, A=A, k=k, j3=j3, c0=c0, n=n, wX=wX: e.matmul(
                            PS[A][:, 0:n], lhsT=wX[:, k, 128 * j3:128 * (j3 + 1)], rhs=xnT[:, k, c0:c0 + n],
                            start=(k == 0), stop=(k == KD - 1)), reads=[kX], writes=["ps%d" % A], sig=(k == KD - 1))
                bB, bC, bH = banks
                S.op("act", lambda e, bC=bC, jj=jj, n=n: e.activation(out=tmpC[jj][:, 0:n], in_=PS[bC][:, 0:n], func=AF.Copy),
                     reads=["ps%d" % bC], writes=["tmpC%d" % jj])
                S.op("act", lambda e, bB=bB, jj=jj, n=n: e.activation(out=tmpB[jj][:, 0:n], in_=PS[bB][:, 0:n], func=AF.Copy),
                     reads=["ps%d" % bB], writes=["tmpB%d" % jj])
                segs = []
                if c0 < L:
                    segs.append((0, min(c0 + n, L) - c0, c0 + 2))
                if c0 + n > L:
                    s0 = max(c0, L)
                    segs.append((s0 - c0, c0 + n - s0, s0 + 4))
                for (so, sn, zc) in segs:
                    S.op("dve", lambda e, bH=bH, jj=jj, so=so, sn=sn, zc=zc, zb=zb: e.tensor_tensor(
                        out=zb[:, zc:zc + sn], in0=tmpC[jj][:, so:so + sn], in1=PS[bH][:, so:so + sn], op=ALU.mult),
                         reads=["tmpC%d" % jj, "ps%d" % bH], writes=[zk])
                for (so, sn, zc) in segs:
                    S.op("dve", lambda e, jj=jj, so=so, sn=sn, zc=zc, zb=zb, c=c: e.tensor_scalar(
                        out=ytmp[jj][:, so:so + sn], in0=zb[:, zc:zc + sn], scalar1=pk[:, 24 + 3 * c + 2:24 + 3 * c + 3],
                        scalar2=pk[:, 36 + c:37 + c], op0=ALU.mult, op1=ALU.add),
                         reads=[zk, "pk"], writes=["ytmp%d" % jj])
                    for tap, sh in ((1, 1), (0, 2)):
                        S.op("dve", lambda e, jj=jj, so=so, sn=sn, zc=zc, zb=zb, c=c, tap=tap, sh=sh: e.scalar_tensor_tensor(
                            out=ytmp[jj][:, so:so + sn], in0=zb[:, zc - sh:zc - sh + sn], scalar=pk[:, 24 + 3 * c + tap:24 + 3 * c + tap + 1],
                            in1=ytmp[jj][:, so:so + sn], op0=ALU.mult, op1=ALU.add),
                             reads=[zk, "pk", "ytmp%d" % jj], writes=["ytmp%d" % jj])
                S.op("dve", lambda e, jj=jj, n=n, c=c, c0=c0: e.tensor_tensor(
                    out=convT[:, c, c0:c0 + n], in0=tmpB[jj][:, 0:n], in1=ytmp[jj][:, 0:n], op=ALU.mult),
                     reads=["tmpB%d" % jj, "ytmp%d" % jj], writes=["convT"])
            S.op("dve", lambda e, zb=zb, c=c: e.tensor_copy(out=zlast[:, c, 0:2], in_=zb[:, 2064:2066]), reads=[zk], writes=["zlast"])
            S.op("dve", lambda e, zb=zb, c=c: e.tensor_copy(out=zlast[:, c, 2:4], in_=zb[:, 2082:2084]), reads=[zk], writes=["zlast"])
            w_done(G_CV[c])
        with nc.allow_non_contiguous_dma(reason="tiny transposed conv-state rows"):
            for c in range(4):
                S.dma("sp", "d_ncp", lambda e, c=c: e.dma_start(
                    out=nc_p[:, 128 * c:128 * (c + 1)].rearrange("r p -> p r"), in_=zlast[:, c, 0:2], allow_slow_non_contiguous=True),
                      reads=["zlast"], writes=["nc_p%d" % c])
                S.dma("sp", "d_ncs", lambda e, c=c: e.dma_start(
                    out=nc_s[:, 128 * c:128 * (c + 1)].rearrange("r p -> p r"), in_=zlast[:, c, 2:4], allow_slow_non_contiguous=True),
                      reads=["zlast"], writes=["nc_s%d" % c])

        chk(9)
        mergedT = M.at(R2, (P, KD, NT), BF16, "mergedT")
        gt = [[M.at(R45 + 24832 + (i * 4 + q_) * 2048, (P, 512), F32, "gt") for q_ in range(4)] for i in range(2)]
        g4 = {"n": 0}
        wbrc, kbrc = w_get(G_BRC)
        wbra, kbra = w_get(G_BRA)
        for pr in range(4):
            wgp, kgp = w_get(G_GP[pr])
            for jj2 in range(2):
                j = 2 * pr + jj2
                for (c0, n) in BLKS_EQ:
                    si = g4["n"] % 2
                    g4["n"] += 1
                    sA, sB, t1, t2 = gt[si]
                    ba = next_ps(); bb = next_ps(); bc = next_ps(); bd = next_ps()
                    for (bank, wv, wk, src, nk, col0) in ((ba, wgp, kgp, xnT, KD, jj2 * 128), (bb, wgp, kgp, xnT, KD, 256 + jj2 * 128),
                                                        (bc, wbrc, kbrc, convT, 4, j * 128), (bd, wbra, kbra, attnT, 4, j * 128)):
                        for k in range(nk):
                            S.op("pe", lambda e, bank=bank, wv=wv, src=src, k=k, nk=nk, col0=col0, c0=c0, n=n: e.matmul(
                                PS[bank][:, 0:n], lhsT=wv[:, k, col0:col0 + 128], rhs=src[:, k, c0:c0 + n],
                                start=(k == 0), stop=(k == nk - 1)),
                                 reads=[wk, "convT"] + ["attnT%d" % i for i in range(18)], writes=["ps%d" % bank], sig=(k == nk - 1))
                    S.op("act", lambda e, ba=ba, sA=sA, n=n: e.activation(out=sA[:, 0:n], in_=PS[ba][:, 0:n], func=AF.Sigmoid),
                         reads=["ps%d" % ba], writes=["gtA%d" % si])
                    S.op("act", lambda e, bb=bb, sB=sB, n=n: e.activation(out=sB[:, 0:n], in_=PS[bb][:, 0:n], func=AF.Sigmoid),
                         reads=["ps%d" % bb], writes=["gtB%d" % si])
                    S.op("dve", lambda e, bc=bc, sA=sA, t1=t1, n=n: e.tensor_tensor(out=t1[:, 0:n], in0=sA[:, 0:n], in1=PS[bc][:, 0:n], op=ALU.mult),
                         reads=["ps%d" % bc, "gtA%d" % si], writes=["gt1%d" % si])
                    S.op("dve", lambda e, bd=bd, sB=sB, t2=t2, n=n: e.tensor_tensor(out=t2[:, 0:n], in0=sB[:, 0:n], in1=PS[bd][:, 0:n], op=ALU.mult),
                         reads=["ps%d" % bd, "gtB%d" % si], writes=["gt2%d" % si])
                    S.op("dve", lambda e, t1=t1, t2=t2, j=j, c0=c0, n=n: e.tensor_tensor(out=mergedT[:, j, c0:c0 + n], in0=t1[:, 0:n], in1=t2[:, 0:n], op=ALU.add),
                         reads=["gt1%d" % si, "gt2%d" % si], writes=["mergedT"])
            w_done(G_GP[pr])
        w_done(G_BRC); w_done(G_BRA)

        chk(10)
        S.barrier()
        yacc = M.at(R1, (P, 16, D), F32, "yacc")
        yacc16 = M.at(R45 + 33280, (P, D), F32, "yacc16")
        hnT = M.at(R45, (P, KD, NT), BF16, "hnT")
        hsb = [M.at(AUG + i * 2048, (P, D), BF16, "hs") for i in range(3)]
        junk2 = M.at(R45 + 41472, (P, D), BF16, "junk2")
        wo0, ko0 = w_get(G_OUT0)
        wo1, ko1 = w_get(G_OUT0 + 1)

        def ytile(i):
            return yacc[:, i, :] if i < 16 else yacc16[:, :]

        def s5A(i):
            r = tile_rows(i)
            c0 = 128 * i
            b = i % 3
            yt = ytile(i)
            for half, (wo, ko) in enumerate(((wo0, ko0), (wo1, ko1))):
                A = next_ps(0, 6, "w5")
                for k in range(KD):
                    S.op("pe", lambda e, A=A, k=k, wo=wo: e.matmul(
                        PS[A][0:r, :], lhsT=mergedT[:, k, c0:c0 + r], rhs=wo[:, k, :], start=(k == 0), stop=(k == KD - 1)),
                         reads=[ko, "mergedT"], writes=["ps%d" % A], sig=(k == KD - 1))
                S.op("dve", lambda e, A=A, half=half: e.tensor_tensor(
                    out=yt[0:r, half * 512:(half + 1) * 512], in0=yt[0:r, half * 512:(half + 1) * 512], in1=PS[A][0:r, :], op=ALU.add),
                     reads=["ps%d" % A, "yacc%d" % i], writes=["yacc%d" % i])
            rms_rstd(yt[0:r, :], r, i, 1.0 / D, "yacc%d" % i, "b", junk=junk2, jkey="junk2")

        def s5A2(i):
            r = tile_rows(i)
            b = i % 3
            yt = ytile(i)
            S.op("dve", lambda e: e.tensor_scalar(out=hsb[b][0:r, :], in0=yt[0:r, :], scalar1=rstd[0:r, i:i + 1],
                                                  scalar2=None, op0=ALU.mult),
                 reads=["yacc%d" % i, "rstdb%d" % i], writes=["hs%d" % b])

        def s5B(i):
            r = tile_rows(i)
            c0 = 128 * i
            b = i % 3
            Tb = next_ps(6, 8, "t5")
            pst = PS[Tb][:, :].bitcast(BF16)
            for k in range(KD):
                S.op("pe", lambda e, k=k: e.transpose(out=pst[:, k * 128:k * 128 + r], in_=hsb[b][0:r, k * 128:(k + 1) * 128],
                                                     identity=identb[0:r, 0:r]),
                     reads=["hs%d" % b, "identb"], writes=["ps%d" % Tb], sig=(k == KD - 1))
            S.op("dve", lambda e: e.tensor_tensor(
                out=hnT[:, :, c0:c0 + r], in0=pst.rearrange("p (k t) -> p k t", k=KD)[:, :, 0:r],
                in1=pk[:, 8:16].unsqueeze(2).to_broadcast([P, KD, r]), op=ALU.mult),
                 reads=["ps%d" % Tb, "pk"], writes=["hnT"])
        for i in range(NTILE):
            load_xtile(i, ytile(i), "yacc%d" % i, "d_xr%d" % i)
        s5A(0)
        s5A(1)
        s5A2(0)
        for i in range(NTILE):
            if i + 2 < NTILE:
                s5A(i + 2)
            if i + 1 < NTILE:
                s5A2(i + 1)
            s5B(i)
        w_done(G_OUT0); w_done(G_OUT0 + 1)

        chk(11)
        S.barrier()
        aTb = [M.at(R2 + i * 16640, (P, 4, NT), BF16, "aT") for i in range(2)]
        rtmp = [M.at(R45 + 37376 + i * 2048, (P, 512), F32, "rtmp") for i in range(2)]
        r6 = {"n": 0}
        for g in range(8):
            wu, ku = w_get(G_MLP + 2 * g)
            wd, kd = w_get(G_MLP + 2 * g + 1)
            aT = aTb[g % 2]
            ak = "aT%d" % (g % 2)
            for fc in range(4):
                for (c0, n) in BLKS_EQ:
                    A = next_ps()
                    for k in range(KD):
                        S.op("pe", lambda e, A=A, k=k, fc=fc, c0=c0, n=n, wu=wu: e.matmul(
                            PS[A][:, 0:n], lhsT=wu[:, k, fc * 128:(fc + 1) * 128], rhs=hnT[:, k, c0:c0 + n],
                            start=(k == 0), stop=(k == KD - 1)), reads=[ku, "hnT"], writes=["ps%d" % A], sig=(k == KD - 1))
                    rj = r6["n"] % 2
                    r6["n"] += 1
                    S.op("act", lambda e, A=A, n=n, rj=rj: e.activation(out=rtmp[rj][:, 0:n], in_=PS[A][:, 0:n], func=AF.Relu),
                         reads=["ps%d" % A], writes=["rtmp%d" % rj])
                    S.op("dve", lambda e, fc=fc, c0=c0, n=n, aT=aT, rj=rj: e.tensor_tensor(
                        out=aT[:, fc, c0:c0 + n], in0=rtmp[rj][:, 0:n], in1=rtmp[rj][:, 0:n], op=ALU.mult),
                         reads=["rtmp%d" % rj], writes=[ak])
            for i in range(NTILE):
                r = tile_rows(i)
                c0 = 128 * i
                yt = ytile(i)
                for half in range(2):
                    A = next_ps()
                    for fc in range(4):
                        S.op("pe", lambda e, A=A, fc=fc, r=r, c0=c0, half=half, aT=aT, wd=wd: e.matmul(
                            PS[A][0:r, :], lhsT=aT[:, fc, c0:c0 + r], rhs=wd[:, fc, half * 512:(half + 1) * 512],
                            start=(fc == 0), stop=(fc == 3)), reads=[kd, ak], writes=["ps%d" % A], sig=(fc == 3))
                    S.op("dve", lambda e, A=A, r=r, yt=yt, half=half: e.tensor_tensor(
                        out=yt[0:r, half * 512:(half + 1) * 512], in0=yt[0:r, half * 512:(half + 1) * 512], in1=PS[A][0:r, :], op=ALU.add),
                         reads=["ps%d" % A, "yacc%d" % i], writes=["yacc%d" % i])
                if g == 7:
                    if i == 0:
                        S.dma("sp", "d_y", lambda e: e.dma_start(out=y_prompt[0:112, :], in_=yacc[16:128, 0, :]), reads=["yacc0"], writes=["y_prompt0"])
                    elif i < 16:
                        S.dma("sp", "d_y", lambda e, i=i: e.dma_start(out=y_prompt[128 * i - 16:128 * i + 112, :], in_=yacc[:, i, :]),
                              reads=["yacc%d" % i], writes=["y_prompt%d" % i])
                    else:
                        S.dma("sp", "d_y", lambda e: e.dma_start(out=y_prompt[2032:2048, :], in_=yacc16[0:16, :]), reads=["yacc16"], writes=["y_prompt16"])
                        S.dma("sp", "d_y", lambda e: e.dma_start(out=y_sample[:, :], in_=yacc16[16:32, :]), reads=["yacc16"], writes=["y_sample"])
            w_done(G_MLP + 2 * g); w_done(G_MLP + 2 * g + 1)

        if "attn_tok" in dbg_out:
            S.dma("sp", "d_dbg", lambda e: e.dma_start(out=dbg_out["attn_tok"][:, :, :], in_=attn_tok[:, :, :]),
                  reads=["attn_tok%d" % i for i in range(NTILE)])
        if "cT24" in dbg_out:
            S.dma("sp", "d_dbg", lambda e: e.dma_start(out=dbg_out["cT24"][:, :], in_=cT24[:, :]), reads=["c_all_T"])
        if "negc" in dbg_out:
            S.dma("sp", "d_dbg", lambda e: e.dma_start(out=dbg_out["negc"][:, :, :], in_=negc[:, :, :]), reads=["c_all_neg"])
        if "qT" in dbg_out:
            S.dma("sp", "d_dbg", lambda e: e.dma_start(out=dbg_out["qT"][:, :, :], in_=qT[:, :, :]), reads=["qT_all"])
        if "c_all" in dbg_out:
            S.dma("sp", "d_dbg", lambda e: e.dma_start(out=dbg_out["c_all"][:, :, :], in_=c_all[:, :, :]), reads=["c_all"])


    except _Stop:
        pass

    S.finish("sp")
    S.emit()
    return nc


def make_in_maps(inputs):
    f = lambda a: np.ascontiguousarray(np.asarray(a, dtype=np.float32))
    x_prompt = f(inputs["x_prompt"]); x_sample = f(inputs["x_sample"])
    ck = f(inputs["cache_k"])[0]; cv = f(inputs["cache_v"])[0]; cl = f(inputs["cache_logf"])[0]
    sc = f(inputs["state_conv"])[0]
    pk = np.zeros((P, 64), np.float32)
    pk[:, 0:8] = f(inputs["norm1_g"])[0].reshape(8, P).T
    pk[:, 8:16] = f(inputs["norm2_g"])[0].reshape(8, P).T
    pk[:, 16:24] = np.broadcast_to(f(inputs["b_f"])[0][None, :], (P, 8))
    cw = f(inputs["conv_w"])[0]
    pk[:, 24:36] = cw.reshape(3, 4, P).transpose(2, 1, 0).reshape(P, 12)
    pk[:, 36:40] = f(inputs["conv_b"])[0].reshape(4, P).T
    pk[:, 40] = np.tile(f(inputs["q_norm_g"])[0], 2)
    pk[:, 41] = np.tile(f(inputs["k_norm_g"])[0], 2)
    maps = []
    for b in range(8):
        stt = np.ascontiguousarray(sc[b].reshape(2, 4, P).transpose(2, 1, 0).reshape(P, 8))
        maps.append({
            "x_prompt": x_prompt[b], "x_sample": x_sample[b],
            "cache_k": ck[b].reshape(PAST, 512), "cache_v": cv[b].reshape(PAST, 512),
            "cache_logf": cl[b], "meta": f(inputs["meta"]),
            "w_in": f(inputs["w_in"])[0], "w_br_conv": f(inputs["w_br_conv"])[0],
            "w_br_attn": f(inputs["w_br_attn"])[0], "w_out": f(inputs["w_out"])[0],
            "w_up": f(inputs["w_up"])[0], "w_down": f(inputs["w_down"])[0],
            "ppk": pk, "state_conv_t": stt,
        })
    return maps


_NC_CACHE = {}


def kernel(**inputs):
    maps = make_in_maps(inputs)
    if "nc" not in _NC_CACHE:
        _NC_CACHE["nc"] = build()
    nc = _NC_CACHE["nc"]
    res = run_bass_kernel_spmd(nc, maps, core_ids=list(range(8)))
    R = res.results
    st = lambda name: np.stack([np.asarray(R[b][name], dtype=np.float32) for b in range(8)])
    y_prompt = st("y_prompt")
    y_sample = st("y_sample")
    nk_p = st("nk_p").reshape(1, 8, L, H, HD)
    nv_p = st("nv_p").reshape(1, 8, L, H, HD)
    nf_p = st("nf_p").reshape(1, 8, L, H)
    nc_p = st("nc_p").reshape(1, 8, 2, DC)
    nk_s = st("nk_s").reshape(1, 8, NS, H, HD)
    nv_s = st("nv_s").reshape(1, 8, NS, H, HD)
    nf_s = st("nf_s").reshape(1, 8, NS, H)
    nc_s = st("nc_s").reshape(1, 8, 2, DC)
    return (y_prompt, y_sample, nk_p, nv_p, nf_p, nc_p, nk_s, nv_s, nf_s, nc_s)
```

```python
import os
import numpy as np
import concourse.bass as bass
import concourse.mybir as mybir
from concourse.bass_utils import run_bass_kernel_spmd

F32 = mybir.dt.float32
BF16 = mybir.dt.bfloat16
AF = mybir.ActivationFunctionType
ALU = mybir.AluOpType
AX = mybir.AxisListType

P = 128
D = 1024
KD = 8
SEQ = 2048
NMETA = 16
L = SEQ + NMETA
NS = 16
NT = L + NS
NTILE = 17
PAST = 2048
H = 8
HD = 64
DC = 512
DA = 512
DFF = 4096
INC = 5128
EPS = 1e-6
C_B, C_C, C_H, C_Q, C_K, C_V, C_F, C_G = 0, 512, 1024, 1536, 2048, 2560, 3072, 3080
BLKS = [(0, 512), (512, 512), (1024, 512), (1536, 512), (2048, 32)]
BLKS_EQ = [(416 * i, 416) for i in range(5)]


def tile_rows(i):
    return 128 if i < 16 else 32


class Sched:
    ENG = ("pe", "act", "dve", "pool", "sp")

    def __init__(self, nc):
        self.nc = nc
        self.q = {e: [] for e in self.ENG}
        self.cnt = {}
        self.sems = {}
        self.lastw = {}
        self.lastr = {}
        self.seen = {e: {} for e in self.ENG}
        self.pending = {e: {} for e in self.ENG}
        for e in ("pe", "act", "dve", "pool"):
            self._sem(e)

    def _sem(self, name):
        if name not in self.sems:
            self.sems[name] = self.nc.alloc_semaphore("s_" + name)
            self.cnt[name] = 0
        return self.sems[name]

    def _deps(self, eng, reads, writes):
        deps = dict(self.pending[eng])
        self.pending[eng] = {}

        def merge(src, raw):
            for s, v in src.items():
                if s == eng and eng == "pe":
                    continue
                if deps.get(s, 0) < v:
                    deps[s] = v
        for k in reads:
            merge(self.lastw.get(k, {}), True)
        for k in writes:
            merge(self.lastw.get(k, {}), False)
            merge(self.lastr.get(k, {}), False)
        out = []
        seen = self.seen[eng]
        for s, v in deps.items():
            if seen.get(s, 0) < v:
                seen[s] = v
                out.append((s, v))
        return out

    def _record(self, s, v, reads, writes):
        for k in reads:
            d = self.lastr.setdefault(k, {})
            if d.get(s, 0) < v:
                d[s] = v
        for k in writes:
            d = self.lastw.setdefault(k, {})
            if d.get(s, 0) < v:
                d[s] = v

    def op(self, eng, fn, reads=(), writes=(), sig=True):
        waits = self._deps(eng, reads, writes)
        if sig:
            self.cnt[eng] += 1
            v = self.cnt[eng]
            inc = (eng, 1)
        else:
            v = self.cnt[eng] + 1
            inc = None
        self._record(eng, v, reads, writes)
        self.q[eng].append((waits, fn, inc))

    def dma(self, queue, sem, fn, reads=(), writes=()):
        self._sem(sem)
        waits = self._deps(queue, reads, writes)
        self.cnt[sem] += 16
        self._record(sem, self.cnt[sem], reads, writes)
        self.q[queue].append((waits, fn, (sem, 16)))

    def barrier(self, engines=("pe", "act", "dve", "sp"), exclude=()):
        snap = {s: v for s, v in self.cnt.items() if v > 0 and s not in exclude
                and not s.startswith(("d_ring", "d_kc", "d_vc", "d_wfl"))}
        for e in engines:
            for s, v in snap.items():
                if s == e and e == "pe":
                    continue
                if self.pending[e].get(s, 0) < v:
                    self.pending[e][s] = v

    def finish(self, eng="sp"):
        waits = []
        for s, v in self.cnt.items():
            if v > 0 and self.seen[eng].get(s, 0) < v:
                waits.append((s, v))
        self.q[eng].append((waits, None, None))

    def replay(self, name, e):
        for waits, fn, inc in self.q[name]:
            for s, v in waits:
                e.wait_ge(self.sems[s], v)
            if fn is None:
                continue
            ins = fn(e)
            if inc is not None:
                ins.then_inc(self.sems[inc[0]], inc[1])

    def emit(self):
        nc = self.nc
        with nc.Block() as block:
            @block.tensor
            def _(e):
                self.replay("pe", e)

            @block.scalar
            def _(e):
                self.replay("act", e)

            @block.vector
            def _(e):
                self.replay("dve", e)

            @block.gpsimd
            def _(e):
                self.replay("pool", e)

            @block.sync
            def _(e):
                self.replay("sp", e)


class Mem:
    def __init__(self, nc):
        self.nc = nc
        self.base = 16512
        self.top = 229344
        self.n = 0

    def at(self, off, shape, dtype, name):
        self.n += 1
        nb = int(np.prod(shape[1:])) * (4 if dtype == F32 else 2)
        assert off % 32 == 0, (name, off)
        assert self.base <= off and off + nb <= self.top, (name, off, nb, self.top)
        return self.nc.alloc_sbuf_tensor_at("%s_%d" % (name, self.n), list(shape), dtype, offset=off)


def build(dbg=None, stop_after=99):
    dbg = dbg or []
    nc = bass.Bass("TRN2", target_bir_lowering=False)
    S = Sched(nc)
    M = Mem(nc)

    def din(name, shape):
        return nc.dram_tensor(name, list(shape), F32, kind="ExternalInput")

    def dout(name, shape):
        return nc.dram_tensor(name, list(shape), F32, kind="ExternalOutput")

    x_prompt = din("x_prompt", (SEQ, D))
    x_sample = din("x_sample", (NS, D))
    cache_k = din("cache_k", (PAST, 512))
    cache_v = din("cache_v", (PAST, 512))
    cache_logf = din("cache_logf", (PAST, H))
    meta = din("meta", (NMETA, D))
    w_in = din("w_in", (D, INC))
    w_br_conv = din("w_br_conv", (DC, D))
    w_br_attn = din("w_br_attn", (DA, D))
    w_out = din("w_out", (D, D))
    w_up = din("w_up", (D, DFF))
    w_down = din("w_down", (DFF, D))
    NPK = 64
    ppk = din("ppk", (P, NPK))
    stc = din("state_conv_t", (P, 8))

    y_prompt = dout("y_prompt", (SEQ, D))
    y_sample = dout("y_sample", (NS, D))
    nk_p = dout("nk_p", (L, 512))
    nv_p = dout("nv_p", (L, 512))
    nf_p = dout("nf_p", (L, H))
    nc_p = dout("nc_p", (2, DC))
    nk_s = dout("nk_s", (NS, 512))
    nv_s = dout("nv_s", (NS, 512))
    nf_s = dout("nf_s", (NS, H))
    nc_s = dout("nc_s", (2, DC))
    dbg_out = {}
    for (name, shape, dt_) in dbg:
        dbg_out[name] = nc.dram_tensor("dbg_" + name, list(shape), dt_, kind="ExternalOutput")

    if os.environ.get("PAIR_EXP", "1") == "1":
        PS2 = [nc.alloc_psum_tensor("ps2_%d" % i, [P, 1024], F32) for i in range(4)]
        PS = [PS2[i // 2][:, (i % 2) * 512:(i % 2 + 1) * 512] for i in range(8)]
    else:
        PS2 = None
        PS = [nc.alloc_psum_tensor("ps%d" % i, [P, 512], F32) for i in range(8)]

    o = M.base
    RING_SLOTS = 5
    ring = [M.at(o + i * 8192, (P, 4096), BF16, "ring") for i in range(RING_SLOTS)]
    o += RING_SLOTS * 8192
    pk = M.at(o, (P, NPK), F32, "pk"); o += NPK * 4
    identb = M.at(o, (P, P), BF16, "identb"); o += 256
    identf = M.at(o, (P, P), F32, "identf"); o += 512
    small = [o]
    o += 13312
    def sm(shape, dtype, name):
        nb = int(np.prod(shape[1:])) * (4 if dtype == F32 else 2)
        nb = (nb + 31) // 32 * 32
        t = M.at(small[0], shape, dtype, name)
        small[0] += nb
        assert small[0] <= o_small_end
        return t
    o_small_end = o
    AUG = o; o += 12544
    R2 = o; o += 33280
    R1 = o; o += 33280
    R3 = o; o += 34304
    R45 = o
    R45_SIZE = M.top - o
    assert R45_SIZE >= 41472, R45_SIZE

    xnT = M.at(R1, (P, KD, NT), BF16, "xnT")
    NXT = 6
    xt = [M.at(R3 + i * 4096, (P, D), F32, "xt") for i in range(2)] + \
         [M.at(R45 + 25600 + i * 4096, (P, D), F32, "xt") for i in range(4)]
    xs = [M.at(R3 + 8192 + i * 2048, (P, D), BF16, "xs") for i in range(2)] + [M.at(R45 + 41984, (P, D), BF16, "xs")]
    junk = M.at(R3 + 12288, (P, D), BF16, "junk")
    ssq = sm((P, 32), F32, "ssq")
    rstd = sm((P, 32), F32, "rstd")

    S.dma("sp", "d_pk", lambda e: e.dma_start(out=pk[:, :], in_=ppk[:, :]), writes=["pk"])
    def mk_ident(t, key):
        S.op("pool", lambda e: e.memset(t[:, :], 1.0), writes=[key])
        S.op("pool", lambda e: e.affine_select(t[:, :], t[:, :], [[-1, P]], ALU.is_equal, 0.0,
                                               base=0, channel_multiplier=1), reads=[key], writes=[key])
    mk_ident(identb, "identb")
    mk_ident(identf, "identf")

    epsc = sm((P, 1), F32, "epsc")
    S.op("pool", lambda e: e.memset(epsc[:, :], EPS), writes=["epsc"])

    def load_xtile(i, buf, key, sem):
        if i == 0:
            S.dma("sp", sem, lambda e: e.dma_start(out=buf[0:16, :], in_=meta[:, :]), writes=[key])
            S.dma("sp", sem, lambda e: e.dma_start(out=buf[16:128, :], in_=x_prompt[0:112, :]), writes=[key])
        elif i < 16:
            S.dma("sp", sem, lambda e: e.dma_start(out=buf[:, :], in_=x_prompt[128 * i - 16:128 * i + 112, :]), writes=[key])
        else:
            S.dma("sp", sem, lambda e: e.dma_start(out=buf[0:16, :], in_=x_prompt[2032:2048, :]), writes=[key])
            S.dma("sp", sem, lambda e: e.dma_start(out=buf[16:32, :], in_=x_sample[:, :]), writes=[key])

    def rms_rstd(src, r, col, inv_n, kin, tagk, junk=junk, jkey="junk"):
        S.op("act", lambda e: e.activation(out=junk[0:r, :], in_=src, func=AF.Square,
                                           accum_out=ssq[0:r, col:col + 1]),
             reads=[kin, "epsc"], writes=[jkey, "ssq%s%d" % (tagk, col)])
        S.op("act", lambda e: e.activation(out=ssq[0:r, col:col + 1], in_=ssq[0:r, col:col + 1], func=AF.Ln,
                                           bias=epsc[0:r, :], scale=inv_n),
             reads=["ssq%s%d" % (tagk, col)], writes=["ssq%s%d" % (tagk, col)])
        S.op("act", lambda e: e.activation(out=rstd[0:r, col:col + 1], in_=ssq[0:r, col:col + 1], func=AF.Exp,
                                           scale=-0.5),
             reads=["ssq%s%d" % (tagk, col)], writes=["rstd%s%d" % (tagk, col)])

    for i in range(NTILE):
        r = tile_rows(i)
        b = i % NXT
        b3 = i % 3
        kx, ks = "xt%d" % b, "xs%d" % b3
        load_xtile(i, xt[b], kx, "d_xt%d" % b)
        rms_rstd(xt[b][0:r, :], r, i, 1.0 / D, kx, "a")
        S.op("dve", lambda e, b=b, b3=b3, r=r, i=i: e.tensor_scalar(out=xs[b3][0:r, :], in0=xt[b][0:r, :],
                                                          scalar1=rstd[0:r, i:i + 1], scalar2=None, op0=ALU.mult),
             reads=[kx, "rstda%d" % i], writes=[ks])
        pb = i % 2
        pst = PS[pb][:, :].bitcast(BF16)
        for k in range(KD):
            S.op("pe", lambda e, k=k, b3=b3, r=r, pst=pst: e.transpose(out=pst[:, k * 128:k * 128 + r],
                                                                    in_=xs[b3][0:r, k * 128:(k + 1) * 128],
                                                                    identity=identb[0:r, 0:r]),
                 reads=[ks, "identb"], writes=["ps%d" % pb], sig=(k == KD - 1))
        c0 = 128 * i
        S.op("dve", lambda e, pst=pst, r=r, c0=c0: e.tensor_tensor(
            out=xnT[:, :, c0:c0 + r],
            in0=pst.rearrange("p (k t) -> p k t", k=KD)[:, :, 0:r],
            in1=pk[:, 0:KD].unsqueeze(2).to_broadcast([P, KD, r]), op=ALU.mult),
             reads=["ps%d" % pb, "pk"], writes=["xnT%d" % i])

    if "xnT" in dbg_out:
        S.dma("sp", "d_dbg", lambda e: e.dma_start(out=dbg_out["xnT"][:, :, :], in_=xnT[:, :, :]),
              reads=["xnT%d" % i for i in range(NTILE)])

    class _Stop(Exception):
        pass

    def chk(st):
        if stop_after < st:
            raise _Stop()

    try:
        XN_ALL = ["xnT%d" % i for i in range(NTILE)]

        def xn_keys(c0, n):
            return ["xnT%d" % i for i in range(c0 // 128, (c0 + n - 1) // 128 + 1)]

        wlist = []

        def wg_cols(w, c0, kch=KD, ncol=512):
            return (lambda t: t[:, 0:kch * ncol].rearrange("p (k c) -> p k c", k=kch),
                    w[0:kch * 128, c0:c0 + ncol].rearrange("(k p) c -> p k c", p=P))

        G_Q, G_K, G_V = 0, 1, 2
        wlist.append(wg_cols(w_in, C_Q))
        wlist.append(wg_cols(w_in, C_K))
        wlist.append(wg_cols(w_in, C_V))
        G_CV = [3, 4, 5, 6]
        for cch in range(4):
            wlist.append((lambda t: t[:, 0:KD * 384].rearrange("p (k c) -> p k c", k=KD),
                          [((128 * j3, 128 * (j3 + 1)),
                            w_in[:, base + 128 * cch:base + 128 * (cch + 1)].rearrange("(k p) c -> p k c", p=P))
                           for j3, base in enumerate((C_B, C_C, C_H))]))
        G_BRC, G_BRA = 7, 8
        G_GP = [9, 10, 11, 12]
        wlist.append(wg_cols(w_br_conv, 0, kch=4, ncol=1024))
        wlist.append(wg_cols(w_br_attn, 0, kch=4, ncol=1024))
        for pr in range(4):
            wlist.append((lambda t: t[:, 0:KD * 512].rearrange("p (k c) -> p k c", k=KD),
                          [((0, 256), w_in[:, C_G + 256 * pr:C_G + 256 * (pr + 1)].rearrange("(k p) c -> p k c", p=P)),
                           ((256, 512), w_in[:, C_G + 1024 + 256 * pr:C_G + 1024 + 256 * (pr + 1)].rearrange("(k p) c -> p k c", p=P))]))
        G_OUT0 = 13
        wlist.append(wg_cols(w_out, 0))
        wlist.append(wg_cols(w_out, 512))
        G_MLP = 15
        for g in range(8):
            wlist.append(wg_cols(w_up, 512 * g))
            wlist.append((lambda t: t[:, :].rearrange("p (k c) -> p k c", k=4),
                          w_down[512 * g:512 * (g + 1), :].rearrange("(k p) c -> p k c", p=P)))
        wstate = {"issued": 0, "free": list(range(RING_SLOTS)), "slot": {}}

        def w_try_issue(limit=None, after=()):
            while wstate["issued"] < len(wlist) and wstate["free"] and (limit is None or wstate["issued"] < limit):
                g = wstate["issued"]
                slot = wstate["free"].pop(0)
                wstate["slot"][g] = slot
                vf, srcs = wlist[g]
                dst = vf(ring[slot])
                if not isinstance(srcs, list):
                    srcs = [(None, srcs)]
                for (sub, src) in srcs:
                    d = dst if sub is None else dst[:, :, sub[0]:sub[1]]
                    S.dma("pool", "d_ring%d" % slot, lambda e, d=d, src=src: e.dma_start(out=d, in_=src),
                          reads=list(after), writes=["ring%d" % slot])
                wstate["issued"] += 1

        def w_get(g):
            if g not in wstate["slot"]:
                w_try_issue(g + 1)
            slot = wstate["slot"][g]
            return wlist[g][0](ring[slot]), "ring%d" % slot

        def w_done(g):
            wstate["free"].append(wstate["slot"][g])
            w_try_issue()

        w_try_issue(1)
        w_try_issue(2, after=["xt%d" % (7 % NXT)])
        w_try_issue(3, after=["xt%d" % (12 % NXT)])

        blockones = sm((P, P), BF16, "blockones")
        S.op("pool", lambda e: e.memset(blockones[:, :], 0.0), writes=["blockones"])
        S.op("pool", lambda e: e.memset(blockones[0:64, 0:64], 1.0), writes=["blockones"])
        S.op("pool", lambda e: e.memset(blockones[64:128, 64:128], 1.0), writes=["blockones"])
        onesf = sm((P, P), F32, "onesf")
        S.op("pool", lambda e: e.memset(onesf[:, :], 1.0), writes=["onesf"])
        trif = sm((P, P), F32, "trif")
        S.op("pool", lambda e: e.memset(trif[:, :], 1.0), writes=["trif"])
        S.op("pool", lambda e: e.affine_select(trif[:, :], trif[:, :], [[1, P]], ALU.is_ge, 0.0,
                                               base=0, channel_multiplier=-1), reads=["trif"], writes=["trif"])
        maskb = sm((P, P), BF16, "maskb")
        S.op("pool", lambda e: e.memset(maskb[:, :], 1.0), writes=["maskb"])
        S.op("pool", lambda e: e.affine_select(maskb[:, :], maskb[:, :], [[1, P]], ALU.is_ge, 0.0,
                                               base=0, channel_multiplier=-1), reads=["maskb"], writes=["maskb"])
        maskneg = sm((P, P), BF16, "maskneg")
        S.op("pool", lambda e: e.memset(maskneg[:, :], 0.0), writes=["maskneg"])
        S.op("pool", lambda e: e.affine_select(maskneg[:, :], maskneg[:, :], [[1, P]], ALU.is_ge, -9984.0,
                                               base=0, channel_multiplier=-1), reads=["maskneg"], writes=["maskneg"])
        ones3 = sm((3, P), BF16, "ones3")
        S.op("pool", lambda e: e.memset(ones3[:, :], 1.0), writes=["ones3"])
        onecol = sm((P, 1), F32, "onecol")
        S.op("pool", lambda e: e.memset(onecol[:, :], 1.0), writes=["onecol"])
        wfl = sm((P, KD, 8), BF16, "wfl")
        if not os.environ.get("SKIP_WFL"):
          S.dma("pool", "d_wfl", lambda e: e.dma_start(out=wfl[:, :, :],
                                                     in_=w_in[:, C_F:C_F + 8].rearrange("(k p) c -> p k c", p=P)),
              writes=["wfl"])
        fl_all = sm((P, NTILE, 8), F32, "fl_all")
        lf_all = sm((P, NTILE, 8), F32, "lf_all")
        S.op("pool", lambda e: e.memset(fl_all[:, :, :], 0.0), writes=["fl_all"])
        S.op("pool", lambda e: e.memset(lf_all[:, :, :], 0.0), writes=["lf_all"])
        c_all = sm((P, NTILE, 8), F32, "c_all")
        S.op("pool", lambda e: e.memset(c_all[:, :, :], 0.0), writes=["c_all"])
        negc = sm((P, NTILE, 8), F32, "negc")
        csplit = sm((P, NTILE, 3, 8), BF16, "csplit")
        cres = sm((P, NTILE, 8), F32, "cres")
        cT24 = M.at(AUG, (24, NT), BF16, "cT24")
        qaug = [M.at(AUG + 4160 + i * 4160, (3, NT), BF16, "qaug") for i in range(2)]

        qT = M.at(R2, (P, 4, NT), BF16, "qT")
        kT = M.at(R2 + 16640, (P, 4, NT), BF16, "kT")
        Vp = M.at(R3 + 14336, (P, NTILE, H, 66), BF16, "Vp")
        sqb = [M.at(R45 + i * 1024, (P, 512), BF16, "sqb") for i in range(2)]
        rsb = [M.at(R45 + 2048 + i * 2048, (P, 512), F32, "rsb") for i in range(2)]
        kf = M.at(R45 + 6144, (P, 4, 512), F32, "kf")
        ktok = [M.at(R45 + 14336 + i * 2048, (P, 512), F32, "ktok") for i in range(2)]
        vtok = [M.at(R45 + 18432 + i * 2048, (P, 512), F32, "vtok") for i in range(2)]

        psn = {"n": 0}

        def next_ps(lo=0, hi=8, key="n"):
            psn[key] = psn.get(key, lo - 1) + 1
            if psn[key] >= hi or psn[key] < lo:
                psn[key] = lo
            return psn[key]

        if not os.environ.get("SKIP_VPMEM"):
            S.op("pool", lambda e: e.memset(Vp[:, :, :, 64:65], 1.0), writes=["Vp_ones"])

        chk(1)
        sqb3 = [M.at(R45 + 22528 + i * 1024, (P, 512), BF16, "sqb3") for i in range(3)]
        cnt1 = {"kt": 0}
        units1 = []
        for which, G in (("q", G_Q), ("k", G_K)):
            for (c0, n) in BLKS:
                for m in range(4):
                    units1.append(dict(which=which, G=G, c0=c0, n=n, m=m, idx=len(units1)))

        def s1A(u):
            which, G, c0, n, m = u["which"], u["G"], u["c0"], u["n"], u["m"]
            wv, wkey = w_get(G)
            A = next_ps(0, 4, "qa")
            sj = u["idx"] % 3
            u["A"], u["sj"] = A, sj
            for k in range(KD):
                S.op("pe", lambda e, k=k: e.matmul(
                    PS[A][:, 0:n], lhsT=wv[:, k, m * 128:(m + 1) * 128], rhs=xnT[:, k, c0:c0 + n],
                    start=(k == 0), stop=(k == KD - 1)),
                     reads=[wkey] + xn_keys(c0, n), writes=["ps%d" % A], sig=(k == KD - 1))
            S.op("act", lambda e: e.activation(out=sqb3[sj][:, 0:n], in_=PS[A][:, 0:n], func=AF.Square),
                 reads=["ps%d" % A], writes=["sqb%d" % sj])

        def s1B(u):
            which, G, c0, n, m = u["which"], u["G"], u["c0"], u["n"], u["m"]
            A, sj = u["A"], u["sj"]
            j = u["idx"] % 2
            B = next_ps(4, 6, "qb")
            S.op("pe", lambda e: e.matmul(PS[B][:, 0:n], lhsT=blockones[:, :], rhs=sqb3[sj][:, 0:n], start=True, stop=True),
                 reads=["sqb%d" % sj, "blockones"], writes=["ps%d" % B])
            S.op("act", lambda e: e.activation(out=rsb[j][:, 0:n], in_=PS[B][:, 0:n], func=AF.Ln,
                                               bias=epsc[:, :], scale=1.0 / HD),
                 reads=["ps%d" % B, "epsc"], writes=["rsb%d" % j])
            S.op("act", lambda e: e.activation(out=rsb[j][:, 0:n], in_=rsb[j][:, 0:n], func=AF.Exp, scale=-0.5),
                 reads=["rsb%d" % j], writes=["rsb%d" % j])
            if which == "q":
                S.op("dve", lambda e: e.scalar_tensor_tensor(
                    out=qT[:, m, c0:c0 + n], in0=PS[A][:, 0:n], scalar=pk[:, 40:41], in1=rsb[j][:, 0:n],
                    op0=ALU.mult, op1=ALU.mult),
                     reads=["ps%d" % A, "rsb%d" % j, "pk"], writes=["qT%d_%d" % (m, c0)])
            else:
                S.op("dve", lambda e: e.scalar_tensor_tensor(
                    out=kf[:, m, 0:n], in0=PS[A][:, 0:n], scalar=pk[:, 41:42], in1=rsb[j][:, 0:n],
                    op0=ALU.mult, op1=ALU.mult),
                     reads=["ps%d" % A, "rsb%d" % j, "pk"], writes=["kf%d" % m])
                S.op("dve", lambda e: e.tensor_copy(out=kT[:, m, c0:c0 + n], in_=kf[:, m, 0:n]),
                     reads=["kf%d" % m], writes=["kT%d_%d" % (m, c0)])
                if m == 3:
                    for tt in range((n + 127) // 128):
                        r = min(128, n - tt * 128)
                        jj = cnt1["kt"] % 2
                        cnt1["kt"] += 1
                        Cb = next_ps(6, 8, "kt")
                        for mm in range(4):
                            S.op("pe", lambda e, Cb=Cb, mm=mm, tt=tt, r=r: e.transpose(
                                out=PS[Cb][0:r, mm * 128:(mm + 1) * 128], in_=kf[:, mm, tt * 128:tt * 128 + r], identity=identf[:, :]),
                                 reads=["kf%d" % mm, "identf"], writes=["ps%d" % Cb], sig=(mm == 3))
                        S.op("dve", lambda e, Cb=Cb, jj=jj, r=r: e.tensor_copy(out=ktok[jj][0:r, :], in_=PS[Cb][0:r, :]),
                             reads=["ps%d" % Cb], writes=["ktok%d" % jj])
                        p0 = c0 + tt * 128
                        if p0 < 2048:
                            S.dma("sp", "d_ktok%d" % jj, lambda e, jj=jj, p0=p0: e.dma_start(out=nk_p[p0:p0 + 128, :], in_=ktok[jj][:, :]),
                                  reads=["ktok%d" % jj], writes=["nk_p_%d" % p0])
                        else:
                            S.dma("sp", "d_ktok%d" % jj, lambda e, jj=jj: e.dma_start(out=nk_p[2048:2064, :], in_=ktok[jj][0:16, :]),
                                  reads=["ktok%d" % jj], writes=["nk_p_%d" % p0])
                            S.dma("sp", "d_ktok%d" % jj, lambda e, jj=jj: e.dma_start(out=nk_s[:, :], in_=ktok[jj][16:32, :]),
                                  reads=["ktok%d" % jj], writes=["nk_s"])
            if m == 3 and c0 == 2048:
                w_done(G)

        LA1 = 2
        for i in range(LA1):
            s1A(units1[i])
        for i in range(len(units1)):
            s1B(units1[i])
            if i + LA1 < len(units1):
                s1A(units1[i + LA1])

        chk(2)
        wv, wkey = w_get(G_V)
        FB = 4
        for i in range(NTILE):
            r = tile_rows(i)
            c0 = 128 * i
            for k in range(KD):
                S.op("pe", lambda e, k=k, r=r, c0=c0, i=i: e.matmul(
                    PS[FB][0:r, 8 * i:8 * i + 8], lhsT=xnT[:, k, c0:c0 + r], rhs=wfl[:, k, :], start=(k == 0), stop=(k == KD - 1)),
                     reads=["wfl", "xnT%d" % i], writes=["ps%d" % FB], sig=(k == KD - 1))
        S.op("dve", lambda e: e.tensor_tensor(out=fl_all[:, 0:16, :], in0=PS[FB][:, 0:128].rearrange("p (i h) -> p i h", h=8),
                                              in1=pk[:, 16:24].unsqueeze(1).to_broadcast([P, 16, 8]), op=ALU.add),
             reads=["ps%d" % FB, "pk"], writes=["fl_all"])
        S.op("dve", lambda e: e.tensor_tensor(out=fl_all[0:32, 16, :], in0=PS[FB][0:32, 128:136], in1=pk[0:32, 16:24], op=ALU.add),
             reads=["ps%d" % FB, "pk"], writes=["fl_all"])

        def logsig(dst, src, r, kin, kout):
            S.op("act", lambda e: e.activation(out=dst, in_=src, func=AF.Exp, scale=-1.0), reads=[kin], writes=[kout])
            S.op("act", lambda e: e.activation(out=dst, in_=dst, func=AF.Ln, bias=onecol[0:r, :], scale=1.0),
                 reads=[kout, "onecol"], writes=[kout])
            S.op("dve", lambda e: e.tensor_scalar(out=dst, in0=dst, scalar1=-1.0, scalar2=None, op0=ALU.mult),
                 reads=[kout], writes=[kout])
        logsig(lf_all[:, 0:16, :], fl_all[:, 0:16, :], P, "fl_all", "lf_all")
        logsig(lf_all[0:32, 16, :], fl_all[0:32, 16, :], 32, "fl_all", "lf_all")
        S.dma("sp", "d_lf", lambda e: e.dma_start(out=nf_p[0:2048, :].rearrange("(i p) h -> p i h", p=P), in_=lf_all[:, 0:16, :]),
              reads=["lf_all"], writes=["nf_p_a"])
        S.dma("sp", "d_lf", lambda e: e.dma_start(out=nf_p[2048:2064, :], in_=lf_all[0:16, 16, :]), reads=["lf_all"], writes=["nf_p_b"])
        S.dma("sp", "d_lf", lambda e: e.dma_start(out=nf_s[:, :], in_=lf_all[16:32, 16, :]), reads=["lf_all"], writes=["nf_s"])

        carr = sm((P, NTILE, 8), F32, "carr")
        lfc = sm((P, 16, 8), F32, "lfc")
        S.dma("sp", "d_lfc", lambda e: e.dma_start(out=lfc[:, :, :], in_=cache_logf[:, :].rearrange("(i p) h -> p i h", p=P)),
              writes=["lfc"])
        c_s = sm((P, NTILE, 8), F32, "c_s")
        S.op("pool", lambda e: e.memset(c_s[:, :, :], 0.0), writes=["c_s"])
        negc_s = sm((P, NTILE, 8), F32, "negc_s")
        msel = sm((32, 16), F32, "msel")
        S.op("pool", lambda e: e.memset(msel[:, :], 1.0), writes=["msel"])
        S.op("pool", lambda e: e.affine_select(msel[:, :], msel[:, :], [[1, 16]], ALU.is_ge, 0.0,
                                               base=16, channel_multiplier=-1), reads=["msel"], writes=["msel"])
        S.op("pool", lambda e: e.memset(msel[0:16, :], 0.0), reads=["msel"], writes=["msel"])
        carr_s = sm((P, 17, 8), F32, "carr_s")
        Cb = 5
        S.op("pe", lambda e: e.matmul(PS[Cb][:, 0:136], lhsT=trif[:, :], rhs=lf_all[:, :, :].rearrange("p i h -> p (i h)"), start=True, stop=True),
             reads=["lf_all", "trif"], writes=["ps%d" % Cb])
        S.op("pe", lambda e: e.matmul(PS[Cb][:, 136:272], lhsT=onesf[:, :], rhs=lf_all[:, :, :].rearrange("p i h -> p (i h)"), start=True, stop=True),
             reads=["lf_all", "onesf"], writes=["ps%d" % Cb])
        Cs = 6
        S.op("pe", lambda e: e.matmul(PS[Cs][:, 0:128], lhsT=trif[:, :], rhs=lfc[:, :, :].rearrange("p i h -> p (i h)"), start=True, stop=True),
             reads=["lfc", "trif"], writes=["ps%d" % Cs])
        S.op("pe", lambda e: e.matmul(PS[Cs][:, 128:256], lhsT=onesf[:, :], rhs=lfc[:, :, :].rearrange("p i h -> p (i h)"), start=True, stop=True),
             reads=["lfc", "onesf"], writes=["ps%d" % Cs])
        S.op("pe", lambda e: e.matmul(PS[Cs][0:16, 256:264], lhsT=msel[:, :], rhs=lf_all[0:32, 16, :], start=True, stop=True),
             reads=["lf_all", "msel"], writes=["ps%d" % Cs])

        chain = []

        def DF(fn, **kw):
            chain.append(lambda: S.op("dve", fn, **kw))
        DF(lambda e: e.memset(carr[:, 0, :], 0.0), writes=["carr"])
        for i in range(1, NTILE):
            DF(lambda e, i=i: e.tensor_tensor(out=carr[:, i, :], in0=carr[:, i - 1, :], in1=PS[Cb][:, 136 + 8 * (i - 1):136 + 8 * i], op=ALU.add),
              reads=["carr", "ps%d" % Cb], writes=["carr"])
        DF(lambda e: e.tensor_tensor(out=c_all[:, :, :], in0=carr[:, :, :], in1=PS[Cb][:, 0:136].rearrange("p (i h) -> p i h", h=8), op=ALU.add),
          reads=["carr", "ps%d" % Cb], writes=["c_all"])
        key = "c_all"
        DF(lambda e: e.tensor_scalar(out=negc[:, :, :], in0=c_all[:, :, :], scalar1=-1.0, scalar2=None, op0=ALU.mult),
          reads=[key], writes=[key + "_neg"])
        DF(lambda e: e.tensor_copy(out=csplit[:, :, 0, :], in_=c_all[:, :, :]), reads=[key], writes=[key + "_s"])
        DF(lambda e: e.tensor_tensor(out=cres[:, :, :], in0=c_all[:, :, :], in1=csplit[:, :, 0, :], op=ALU.subtract),
          reads=[key, key + "_s"], writes=[key + "_r"])
        DF(lambda e: e.tensor_copy(out=csplit[:, :, 1, :], in_=cres[:, :, :]), reads=[key + "_r"], writes=[key + "_s"])
        DF(lambda e: e.tensor_tensor(out=cres[:, :, :], in0=cres[:, :, :], in1=csplit[:, :, 1, :], op=ALU.subtract),
          reads=[key + "_r", key + "_s"], writes=[key + "_r"])
        DF(lambda e: e.tensor_copy(out=csplit[:, :, 2, :], in_=cres[:, :, :]), reads=[key + "_r"], writes=[key + "_s"])
        DF(lambda e: e.memset(carr_s[:, 0, :], 0.0), writes=["carr_s"])
        for i in range(1, 17):
            DF(lambda e, i=i: e.tensor_tensor(out=carr_s[:, i, :], in0=carr_s[:, i - 1, :], in1=PS[Cs][:, 128 + 8 * (i - 1):128 + 8 * i], op=ALU.add),
              reads=["carr_s", "ps%d" % Cs], writes=["carr_s"])
        DF(lambda e: e.tensor_tensor(out=c_s[:, 0:16, :], in0=carr_s[:, 0:16, :], in1=PS[Cs][:, 0:128].rearrange("p (i h) -> p i h", h=8), op=ALU.add),
          reads=["carr_s", "ps%d" % Cs], writes=["c_s"])
        DF(lambda e: e.tensor_tensor(out=c_s[:, 0:16, :], in0=c_s[:, 0:16, :],
                                    in1=carr_s[:, 16, :].unsqueeze(1).to_broadcast([P, 16, 8]), op=ALU.subtract),
          reads=["carr_s", "c_s"], writes=["c_s"])
        DF(lambda e: e.tensor_copy(out=c_s[0:16, 16, :], in_=PS[Cs][0:16, 256:264]), reads=["ps%d" % Cs], writes=["c_s"])
        DF(lambda e: e.tensor_scalar(out=negc_s[:, :, :], in0=c_s[:, :, :], scalar1=-1.0, scalar2=None, op0=ALU.mult),
          reads=["c_s"], writes=["c_s_neg"])

        def run_chain(n):
            for _ in range(n):
                if chain:
                    chain.pop(0)()

        for i in range(NTILE):
            r = tile_rows(i)
            c0 = 128 * i
            jj = i % 2
            A = next_ps(0, 4, "vp")
            for k in range(KD):
                S.op("pe", lambda e, A=A, k=k, r=r, c0=c0, wv=wv: e.matmul(
                    PS[A][0:r, :], lhsT=xnT[:, k, c0:c0 + r], rhs=wv[:, k, :], start=(k == 0), stop=(k == KD - 1)),
                     reads=[wkey, "xnT%d" % i], writes=["ps%d" % A], sig=(k == KD - 1))
            S.op("dve", lambda e, A=A, jj=jj, r=r: e.tensor_copy(out=vtok[jj][0:r, :], in_=PS[A][0:r, :]),
                 reads=["ps%d" % A], writes=["vtok%d" % jj])
            S.op("dve", lambda e, A=A, r=r, i=i: e.tensor_copy(
                out=Vp[0:r, i, :, 0:64], in_=PS[A][0:r, :].rearrange("p (h d) -> p h d", h=H)),
                 reads=["ps%d" % A], writes=["Vp%d" % i])
            if i < 16:
                S.dma("sp", "d_vtok%d" % jj, lambda e, jj=jj, c0=c0: e.dma_start(out=nv_p[c0:c0 + 128, :], in_=vtok[jj][:, :]),
                      reads=["vtok%d" % jj], writes=["nv_p_%d" % i])
            else:
                S.dma("sp", "d_vtok%d" % jj, lambda e, jj=jj: e.dma_start(out=nv_p[2048:2064, :], in_=vtok[jj][0:16, :]),
                      reads=["vtok%d" % jj], writes=["nv_p_%d" % i])
                S.dma("sp", "d_vtok%d" % jj, lambda e, jj=jj: e.dma_start(out=nv_s[:, :], in_=vtok[jj][16:32, :]),
                      reads=["vtok%d" % jj], writes=["nv_s"])
            run_chain(4)
        run_chain(len(chain))
        Vsn = sm((16, H, 66), BF16, "Vsn")
        S.op("pool", lambda e: e.memset(Vsn[:, :, 64:65], 1.0), writes=["Vsn_ones"])
        A = next_ps(0, 4, "vp")
        for k in range(KD):
            S.op("pe", lambda e, A=A, k=k, wv=wv: e.matmul(PS[A][0:16, :], lhsT=xnT[:, k, L:NT], rhs=wv[:, k, :],
                                                          start=(k == 0), stop=(k == KD - 1)),
                 reads=[wkey, "xnT16"], writes=["ps%d" % A], sig=(k == KD - 1))
        S.op("dve", lambda e, A=A: e.tensor_copy(out=Vsn[:, :, 0:64], in_=PS[A][0:16, :].rearrange("p (h d) -> p h d", h=H)),
             reads=["ps%d" % A], writes=["Vsn"])
        w_done(G_V)

        for bnk in range(3):
            Tb = next_ps(4, 8, "ct")
            pst = PS[Tb][:, :].bitcast(BF16)
            tiles = list(range(8 * bnk, min(NTILE, 8 * bnk + 8)))
            for i in tiles:
                r = 128 if i < 16 else 16
                S.op("pe", lambda e, i=i, r=r, pst=pst: e.transpose(
                    out=pst[0:24, 128 * (i % 8):128 * (i % 8) + r], in_=csplit[0:r, i, :, :].rearrange("p j h -> p (j h)"),
                    identity=identb[0:r, 0:r]),
                     reads=["c_all_s", "identb"], writes=["ps%d" % Tb], sig=(i == tiles[-1]))
            w0 = 128 * tiles[0]
            wn = sum(128 if i < 16 else 16 for i in tiles)
            S.op("dve", lambda e, pst=pst, w0=w0, wn=wn: e.tensor_scalar(out=cT24[:, w0:w0 + wn], in0=pst[0:24, 0:wn],
                                                                  scalar1=8.0, scalar2=None, op0=ALU.mult),
                 reads=["ps%d" % Tb], writes=["c_all_T"])

        chk(6)
        S.barrier(engines=("pe", "act", "dve", "sp", "pool"))

        attnT = M.at(R45, (P, 4, NT), BF16, "attnT")
        attn_tok = M.at(R45 + 16640, (P, NTILE, 512), BF16, "attn_tok")
        NPB = 4
        Pb = [M.at(R45 + 34048 + i * 2048, (P, 1024), BF16, "Pb") for i in range(NPB)]
        Qh = [M.at(R45 + i * 4160, (P, NT), BF16, "Qh") for i in range(2)]
        Kh = [M.at(R45 + 8320 + i * 4160, (P, NT), BF16, "Kh") for i in range(2)]
        ncT24 = M.at(AUG + 4160, (24, NT), BF16, "ncT24")
        S.op("dve", lambda e: e.tensor_scalar(out=ncT24[:, 0:L], in0=cT24[:, 0:L], scalar1=-1.0, scalar2=None, op0=ALU.mult),
             reads=["c_all_T"], writes=["ncT24"])
        for i in range(2):
            S.op("pool", lambda e, i=i: e.memset(Qh[i][64:128, :], 0.0), writes=["Qh%d" % i])
            S.op("pool", lambda e, i=i: e.memset(Kh[i][64:128, :], 0.0), writes=["Kh%d" % i])
        S.op("dve", lambda e: e.memset(Qh[0][64:70, :], 1.0), writes=["Qh0"])
        S.dma("sp", "d_qh1", lambda e: e.dma_start(out=Qh[1][67:70, 0:L], in_=Qh[0][67:70, 0:L]), reads=["Qh0"], writes=["Qh1"])
        S.dma("sp", "d_kh0", lambda e: e.dma_start(out=Kh[0][64:67, 0:L], in_=Qh[0][67:70, 0:L]), reads=["Qh0"], writes=["Kh0"])
        S.dma("sp", "d_kh1", lambda e: e.dma_start(out=Kh[1][64:67, 0:L], in_=Qh[0][67:70, 0:L]), reads=["Qh0"], writes=["Kh1"])
        rsum = sm((P, 4), F32, "rsum")
        QG = [(0, 512), (512, 512), (1024, 512), (1536, 512), (2048, 16)]
        units = []
        for h in range(H):
            for gi, (q0, qn) in enumerate(QG):
                kt_last = (q0 + qn - 1) // 128
                kts = list(range(kt_last + 1))
                groups = []
                full = [kt for kt in kts if kt * 128 < q0 and qn == 512]
                rest = [kt for kt in kts if kt not in full]
                for j in range(0, len(full), 2):
                    groups.append(full[j:j + 2])
                packed_from = len(groups)
                if qn < 128:
                    groups.append(rest)
                else:
                    groups.append(rest[:-2])
                    packed_from = len(groups)
                    groups.append(rest[-2:])
                for gj, g in enumerate(groups):
                    units.append(dict(h=h, q0=q0, qn=qn, kts=g, packed=(gj >= packed_from), first_h=(gi == 0 and gj == 0), first_g=(gj == 0),
                                      last_g=(gj == len(groups) - 1), idx=len(units)))
        ostate = {}

        def emitA(u):
            h, q0, qn, kts = u["h"], u["q0"], u["qn"], u["kts"]
            hp, hoff = h // 2, (h % 2) * 64
            hb = h % 2
            nqb = (qn + 127) // 128
            if u["first_h"]:
                S.dma("sp", "d_qh%d" % hb, lambda e: e.dma_start(out=Qh[hb][0:64, :], in_=qT[hoff:hoff + 64, hp, :]),
                      reads=["qT_all"], writes=["Qh%d" % hb])
                S.dma("sp", "d_kh%d" % hb, lambda e: e.dma_start(out=Kh[hb][0:64, :], in_=kT[hoff:hoff + 64, hp, :]),
                      reads=["kT_all"], writes=["Kh%d" % hb])
                for j3 in range(3):
                    S.dma("sp", "d_qh%d" % hb, lambda e, j3=j3: e.dma_start(
                        out=Qh[hb][64 + j3:65 + j3, 0:L], in_=cT24[8 * j3 + h:8 * j3 + h + 1, 0:L]),
                          reads=["c_all_T"], writes=["Qh%d" % hb])
                    S.dma("sp", "d_kh%d" % hb, lambda e, j3=j3: e.dma_start(
                        out=Kh[hb][67 + j3:68 + j3, 0:L], in_=ncT24[8 * j3 + h:8 * j3 + h + 1, 0:L]),
                          reads=["ncT24"], writes=["Kh%d" % hb])
            if u["first_g"]:
                if qn < 128:
                    ostate[(h, q0)] = (ostate[(h, QG[3][0])][0], 260)
                else:
                    O = next_ps(6, 8, "o")
                    ostate[(h, q0)] = (O, 0)
                    zc = 325 if q0 == QG[3][0] else 65 * nqb
                    S.op("dve", lambda e, O=O, zc=zc: e.memset(PS[O][:, 0:zc], 0.0), writes=["ps%d" % O])
            pp = next_ps(0, 3, "sp2")
            pj = u["idx"] % NPB
            u["pj"] = pj
            u["geo"] = []
            multi = u["packed"]
            cum = 0
            for hf, kt in enumerate(kts):
                kr = 128 if kt < 16 else 16
                qs = max(q0, kt * 128)
                nn = q0 + qn - qs
                coff = cum if multi else hf * 512
                cum += nn
                u["geo"].append((kt, kr, qs, nn, coff))
                diag = (kt * 128 >= q0)
                dst = PS2[pp][0:kr, coff:coff + nn]
                S.op("pe", lambda e, dst=dst, kt=kt, kr=kr, qs=qs, nn=nn, diag=diag: e.matmul(
                    dst, lhsT=Kh[hb][:, kt * 128:kt * 128 + kr], rhs=Qh[hb][:, qs:qs + nn], start=True, stop=not diag),
                     reads=["Kh%d" % hb, "Qh%d" % hb], writes=["ps%d" % (2 * pp), "ps%d" % (2 * pp + 1)], sig=not diag)
                if diag:
                    dn = min(128, nn)
                    S.op("pe", lambda e, kr=kr, dn=dn, coff=coff: e.matmul(
                        PS2[pp][0:kr, coff:coff + dn],
                        lhsT=identb[0:kr, 0:kr], rhs=maskneg[0:kr, 0:dn], start=False, stop=True),
                         reads=["identb", "maskneg"], writes=["ps%d" % (2 * pp), "ps%d" % (2 * pp + 1)])
            if len(kts) == 2 and not multi:
                S.op("act", lambda e: e.activation(out=Pb[pj][:, :], in_=PS2[pp][:, :], func=AF.Exp, scale=0.125),
                     reads=["ps%d" % (2 * pp), "ps%d" % (2 * pp + 1)], writes=["Pb%d" % pj])
            elif False:
                for hf in range(2):
                    S.op("act", lambda e, hf=hf: e.activation(out=Pb[pj][:, hf * 512:(hf + 1) * 512],
                                                             in_=(PS2[pp][:, hf * 512:(hf + 1) * 512] if PS2 is not None else PS[2 * pp + hf][:, :]),
                                                             func=AF.Exp, scale=0.125),
                         reads=["ps%d" % (2 * pp), "ps%d" % (2 * pp + 1)], writes=["Pb%d" % pj])
            elif multi:
                wtot = u["geo"][-1][4] + u["geo"][-1][3]
                S.op("act", lambda e: e.activation(out=Pb[pj][:, 0:wtot], in_=PS2[pp][:, 0:wtot], func=AF.Exp, scale=0.125),
                     reads=["ps%d" % (2 * pp), "ps%d" % (2 * pp + 1)], writes=["Pb%d" % pj])
            else:
                kt, kr, qs, nn, coff = u["geo"][0]
                S.op("act", lambda e: e.activation(out=Pb[pj][0:kr, 0:nn], in_=PS2[pp][0:kr, 0:nn],
                                                   func=AF.Exp, scale=0.125),
                     reads=["ps%d" % (2 * pp), "ps%d" % (2 * pp + 1)], writes=["Pb%d" % pj])

        def emitB(u):
            h, q0, qn = u["h"], u["q0"], u["qn"]
            pj = u["pj"]
            nqb = (qn + 127) // 128
            O, oc0 = ostate[(h, q0)]
            nk = len(u["geo"])
            for hf, (kt, kr, qs, nn, coff) in enumerate(u["geo"]):
                for qb in range(nqb):
                    qcol = q0 + qb * 128
                    qr = min(128, q0 + qn - qcol)
                    if qcol + qr - 1 < kt * 128:
                        continue
                    S.op("pe", lambda e, qb=qb, qr=qr, qcol=qcol, coff=coff, kt=kt, kr=kr, qs=qs: e.matmul(
                        PS[O][0:qr, oc0 + 65 * qb:oc0 + 65 * qb + 65], lhsT=Pb[pj][0:kr, coff + qcol - qs:coff + qcol - qs + qr],
                        rhs=Vp[0:kr, kt, h, 0:65], start=False, stop=(kt == qcol // 128), skip_group_check=True),
                         reads=["Pb%d" % pj, "Vp%d" % kt, "Vp_ones"], writes=["ps%d" % O],
                         sig=(qb == nqb - 1 and hf == nk - 1))
            if u["last_g"]:
                for qb in range(nqb):
                    qcol = q0 + qb * 128
                    qr = min(128, q0 + qn - qcol)
                    S.op("dve", lambda e, qb=qb, qr=qr: e.reciprocal(out=rsum[0:qr, qb:qb + 1], in_=PS[O][0:qr, oc0 + 65 * qb + 64:oc0 + 65 * qb + 65]),
                         reads=["ps%d" % O], writes=["rsum%d" % qb])
                    S.op("dve", lambda e, qb=qb, qr=qr, qcol=qcol: e.tensor_scalar(
                        out=attn_tok[0:qr, qcol // 128, h * 64:(h + 1) * 64], in0=PS[O][0:qr, oc0 + 65 * qb:oc0 + 65 * qb + 64],
                        scalar1=rsum[0:qr, qb:qb + 1], scalar2=None, op0=ALU.mult),
                         reads=["ps%d" % O, "rsum%d" % qb], writes=["attn_tok%d" % (qcol // 128)])

        LA = 3
        for i in range(min(LA, len(units))):
            emitA(units[i])
        for i in range(len(units)):
            emitB(units[i])
            if i + LA < len(units):
                emitA(units[i + LA])

        chk(7)
        KC = 256
        kc_tm = [M.at(R3 + i * 2048, (P, 2, 512), BF16, "kc_tm") for i in range(2)]
        vc_tm = [M.at(R3 + 4096 + i * 2048, (P, 2, 512), BF16, "vc_tm") for i in range(2)]
        KcT = [M.at(R3 + 8192 + i * 2048, (P, 2, 4, 128), BF16, "KcT") for i in range(2)]
        Psb = [M.at(R3 + 12288 + i * 512, (P, 2, H, 16), BF16, "Psb") for i in range(2)]
        Vw = [M.at(AUG + 8320 + i * 2112, (P, 2, H, 66), BF16, "Vw") for i in range(2)]
        attn_s = sm((16, 512), BF16, "attn_s")
        rsum_s = sm((16, 8), F32, "rsum_s")
        wts = sm((P, NTILE, 8), F32, "wts")
        Qz = sm((P, H, 16), BF16, "Qz")
        S.op("act", lambda e: e.activation(out=wts[:, :, :].rearrange("p i h -> p (i h)"), in_=negc_s[:, :, :].rearrange("p i h -> p (i h)"), func=AF.Exp),
             reads=["c_s_neg"], writes=["wts"])
        S.op("dve", lambda e: e.memset(Qz[:, :, :], 0.0), writes=["Qz"])
        for h in range(H):
            hp, hoff = h // 2, (h % 2) * 64
            S.op("dve", lambda e, h=h, hp=hp, hoff=hoff: e.tensor_copy(out=Qz[hoff:hoff + 64, h, :], in_=qT[hoff:hoff + 64, hp, L:NT]),
                 reads=["qT_all"], writes=["Qz"])

        def load_cache_chunk(c):
            j = c % 2
            S.dma("pool", "d_kc%d" % j, lambda e, c=c, j=j: e.dma_start(
                out=kc_tm[j][:, :, :], in_=cache_k[KC * c:KC * (c + 1), :].rearrange("(i p) d -> p i d", p=P)),
                  writes=["kc_tm%d" % j])
            S.dma("pool", "d_vc%d" % j, lambda e, c=c, j=j: e.dma_start(
                out=vc_tm[j][:, :, :], in_=cache_v[KC * c:KC * (c + 1), :].rearrange("(i p) d -> p i d", p=P)),
                  writes=["vc_tm%d" % j])
        load_cache_chunk(0)
        load_cache_chunk(1)
        OS = (6, 7)
        for ob in OS:
            S.op("dve", lambda e, ob=ob: e.memset(PS[ob][0:16, 0:260], 0.0), writes=["ps%d" % ob])
        NCH = PAST // KC

        def sA(c):
            j = c % 2
            Tk = next_ps(0, 2, "tk")
            pstk = PS[Tk][:, :].bitcast(BF16)
            for t in range(2):
                for hp in range(4):
                    S.op("pe", lambda e, t=t, hp=hp: e.transpose(
                        out=pstk[:, (t * 4 + hp) * 128:(t * 4 + hp + 1) * 128], in_=kc_tm[j][:, t, hp * 128:(hp + 1) * 128],
                        identity=identb[:, :]),
                         reads=["kc_tm%d" % j, "identb"], writes=["ps%d" % Tk], sig=(t == 1 and hp == 3))
            S.op("act", lambda e: e.activation(out=KcT[j][:, :, :, :].rearrange("p t h k -> p (t h k)"), in_=pstk[:, :], func=AF.Copy),
                 reads=["ps%d" % Tk], writes=["KcT%d" % j])
            for t in range(2):
                kt = 2 * c + t
                S.op("dve", lambda e, t=t, kt=kt: e.tensor_tensor(
                    out=Vw[j][:, t, :, 0:64], in0=vc_tm[j][:, t, :].rearrange("p (h d) -> p h d", h=H),
                    in1=wts[:, kt, :].unsqueeze(2).to_broadcast([P, H, 64]), op=ALU.mult),
                     reads=["vc_tm%d" % j, "wts"], writes=["Vw%d" % j])
                S.op("dve", lambda e, t=t, kt=kt: e.tensor_copy(out=Vw[j][:, t, :, 64], in_=wts[:, kt, :]),
                     reads=["wts"], writes=["Vw%d" % j])
            Sb = next_ps(2, 4, "sb")
            for t in range(2):
                for h in range(H):
                    hp = h // 2
                    col = (t * H + h) * 16
                    S.op("pe", lambda e, col=col, t=t, hp=hp, h=h: e.matmul(
                        PS[Sb][:, col:col + 16], lhsT=KcT[j][:, t, hp, :], rhs=Qz[:, h, :], start=True, stop=True),
                         reads=["KcT%d" % j, "Qz"], writes=["ps%d" % Sb], sig=(t == 1 and h == H - 1))
            S.op("act", lambda e: e.activation(out=Psb[j][:, :, :, :].rearrange("p t h q -> p (t h q)"), in_=PS[Sb][:, 0:256],
                                               func=AF.Exp, scale=0.125),
                 reads=["ps%d" % Sb], writes=["Psb%d" % j])

        def sB(c):
            j = c % 2
            for t in range(2):
                for h in range(H):
                    ob = OS[h // 4]
                    oc = (h % 4) * 65
                    S.op("pe", lambda e, ob=ob, oc=oc, t=t, h=h: e.matmul(
                        PS[ob][0:16, oc:oc + 65], lhsT=Psb[j][:, t, h, :], rhs=Vw[j][:, t, h, 0:65],
                        start=False, stop=False, skip_group_check=True),
                         reads=["Psb%d" % j, "Vw%d" % j], writes=["ps%d" % ob], sig=(t == 1 and h == H - 1))
            if c + 2 < NCH:
                load_cache_chunk(c + 2)
        sA(0)
        for c in range(NCH):
            if c + 1 < NCH:
                sA(c + 1)
            sB(c)
        Sb = next_ps(2, 4, "sb")
        Pn = sm((16, H, 16), BF16, "Pn")
        Vsw = sm((16, H, 66), BF16, "Vsw")
        for h in range(H):
            hp = h // 2
            S.op("pe", lambda e, h=h, hp=hp: e.matmul(
                PS[Sb][0:16, 16 * h:16 * h + 16], lhsT=kT[:, hp, L:NT], rhs=Qz[:, h, :], start=True, stop=False),
                 reads=["Qz"], writes=["ps%d" % Sb], sig=False)
            S.op("pe", lambda e, h=h: e.matmul(
                PS[Sb][0:16, 16 * h:16 * h + 16], lhsT=identb[0:16, 0:16], rhs=maskneg[0:16, 0:16], start=False, stop=True),
                 reads=["identb", "maskneg"], writes=["ps%d" % Sb], sig=(h == H - 1))
        S.op("act", lambda e: e.activation(out=Pn[:, :, :].rearrange("p h q -> p (h q)"), in_=PS[Sb][0:16, 0:128], func=AF.Exp, scale=0.125),
             reads=["ps%d" % Sb], writes=["Pn"])
        S.op("dve", lambda e: e.tensor_tensor(out=Vsw[:, :, 0:65], in0=Vsn[:, :, 0:65],
                                              in1=wts[0:16, 16, :].unsqueeze(2).to_broadcast([16, H, 65]), op=ALU.mult),
             reads=["Vsn", "Vsn_ones", "wts"], writes=["Vsw"])
        for h in range(H):
            ob = OS[h // 4]
            oc = (h % 4) * 65
            S.op("pe", lambda e, ob=ob, oc=oc, h=h: e.matmul(
                PS[ob][0:16, oc:oc + 65], lhsT=Pn[:, h, :], rhs=Vsw[:, h, 0:65], start=False, stop=True, skip_group_check=True),
                 reads=["Pn", "Vsw"], writes=["ps%d" % ob], sig=(h % 4 == 3))
        for h in range(H):
            ob = OS[h // 4]
            oc = (h % 4) * 65
            S.op("dve", lambda e, ob=ob, oc=oc, h=h: e.reciprocal(out=rsum_s[:, h:h + 1], in_=PS[ob][0:16, oc + 64:oc + 65]),
                 reads=["ps%d" % ob], writes=["rsum_s"])
            S.op("dve", lambda e, ob=ob, oc=oc, h=h: e.tensor_scalar(
                out=attn_s[:, h * 64:(h + 1) * 64], in0=PS[ob][0:16, oc:oc + 64], scalar1=rsum_s[:, h:h + 1], scalar2=None, op0=ALU.mult),
                 reads=["ps%d" % ob, "rsum_s"], writes=["attn_s"])

        for i in range(NTILE):
            r = 128 if i < 16 else 16
            Tb = next_ps(0, 4, "ta")
            pst = PS[Tb][:, :].bitcast(BF16)
            for hp in range(4):
                S.op("pe", lambda e, i=i, r=r, hp=hp, pst=pst: e.transpose(
                    out=pst[:, hp * 128:hp * 128 + r], in_=attn_tok[0:r, i, hp * 128:(hp + 1) * 128], identity=identb[0:r, 0:r]),
                     reads=["attn_tok%d" % i, "identb"], writes=["ps%d" % Tb], sig=(hp == 3))
            S.op("dve", lambda e, i=i, r=r, pst=pst: e.tensor_copy(
                out=attnT[:, :, 128 * i:128 * i + r], in_=pst[:, 0:512].rearrange("p (h t) -> p h t", h=4)[:, :, 0:r]),
                 reads=["ps%d" % Tb], writes=["attnT%d" % i])
        Tb = next_ps(0, 4, "ta")
        pst = PS[Tb][:, :].bitcast(BF16)
        for hp in range(4):
            S.op("pe", lambda e, hp=hp, pst=pst: e.transpose(
                out=pst[:, hp * 128:hp * 128 + 16], in_=attn_s[0:16, hp * 128:(hp + 1) * 128], identity=identb[0:16, 0:16]),
                 reads=["attn_s", "identb"], writes=["ps%d" % Tb], sig=(hp == 3))
        S.op("dve", lambda e, pst=pst: e.tensor_copy(
            out=attnT[:, :, L:NT], in_=pst[:, 0:512].rearrange("p (h t) -> p h t", h=4)[:, :, 0:16]),
             reads=["ps%d" % Tb], writes=["attnT17"])
        if "attnT" in dbg_out:
            S.dma("sp", "d_dbg", lambda e: e.dma_start(out=dbg_out["attnT"][:, :, :], in_=attnT[:, :, :]),
                  reads=["attnT%d" % i for i in range(18)])

        chk(8)
        S.barrier()
        ZW = 2088
        zbuf = [M.at(R3 + i * (ZW * 4), (P, ZW), F32, "zbuf") for i in range(2)]
        convT = M.at(R3 + 16896, (P, 4, NT), BF16, "convT")
        tmpC = [M.at(R45 + 16640 + i * 2048, (P, 512), F32, "tmpC") for i in range(2)]
        ytmp = [M.at(R45 + 20736 + i * 2048, (P, 512), F32, "ytmp") for i in range(2)]
        tmpB = [M.at(AUG + 4160 + i * 2048, (P, 512), F32, "tmpB") for i in range(2)]
        stc_sb = sm((P, 8), F32, "stc_sb")
        zlast = sm((P, 4, 4), F32, "zlast")
        S.dma("sp", "d_stc", lambda e: e.dma_start(out=stc_sb[:, :], in_=stc[:, :]), writes=["stc_sb"])
        cc3 = {"n": 0}
        for c in range(4):
            wX, kX = w_get(G_CV[c])
            zb = zbuf[c % 2]
            zk = "zbuf%d" % (c % 2)
            S.op("dve", lambda e, zb=zb: e.memset(zb[:, 0:2], 0.0), writes=[zk])
            S.op("dve", lambda e, zb=zb, c=c: e.tensor_copy(out=zb[:, 2066:2068], in_=stc_sb[:, 2 * c:2 * c + 2]), reads=["stc_sb"], writes=[zk])
            for (c0, n) in BLKS_EQ:
                jj = cc3["n"] % 2
                cc3["n"] += 1
                banks = []
                for j3 in range(3):
                    A = next_ps()
                    banks.append(A)
                    for k in range(KD):
                        S.op("pe", lambda e, A=A, k=k, j3=j3, c0=c0, n=n, wX=wX: e.matmul(
                            PS[A][:, 0:n], lhsT=wX[:, k, 128 * j3:128 * (j3 + 1)], rhs=xnT[:, k, c0:c0 + n],
                            start=(k == 0), stop=(k == KD - 1)), reads=[kX], writes=["ps%d" % A], sig=(k == KD - 1))
                bB, bC, bH = banks
                S.op("act", lambda e, bC=bC, jj=jj, n=n: e.activation(out=tmpC[jj][:, 0:n], in_=PS[bC][:, 0:n], func=AF.Copy),
                     reads=["ps%d" % bC], writes=["tmpC%d" % jj])
                S.op("act", lambda e, bB=bB, jj=jj, n=n: e.activation(out=tmpB[jj][:, 0:n], in_=PS[bB][:, 0:n], func=AF.Copy),
                     reads=["ps%d" % bB], writes=["tmpB%d" % jj])
                segs = []
                if c0 < L:
                    segs.append((0, min(c0 + n, L) - c0, c0 + 2))
                if c0 + n > L:
                    s0 = max(c0, L)
                    segs.append((s0 - c0, c0 + n - s0, s0 + 4))
                for (so, sn, zc) in segs:
                    S.op("dve", lambda e, bH=bH, jj=jj, so=so, sn=sn, zc=zc, zb=zb: e.tensor_tensor(
                        out=zb[:, zc:zc + sn], in0=tmpC[jj][:, so:so + sn], in1=PS[bH][:, so:so + sn], op=ALU.mult),
                         reads=["tmpC%d" % jj, "ps%d" % bH], writes=[zk])
                for (so, sn, zc) in segs:
                    S.op("dve", lambda e, jj=jj, so=so, sn=sn, zc=zc, zb=zb, c=c: e.tensor_scalar(
                        out=ytmp[jj][:, so:so + sn], in0=zb[:, zc:zc + sn], scalar1=pk[:, 24 + 3 * c + 2:24 + 3 * c + 3],
                        scalar2=pk[:, 36 + c:37 + c], op0=ALU.mult, op1=ALU.add),
                         reads=[zk, "pk"], writes=["ytmp%d" % jj])
                    for tap, sh in ((1, 1), (0, 2)):
                        S.op("dve", lambda e, jj=jj, so=so, sn=sn, zc=zc, zb=zb, c=c, tap=tap, sh=sh: e.scalar_tensor_tensor(
                            out=ytmp[jj][:, so:so + sn], in0=zb[:, zc - sh:zc - sh + sn], scalar=pk[:, 24 + 3 * c + tap:24 + 3 * c + tap + 1],
                            in1=ytmp[jj][:, so:so + sn], op0=ALU.mult, op1=ALU.add),
                             reads=[zk, "pk", "ytmp%d" % jj], writes=["ytmp%d" % jj])
                S.op("dve", lambda e, jj=jj, n=n, c=c, c0=c0: e.tensor_tensor(
                    out=convT[:, c, c0:c0 + n], in0=tmpB[jj][:, 0:n], in1=ytmp[jj][:, 0:n], op=ALU.mult),
                     reads=["tmpB%d" % jj, "ytmp%d" % jj], writes=["convT"])
            S.op("dve", lambda e, zb=zb, c=c: e.tensor_copy(out=zlast[:, c, 0:2], in_=zb[:, 2064:2066]), reads=[zk], writes=["zlast"])
            S.op("dve", lambda e, zb=zb, c=c: e.tensor_copy(out=zlast[:, c, 2:4], in_=zb[:, 2082:2084]), reads=[zk], writes=["zlast"])
            w_done(G_CV[c])
        with nc.allow_non_contiguous_dma(reason="tiny transposed conv-state rows"):
            for c in range(4):
                S.dma("sp", "d_ncp", lambda e, c=c: e.dma_start(
                    out=nc_p[:, 128 * c:128 * (c + 1)].rearrange("r p -> p r"), in_=zlast[:, c, 0:2], allow_slow_non_contiguous=True),
                      reads=["zlast"], writes=["nc_p%d" % c])
                S.dma("sp", "d_ncs", lambda e, c=c: e.dma_start(
                    out=nc_s[:, 128 * c:128 * (c + 1)].rearrange("r p -> p r"), in_=zlast[:, c, 2:4], allow_slow_non_contiguous=True),
                      reads=["zlast"], writes=["nc_s%d" % c])

        chk(9)
        mergedT = M.at(R2, (P, KD, NT), BF16, "mergedT")
        gt = [[M.at(R45 + 24832 + (i * 4 + q_) * 2048, (P, 512), F32, "gt") for q_ in range(4)] for i in range(2)]
        g4 = {"n": 0}
        wbrc, kbrc = w_get(G_BRC)
        wbra, kbra = w_get(G_BRA)
        for pr in range(4):
            wgp, kgp = w_get(G_GP[pr])
            for jj2 in range(2):
                j = 2 * pr + jj2
                for (c0, n) in BLKS_EQ:
                    si = g4["n"] % 2
                    g4["n"] += 1
                    sA, sB, t1, t2 = gt[si]
                    ba = next_ps(); bb = next_ps(); bc = next_ps(); bd = next_ps()
                    for (bank, wv, wk, src, nk, col0) in ((ba, wgp, kgp, xnT, KD, jj2 * 128), (bb, wgp, kgp, xnT, KD, 256 + jj2 * 128),
                                                        (bc, wbrc, kbrc, convT, 4, j * 128), (bd, wbra, kbra, attnT, 4, j * 128)):
                        for k in range(nk):
                            S.op("pe", lambda e, bank=bank, wv=wv, src=src, k=k, nk=nk, col0=col0, c0=c0, n=n: e.matmul(
                                PS[bank][:, 0:n], lhsT=wv[:, k, col0:col0 + 128], rhs=src[:, k, c0:c0 + n],
                                start=(k == 0), stop=(k == nk - 1)),
                                 reads=[wk, "convT"] + ["attnT%d" % i for i in range(18)], writes=["ps%d" % bank], sig=(k == nk - 1))
                    S.op("act", lambda e, ba=ba, sA=sA, n=n: e.activation(out=sA[:, 0:n], in_=PS[ba][:, 0:n], func=AF.Sigmoid),
                         reads=["ps%d" % ba], writes=["gtA%d" % si])
                    S.op("act", lambda e, bb=bb, sB=sB, n=n: e.activation(out=sB[:, 0:n], in_=PS[bb][:, 0:n], func=AF.Sigmoid),
                         reads=["ps%d" % bb], writes=["gtB%d" % si])
                    S.op("dve", lambda e, bc=bc, sA=sA, t1=t1, n=n: e.tensor_tensor(out=t1[:, 0:n], in0=sA[:, 0:n], in1=PS[bc][:, 0:n], op=ALU.mult),
                         reads=["ps%d" % bc, "gtA%d" % si], writes=["gt1%d" % si])
                    S.op("dve", lambda e, bd=bd, sB=sB, t2=t2, n=n: e.tensor_tensor(out=t2[:, 0:n], in0=sB[:, 0:n], in1=PS[bd][:, 0:n], op=ALU.mult),
                         reads=["ps%d" % bd, "gtB%d" % si], writes=["gt2%d" % si])
                    S.op("dve", lambda e, t1=t1, t2=t2, j=j, c0=c0, n=n: e.tensor_tensor(out=mergedT[:, j, c0:c0 + n], in0=t1[:, 0:n], in1=t2[:, 0:n], op=ALU.add),
                         reads=["gt1%d" % si, "gt2%d" % si], writes=["mergedT"])
            w_done(G_GP[pr])
        w_done(G_BRC); w_done(G_BRA)

        chk(10)
        S.barrier()
        yacc = M.at(R1, (P, 16, D), F32, "yacc")
        yacc16 = M.at(R45 + 33280, (P, D), F32, "yacc16")
        hnT = M.at(R45, (P, KD, NT), BF16, "hnT")
        hsb = [M.at(AUG + i * 2048, (P, D), BF16, "hs") for i in range(3)]
        junk2 = M.at(R45 + 41472, (P, D), BF16, "junk2")
        wo0, ko0 = w_get(G_OUT0)
        wo1, ko1 = w_get(G_OUT0 + 1)

        def ytile(i):
            return yacc[:, i, :] if i < 16 else yacc16[:, :]

        def s5A(i):
            r = tile_rows(i)
            c0 = 128 * i
            b = i % 3
            yt = ytile(i)
            for half, (wo, ko) in enumerate(((wo0, ko0), (wo1, ko1))):
                A = next_ps(0, 6, "w5")
                for k in range(KD):
                    S.op("pe", lambda e, A=A, k=k, wo=wo: e.matmul(
                        PS[A][0:r, :], lhsT=mergedT[:, k, c0:c0 + r], rhs=wo[:, k, :], start=(k == 0), stop=(k == KD - 1)),
                         reads=[ko, "mergedT"], writes=["ps%d" % A], sig=(k == KD - 1))
                S.op("dve", lambda e, A=A, half=half: e.tensor_tensor(
                    out=yt[0:r, half * 512:(half + 1) * 512], in0=yt[0:r, half * 512:(half + 1) * 512], in1=PS[A][0:r, :], op=ALU.add),
                     reads=["ps%d" % A, "yacc%d" % i], writes=["yacc%d" % i])
            rms_rstd(yt[0:r, :], r, i, 1.0 / D, "yacc%d" % i, "b", junk=junk2, jkey="junk2")

        def s5A2(i):
            r = tile_rows(i)
            b = i % 3
            yt = ytile(i)
            S.op("dve", lambda e: e.tensor_scalar(out=hsb[b][0:r, :], in0=yt[0:r, :], scalar1=rstd[0:r, i:i + 1],
                                                  scalar2=None, op0=ALU.mult),
                 reads=["yacc%d" % i, "rstdb%d" % i], writes=["hs%d" % b])

        def s5B(i):
            r = tile_rows(i)
            c0 = 128 * i
            b = i % 3
            Tb = next_ps(6, 8, "t5")
            pst = PS[Tb][:, :].bitcast(BF16)
            for k in range(KD):
                S.op("pe", lambda e, k=k: e.transpose(out=pst[:, k * 128:k * 128 + r], in_=hsb[b][0:r, k * 128:(k + 1) * 128],
                                                     identity=identb[0:r, 0:r]),
                     reads=["hs%d" % b, "identb"], writes=["ps%d" % Tb], sig=(k == KD - 1))
            S.op("dve", lambda e: e.tensor_tensor(
                out=hnT[:, :, c0:c0 + r], in0=pst.rearrange("p (k t) -> p k t", k=KD)[:, :, 0:r],
                in1=pk[:, 8:16].unsqueeze(2).to_broadcast([P, KD, r]), op=ALU.mult),
                 reads=["ps%d" % Tb, "pk"], writes=["hnT"])
        for i in range(NTILE):
            load_xtile(i, ytile(i), "yacc%d" % i, "d_xr%d" % i)
        s5A(0)
        s5A(1)
        s5A2(0)
        for i in range(NTILE):
            if i + 2 < NTILE:
                s5A(i + 2)
            if i + 1 < NTILE:
                s5A2(i + 1)
            s5B(i)
        w_done(G_OUT0); w_done(G_OUT0 + 1)

        chk(11)
        S.barrier()
        aTb = [M.at(R2 + i * 16640, (P, 4, NT), BF16, "aT") for i in range(2)]
        rtmp = [M.at(R45 + 37376 + i * 2048, (P, 512), F32, "rtmp") for i in range(2)]
        r6 = {"n": 0}
        for g in range(8):
            wu, ku = w_get(G_MLP + 2 * g)
            wd, kd = w_get(G_MLP + 2 * g + 1)
            aT = aTb[g % 2]
            ak = "aT%d" % (g % 2)
            for fc in range(4):
                for (c0, n) in BLKS_EQ:
                    A = next_ps()
                    for k in range(KD):
                        S.op("pe", lambda e, A=A, k=k, fc=fc, c0=c0, n=n, wu=wu: e.matmul(
                            PS[A][:, 0:n], lhsT=wu[:, k, fc * 128:(fc + 1) * 128], rhs=hnT[:, k, c0:c0 + n],
                            start=(k == 0), stop=(k == KD - 1)), reads=[ku, "hnT"], writes=["ps%d" % A], sig=(k == KD - 1))
                    rj = r6["n"] % 2
                    r6["n"] += 1
                    S.op("act", lambda e, A=A, n=n, rj=rj: e.activation(out=rtmp[rj][:, 0:n], in_=PS[A][:, 0:n], func=AF.Relu),
                         reads=["ps%d" % A], writes=["rtmp%d" % rj])
                    S.op("dve", lambda e, fc=fc, c0=c0, n=n, aT=aT, rj=rj: e.tensor_tensor(
                        out=aT[:, fc, c0:c0 + n], in0=rtmp[rj][:, 0:n], in1=rtmp[rj][:, 0:n], op=ALU.mult),
                         reads=["rtmp%d" % rj], writes=[ak])
            for i in range(NTILE):
                r = tile_rows(i)
                c0 = 128 * i
                yt = ytile(i)
                for half in range(2):
                    A = next_ps()
                    for fc in range(4):
                        S.op("pe", lambda e, A=A, fc=fc, r=r, c0=c0, half=half, aT=aT, wd=wd: e.matmul(
                            PS[A][0:r, :], lhsT=aT[:, fc, c0:c0 + r], rhs=wd[:, fc, half * 512:(half + 1) * 512],
                            start=(fc == 0), stop=(fc == 3)), reads=[kd, ak], writes=["ps%d" % A], sig=(fc == 3))
                    S.op("dve", lambda e, A=A, r=r, yt=yt, half=half: e.tensor_tensor(
                        out=yt[0:r, half * 512:(half + 1) * 512], in0=yt[0:r, half * 512:(half + 1) * 512], in1=PS[A][0:r, :], op=ALU.add),
                         reads=["ps%d" % A, "yacc%d" % i], writes=["yacc%d" % i])
                if g == 7:
                    if i == 0:
                        S.dma("sp", "d_y", lambda e: e.dma_start(out=y_prompt[0:112, :], in_=yacc[16:128, 0, :]), reads=["yacc0"], writes=["y_prompt0"])
                    elif i < 16:
                        S.dma("sp", "d_y", lambda e, i=i: e.dma_start(out=y_prompt[128 * i - 16:128 * i + 112, :], in_=yacc[:, i, :]),
                              reads=["yacc%d" % i], writes=["y_prompt%d" % i])
                    else:
                        S.dma("sp", "d_y", lambda e: e.dma_start(out=y_prompt[2032:2048, :], in_=yacc16[0:16, :]), reads=["yacc16"], writes=["y_prompt16"])
                        S.dma("sp", "d_y", lambda e: e.dma_start(out=y_sample[:, :], in_=yacc16[16:32, :]), reads=["yacc16"], writes=["y_sample"])
            w_done(G_MLP + 2 * g); w_done(G_MLP + 2 * g + 1)

        if "attn_tok" in dbg_out:
            S.dma("sp", "d_dbg", lambda e: e.dma_start(out=dbg_out["attn_tok"][:, :, :], in_=attn_tok[:, :, :]),
                  reads=["attn_tok%d" % i for i in range(NTILE)])
        if "cT24" in dbg_out:
            S.dma("sp", "d_dbg", lambda e: e.dma_start(out=dbg_out["cT24"][:, :], in_=cT24[:, :]), reads=["c_all_T"])
        if "negc" in dbg_out:
            S.dma("sp", "d_dbg", lambda e: e.dma_start(out=dbg_out["negc"][:, :, :], in_=negc[:, :, :]), reads=["c_all_neg"])
        if "qT" in dbg_out:
            S.dma("sp", "d_dbg", lambda e: e.dma_start(out=dbg_out["qT"][:, :, :], in_=qT[:, :, :]), reads=["qT_all"])
        if "c_all" in dbg_out:
            S.dma("sp", "d_dbg", lambda e: e.dma_start(out=dbg_out["c_all"][:, :, :], in_=c_all[:, :, :]), reads=["c_all"])


    except _Stop:
        pass

    S.finish("sp")
    S.emit()
    return nc


def make_in_maps(inputs):
    f = lambda a: np.ascontiguousarray(np.asarray(a, dtype=np.float32))
    x_prompt = f(inputs["x_prompt"]); x_sample = f(inputs["x_sample"])
    ck = f(inputs["cache_k"])[0]; cv = f(inputs["cache_v"])[0]; cl = f(inputs["cache_logf"])[0]
    sc = f(inputs["state_conv"])[0]
    pk = np.zeros((P, 64), np.float32)
    pk[:, 0:8] = f(inputs["norm1_g"])[0].reshape(8, P).T
    pk[:, 8:16] = f(inputs["norm2_g"])[0].reshape(8, P).T
    pk[:, 16:24] = np.broadcast_to(f(inputs["b_f"])[0][None, :], (P, 8))
    cw = f(inputs["conv_w"])[0]
    pk[:, 24:36] = cw.reshape(3, 4, P).transpose(2, 1, 0).reshape(P, 12)
    pk[:, 36:40] = f(inputs["conv_b"])[0].reshape(4, P).T
    pk[:, 40] = np.tile(f(inputs["q_norm_g"])[0], 2)
    pk[:, 41] = np.tile(f(inputs["k_norm_g"])[0], 2)
    maps = []
    for b in range(8):
        stt = np.ascontiguousarray(sc[b].reshape(2, 4, P).transpose(2, 1, 0).reshape(P, 8))
        maps.append({
            "x_prompt": x_prompt[b], "x_sample": x_sample[b],
            "cache_k": ck[b].reshape(PAST, 512), "cache_v": cv[b].reshape(PAST, 512),
            "cache_logf": cl[b], "meta": f(inputs["meta"]),
            "w_in": f(inputs["w_in"])[0], "w_br_conv": f(inputs["w_br_conv"])[0],
            "w_br_attn": f(inputs["w_br_attn"])[0], "w_out": f(inputs["w_out"])[0],
            "w_up": f(inputs["w_up"])[0], "w_down": f(inputs["w_down"])[0],
            "ppk": pk, "state_conv_t": stt,
        })
    return maps


_NC_CACHE = {}


def kernel(**inputs):
    maps = make_in_maps(inputs)
    if "nc" not in _NC_CACHE:
        _NC_CACHE["nc"] = build()
    nc = _NC_CACHE["nc"]
    res = run_bass_kernel_spmd(nc, maps, core_ids=list(range(8)))
    R = res.results
    st = lambda name: np.stack([np.asarray(R[b][name], dtype=np.float32) for b in range(8)])
    y_prompt = st("y_prompt")
    y_sample = st("y_sample")
    nk_p = st("nk_p").reshape(1, 8, L, H, HD)
    nv_p = st("nv_p").reshape(1, 8, L, H, HD)
    nf_p = st("nf_p").reshape(1, 8, L, H)
    nc_p = st("nc_p").reshape(1, 8, 2, DC)
    nk_s = st("nk_s").reshape(1, 8, NS, H, HD)
    nv_s = st("nv_s").reshape(1, 8, NS, H, HD)
    nf_s = st("nf_s").reshape(1, 8, NS, H)
    nc_s = st("nc_s").reshape(1, 8, 2, DC)
    return (y_prompt, y_sample, nk_p, nv_p, nf_p, nc_p, nk_s, nv_s, nf_s, nc_s)
```
